# Optimizing a Trainium2 kernel written in Bass

```python
import math, functools
import jax, jax.numpy as jnp
from jax import lax
import numpy as np

D_MODEL = 1024
BATCH = 2
SEQ = 8192
DEPTH = 2
DEC_BATCH = 128
DEC_SEQ = 1
PAST_LEN = 8192
PAGE_SIZE = 128

BRANCH_W = D_MODEL // 2
HEAD_DIM = 64
N_HEADS = BRANCH_W // HEAD_DIM
KV_HEADS = 2
GQA_GROUP = N_HEADS // KV_HEADS
WINDOW = 128
ATTN_BLOCK = WINDOW
Q_W = N_HEADS * HEAD_DIM
KV_W = KV_HEADS * HEAD_DIM
SSM_GROUP_CH = 16
SSM_GROUPS = BRANCH_W // SSM_GROUP_CH
SSM_STATE = 64
CHUNK = 128
SGU_GROUPS = 4
N_BRANCH = 3
IN_SIZES = (Q_W, KV_W, KV_W, BRANCH_W, BRANCH_W, BRANCH_W, BRANCH_W, BRANCH_W, BRANCH_W, N_BRANCH * D_MODEL)
N_IN = sum(IN_SIZES)
ALPHA = (2 * DEPTH) ** 0.25
BETA = (8 * DEPTH) ** -0.25
LN_EPS = 1e-5
ATTN_SCALE = HEAD_DIM ** -0.5
NEG_INF = -1e30

kernel_name = 'hybrid_swa_s5_sgu_decoder_step'


def _layernorm(x, g, b):
    xf = x.astype(jnp.float32)
    xc = xf - xf.mean(-1, keepdims=True)
    var = (xc * xc).mean(-1, keepdims=True)
    y = xc * lax.rsqrt(var + LN_EPS) * g.astype(jnp.float32) + b.astype(jnp.float32)
    return y.astype(x.dtype)


def _alibi_slopes():
    h = jnp.arange(1, N_HEADS + 1, dtype=jnp.float32)
    return jnp.exp2(-8.0 * h / N_HEADS)


def _window_attention(q, k, v, qpos, kpos, sinks):
    nb, m, tq = q.shape[:3]
    qg = q.reshape(nb, m, tq, KV_HEADS, GQA_GROUP, HEAD_DIM)
    s = jnp.einsum('bmqkgd,bmskd->bmkgqs', qg, k, preferred_element_type=jnp.float32) * ATTN_SCALE
    dist = qpos[:, :, None] - kpos[:, None, :]
    valid = (dist >= 0) & (dist <= WINDOW) & (kpos[:, None, :] >= 0)
    slopes = _alibi_slopes().reshape(KV_HEADS, GQA_GROUP)
    bias = -slopes[None, None, :, :, None, None] * dist.astype(jnp.float32)[None, :, None, None]
    s = jnp.where(valid[None, :, None, None], s + bias, NEG_INF)
    sink = sinks.astype(jnp.float32).reshape(KV_HEADS, GQA_GROUP)[None, None, :, :, None, None]
    mx = jnp.maximum(s.max(-1, keepdims=True), sink)
    p = jnp.exp(s - mx)
    denom = p.sum(-1, keepdims=True) + jnp.exp(sink - mx)
    p = (p / denom).astype(v.dtype)
    o = jnp.einsum('bmkgqs,bmskd->bmqkgd', p, v)
    return o.reshape(nb, m, tq, Q_W)


def _attend_prompt(q, k, v, sinks):
    n, t = q.shape[:2]
    nb = t // ATTN_BLOCK
    qb = q.reshape(n, nb, ATTN_BLOCK, N_HEADS, HEAD_DIM)
    kb = k.reshape(n, nb, ATTN_BLOCK, KV_HEADS, HEAD_DIM)
    vb = v.reshape(n, nb, ATTN_BLOCK, KV_HEADS, HEAD_DIM)
    shift = ((0, 0), (1, 0), (0, 0), (0, 0), (0, 0))
    kcat = jnp.concatenate([jnp.pad(kb[:, :-1], shift), kb], axis=2)
    vcat = jnp.concatenate([jnp.pad(vb[:, :-1], shift), vb], axis=2)
    pos = jnp.arange(t, dtype=jnp.int32).reshape(nb, ATTN_BLOCK)
    kpos = jnp.concatenate([pos - ATTN_BLOCK, pos], axis=1)
    o = _window_attention(qb, kcat, vcat, pos, kpos, sinks).reshape(n, t, Q_W)
    keep = min(WINDOW, t)
    return o, k[:, t - keep:], v[:, t - keep:]


def _attend_sample(q, k, v, sinks, cache_k, cache_v):
    t = q.shape[1]
    w = cache_k.shape[1]
    kcat = jnp.concatenate([cache_k.astype(k.dtype), k], axis=1)[:, None]
    vcat = jnp.concatenate([cache_v.astype(v.dtype), v], axis=1)[:, None]
    qpos = PAST_LEN + jnp.arange(t, dtype=jnp.int32)
    kpos = jnp.concatenate([PAST_LEN - w + jnp.arange(w, dtype=jnp.int32), qpos])
    o = _window_attention(q[:, None], kcat, vcat, qpos[None], kpos[None], sinks)[:, 0]
    return o, k, v


def _complex_affine_combine(e1, e2):
    a1r, a1i, b1r, b1i = e1
    a2r, a2i, b2r, b2i = e2
    return (a2r * a1r - a2i * a1i,
            a2r * a1i + a2i * a1r,
            a2r * b1r - a2i * b1i + b2r,
            a2r * b1i + a2i * b1r + b2i)


def _s5(u, h0_re, h0_im, lam_re, lam_im, log_dt, b_re, b_im, c_re, c_im, d_skip, glu_w, glu_b):
    f32 = jnp.float32
    n, t, _ = u.shape
    uf = u.astype(f32)
    ug = uf.reshape(n, t, SSM_GROUPS, SSM_GROUP_CH)
    lr = lam_re.astype(f32)
    li = lam_im.astype(f32)
    dt = jnp.exp(log_dt.astype(f32))[:, None]
    mag = jnp.exp(lr * dt)
    ar = mag * jnp.cos(li * dt)
    ai = mag * jnp.sin(li * dt)
    den = lr * lr + li * li
    cr = ((ar - 1.0) * lr + ai * li) / den
    ci = (ai * lr - (ar - 1.0) * li) / den
    br = b_re.astype(f32)
    bi = b_im.astype(f32)
    bbr = cr[..., None] * br - ci[..., None] * bi
    bbi = cr[..., None] * bi + ci[..., None] * br
    xr = jnp.einsum('ntgh,gph->ntgp', ug, bbr)
    xi = jnp.einsum('ntgh,gph->ntgp', ug, bbi)
    h0r = h0_re.astype(f32)
    h0i = h0_im.astype(f32)
    xr = xr.at[:, 0].add(ar * h0r - ai * h0i)
    xi = xi.at[:, 0].add(ar * h0i + ai * h0r)
    a_r = jnp.broadcast_to(ar, xr.shape)
    a_i = jnp.broadcast_to(ai, xi.shape)
    _, _, hr, hi = lax.associative_scan(_complex_affine_combine, (a_r, a_i, xr, xi), axis=1)
    y = (jnp.einsum('ntgp,ghp->ntgh', hr, c_re.astype(f32))
         - jnp.einsum('ntgp,ghp->ntgh', hi, c_im.astype(f32)))
    y = y.reshape(n, t, BRANCH_W) + d_skip.astype(f32) * uf
    y = jax.nn.gelu(y)
    y = y * jax.nn.sigmoid(jnp.einsum('ntw,wv->ntv', y, glu_w.astype(f32)) + glu_b.astype(f32))
    return y.astype(u.dtype), hr[:, -1], hi[:, -1]


def _chunk_sgu(u, v, ln_g, ln_b, ws, bs):
    n, t, w = v.shape
    vn = _layernorm(v, ln_g, ln_b)
    pad = (-t) % CHUNK
    vp = jnp.pad(vn, ((0, 0), (0, pad), (0, 0)))
    nc = (t + pad) // CHUNK
    vc = vp.reshape(n, nc, CHUNK, SGU_GROUPS, w // SGU_GROUPS)
    tril = jnp.tril(jnp.ones((CHUNK, CHUNK), dtype=bool))
    wm = jnp.where(tril[None], ws, jnp.zeros_like(ws))
    s = jnp.einsum('gts,ncsgw->nctgw', wm, vc) + bs.T[None, None, :, :, None]
    s = s.reshape(n, nc * CHUNK, w)[:, :t]
    return u * s, vn


def _layer(x, attend, h0_re, h0_im, w_in, sinks, lam_re, lam_im, log_dt, b_re, b_im, c_re, c_im,
           d_skip, glu_w, glu_b, sgu_ln_g, sgu_ln_b, sgu_w, sgu_b, w_read, w_o, ln_g, ln_b):
    n, t, _ = x.shape
    proj = jnp.einsum('ntd,de->nte', x, w_in)
    cuts = [int(c) for c in np.cumsum(IN_SIZES)[:-1]]
    q, k, v, z_a, u_b, z_b, u_c, v_c, z_c, g = jnp.split(proj, cuts, axis=-1)
    q = q.reshape(n, t, N_HEADS, HEAD_DIM)
    k = k.reshape(n, t, KV_HEADS, HEAD_DIM)
    v = v.reshape(n, t, KV_HEADS, HEAD_DIM)
    y_a, k_rows, v_rows = attend(q, k, v, sinks)
    y_b, h_re, h_im = _s5(u_b, h0_re, h0_im, lam_re, lam_im, log_dt, b_re, b_im, c_re, c_im,
                          d_skip, glu_w, glu_b)
    y_c, vn_rows = _chunk_sgu(u_c, v_c, sgu_ln_g, sgu_ln_b, sgu_w, sgu_b)
    ys = jnp.stack([y_a * jax.nn.silu(z_a), y_b * jax.nn.silu(z_b), y_c * jax.nn.silu(z_c)], axis=2)
    branch = jnp.einsum('ntbw,bwd->ntbd', ys, w_read)
    gates = jax.nn.sigmoid(g.reshape(n, t, N_BRANCH, D_MODEL))
    merged = (gates * branch).sum(axis=2)
    out = jnp.einsum('ntd,de->nte', merged, w_o)
    x = _layernorm(ALPHA * x + out, ln_g, ln_b)
    return x, (k_rows, v_rows, h_re, h_im, vn_rows)


def setup_inputs(seed: int = 0) -> dict:
    key = jax.random.key(seed)
    ks = jax.random.split(key, 26)
    f32 = jnp.float32

    def nrm(k, shape, scale):
        return scale * jax.random.normal(k, shape, f32)

    win_buf = min(WINDOW, PAST_LEN)
    lam_im_base = jnp.pi * jnp.arange(SSM_STATE, dtype=f32)
    return {
        'x_prompt': nrm(ks[0], (BATCH, SEQ, D_MODEL), 1.0),
        'x_sample': nrm(ks[1], (DEC_BATCH, DEC_SEQ, D_MODEL), 1.0),
        'cache_k_win': nrm(ks[2], (DEPTH, DEC_BATCH, win_buf, KV_HEADS, HEAD_DIM), 1.0),
        'cache_v_win': nrm(ks[3], (DEPTH, DEC_BATCH, win_buf, KV_HEADS, HEAD_DIM), 1.0),
        'state_ssm_re': nrm(ks[4], (DEPTH, DEC_BATCH, SSM_GROUPS, SSM_STATE), 0.3),
        'state_ssm_im': nrm(ks[5], (DEPTH, DEC_BATCH, SSM_GROUPS, SSM_STATE), 0.3),
        'w_in': nrm(ks[6], (DEPTH, D_MODEL, N_IN), D_MODEL ** -0.5),
        'attn_sinks': nrm(ks[7], (DEPTH, N_HEADS), 0.5),
        'ssm_lambda_re': -0.5 + nrm(ks[8], (DEPTH, SSM_GROUPS, SSM_STATE), 0.01),
        'ssm_lambda_im': lam_im_base + nrm(ks[9], (DEPTH, SSM_GROUPS, SSM_STATE), 0.01),
        'ssm_log_dt': jax.random.uniform(ks[10], (DEPTH, SSM_GROUPS), f32,
                                         minval=math.log(1e-3), maxval=math.log(1e-1)),
        'ssm_b_re': nrm(ks[11], (DEPTH, SSM_GROUPS, SSM_STATE, SSM_GROUP_CH), (2 * SSM_GROUP_CH) ** -0.5),
        'ssm_b_im': nrm(ks[12], (DEPTH, SSM_GROUPS, SSM_STATE, SSM_GROUP_CH), (2 * SSM_GROUP_CH) ** -0.5),
        'ssm_c_re': nrm(ks[13], (DEPTH, SSM_GROUPS, SSM_GROUP_CH, SSM_STATE), SSM_STATE ** -0.5),
        'ssm_c_im': nrm(ks[14], (DEPTH, SSM_GROUPS, SSM_GROUP_CH, SSM_STATE), SSM_STATE ** -0.5),
        'ssm_d': nrm(ks[15], (DEPTH, BRANCH_W), 1.0),
        'glu_w': nrm(ks[16], (DEPTH, BRANCH_W, BRANCH_W), BRANCH_W ** -0.5),
        'glu_b': nrm(ks[17], (DEPTH, BRANCH_W), 0.02),
        'sgu_ln_g': 1.0 + nrm(ks[18], (DEPTH, BRANCH_W), 0.02),
        'sgu_ln_b': nrm(ks[19], (DEPTH, BRANCH_W), 0.02),
        'sgu_w': nrm(ks[20], (DEPTH, SGU_GROUPS, CHUNK, CHUNK), 0.5 * CHUNK ** -0.5),
        'sgu_b': 1.0 + nrm(ks[21], (DEPTH, SGU_GROUPS, CHUNK), 0.02),
        'w_read': nrm(ks[22], (DEPTH, N_BRANCH, BRANCH_W, D_MODEL), BRANCH_W ** -0.5),
        'w_o': nrm(ks[23], (DEPTH, D_MODEL, D_MODEL), BETA * D_MODEL ** -0.5),
        'ln_g': 1.0 + nrm(ks[24], (DEPTH, D_MODEL), 0.02),
        'ln_b': nrm(ks[25], (DEPTH, D_MODEL), 0.02),
    }


def reference(x_prompt, x_sample, cache_k_win, cache_v_win, state_ssm_re, state_ssm_im,
              w_in, attn_sinks, ssm_lambda_re, ssm_lambda_im, ssm_log_dt, ssm_b_re, ssm_b_im,
              ssm_c_re, ssm_c_im, ssm_d, glu_w, glu_b, sgu_ln_g, sgu_ln_b, sgu_w, sgu_b,
              w_read, w_o, ln_g, ln_b):
    xp = x_prompt
    xs = x_sample
    h0 = jnp.zeros((x_prompt.shape[0], SSM_GROUPS, SSM_STATE), jnp.float32)
    kp, vp, hrp, hip = [], [], [], []
    ksm, vsm, hrs, his, vcs = [], [], [], [], []
    for l in range(DEPTH):
        lw = (w_in[l], attn_sinks[l], ssm_lambda_re[l], ssm_lambda_im[l], ssm_log_dt[l],
              ssm_b_re[l], ssm_b_im[l], ssm_c_re[l], ssm_c_im[l], ssm_d[l], glu_w[l], glu_b[l],
              sgu_ln_g[l], sgu_ln_b[l], sgu_w[l], sgu_b[l], w_read[l], w_o[l], ln_g[l], ln_b[l])
        xp, (k_r, v_r, h_r, h_i, _) = _layer(xp, _attend_prompt, h0, h0, *lw)
        kp.append(k_r)
        vp.append(v_r)
        hrp.append(h_r)
        hip.append(h_i)
        attend_s = functools.partial(_attend_sample, cache_k=cache_k_win[l], cache_v=cache_v_win[l])
        xs, (k_r, v_r, h_r, h_i, vc) = _layer(xs, attend_s, state_ssm_re[l], state_ssm_im[l], *lw)
        ksm.append(k_r)
        vsm.append(v_r)
        hrs.append(h_r)
        his.append(h_i)
        vcs.append(vc)
    return (xp, xs, jnp.stack(kp), jnp.stack(vp), jnp.stack(hrp), jnp.stack(hip),
            jnp.stack(ksm), jnp.stack(vsm), jnp.stack(hrs), jnp.stack(his), jnp.stack(vcs))
```

```python
import math
from contextlib import ExitStack

import numpy as np
import concourse.bass as bass
import concourse.mybir as mybir
from concourse.bass_utils import run_bass_kernel_spmd

F32 = mybir.dt.float32
BF16 = mybir.dt.bfloat16
AF = mybir.ActivationFunctionType
ALU = mybir.AluOpType

D = 1024
SEQ = 8192
DEPTH = 2
NS = 16
BW = 512
TT = 256
NBK = TT // 128
NTILE = SEQ // TT
NCH = 24
ALPHA = (2 * DEPTH) ** 0.25
LN_EPS = 1e-5
GK = 0.044715
GS = 2.0 * math.sqrt(2.0 / math.pi)
NDS = 24
DBG_TILES = None
REST_STEPS = 2
DBG_NOKV = False
DBG_SKIPKV = False

C_Q, C_KD, C_KV, C_ZA, C_UB, C_ZB, C_UC, C_VC, C_ZC = range(9)
C_G = 9
C_R = 15
C_O = 21
C_GLU = 23


class Buf:
    __slots__ = ("name", "w", "r", "excl")

    def __init__(self, name, excl=False):
        self.name = name
        self.w = None
        self.r = {}
        self.excl = excl


class Sched:
    def __init__(self, nc, es):
        self.nc = nc
        self.engs = {"pe": nc.tensor, "dve": nc.vector, "act": nc.scalar, "pool": nc.gpsimd, "sp": nc.sync}
        self.sem = {k: es.enter_context(nc.semaphore("sem_" + k)) for k in ["pe", "dve", "act", "pool"]}
        self.cnt = {k: 0 for k in self.sem}
        self.seen = {e: {} for e in self.engs}
        self.dsem = [es.enter_context(nc.semaphore("dsem%d" % i)) for i in range(NDS)]
        self.dcnt = [0] * NDS
        self.drange = {"sp": (0, 16), "pool": (16, NDS)}
        self.drr = {"sp": 0, "pool": 16}
        self.pend = []
        self.pendset = set()
        self.ninst = 0

    def _wait(self, e, ev):
        if ev is None:
            return
        key, val = ev
        if key == e and e == "pe":
            return
        if self.seen[e].get(key, 0) >= val:
            return
        sem = self.sem[key] if isinstance(key, str) else self.dsem[key[1]]
        self.engs[e].wait_ge(sem, val)
        self.seen[e][key] = val

    def _deps(self, e, r, w):
        for b in r:
            assert id(b) not in self.pendset or e == "pe", b.name
            self._wait(e, b.w)
            if b.excl:
                for k, v in list(b.r.items()):
                    if k != e:
                        self._wait(e, (k, v))
        for b in w:
            assert id(b) not in self.pendset or e == "pe", b.name
            self._wait(e, b.w)
            for k, v in list(b.r.items()):
                self._wait(e, (k, v))

    def _commit(self, ev, r, w):
        for b in w:
            b.w = ev
            b.r = {}
        for b in r:
            if all(b is not x for x in w):
                b.r[ev[0]] = ev[1]

    def op(self, e, fn, r=(), w=(), sig=True):
        self._deps(e, r, w)
        inst = fn(self.engs[e])
        self.ninst += 1
        if e == "pe" and not sig:
            for b in r:
                self.pend.append((b, 0))
                self.pendset.add(id(b))
            for b in w:
                self.pend.append((b, 1))
                self.pendset.add(id(b))
            return inst
        self.cnt[e] += 1
        inst.then_inc(self.sem[e], 1)
        ev = (e, self.cnt[e])
        rr = list(r)
        ww = list(w)
        if e == "pe":
            for b, k in self.pend:
                (ww if k else rr).append(b)
            self.pend = []
            self.pendset = set()
        self._commit(ev, rr, ww)
        return inst

    def dma(self, q, out_ap, in_ap, r=(), w=()):
        lo, hi = self.drange[q]
        i = self.drr[q]
        self.drr[q] = lo + (i + 1 - lo) % (hi - lo)
        if self.dcnt[i] > 0:
            self._wait(q, (("d", i), self.dcnt[i]))
        self._deps(q, r, w)
        inst = self.engs[q].dma_start(out=out_ap, in_=in_ap)
        self.ninst += 1
        self.dcnt[i] += 16
        inst.then_inc(self.dsem[i], 16)
        self._commit((("d", i), self.dcnt[i]), list(r), list(w))
        return inst

    def finish(self, bufs):
        for b in bufs:
            self._wait("sp", b.w)


class T:
    def __init__(self, t, name):
        self.t = t
        self.b = Buf(name)

    def __getitem__(self, k):
        return self.t[k]


def build_program():
    rec = []
    _build(rec, True)
    return _build(rec, False)


def _build(order, recording):
    nc = bass.Bass("TRN2", target_bir_lowering=False)
    es = ExitStack()
    with es:
        def din(name, shape):
            return nc.dram_tensor(name, list(shape), F32, kind="ExternalInput").ap()

        def dout(name, shape):
            return nc.dram_tensor(name, list(shape), F32, kind="ExternalOutput").ap()

        xp = din("xp", [SEQ, D])
        xs = din("xs", [NS, D])
        wch = din("wch", [DEPTH, NCH, 128, 4096])
        ident_d = din("ident", [128, 128])
        swap_d = din("swapm", [128, 128])
        masks_d = din("masks", [2, 128, 512])
        ak_d = din("ak", [2, 256])
        aq_d = din("aq", [2, 1024])
        colp_d = din("colp", [DEPTH, 128, 3 * 32])
        rowp_d = din("rowp", [DEPTH, 3, 128, 2048])
        bpad_d = din("bpad", [DEPTH, 2, 32, 128, 64])
        cpad_d = din("cpad", [DEPTH, 32, 128, 128])
        rmask_d = din("rmask", [128, 8])
        dsk_d = din("dsk", [DEPTH, 128, 4])
        glub_d = din("glub", [DEPTH, 128, 4])
        sgn_d = din("sgn", [DEPTH, 2, 128, 512])
        sgw_d = din("sgw", [DEPTH, 4, 128, 128])
        triu_d = din("triu", [128, 128])
        sgb_d = din("sgb", [DEPTH, 128, 512])
        lng_d = din("lng", [DEPTH, 2, 128, 1024])
        snk_d = din("snk", [DEPTH, 128, 8])

        ckT_d = din("ckT", [DEPTH, 128, 32, 128])
        cvd_d = din("cvd", [DEPTH, 128, 32, 128])
        h0_d = din("h0", [DEPTH, 128, 32, NS])
        h0s_d = din("h0s", [DEPTH, 128, 32, NS])
        aqs_d = din("aqs", [2, 8])
        hsel_d = din("hsel", [128, 2])
        hselT_d = din("hselT", [2, 128])
        signc_d = din("signc", [128, 1])
        w00_d = din("w00", [DEPTH, NS, 4])
        b00_d = din("b00", [DEPTH, 128, 4])
        esks_d = din("esks", [DEPTH, 128, 4])
        ysm = dout("ysm", [NS, D])
        kvs = dout("kvs", [DEPTH, NS, 256])
        hss = dout("hss", [DEPTH, 128, 32, NS])
        vns = dout("vns", [DEPTH, NS, 512])
        xs1 = nc.dram_tensor("xs1", [NS, D], F32).ap()
        yp = dout("yp", [SEQ, D])
        kvw = dout("kvw", [DEPTH, 128, 256])
        hst_o = dout("hst", [DEPTH, 128, 32])

        x1 = nc.dram_tensor("x1", [SEQ, D], F32).ap()
        wbf = nc.dram_tensor("wbf", [DEPTH, NCH, 128, 4096], BF16).ap()

        S = Sched(nc, es)

        def sb(name, shape, dt=F32):
            return T(es.enter_context(nc.sbuf_tensor("s_" + name, list(shape), dt)), name)

        ident = sb("ident", [128, 128])
        swapm = sb("swapm", [128, 128])
        masks = sb("masks", [128, 2, 128], BF16)
        ak = sb("ak", [2, 256], BF16)
        aq = sb("aq", [2, 1024], BF16)
        ones = sb("ones", [128, 128], BF16)
        rmask = sb("rmask", [128, 8])
        xtok = [sb("xtok%d" % i, [128, D]) for i in range(2 * NBK)]
        xT = sb("xT", [128, 8, TT], BF16)
        wbuf = [sb("wbuf%d" % i, [128, 4096], BF16) for i in range(3)]
        qT = sb("qT", [128, 4, TT], BF16)
        kT = sb("kT", [128, 2, TT + 128], BF16)
        vtok = sb("vtok", [128, NBK + 1, 2, 128], BF16)
        sza = sb("sza", [128, 4, TT], BF16)
        szb = sb("szb", [128, 4, TT], BF16)
        uT = sb("uT", [128, 4, TT], BF16)
        ucz = sb("ucz", [128, 4, TT], BF16)
        vn = sb("vn", [128, 4, BW], BF16)
        ys = sb("ys", [128, 12, TT], BF16)
        ypre = sb("ypre", [128, 4, TT])
        gtmp = sb("gtmp", [128, 4, TT])
        gtmp2 = sb("gtmp2", [128, 4, TT], BF16)
        sig = sb("sig", [128, 4, TT], BF16)
        mtmp = [sb("mtmp%d" % i, [128, 512]) for i in range(2)]
        mrgT = sb("mrgT", [128, 8, TT], BF16)
        pT = [[sb("pT%d%d" % (kv, pt), [128, 512], BF16) for pt in range(2)] for kv in range(2)]
        rden = [sb("rden%d" % kv, [128, 512]) for kv in range(2)]
        otmp = [sb("otmp%d" % kv, [128, 512], BF16) for kv in range(2)]
        lnst = sb("lnst", [128, 12])
        lnmv = sb("lnmv", [128, 4])
        lntmp = sb("lntmp", [128, 512])
        lnout = [sb("lnout%d" % i, [128, D]) for i in range(2)]
        colp = sb("colp", [128, 96])
        cdt = sb("cdt", [128, 32])
        crr = sb("crr", [128, 32])
        cth = sb("cth", [128, 32])
        ctmp = sb("ctmp", [128, 32])
        cc1 = sb("cc1", [128, 32])
        ctmp2 = sb("ctmp2", [128, 32])
        citmp = sb("citmp", [128, 32], mybir.dt.int32)
        cs1 = sb("cs1", [128, 32])
        tabc = sb("tabc", [128, 32, TT], BF16)
        tabs = sb("tabs", [128, 32, TT], BF16)
        bbw = sb("bbw", [128, 32, 128], BF16)
        bbs = sb("bbs", [128, 32, 128], BF16)
        cpw = sb("cpw", [128, 32, 128], BF16)
        dsk = sb("dsk", [128, 4])
        glub = sb("glub", [128, 4])
        sgn = sb("sgn", [128, 2, 512])
        sgw = sb("sgw", [128, 4, 128], BF16)
        triu = sb("triu", [128, 128])
        sgb = sb("sgb", [128, 512])
        lng = sb("lng", [128, 2, 1024])
        esk = sb("esk", [128, 8])
        sigB = sb("sigB", [128, 8, TT], BF16)
        hst = sb("hst", [128, 32])
        s5a = [sb("s5a%d" % i, [128, TT]) for i in range(2)]
        s5b = [sb("s5b%d" % i, [128, TT]) for i in range(2)]
        s5v = [sb("s5v%d" % i, [128, TT]) for i in range(2)]
        s5g = [sb("s5g%d" % i, [128, TT]) for i in range(2)]
        s5c = [sb("s5c%d" % i, [128, TT]) for i in range(3)]
        s5d = [sb("s5d%d" % i, [128, TT]) for i in range(2)]
        s5hb = [sb("s5hb%d" % i, [128, TT], BF16) for i in range(2)]

        aqs = sb("aqs", [2, 8], BF16)
        hsel = sb("hsel", [128, 2], BF16)
        hselT = sb("hselT", [2, 128], BF16)
        signc = sb("signc", [128, 1])
        w00 = sb("w00", [NS, 4])
        w00I = sb("w00I", [NS, 4, NS], BF16)
        b00 = sb("b00", [128, 4])
        esks = sb("esks", [128, 4])
        car = sb("car", [128, 32])
        cais = sb("cais", [128, 32])
        ksd = sb("ksd", [128, 2, NS], BF16)
        vsd = sb("vsd", [128, 2, NS])
        prod = sb("prod", [128, 4, NS], BF16)
        pself = sb("pself", [2, 64], BF16)
        pbs = sb("pbs", [128, 4, NS])
        st1 = sb("st1", [128, NS])
        st2 = sb("st2", [128, NS])

        class AV:
            def __init__(self, ap, b):
                self.ap = ap
                self.b = b

            def __getitem__(self, k):
                return self.ap[k]

        ck = AV(tabc[:, :, 0:128], tabc.b)
        kvout = AV(lnout[1][:, 0:256], lnout[1].b)
        kvs_sb = AV(rden[0][0:NS, 0:256], rden[0].b)
        vnS = AV(otmp[0][0:NS, :], otmp[0].b)
        cv = AV(tabs[:, :, 0:128], tabs.b)
        h0 = AV(mtmp[0][:, 0:32 * NS].rearrange("p (g n) -> p g n", n=NS), mtmp[0].b)
        h0s = AV(mtmp[1][:, 0:32 * NS].rearrange("p (g n) -> p g n", n=NS), mtmp[1].b)
        hbs = AV(pT[0][0][:, 0:32 * NS].rearrange("p (g n) -> p g n", n=NS), pT[0][0].b)
        pSb = AV(pT[0][1][:, 0:128], pT[0][1].b)
        sdn = AV(rden[0][:, 0:128], rden[0].b)
        sso = AV(rden[1][:, 0:128], rden[1].b)

        psum = [T(es.enter_context(nc.psum_tensor("ps%d" % i, [128, 512], F32)), "ps%d" % i) for i in range(8)]
        for p_ in psum:
            p_.b.excl = True
        pools = {"A": [0, 1], "B": [2, 3], "C": [4], "Y": [5], "D": [6, 7]}
        prr = {k: 0 for k in pools}

        def ps_alloc(pool):
            i = prr[pool]
            prr[pool] = (i + 1) % len(pools[pool])
            return psum[pools[pool][i]]

        rr = {}

        def rot(lst, key):
            i = rr.get(key, 0)
            rr[key] = (i + 1) % len(lst)
            return lst[i]

        wbf_b = [[Buf("wbf%d_%d" % (l, c)) for c in range(NCH)] for l in range(DEPTH)]
        for l in range(DEPTH):
            for c in range(NCH):
                S.dma("pool", wbf[l, c], wch[l, c], w=[wbf_b[l][c]])

        S.dma("sp", ident[:], ident_d, w=[ident.b])
        S.dma("sp", swapm[:], swap_d, w=[swapm.b])
        S.dma("pool", masks[:], masks_d.rearrange("a p c -> p a c")[:, :, 0:128], w=[masks.b])
        S.dma("pool", ak[:], ak_d, w=[ak.b])
        S.dma("pool", aq[:], aq_d, w=[aq.b])
        S.dma("sp", rmask[:], rmask_d, w=[rmask.b])
        S.dma("sp", triu[:], triu_d, w=[triu.b])
        S.op("dve", lambda e: e.memset(ones[:], 1.0), w=[ones.b])
        S.dma("pool", aqs[:], aqs_d, w=[aqs.b])
        S.dma("pool", hsel[:], hsel_d, w=[hsel.b])
        S.dma("pool", hselT[:], hselT_d, w=[hselT.b])
        S.dma("sp", signc[:], signc_d, w=[signc.b])

        def nk_of(c):
            return 4 if (c == C_GLU or C_R <= c < C_R + 6) else 8

        wst = {"issued": 0, "used": 0, "hold": False}

        def w_issue():
            i = wst["issued"]
            if i >= len(order):
                return
            l, c, nk = order[i]
            wb = wbuf[i % 3]
            S.dma("sp", wb[:, 0:nk * 512], wbf[l, c, :, 0:nk * 512], r=[wbf_b[l][c]], w=[wb.b])
            wst["issued"] = i + 1

        def w_next(l, c):
            i = wst["used"]
            if recording:
                order.append((l, c, nk_of(c)))
            assert order[i][0] == l and order[i][1] == c, (order[i], l, c)
            while wst["issued"] < min(i + (1 if recording else 2), len(order)):
                w_issue()
            wst["used"] = i + 1
            wst["hold"] = True
            return wbuf[i % 3]

        def w_done():
            wst["hold"] = False
            if recording:
                return
            while wst["issued"] < min(wst["used"] + 2, len(order)):
                w_issue()

        def evac(eng, fn, r, w):
            S.op(eng, fn, r=r, w=w)

        def mm_fm(ps, wb, j, nk, rhs_fn, rbufs, n):
            for kb in range(nk):
                S.op("pe", lambda e, kb=kb: e.matmul(ps[:, 0:n], lhsT=wb[:, kb * 512 + j * 128: kb * 512 + (j + 1) * 128],
                                                     rhs=rhs_fn(kb), start=(kb == 0), stop=(kb == nk - 1)),
                     r=[wb.b] + rbufs, w=[ps.b], sig=(kb == nk - 1))

        def v2(t, width):
            ap = t[:]
            if len(ap.shape) == 3:
                ap = ap.rearrange("p a b -> p (a b)")
            return ap[:, 0:width]

        def setup_layer(l):
            S.dma("sp", colp[:], colp_d[l], w=[colp.b])
            S.dma("pool", cpw[:], cpad_d[l].rearrange("g p m -> p g m"), w=[cpw.b])
            S.dma("sp", dsk[:], dsk_d[l], w=[dsk.b])
            S.dma("sp", glub[:], glub_d[l], w=[glub.b])
            S.dma("sp", sgn[:], sgn_d[l].rearrange("a p c -> p a c"), w=[sgn.b])
            sgw32 = v2(ypre, 512).rearrange("p (g t) -> p g t", g=4)
            S.dma("sp", sgw32, sgw_d[l].rearrange("g s t -> s g t"), w=[ypre.b])
            S.dma("sp", sgb[:], sgb_d[l], w=[sgb.b])
            S.dma("sp", lng[:], lng_d[l].rearrange("a p c -> p a c"), w=[lng.b])
            S.dma("sp", esk[:], snk_d[l], w=[esk.b])
            for g4 in range(4):
                S.op("dve", lambda e, g4=g4: e.tensor_tensor(out=sgw[:, g4, :], in0=sgw32[:, g4, :], in1=triu[:], op=ALU.mult),
                     r=[ypre.b, triu.b], w=[sgw.b])
            S.op("act", lambda e: e.activation(out=esk[:], in_=esk[:], func=AF.Exp), r=[esk.b], w=[esk.b])
            S.op("dve", lambda e: e.tensor_scalar(out=cpw[64:128], in0=cpw[64:128], scalar1=-1.0, scalar2=None, op0=ALU.mult),
                 r=[cpw.b], w=[cpw.b])
            S.op("dve", lambda e: e.memset(hst[:], 0.0), w=[hst.b] + hstb)

            lr_c, li_c, ld_c = colp[:, 0:32], colp[:, 32:64], colp[:, 64:96]
            S.op("act", lambda e: e.activation(out=cdt[:], in_=ld_c, func=AF.Exp), r=[colp.b], w=[cdt.b])
            S.op("dve", lambda e: e.tensor_tensor(out=ctmp[:], in0=lr_c, in1=cdt[:], op=ALU.mult), r=[colp.b, cdt.b], w=[ctmp.b])
            S.op("act", lambda e: e.activation(out=crr[:], in_=ctmp[:], func=AF.Exp), r=[ctmp.b], w=[crr.b])
            S.op("dve", lambda e: e.tensor_tensor(out=cth[:], in0=li_c, in1=cdt[:], op=ALU.mult), r=[colp.b, cdt.b], w=[cth.b])

            def sincos(out_s, out_c, th, thb, tmp, tb, m, mb, itmp, ib, sb_, cb):
                I32 = mybir.dt.int32
                for dst, db, shift in ((out_s, sb_, 0.0), (out_c, cb, 0.5 * math.pi)):
                    S.op("dve", lambda e, shift=shift: e.tensor_scalar(out=tmp, in0=th, scalar1=shift, scalar2=1.0 / (2 * math.pi),
                                                                       op0=ALU.add, op1=ALU.mult), r=[thb], w=[tb])
                    S.op("dve", lambda e: e.tensor_copy(out=itmp, in_=tmp), r=[tb], w=[ib])
                    S.op("dve", lambda e: e.tensor_copy(out=tmp, in_=itmp), r=[ib], w=[tb])
                    S.op("dve", lambda e: e.scalar_tensor_tensor(out=tmp, in0=tmp, scalar=-2 * math.pi, in1=th, op0=ALU.mult, op1=ALU.add),
                         r=[tb, thb], w=[tb])
                    S.op("dve", lambda e, shift=shift: e.tensor_scalar(out=tmp, in0=tmp, scalar1=shift, scalar2=None, op0=ALU.add),
                         r=[tb], w=[tb])
                    S.op("dve", lambda e: e.tensor_scalar(out=m, in0=tmp, scalar1=math.pi, scalar2=-2 * math.pi, op0=ALU.is_gt, op1=ALU.mult),
                         r=[tb], w=[mb])
                    S.op("dve", lambda e: e.tensor_tensor(out=tmp, in0=tmp, in1=m, op=ALU.add), r=[tb, mb], w=[tb])
                    S.op("dve", lambda e: e.tensor_scalar(out=m, in0=tmp, scalar1=-math.pi, scalar2=2 * math.pi, op0=ALU.is_lt, op1=ALU.mult),
                         r=[tb], w=[mb])
                    S.op("dve", lambda e: e.tensor_tensor(out=tmp, in0=tmp, in1=m, op=ALU.add), r=[tb, mb], w=[tb])
                    S.op("dve", lambda e: e.tensor_scalar(out=tmp, in0=tmp, scalar1=math.pi, scalar2=-math.pi, op0=ALU.min, op1=ALU.max),
                         r=[tb], w=[tb])
                    S.op("act", lambda e, dst=dst: e.activation(out=dst, in_=tmp, func=AF.Sin), r=[tb], w=[db])

            sincos(cs1[:], cc1[:], cth[:], cth.b, ctmp[:], ctmp.b, ctmp2[:], ctmp2.b, citmp[:], citmp.b, cs1.b, cc1.b)
            S.op("dve", lambda e: e.tensor_tensor(out=car[:], in0=crr[:], in1=cc1[:], op=ALU.mult), r=[crr.b, cc1.b], w=[car.b])
            S.op("dve", lambda e: e.tensor_tensor(out=cais[:], in0=crr[:], in1=cs1[:], op=ALU.mult), r=[crr.b, cs1.b], w=[cais.b])
            S.op("dve", lambda e: e.tensor_scalar(out=cais[:], in0=cais[:], scalar1=signc[:, 0:1], scalar2=None, op0=ALU.mult),
                 r=[cais.b, signc.b], w=[cais.b])
            S.dma("sp", w00[:], w00_d[l], w=[w00.b])
            S.dma("sp", b00[:], b00_d[l], w=[b00.b])
            S.dma("sp", esks[:], esks_d[l], w=[esks.b])
            S.op("act", lambda e: e.activation(out=esks[:], in_=esks[:], func=AF.Exp), r=[esks.b], w=[esks.b])
            for g4 in range(4):
                S.op("dve", lambda e, g4=g4: e.tensor_scalar(out=w00I[:, g4, :], in0=ident[0:NS, 0:NS], scalar1=w00[:, g4:g4 + 1],
                                                             scalar2=None, op0=ALU.mult), r=[ident.b, w00.b], w=[w00I.b])
            for qq in range(8):
                gs = slice(qq * 4, (qq + 1) * 4)
                ccos = v2(ypre, 4 * TT).rearrange("p (g t) -> p g t", g=4)
                csin = v2(gtmp, 4 * TT).rearrange("p (g t) -> p g t", g=4)
                ctm = v2(mtmp[0], 2 * TT).rearrange("p (g t) -> p g t", g=4)
                ctm2 = v2(mtmp[1], 2 * TT).rearrange("p (g t) -> p g t", g=4)
                S.op("dve", lambda e: e.tensor_copy(out=ccos[:, :, 0], in_=cc1[:, gs]), r=[cc1.b], w=[ypre.b])
                S.op("dve", lambda e: e.tensor_copy(out=csin[:, :, 0], in_=cs1[:, gs]), r=[cs1.b], w=[gtmp.b])
                n = 1
                while n < TT:
                    cn = ccos[:, :, n - 1:n].to_broadcast([128, 4, n])
                    sn = csin[:, :, n - 1:n].to_broadcast([128, 4, n])
                    c0 = ccos[:, :, 0:n]
                    s0 = csin[:, :, 0:n]
                    t1 = ctm[:, :, 0:n]
                    t2 = ctm2[:, :, 0:n]
                    S.op("dve", lambda e, c0=c0, cn=cn, t1=t1: e.tensor_tensor(out=t1, in0=c0, in1=cn, op=ALU.mult), r=[ypre.b], w=[mtmp[0].b])
                    S.op("dve", lambda e, s0=s0, sn=sn, t2=t2: e.tensor_tensor(out=t2, in0=s0, in1=sn, op=ALU.mult), r=[gtmp.b], w=[mtmp[1].b])
                    S.op("dve", lambda e, t1=t1, t2=t2, n=n: e.tensor_tensor(out=ccos[:, :, n:2 * n], in0=t1, in1=t2, op=ALU.subtract),
                         r=[mtmp[0].b, mtmp[1].b], w=[ypre.b])
                    S.op("dve", lambda e, c0=c0, sn=sn, t1=t1: e.tensor_tensor(out=t1, in0=c0, in1=sn, op=ALU.mult), r=[ypre.b, gtmp.b], w=[mtmp[0].b])
                    S.op("dve", lambda e, s0=s0, cn=cn, t2=t2: e.tensor_tensor(out=t2, in0=s0, in1=cn, op=ALU.mult), r=[ypre.b, gtmp.b], w=[mtmp[1].b])
                    S.op("dve", lambda e, t1=t1, t2=t2, n=n: e.tensor_tensor(out=csin[:, :, n:2 * n], in0=t1, in1=t2, op=ALU.add),
                         r=[mtmp[0].b, mtmp[1].b], w=[gtmp.b])
                    n *= 2
                S.op("act", lambda e: e.copy(out=tabc[:, gs, :], in_=ccos), r=[ypre.b], w=[tabc.b])
                S.op("act", lambda e: e.copy(out=tabs[:, gs, :], in_=csin), r=[gtmp.b], w=[tabs.b])

            tl = [T(None, "x")] * 0
            hold = [(xtok[0], 0), (xtok[0], 512), (xtok[1], 0), (xtok[1], 512), (lnout[0], 0), (lnout[0], 512),
                    (lnout[1], 0), (lnout[1], 512)]

            class V:
                def __init__(self, t, off):
                    self.ap = t[:, off:off + 512]
                    self.b = t.b

            for qq in range(4):
                gs = slice(qq * 8, (qq + 1) * 8)
                cs_q = slice(qq * 512, (qq + 1) * 512)
                lr, li, ld, dt_, mg, th, sn_, cs_ = [V(t, o) for t, o in hold]
                for i, dst in enumerate((lr, li, ld)):
                    S.dma("sp", dst.ap, rowp_d[l, i][:, cs_q], w=[dst.b])
                bld = v2(xtok[2], 1024).rearrange("k (a g p) -> k a g p", a=2, g=8)
                for a in range(2):
                    S.dma("sp", bld[:, a], bpad_d[l, a, qq * 8:(qq + 1) * 8].rearrange("g k p -> k g p"), w=[xtok[2].b])

                def tt(out, a, b_, op):
                    S.op("dve", lambda e: e.tensor_tensor(out=out.ap, in0=a.ap, in1=b_.ap, op=op), r=[a.b, b_.b], w=[out.b])

                S.op("act", lambda e: e.activation(out=dt_.ap, in_=ld.ap, func=AF.Exp), r=[ld.b], w=[dt_.b])
                tt(mg, lr, dt_, ALU.mult)
                S.op("act", lambda e: e.activation(out=mg.ap, in_=mg.ap, func=AF.Exp), r=[mg.b], w=[mg.b])
                tt(th, li, dt_, ALU.mult)
                sincos(sn_.ap, cs_.ap, th.ap, th.b, ld.ap, ld.b, v2(gtmp, 512), gtmp.b, v2(ypre, 512).bitcast(mybir.dt.int32), ypre.b, sn_.b, cs_.b)
                tt(cs_, cs_, mg, ALU.mult)
                S.op("dve", lambda e: e.tensor_scalar(out=cs_.ap, in0=cs_.ap, scalar1=-1.0, scalar2=None, op0=ALU.add), r=[cs_.b], w=[cs_.b])
                tt(sn_, sn_, mg, ALU.mult)
                tt(mg, lr, lr, ALU.mult)
                tt(th, li, li, ALU.mult)
                tt(mg, mg, th, ALU.add)
                S.op("dve", lambda e: e.reciprocal(out=mg.ap, in_=mg.ap), r=[mg.b], w=[mg.b])
                tt(dt_, cs_, lr, ALU.mult)
                tt(ld, sn_, li, ALU.mult)
                tt(dt_, dt_, ld, ALU.add)
                tt(dt_, dt_, mg, ALU.mult)
                tt(th, sn_, lr, ALU.mult)
                tt(ld, cs_, li, ALU.mult)
                tt(th, th, ld, ALU.subtract)
                tt(th, th, mg, ALU.mult)
                crv = dt_.ap.rearrange("k (g p) -> k g p", p=64)
                civ = th.ap.rearrange("k (g p) -> k g p", p=64)
                t1 = sn_.ap.rearrange("k (g p) -> k g p", p=64)
                t2 = cs_.ap.rearrange("k (g p) -> k g p", p=64)
                bre = bld[:, 0]
                bim = bld[:, 1]
                S.op("dve", lambda e: e.tensor_tensor(out=t1, in0=crv, in1=bre, op=ALU.mult), r=[dt_.b, xtok[2].b], w=[sn_.b])
                S.op("dve", lambda e: e.tensor_tensor(out=t2, in0=civ, in1=bim, op=ALU.mult), r=[th.b, xtok[2].b], w=[cs_.b])
                S.op("dve", lambda e: e.tensor_tensor(out=bbw[:, gs, 0:64], in0=t1, in1=t2, op=ALU.subtract), r=[sn_.b, cs_.b], w=[bbw.b])
                S.op("dve", lambda e: e.tensor_tensor(out=bbs[:, gs, 64:128], in0=t2, in1=t1, op=ALU.subtract), r=[sn_.b, cs_.b], w=[bbs.b])
                S.op("dve", lambda e: e.tensor_tensor(out=t1, in0=crv, in1=bim, op=ALU.mult), r=[dt_.b, xtok[2].b], w=[sn_.b])
                S.op("dve", lambda e: e.tensor_tensor(out=t2, in0=civ, in1=bre, op=ALU.mult), r=[th.b, xtok[2].b], w=[cs_.b])
                S.op("dve", lambda e: e.tensor_tensor(out=bbw[:, gs, 64:128], in0=t1, in1=t2, op=ALU.add), r=[sn_.b, cs_.b], w=[bbw.b])
                S.op("dve", lambda e: e.tensor_tensor(out=bbs[:, gs, 0:64], in0=t1, in1=t2, op=ALU.add), r=[sn_.b, cs_.b], w=[bbs.b])

        hstb = [Buf("hst%d" % g) for g in range(32)]
        ysb = [Buf("ys_a"), Buf("ys_b"), Buf("ys_c")]

        def s5_pipe(n):
            cols = slice(0, n)
            st = {}
            psy_cur = {}
            for it in range(32 + 7):
                g = it - 7
                if 0 <= g < 32:
                    fb, gl = g // 8, g % 8
                    if gl == 0:
                        psy_cur["p"] = ps_alloc("Y")
                    psy = psy_cur["p"]
                    hb = st[g]["hb"]
                    S.op("pe", lambda e, g=g, gl=gl, psy=psy, hb=hb: e.matmul(psy[:, 0:n], lhsT=cpw[:, g, :], rhs=hb[:, 0:n],
                                                                               start=(gl == 0), stop=(gl == 7)),
                         r=[cpw.b, hb.b], w=[psy.b], sig=True)
                    if gl == 7:
                        S.op("dve", lambda e, fb=fb, psy=psy: e.scalar_tensor_tensor(out=ypre[:, fb, cols], in0=uT[:, fb, cols],
                                                                                    scalar=dsk[:, fb:fb + 1], in1=psy[:, 0:n],
                                                                                    op0=ALU.mult, op1=ALU.add),
                             r=[uT.b, dsk.b, psy.b], w=[ypre.b])
                    del st[g]
                g = it - 6
                if 0 <= g < 32:
                    c, d = st[g]["c"], st[g]["d"]
                    hb = rot(s5hb, "hb")
                    st[g]["hb"] = hb
                    S.op("pool", lambda e, c=c, d=d, hb=hb: e.tensor_tensor(out=hb[:, 0:n], in0=c[:, 0:n], in1=d[:, 0:n], op=ALU.add),
                         r=[c.b, d.b], w=[hb.b])
                    S.op("pool", lambda e, c=c, d=d, g=g: e.tensor_tensor(out=hst[:, g:g + 1], in0=c[:, n - 1:n], in1=d[:, n - 1:n], op=ALU.add),
                         r=[c.b, d.b], w=[hstb[g]])
                g = it - 5
                if 0 <= g < 32:
                    psw = st[g]["psw"]
                    d = rot(s5d, "d")
                    st[g]["d"] = d
                    S.op("dve", lambda e, g=g, psw=psw, d=d: e.tensor_tensor(out=d[:, 0:n], in0=psw[:, 0:n], in1=tabs[:, g, 0:n], op=ALU.mult),
                         r=[psw.b, tabs.b], w=[d.b])
                g = it - 4
                if 0 <= g < 32:
                    gg = st[g]["gg"]
                    psw = ps_alloc("C")
                    st[g]["psw"] = psw
                    S.op("pe", lambda e, gg=gg, psw=psw: e.matmul(psw[:, 0:n], lhsT=swapm[:], rhs=gg[:, 0:n], start=True, stop=True),
                         r=[swapm.b, gg.b], w=[psw.b])
                    c = rot(s5c, "c")
                    st[g]["c"] = c
                    S.op("dve", lambda e, g=g, gg=gg, c=c: e.tensor_tensor(out=c[:, 0:n], in0=gg[:, 0:n], in1=tabc[:, g, 0:n], op=ALU.mult),
                         r=[gg.b, tabc.b], w=[c.b])
                g = it - 3
                if 0 <= g < 32:
                    v = st[g]["v"]
                    gg = rot(s5g, "g")
                    st[g]["gg"] = gg
                    S.op("dve", lambda e, g=g, v=v, gg=gg: e.tensor_tensor_scan(out=gg[:, 0:n], data0=crr[:, g:g + 1].to_broadcast([128, n]),
                                                                               data1=v[:, 0:n], initial=hst[:, g:g + 1], op0=ALU.mult, op1=ALU.add),
                         r=[crr.b, v.b, hstb[g]], w=[gg.b])
                g = it - 2
                if 0 <= g < 32:
                    a, b_ = st[g]["a"], st[g]["b"]
                    v = rot(s5v, "v")
                    st[g]["v"] = v
                    S.op("pool", lambda e, a=a, b_=b_, v=v: e.tensor_tensor(out=v[:, 0:n], in0=a[:, 0:n], in1=b_[:, 0:n], op=ALU.add),
                         r=[a.b, b_.b], w=[v.b])
                g = it - 1
                if 0 <= g < 32:
                    pv = st[g]["pv"]
                    a = rot(s5a, "a")
                    b_ = rot(s5b, "b")
                    st[g]["a"], st[g]["b"] = a, b_
                    S.op("dve", lambda e, g=g, pv=pv, a=a: e.tensor_tensor(out=a[:, 0:n], in0=pv[:, 0:n], in1=tabc[:, g, 0:n], op=ALU.mult),
                         r=[pv.b, tabc.b], w=[a.b])
                    S.op("dve", lambda e, g=g, pv=pv, b_=b_: e.tensor_tensor(out=b_[:, 0:n], in0=pv[:, TT:TT + n], in1=tabs[:, g, 0:n], op=ALU.mult),
                         r=[pv.b, tabs.b], w=[b_.b])
                g = it
                if 0 <= g < 32:
                    fb = g // 8
                    pv = ps_alloc("B")
                    st[g] = {"pv": pv}
                    S.op("pe", lambda e, g=g, fb=fb, pv=pv: e.matmul(pv[:, 0:n], lhsT=bbw[:, g, :], rhs=uT[:, fb, cols], start=True, stop=True),
                         r=[bbw.b, uT.b], w=[pv.b], sig=False)
                    S.op("pe", lambda e, g=g, fb=fb, pv=pv: e.matmul(pv[:, TT:TT + n], lhsT=bbs[:, g, :], rhs=uT[:, fb, cols], start=True, stop=True),
                         r=[bbs.b, uT.b], w=[pv.b])
                yield

        def gelu_glu_gen(l, n):
            yv = ypre[:, :, 0:n]
            g1 = gtmp[:, :, 0:n]
            S.op("dve", lambda e: e.tensor_tensor(out=g1, in0=yv, in1=yv, op=ALU.mult), r=[ypre.b], w=[gtmp.b])
            S.op("dve", lambda e: e.tensor_scalar(out=g1, in0=g1, scalar1=GK, scalar2=1.0, op0=ALU.mult, op1=ALU.add), r=[gtmp.b], w=[gtmp.b])
            yield
            S.op("pool", lambda e: e.tensor_tensor(out=g1, in0=g1, in1=yv, op=ALU.mult), r=[gtmp.b, ypre.b], w=[gtmp.b])
            yield
            S.op("act", lambda e: e.activation(out=g1, in_=g1, func=AF.Sigmoid, scale=GS), r=[gtmp.b], w=[gtmp.b])
            yield
            S.op("dve", lambda e: e.tensor_tensor(out=yv, in0=yv, in1=g1, op=ALU.mult), r=[gtmp.b, ypre.b], w=[ypre.b])
            yield
            S.op("act", lambda e: e.copy(out=gtmp2[:, :, 0:n], in_=yv), r=[ypre.b], w=[gtmp2.b])
            yield
            wb = w_next(l, C_GLU)
            for j in range(4):
                ps = ps_alloc("A")
                mm_fm(ps, wb, j, 4, lambda kb: gtmp2[:, kb, 0:n], [gtmp2.b], n)
                S.op("act", lambda e, j=j, ps=ps: e.activation(out=gtmp[:, j, 0:n], in_=ps[:, 0:n], func=AF.Sigmoid, bias=glub[:, j:j + 1]),
                     r=[ps.b, glub.b], w=[gtmp.b])
            w_done()
            yield
            S.op("dve", lambda e: e.tensor_tensor(out=yv, in0=yv, in1=g1, op=ALU.mult), r=[gtmp.b, ypre.b], w=[ypre.b])
            yield
            S.op("pool", lambda e: e.tensor_tensor(out=ys[:, 4:8, 0:n], in0=yv, in1=szb[:, :, 0:n], op=ALU.mult),
                 r=[ypre.b, szb.b], w=[ysb[1]])

        def gelu_glu(l, n):
            for _ in gelu_glu_gen(l, n):
                pass

        def layernorm_rows(src, nrow, width, gam, bet, dst, tmp=None, eng2="pool"):
            if tmp is None:
                tmp = (lntmp.b, lntmp[0:nrow, 0:width])
            nchk = width // 512
            for i in range(nchk):
                S.op("dve", lambda e, i=i: e.bn_stats(out=lnst[0:nrow, i * 6:(i + 1) * 6], in_=src[1][:, i * 512:(i + 1) * 512]),
                     r=[src[0]], w=[lnst.b])
            S.op("dve", lambda e: e.bn_aggr(out=lnmv[0:nrow, 0:2], in_=lnst[0:nrow, 0:6 * nchk]), r=[lnst.b], w=[lnmv.b])
            S.op("dve", lambda e: e.tensor_scalar(out=lnmv[0:nrow, 2:3], in0=lnmv[0:nrow, 1:2], scalar1=LN_EPS, scalar2=None,
                                                  op0=ALU.add), r=[lnmv.b], w=[lnmv.b])
            S.op("act", lambda e: e.sqrt(out=lnmv[0:nrow, 2:3], in_=lnmv[0:nrow, 2:3]), r=[lnmv.b], w=[lnmv.b])
            S.op("dve", lambda e: e.reciprocal(out=lnmv[0:nrow, 2:3], in_=lnmv[0:nrow, 2:3]), r=[lnmv.b], w=[lnmv.b])
            S.op("dve", lambda e: e.tensor_scalar(out=tmp[1], in0=src[1], scalar1=lnmv[0:nrow, 0:1],
                                                  scalar2=lnmv[0:nrow, 2:3], op0=ALU.subtract, op1=ALU.mult),
                 r=[src[0], lnmv.b], w=[tmp[0]])
            S.op("dve", lambda e: e.tensor_tensor(out=tmp[1], in0=tmp[1], in1=gam[1], op=ALU.mult),
                 r=[tmp[0], gam[0]], w=[tmp[0]])
            S.op(eng2, lambda e: e.tensor_tensor(out=dst[1], in0=tmp[1], in1=bet[1], op=ALU.add),
                 r=[tmp[0], bet[0]], w=[dst[0]])

        def xload(l, ti):
            t0 = ti * TT
            src = xp if l == 0 else x1
            for b in range(NBK):
                xt = xtok[(ti % 2) * NBK + b]
                S.dma("sp", xt[:], src[t0 + b * 128: t0 + (b + 1) * 128, :], r=([x1_b] if l == 1 else []), w=[xt.b])

        def front(l, ti):
            n = TT
            xts = [xtok[(ti % 2) * NBK + b] for b in range(NBK)]
            for kb in range(8):
                ps = ps_alloc("A")
                for b in range(NBK):
                    S.op("pe", lambda e, b=b, ps=ps, kb=kb: e.transpose(out=ps[:, b * 128:(b + 1) * 128],
                                                                       in_=xts[b][:, kb * 128:(kb + 1) * 128], identity=ident[:]),
                         r=[xts[b].b, ident.b], w=[ps.b], sig=(b == NBK - 1))
                S.op("act" if kb % 2 else "dve",
                     (lambda e, ps=ps, kb=kb: e.copy(out=xT[:, kb, :], in_=ps[:, 0:TT])) if kb % 2 else
                     (lambda e, ps=ps, kb=kb: e.tensor_copy(out=xT[:, kb, :], in_=ps[:, 0:TT])),
                     r=[ps.b], w=[xT.b])
            xrhs = lambda kb: xT[:, kb, :]
            wb = w_next(l, C_UB)
            for j in range(4):
                ps = ps_alloc("A")
                mm_fm(ps, wb, j, 8, xrhs, [xT.b], n)
                S.op("act", lambda e, j=j, ps=ps: e.copy(out=uT[:, j, :], in_=ps[:, 0:TT]), r=[ps.b], w=[uT.b])
            w_done()

        def tail(l, ti):
            t0 = ti * TT
            dst = x1 if l == 0 else yp
            for _ in gelu_glu_gen(l, TT):
                yield
            yield
            for _ in readout_br(l, TT, (1,), False, pre=True):
                yield
            blocks = [(xtok[(ti % 2) * NBK + b], 128, dst[t0 + b * 128: t0 + (b + 1) * 128, :]) for b in range(NBK)]
            for _ in wo_ln(l, TT, blocks):
                yield
            if ti + 2 < (NTILE if DBG_TILES is None else DBG_TILES):
                xload(l, ti + 2)

        def rest_gen(l, ti):
            n = TT
            xrhs = lambda kb: xT[:, kb, :]

            wb = w_next(l, C_Q)
            for j in range(4):
                ps = ps_alloc("A")
                mm_fm(ps, wb, j, 8, xrhs, [xT.b], n)
                S.op("act", lambda e, j=j, ps=ps: e.activation(out=qT[:, j, :], in_=ps[:, 0:TT], func=AF.Copy, scale=0.125),
                     r=[ps.b], w=[qT.b])
                yield
            w_done()
            yield
            wb = w_next(l, C_KD)
            for j in range(2):
                ps = ps_alloc("A")
                mm_fm(ps, wb, j, 8, xrhs, [xT.b], n)
                S.op("act", lambda e, j=j, ps=ps: e.copy(out=kT[:, j, 128:128 + TT], in_=ps[:, 0:TT]), r=[ps.b], w=[kT.b])
                yield
            w_done()
            yield
            wb = w_next(l, C_KV)
            for b in range(NBK):
                ps = ps_alloc("A")
                for kb in range(8):
                    S.op("pe", lambda e, kb=kb, b=b, ps=ps: e.matmul(ps[:, 0:256], lhsT=xT[:, kb, b * 128:(b + 1) * 128],
                                                                    rhs=wb[:, kb * 512: kb * 512 + 256], start=(kb == 0), stop=(kb == 7)),
                         r=[wb.b, xT.b], w=[ps.b], sig=(kb == 7))
                vsrc = ps[:, 128:256].rearrange("p (k d) -> p k d", k=2)
                S.op("act", lambda e, b=b, vsrc=vsrc: e.copy(out=vtok[:, b + 1, :, 0:64], in_=vsrc), r=[ps.b], w=[vtok.b])
                S.op("act", lambda e, b=b, vsrc=vsrc: e.copy(out=vtok[:, b + 1, :, 64:128], in_=vsrc), r=[ps.b], w=[vtok.b])
                if ti == (NTILE if DBG_TILES is None else DBG_TILES) - 1 and b == NBK - 1:
                    S.op("act", lambda e, ps=ps: e.copy(out=kvout[:], in_=ps[:, 0:256]), r=[ps.b], w=[kvout.b])
                    S.dma("pool", kvw[l], kvout[:], r=[kvout.b])
                yield
            w_done()
            yield
            for cid, dstT, fn in ((C_ZA, sza, AF.Silu), (C_UC, ucz, None)):
                wb = w_next(l, cid)
                for j in range(4):
                    ps = ps_alloc("A")
                    mm_fm(ps, wb, j, 8, xrhs, [xT.b], n)
                    if fn is None:
                        S.op("act", lambda e, j=j, ps=ps, dstT=dstT: e.copy(out=dstT[:, j, :], in_=ps[:, 0:TT]), r=[ps.b], w=[dstT.b])
                    else:
                        S.op("act", lambda e, j=j, ps=ps, dstT=dstT, fn=fn: e.activation(out=dstT[:, j, :], in_=ps[:, 0:TT], func=fn),
                             r=[ps.b], w=[dstT.b])
                    yield
                w_done()
                yield
            wb = w_next(l, C_VC)
            for b in range(NBK):
                ps = ps_alloc("A")
                for kb in range(8):
                    S.op("pe", lambda e, kb=kb, b=b, ps=ps: e.matmul(ps[:, :], lhsT=xT[:, kb, b * 128:(b + 1) * 128],
                                                                    rhs=wb[:, kb * 512:(kb + 1) * 512], start=(kb == 0), stop=(kb == 7)),
                         r=[wb.b, xT.b], w=[ps.b], sig=(kb == 7))
                layernorm_rows((ps.b, ps[:, :]), 128, 512, (sgn.b, sgn[:, 0, :]), (sgn.b, sgn[:, 1, :]), (vn.b, vn[:, b, :]))
                yield
            w_done()
            yield
            wb = w_next(l, C_ZC)
            for j in range(4):
                ps = ps_alloc("A")
                mm_fm(ps, wb, j, 8, xrhs, [xT.b], n)
                S.op("act", lambda e, j=j, ps=ps: e.activation(out=sig[:, j, :], in_=ps[:, 0:TT], func=AF.Silu), r=[ps.b], w=[sig.b])
                yield
            w_done()
            S.op("pool", lambda e: e.tensor_tensor(out=ucz[:], in0=ucz[:], in1=sig[:], op=ALU.mult), r=[ucz.b, sig.b], w=[ucz.b])
            yield

            for b in range(NBK):
                for g4 in range(4):
                    ps = ps_alloc("A")
                    S.op("pe", lambda e, b=b, g4=g4, ps=ps: e.matmul(ps[:, 0:128], lhsT=vn[:, b, g4 * 128:(g4 + 1) * 128],
                                                                    rhs=sgw[:, g4, :], start=True, stop=True),
                         r=[vn.b, sgw.b], w=[ps.b])
                    mt = rot(mtmp, "mt")
                    S.op("dve", lambda e, g4=g4, ps=ps, mt=mt: e.tensor_tensor(out=mt[:, 0:128], in0=ps[:, 0:128],
                                                                              in1=sgb[:, g4 * 128:(g4 + 1) * 128], op=ALU.add),
                         r=[ps.b, sgb.b], w=[mt.b])
                    S.op("pool", lambda e, b=b, g4=g4, mt=mt: e.tensor_tensor(out=ys[:, 8 + g4, b * 128:(b + 1) * 128], in0=mt[:, 0:128],
                                                                             in1=ucz[:, g4, b * 128:(b + 1) * 128], op=ALU.mult),
                         r=[mt.b, ucz.b], w=[ysb[2]])
                yield

            for b in range(NBK):
                first = (ti == 0 and b == 0)
                parts = [1] if first else [0, 1]
                for kv in range(2):
                    for pt in parts:
                        pss = ps_alloc("D")
                        kc = b * 128 + pt * 128
                        for par in range(2):
                            rows = slice(par * 64, (par + 1) * 64)
                            o = pss[:, par * 256:(par + 1) * 256].rearrange("p (j q) -> p j q", j=2)
                            S.op("pe", lambda e, o=o, rows=rows, kc=kc, kv=kv, b=b: e.matmul(
                                o, lhsT=kT[rows, kv, kc:kc + 128], rhs=qT[rows, 2 * kv:2 * kv + 2, b * 128:(b + 1) * 128],
                                start=True, stop=False), r=[kT.b, qT.b], w=[pss.b], sig=False)
                            aqv = aq[:, (kv * 2 + par) * 256:(kv * 2 + par + 1) * 256].rearrange("p (j q) -> p j q", j=2)
                            S.op("pe", lambda e, o=o, aqv=aqv, pt=pt: e.matmul(o, lhsT=ak[:, pt * 128:(pt + 1) * 128], rhs=aqv,
                                                                               start=False, stop=True),
                                 r=[ak.b, aq.b], w=[pss.b], sig=(par == 1))
                        S.op("act", lambda e, pss=pss, pt=pt, kv=kv: e.activation(out=pT[kv][pt][:], in_=pss[:, :], func=AF.Exp),
                             r=[pss.b], w=[pT[kv][pt].b])
                        S.op("pool", lambda e, pt=pt, kv=kv: e.tensor_tensor(
                            out=pT[kv][pt][:].rearrange("p (h q) -> p h q", h=4), in0=pT[kv][pt][:].rearrange("p (h q) -> p h q", h=4),
                            in1=masks[:, pt, :].unsqueeze(1).to_broadcast([128, 4, 128]), op=ALU.mult),
                             r=[pT[kv][pt].b, masks.b], w=[pT[kv][pt].b])
                    yield
                    psd = ps_alloc("D")
                    for i, pt in enumerate(parts):
                        S.op("pe", lambda e, psd=psd, pt=pt, i=i, kv=kv: e.matmul(psd[:, :], lhsT=ones[:], rhs=pT[kv][pt][:],
                                                                                 start=(i == 0), stop=(i == len(parts) - 1)),
                             r=[ones.b, pT[kv][pt].b], w=[psd.b], sig=(i == len(parts) - 1))
                    S.op("dve", lambda e, psd=psd, kv=kv: e.tensor_tensor(
                        out=rden[kv][:].rearrange("p (h q) -> p h q", h=4), in0=psd[:, :].rearrange("p (h q) -> p h q", h=4),
                        in1=esk[:, kv * 4:(kv + 1) * 4].unsqueeze(2).to_broadcast([128, 4, 128]), op=ALU.add),
                         r=[psd.b, esk.b], w=[rden[kv].b])
                    S.op("act", lambda e, kv=kv: e.activation(out=rden[kv][:], in_=rden[kv][:], func=AF.Ln), r=[rden[kv].b], w=[rden[kv].b])
                    S.op("act", lambda e, kv=kv: e.activation(out=rden[kv][:], in_=rden[kv][:], func=AF.Exp, scale=-1.0),
                         r=[rden[kv].b], w=[rden[kv].b])
                    pso = ps_alloc("D")
                    for hh in range(4):
                        for i, pt in enumerate(parts):
                            S.op("pe", lambda e, pso=pso, hh=hh, pt=pt, i=i, kv=kv, b=b: e.matmul(
                                pso[:, hh * 128:(hh + 1) * 128], lhsT=vtok[:, b + pt, kv, :], rhs=pT[kv][pt][:, hh * 128:(hh + 1) * 128],
                                start=(i == 0), stop=(i == len(parts) - 1)),
                                 r=[vtok.b, pT[kv][pt].b], w=[pso.b], sig=(hh == 3 and i == len(parts) - 1))
                    S.op("dve", lambda e, pso=pso, kv=kv: e.tensor_tensor(out=otmp[kv][:], in0=pso[:, :], in1=rden[kv][:], op=ALU.mult),
                         r=[pso.b, rden[kv].b], w=[otmp[kv].b])
                    for par in range(2):
                        rows = slice(par * 64, (par + 1) * 64)
                        ov = otmp[kv][rows, par * 256:(par + 1) * 256].rearrange("p (j q) -> p j q", j=2)
                        S.op("pool", lambda e, rows=rows, ov=ov, kv=kv, b=b: e.tensor_tensor(
                            out=ys[rows, 2 * kv:2 * kv + 2, b * 128:(b + 1) * 128], in0=ov,
                            in1=sza[rows, 2 * kv:2 * kv + 2, b * 128:(b + 1) * 128], op=ALU.mult),
                             r=[otmp[kv].b, sza.b], w=[ysb[0]])
                    yield
            S.op("act", lambda e: e.copy(out=kT[:, :, 0:128], in_=kT[:, :, TT:TT + 128]), r=[kT.b], w=[kT.b])
            S.op("pool", lambda e: e.tensor_copy(out=vtok[:, 0], in_=vtok[:, NBK]), r=[vtok.b], w=[vtok.b])
            for _ in readout_br(l, n, (0, 2), True):
                yield
            for _ in gates_b(l, n):
                yield
            wb = w_next(l, C_ZB)
            for j in range(4):
                ps = ps_alloc("A")
                mm_fm(ps, wb, j, 8, xrhs, [xT.b], n)
                S.op("act", lambda e, j=j, ps=ps: e.activation(out=szb[:, j, :], in_=ps[:, 0:TT], func=AF.Silu), r=[ps.b], w=[szb.b])
            w_done()
            yield


        def gates_b(l, n):
            xrhs = lambda kb: xT[:, kb, 0:n]
            for hf in range(2):
                wb = w_next(l, C_G + hf * 3 + 1)
                for j in range(4):
                    ps = ps_alloc("A")
                    mm_fm(ps, wb, j, 8, xrhs, [xT.b], n)
                    S.op("act", lambda e, j=j, ps=ps: e.activation(out=sigB[:, hf * 4 + j, 0:n], in_=ps[:, 0:n], func=AF.Sigmoid),
                         r=[ps.b], w=[sigB.b])
                    yield
                w_done()
                yield

        def readout_br(l, n, branches, init_first, pre=False):
            xrhs = lambda kb: xT[:, kb, 0:n]
            for hf in range(2):
                for bi, b in enumerate(branches):
                    if pre:
                        gsrc, gbuf, goff = sigB, sigB.b, hf * 4
                    else:
                        gsrc, gbuf, goff = sig, sig.b, 0
                        wb = w_next(l, C_G + hf * 3 + b)
                        for j in range(4):
                            ps = ps_alloc("A")
                            mm_fm(ps, wb, j, 8, xrhs, [xT.b], n)
                            S.op("act", lambda e, j=j, ps=ps: e.activation(out=sig[:, j, 0:n], in_=ps[:, 0:n], func=AF.Sigmoid),
                                 r=[ps.b], w=[sig.b])
                            yield
                        w_done()
                        yield
                    wb = w_next(l, C_R + hf * 3 + b)
                    for j in range(4):
                        ps = ps_alloc("A")
                        mm_fm(ps, wb, j, 4, lambda kb, b=b: ys[:, b * 4 + kb, 0:n], [ysb[b]], n)
                        if init_first and bi == 0:
                            S.op("dve", lambda e, j=j, ps=ps, gsrc=gsrc, goff=goff: e.tensor_tensor(
                                out=mrgT[:, hf * 4 + j, 0:n], in0=ps[:, 0:n], in1=gsrc[:, goff + j, 0:n], op=ALU.mult),
                                 r=[ps.b, gbuf], w=[mrgT.b])
                        else:
                            mt = rot(mtmp, "mt")
                            S.op("dve", lambda e, j=j, ps=ps, mt=mt, gsrc=gsrc, goff=goff: e.tensor_tensor(
                                out=mt[:, 0:n], in0=ps[:, 0:n], in1=gsrc[:, goff + j, 0:n], op=ALU.mult),
                                 r=[ps.b, gbuf], w=[mt.b])
                            S.op("pool", lambda e, j=j, mt=mt: e.tensor_tensor(out=mrgT[:, hf * 4 + j, 0:n], in0=mrgT[:, hf * 4 + j, 0:n],
                                                                              in1=mt[:, 0:n], op=ALU.add),
                                 r=[mrgT.b, mt.b], w=[mrgT.b])
                        yield
                    w_done()
                    yield

        def drain(gen):
            for _ in gen:
                pass

        def wo_ln(l, n, blocks):
            for hf in range(2):
                wb = w_next(l, C_O + hf)
                c0 = 0
                for (xt, nrow, _) in blocks:
                    ps = ps_alloc("A")
                    for kb in range(8):
                        S.op("pe", lambda e, kb=kb, ps=ps, c0=c0, nrow=nrow: e.matmul(
                            ps[0:nrow, :], lhsT=mrgT[:, kb, c0:c0 + nrow], rhs=wb[:, kb * 512:(kb + 1) * 512],
                            start=(kb == 0), stop=(kb == 7)), r=[wb.b, mrgT.b], w=[ps.b], sig=(kb == 7))
                    S.op("dve", lambda e, ps=ps, xt=xt, nrow=nrow: e.scalar_tensor_tensor(
                        out=xt[0:nrow, hf * 512:(hf + 1) * 512], in0=xt[0:nrow, hf * 512:(hf + 1) * 512], scalar=ALPHA,
                        in1=ps[0:nrow, :], op0=ALU.mult, op1=ALU.add), r=[ps.b, xt.b], w=[xt.b])
                    c0 += nrow
                    yield
                w_done()
                yield
            for (xt, nrow, dst_ap) in blocks:
                lo = rot(lnout, "lo")
                layernorm_rows((xt.b, xt[0:nrow, :]), nrow, D, (lng.b, lng[0:nrow, 0, :]), (lng.b, lng[0:nrow, 1, :]),
                               (lo.b, lo[0:nrow, :]), tmp=(xt.b, xt[0:nrow, :]))
                S.dma("pool", dst_ap, lo[0:nrow, :], r=[lo.b], w=[x1_b])
                yield

        def sample_tile(l):
            n = NS
            src = xs if l == 0 else xs1
            dst = xs1 if l == 0 else ysm
            xt = xtok[0]
            S.dma("sp", xt[0:n, :], src, r=([x1_b] if l == 1 else []), w=[xt.b])
            for kb in range(8):
                ps = ps_alloc("A")
                S.op("pe", lambda e, ps=ps, kb=kb: e.transpose(out=ps[:, 0:n], in_=xt[0:n, kb * 128:(kb + 1) * 128], identity=ident[0:n, 0:n]),
                     r=[xt.b, ident.b], w=[ps.b])
                S.op("dve", lambda e, ps=ps, kb=kb: e.tensor_copy(out=xT[:, kb, 0:n], in_=ps[:, 0:n]), r=[ps.b], w=[xT.b])
            xrhs = lambda kb: xT[:, kb, 0:n]
            wb = w_next(l, C_UB)
            for j in range(4):
                ps = ps_alloc("A")
                mm_fm(ps, wb, j, 8, xrhs, [xT.b], n)
                S.op("dve", lambda e, j=j, ps=ps: e.tensor_copy(out=uT[:, j, 0:n], in_=ps[:, 0:n]), r=[ps.b], w=[uT.b])
            w_done()
            wb = w_next(l, C_Q)
            for j in range(4):
                ps = ps_alloc("A")
                mm_fm(ps, wb, j, 8, xrhs, [xT.b], n)
                S.op("act", lambda e, j=j, ps=ps: e.activation(out=qT[:, j, 0:n], in_=ps[:, 0:n], func=AF.Copy, scale=0.125), r=[ps.b], w=[qT.b])
            w_done()
            wb = w_next(l, C_KD)
            for j in range(4):
                ps = ps_alloc("A")
                mm_fm(ps, wb, j, 8, xrhs, [xT.b], n)
                if j < 2:
                    S.op("dve", lambda e, j=j, ps=ps: e.tensor_copy(out=ksd[:, j, :], in_=ps[:, 0:n]), r=[ps.b], w=[ksd.b])
                else:
                    S.op("dve", lambda e, j=j, ps=ps: e.tensor_copy(out=vsd[:, j - 2, :], in_=ps[:, 0:n]), r=[ps.b], w=[vsd.b])
            w_done()
            wb = w_next(l, C_KV)
            ps = ps_alloc("A")
            for kb in range(8):
                S.op("pe", lambda e, kb=kb, ps=ps: e.matmul(ps[0:n, 0:256], lhsT=xT[:, kb, 0:n], rhs=wb[:, kb * 512: kb * 512 + 256],
                                                           start=(kb == 0), stop=(kb == 7)), r=[wb.b, xT.b], w=[ps.b], sig=(kb == 7))
            S.op("dve", lambda e, ps=ps: e.tensor_copy(out=kvs_sb[:], in_=ps[0:n, 0:256]), r=[ps.b], w=[kvs_sb.b])
            S.dma("pool", kvs[l], kvs_sb[:], r=[kvs_sb.b])
            w_done()
            for cid, dstT, fn in ((C_ZA, sza, AF.Silu), (C_ZB, szb, AF.Silu), (C_UC, ucz, None)):
                wb = w_next(l, cid)
                for j in range(4):
                    ps = ps_alloc("A")
                    mm_fm(ps, wb, j, 8, xrhs, [xT.b], n)
                    if fn is None:
                        S.op("dve", lambda e, j=j, ps=ps, dstT=dstT: e.tensor_copy(out=dstT[:, j, 0:n], in_=ps[:, 0:n]), r=[ps.b], w=[dstT.b])
                    else:
                        S.op("act", lambda e, j=j, ps=ps, dstT=dstT, fn=fn: e.activation(out=dstT[:, j, 0:n], in_=ps[:, 0:n], func=fn),
                             r=[ps.b], w=[dstT.b])
                w_done()
            wb = w_next(l, C_VC)
            ps = ps_alloc("A")
            for kb in range(8):
                S.op("pe", lambda e, kb=kb, ps=ps: e.matmul(ps[0:n, :], lhsT=xT[:, kb, 0:n], rhs=wb[:, kb * 512:(kb + 1) * 512],
                                                           start=(kb == 0), stop=(kb == 7)), r=[wb.b, xT.b], w=[ps.b], sig=(kb == 7))
            lo = rot(lnout, "lo")
            layernorm_rows((ps.b, ps[0:n, :]), n, 512, (sgn.b, sgn[0:n, 0, :]), (sgn.b, sgn[0:n, 1, :]), (lo.b, lo[0:n, 0:512]))
            S.dma("pool", vns[l], lo[0:n, 0:512], r=[lo.b])
            S.op("act", lambda e: e.copy(out=vnS[:], in_=lo[0:n, 0:512]), r=[lo.b], w=[vnS.b])
            w_done()
            wb = w_next(l, C_ZC)
            for j in range(4):
                ps = ps_alloc("A")
                mm_fm(ps, wb, j, 8, xrhs, [xT.b], n)
                S.op("act", lambda e, j=j, ps=ps: e.activation(out=gtmp2[:, j, 0:n], in_=ps[:, 0:n], func=AF.Silu), r=[ps.b], w=[gtmp2.b])
            w_done()
            S.op("pool", lambda e: e.tensor_tensor(out=ucz[:, :, 0:n], in0=ucz[:, :, 0:n], in1=gtmp2[:, :, 0:n], op=ALU.mult),
                 r=[ucz.b, gtmp2.b], w=[ucz.b])

            S.dma("sp", h0[:], h0_d[l], w=[h0.b])
            S.dma("sp", h0s[:], h0s_d[l], w=[h0s.b])
            psv = ps_alloc("B")
            for g in range(32):
                S.op("pe", lambda e, g=g: e.matmul(psv[:, g * n:(g + 1) * n], lhsT=bbw[:, g, :], rhs=uT[:, g // 8, 0:n], start=True, stop=True),
                     r=[bbw.b, uT.b], w=[psv.b], sig=(g == 31))
            S.op("dve", lambda e: e.tensor_tensor(out=h0[:], in0=h0[:], in1=car[:].unsqueeze(2).to_broadcast([128, 32, n]), op=ALU.mult),
                 r=[h0.b, car.b], w=[h0.b])
            S.op("dve", lambda e: e.tensor_tensor(out=h0s[:], in0=h0s[:], in1=cais[:].unsqueeze(2).to_broadcast([128, 32, n]), op=ALU.mult),
                 r=[h0s.b, cais.b], w=[h0s.b])
            S.op("dve", lambda e: e.tensor_tensor(out=h0[:], in0=h0[:], in1=h0s[:], op=ALU.add), r=[h0.b, h0s.b], w=[h0.b])
            S.op("dve", lambda e: e.tensor_tensor(out=h0[:], in0=h0[:], in1=psv[:, 0:32 * n].rearrange("p (g n) -> p g n", n=n), op=ALU.add),
                 r=[h0.b, psv.b], w=[h0.b])
            S.dma("pool", hss[l], h0[:], r=[h0.b])
            S.op("act", lambda e: e.copy(out=hbs[:], in_=h0[:]), r=[h0.b], w=[hbs.b])
            for fb in range(4):
                psy = ps_alloc("Y")
                for gl in range(8):
                    g = fb * 8 + gl
                    S.op("pe", lambda e, g=g, gl=gl, psy=psy: e.matmul(psy[:, 0:n], lhsT=cpw[:, g, :], rhs=hbs[:, g, :], start=(gl == 0), stop=(gl == 7)),
                         r=[cpw.b, hbs.b], w=[psy.b], sig=(gl == 7))
                S.op("dve", lambda e, fb=fb, psy=psy: e.scalar_tensor_tensor(out=ypre[:, fb, 0:n], in0=uT[:, fb, 0:n], scalar=dsk[:, fb:fb + 1],
                                                                            in1=psy[:, 0:n], op0=ALU.mult, op1=ALU.add),
                     r=[uT.b, dsk.b, psy.b], w=[ypre.b])

            for g4 in range(4):
                ps = ps_alloc("A")
                S.op("pe", lambda e, g4=g4, ps=ps: e.matmul(ps[:, 0:n], lhsT=vnS[:, g4 * 128:(g4 + 1) * 128], rhs=w00I[:, g4, :], start=True, stop=True),
                     r=[vnS.b, w00I.b], w=[ps.b])
                S.op("dve", lambda e, g4=g4, ps=ps: e.tensor_scalar(out=st1[:], in0=ps[:, 0:n], scalar1=b00[:, g4:g4 + 1], scalar2=None, op0=ALU.add),
                     r=[ps.b, b00.b], w=[st1.b])
                S.op("dve", lambda e, g4=g4: e.tensor_tensor(out=ys[:, 8 + g4, 0:n], in0=st1[:], in1=ucz[:, g4, 0:n], op=ALU.mult),
                     r=[st1.b, ucz.b], w=[ysb[2]])

            S.dma("pool", ck[:], ckT_d[l], w=[ck.b])
            S.dma("pool", cv[:], cvd_d[l], w=[cv.b])
            pss = ps_alloc("D")
            for nn in range(n):
                for kv in range(2):
                    for par in range(2):
                        rows = slice(par * 64, (par + 1) * 64)
                        c0 = (kv * 2 + par) * 32
                        o = pss[:, c0:c0 + 32].rearrange("p (j n) -> p j n", j=2)[:, :, nn]
                        last = (nn == n - 1 and kv == 1 and par == 1)
                        S.op("pe", lambda e, o=o, rows=rows, nn=nn, kv=kv: e.matmul(o, lhsT=ck[rows, nn * 2 + kv, :], rhs=qT[rows, 2 * kv:2 * kv + 2, nn],
                                                                                   start=True, stop=False), r=[ck.b, qT.b], w=[pss.b], sig=False)
                        S.op("pe", lambda e, o=o, c0=c0: e.matmul(o, lhsT=ak[:, 0:128], rhs=aqs[:, c0 // 16: c0 // 16 + 2], start=False, stop=True),
                             r=[ak.b, aqs.b], w=[pss.b], sig=last)
            S.op("act", lambda e: e.activation(out=pSb[:], in_=pss[:, 0:128], func=AF.Exp), r=[pss.b], w=[pSb.b])
            for j in range(4):
                S.op("dve", lambda e, j=j: e.tensor_tensor(out=prod[:, j, :], in0=qT[:, j, 0:n], in1=ksd[:, j // 2, :], op=ALU.mult),
                     r=[qT.b, ksd.b], w=[prod.b])
            ps1 = ps_alloc("A")
            S.op("pe", lambda e: e.matmul(ps1[0:2, 0:64], lhsT=hsel[:], rhs=prod[:].rearrange("p j n -> p (j n)"), start=True, stop=True),
                 r=[hsel.b, prod.b], w=[ps1.b])
            S.op("act", lambda e: e.activation(out=pself[:], in_=ps1[0:2, 0:64], func=AF.Exp), r=[ps1.b], w=[pself.b])
            ps2 = ps_alloc("A")
            S.op("pe", lambda e: e.matmul(ps2[:, 0:64], lhsT=hselT[:], rhs=pself[:], start=True, stop=True), r=[hselT.b, pself.b], w=[ps2.b])
            S.op("dve", lambda e: e.tensor_copy(out=pbs[:].rearrange("p j n -> p (j n)"), in_=ps2[:, 0:64]), r=[ps2.b], w=[pbs.b])
            psd = ps_alloc("A")
            S.op("pe", lambda e: e.matmul(psd[:, 0:128], lhsT=ones[:], rhs=pSb[:], start=True, stop=True), r=[ones.b, pSb.b], w=[psd.b])
            S.op("dve", lambda e: e.tensor_copy(out=sdn[:], in_=psd[:, 0:128]), r=[psd.b], w=[sdn.b])
            pso = ps_alloc("D")
            for nn in range(n):
                for kv in range(2):
                    o = pso[:, kv * 64:(kv + 1) * 64].rearrange("p (a n) -> p a n", n=n)[:, :, nn]
                    r_ = pSb[:, kv * 64:(kv + 1) * 64].rearrange("p (a n) -> p a n", n=n)[:, :, nn]
                    S.op("pe", lambda e, o=o, r_=r_, nn=nn, kv=kv: e.matmul(o, lhsT=cv[:, nn * 2 + kv, :], rhs=r_, start=True, stop=True),
                         r=[cv.b, pSb.b], w=[pso.b], sig=(nn == n - 1 and kv == 1))
            S.op("dve", lambda e: e.tensor_copy(out=sso[:], in_=pso[:, 0:128]), r=[pso.b], w=[sso.b])
            for par in range(2):
                rows = slice(par * 64, (par + 1) * 64)
                for kv in range(2):
                    for jj in range(2):
                        blk = 2 * kv + jj
                        c0 = ((kv * 2 + par) * 2 + jj) * n
                        a1 = st1[rows, :]
                        a2 = st2[rows, :]
                        S.op("dve", lambda e, a1=a1, rows=rows, blk=blk, kv=kv: e.tensor_tensor(out=a1, in0=pbs[rows, blk, :], in1=vsd[rows, kv, :], op=ALU.mult),
                             r=[pbs.b, vsd.b], w=[st1.b])
                        S.op("dve", lambda e, a1=a1, rows=rows, c0=c0: e.tensor_tensor(out=a1, in0=a1, in1=sso[rows, c0:c0 + n], op=ALU.add),
                             r=[st1.b, sso.b], w=[st1.b])
                        S.op("dve", lambda e, a2=a2, rows=rows, blk=blk, c0=c0: e.tensor_tensor(out=a2, in0=pbs[rows, blk, :], in1=sdn[rows, c0:c0 + n], op=ALU.add),
                             r=[pbs.b, sdn.b], w=[st2.b])
                        S.op("dve", lambda e, a2=a2, rows=rows, blk=blk: e.tensor_scalar(out=a2, in0=a2, scalar1=esks[rows, blk:blk + 1], scalar2=None, op0=ALU.add),
                             r=[st2.b, esks.b], w=[st2.b])
                        S.op("dve", lambda e, a2=a2: e.reciprocal(out=a2, in_=a2), r=[st2.b], w=[st2.b])
                        S.op("dve", lambda e, a1=a1, a2=a2: e.tensor_tensor(out=a1, in0=a1, in1=a2, op=ALU.mult), r=[st1.b, st2.b], w=[st1.b])
                        S.op("dve", lambda e, a1=a1, rows=rows, blk=blk: e.tensor_tensor(out=ys[rows, blk, 0:n], in0=a1, in1=sza[rows, blk, 0:n], op=ALU.mult),
                             r=[st1.b, sza.b], w=[ysb[0]])
            drain(readout_br(l, n, (0, 2), True))
            gelu_glu(l, n)
            drain(readout_br(l, n, (1,), False))
            drain(wo_ln(l, n, [(xt, n, dst)]))

        x1_b = Buf("x1")

        def roundrobin(gens):
            gens = [list(x) for x in gens]
            while gens:
                for item in list(gens):
                    g_, wgt = item
                    try:
                        for _ in range(wgt):
                            next(g_)
                            while wst["hold"]:
                                next(g_)
                    except StopIteration:
                        gens.remove(item)

        for l in range(DEPTH):
            setup_layer(l)
            nt = NTILE if DBG_TILES is None else DBG_TILES
            xload(l, 0)
            if nt > 1:
                xload(l, 1)
            front(l, 0)
            prev_tail = None
            for ti in range(nt):
                gens = [(s5_pipe(TT), 1), (rest_gen(l, ti), REST_STEPS)]
                if prev_tail is not None:
                    gens.append((prev_tail, 3))
                roundrobin(gens)
                if ti + 1 < nt:
                    front(l, ti + 1)
                prev_tail = tail(l, ti)
            if prev_tail is not None:
                drain(prev_tail)
            S.dma("pool", hst_o[l], hst[:], r=[hst.b] + hstb)
            sample_tile(l)
        allb = [x1_b, hst.b, h0.b, kvs_sb.b] + [t.b for t in lnout] + ([] if DBG_NOKV else [kvout.b])
        S.finish(allb)
        for i in range(NDS):
            if S.dcnt[i] > 0:
                S._wait("sp", (("d", i), S.dcnt[i]))
        if not recording:
            print("instructions:", S.ninst, "sem counts:", S.cnt)
    return nc


_PROG = None


def _consts():
    ident = np.eye(128, dtype=np.float32)
    swapm = np.zeros((128, 128), np.float32)
    for m in range(64):
        swapm[64 + m, m] = -1.0
        swapm[m, 64 + m] = 1.0
    s = np.arange(128)[:, None]
    q = np.arange(128)[None, :]
    mprev = (s >= q).astype(np.float32)
    mcur = (s <= q).astype(np.float32)
    masks = np.stack([np.tile(mprev, (1, 4)), np.tile(mcur, (1, 4))]).astype(np.float32)
    ak = np.zeros((2, 256), np.float32)
    ak[0, :128] = np.arange(128) - 128
    ak[0, 128:] = np.arange(128)
    ak[1, :] = 1.0
    slopes = 2.0 ** (-(np.arange(1, 9)))
    aq = np.zeros((2, 2, 2, 2, 128), np.float32)
    for kv in range(2):
        for par in range(2):
            for jj in range(2):
                h = 4 * kv + 2 * jj + par
                aq[0, kv, par, jj, :] = slopes[h]
                aq[1, kv, par, jj, :] = -slopes[h] * np.arange(128)
    rmask = np.zeros((128, 8), np.float32)
    for gl in range(8):
        rmask[gl * 16:(gl + 1) * 16, gl] = 1.0
    triu = (s <= q).astype(np.float32)
    aqs = np.zeros((2, 8), np.float32)
    for kv in range(2):
        for par in range(2):
            for jj in range(2):
                aqs[0, (kv * 2 + par) * 2 + jj] = slopes[4 * kv + 2 * jj + par]
    hsel = np.zeros((128, 2), np.float32)
    hsel[0:64, 0] = 1.0
    hsel[64:128, 1] = 1.0
    signc = np.ones((128, 1), np.float32)
    signc[0:64] = -1.0
    return dict(ident=ident, swapm=swapm, masks=masks, ak=ak, aq=aq.reshape(2, 1024), rmask=rmask, triu=triu,
                aqs=aqs, hsel=hsel, hselT=np.ascontiguousarray(hsel.T), signc=signc)


def _layout_weights(w_in, w_read, w_o, glu_w):
    out = np.zeros((DEPTH, NCH, 128, 8, 512), np.float32)

    def put(l, c, mat, kb0=0):
        k = mat.shape[0] // 128
        out[l, c, :, kb0:kb0 + k, :mat.shape[1]] = mat.reshape(k, 128, mat.shape[1]).transpose(1, 0, 2)

    for l in range(DEPTH):
        W = w_in[l]
        put(l, C_Q, W[:, 0:512])
        k0, k1 = W[:, 512:576], W[:, 576:640]
        v0, v1 = W[:, 640:704], W[:, 704:768]
        put(l, C_KD, np.concatenate([k0, k0, k1, k1, v0, v0, v1, v1], axis=1))
        put(l, C_KV, W[:, 512:768])
        put(l, C_ZA, W[:, 768:1280])
        put(l, C_UB, W[:, 1280:1792])
        put(l, C_ZB, W[:, 1792:2304])
        put(l, C_UC, W[:, 2304:2816])
        put(l, C_VC, W[:, 2816:3328])
        put(l, C_ZC, W[:, 3328:3840])
        for hf in range(2):
            for b in range(3):
                c0 = 3840 + b * 1024 + hf * 512
                put(l, C_G + hf * 3 + b, W[:, c0:c0 + 512])
                put(l, C_R + hf * 3 + b, w_read[l, b][:, hf * 512:(hf + 1) * 512])
            put(l, C_O + hf, w_o[l][:, hf * 512:(hf + 1) * 512])
        put(l, C_GLU, glu_w[l])
    return out.reshape(DEPTH, NCH, 128, 4096)


def _layout_params(inp):
    f = np.float32
    lam_re, lam_im, log_dt = inp["ssm_lambda_re"], inp["ssm_lambda_im"], inp["ssm_log_dt"]
    colp = np.zeros((DEPTH, 128, 96), f)
    rowp = np.zeros((DEPTH, 3, 128, 2048), f)
    for l in range(DEPTH):
        colp[l, :, 0:32] = np.concatenate([lam_re[l].T, lam_re[l].T], 0)
        colp[l, :, 32:64] = np.concatenate([lam_im[l].T, lam_im[l].T], 0)
        colp[l, :, 64:96] = log_dt[l][None, :]
        rowp[l, 0] = lam_re[l].reshape(1, 2048)
        rowp[l, 1] = lam_im[l].reshape(1, 2048)
        rowp[l, 2] = np.repeat(log_dt[l], 64)[None, :]
    bpad = np.zeros((DEPTH, 2, 32, 128, 64), f)
    cpad = np.zeros((DEPTH, 32, 128, 128), f)
    for l in range(DEPTH):
        for g in range(32):
            gl = g % 8
            bpad[l, 0, g, gl * 16:(gl + 1) * 16, :] = inp["ssm_b_re"][l, g].T
            bpad[l, 1, g, gl * 16:(gl + 1) * 16, :] = inp["ssm_b_im"][l, g].T
            cpad[l, g, 0:64, gl * 16:(gl + 1) * 16] = inp["ssm_c_re"][l, g].T
            cpad[l, g, 64:128, gl * 16:(gl + 1) * 16] = inp["ssm_c_im"][l, g].T
    dsk = inp["ssm_d"].reshape(DEPTH, 4, 128).transpose(0, 2, 1).copy()
    glub = inp["glu_b"].reshape(DEPTH, 4, 128).transpose(0, 2, 1).copy()
    sgn = np.zeros((DEPTH, 2, 128, 512), f)
    sgn[:, 0] = inp["sgu_ln_g"][:, None, :]
    sgn[:, 1] = inp["sgu_ln_b"][:, None, :]
    sgw = inp["sgu_w"].transpose(0, 1, 3, 2).copy()
    sgb = np.broadcast_to(inp["sgu_b"].reshape(DEPTH, 1, 512), (DEPTH, 128, 512)).copy()
    lng = np.zeros((DEPTH, 2, 128, 1024), f)
    lng[:, 0] = inp["ln_g"][:, None, :]
    lng[:, 1] = inp["ln_b"][:, None, :]
    snk = np.zeros((DEPTH, 128, 2, 2, 2), f)
    for kv in range(2):
        for par in range(2):
            for jj in range(2):
                h = 4 * kv + 2 * jj + par
                snk[:, :, kv, par, jj] = inp["attn_sinks"][:, h][:, None]
    return dict(colp=colp, rowp=rowp, bpad=bpad, cpad=cpad, dsk=dsk, glub=glub, sgn=sgn, sgw=sgw, sgb=sgb, lng=lng,
                snk=snk.reshape(DEPTH, 128, 8))


def kernel(**inp):
    global _PROG
    inp = {k: np.asarray(v) for k, v in inp.items()}
    if _PROG is None:
        _PROG = build_program()
    nc = _PROG
    consts = _consts()
    wch = _layout_weights(inp["w_in"], inp["w_read"], inp["w_o"], inp["glu_w"])
    prm = _layout_params(inp)
    in_maps = []
    for c in range(8):
        m = dict(consts)
        m.update(prm)
        m["wch"] = wch
        m["xp"] = np.ascontiguousarray(inp["x_prompt"][c % 2])
        sl = slice(c * NS, (c + 1) * NS)
        m["xs"] = np.ascontiguousarray(inp["x_sample"][sl, 0, :])
        ckc = inp["cache_k_win"][:, sl]
        t = ckc.transpose(0, 4, 1, 3, 2).reshape(DEPTH, 64, 32, 128)
        m["ckT"] = np.ascontiguousarray(np.concatenate([t, t], axis=1))
        cvc = inp["cache_v_win"][:, sl]
        t = cvc.transpose(0, 2, 1, 3, 4).reshape(DEPTH, 128, 32, 64)
        m["cvd"] = np.ascontiguousarray(np.concatenate([t, t], axis=3))
        sr = inp["state_ssm_re"][:, sl].transpose(0, 3, 2, 1)
        si = inp["state_ssm_im"][:, sl].transpose(0, 3, 2, 1)
        m["h0"] = np.ascontiguousarray(np.concatenate([sr, si], axis=1))
        m["h0s"] = np.ascontiguousarray(np.concatenate([si, sr], axis=1))
        m["w00"] = np.ascontiguousarray(np.broadcast_to(inp["sgu_w"][:, None, :, 0, 0], (DEPTH, NS, 4)))
        m["b00"] = np.ascontiguousarray(np.broadcast_to(inp["sgu_b"][:, None, :, 0], (DEPTH, 128, 4)))
        es_ = np.zeros((DEPTH, 128, 4), np.float32)
        for blk in range(4):
            es_[:, 0:64, blk] = inp["attn_sinks"][:, 2 * blk][:, None]
            es_[:, 64:128, blk] = inp["attn_sinks"][:, 2 * blk + 1][:, None]
        m["esks"] = es_
        in_maps.append(m)
    res = run_bass_kernel_spmd(nc, in_maps, core_ids=list(range(8)))
    R = res.results
    y_prompt = np.stack([R[0]["yp"], R[1]["yp"]])
    kvw = np.stack([R[0]["kvw"], R[1]["kvw"]], axis=1)
    k_win = kvw[..., 0:128].reshape(DEPTH, 2, 128, 2, 64)
    v_win = kvw[..., 128:256].reshape(DEPTH, 2, 128, 2, 64)
    hs = np.stack([R[0]["hst"], R[1]["hst"]], axis=1)
    h_re = hs[:, :, 0:64, :].transpose(0, 1, 3, 2)
    h_im = hs[:, :, 64:128, :].transpose(0, 1, 3, 2)
    y_sample = np.concatenate([R[c]["ysm"] for c in range(8)], axis=0).reshape(128, 1, D)
    kvsm = np.concatenate([R[c]["kvs"] for c in range(8)], axis=1)
    k_s = kvsm[..., 0:128].reshape(DEPTH, 128, 1, 2, 64)
    v_s = kvsm[..., 128:256].reshape(DEPTH, 128, 1, 2, 64)
    hsm = np.concatenate([R[c]["hss"] for c in range(8)], axis=3)
    hs_re = hsm[:, 0:64].transpose(0, 3, 2, 1)
    hs_im = hsm[:, 64:128].transpose(0, 3, 2, 1)
    vn_s = np.concatenate([R[c]["vns"] for c in range(8)], axis=1).reshape(DEPTH, 128, 1, 512)
    f = np.float32
    outs = (y_prompt, y_sample, k_win, v_win, h_re, h_im, k_s, v_s, hs_re, hs_im, vn_s)
    return tuple(np.ascontiguousarray(o, dtype=f) for o in outs)
```

```python
import math
from contextlib import ExitStack

import numpy as np
import concourse.bass as bass
import concourse.mybir as mybir
from concourse.bass_utils import run_bass_kernel_spmd

F32 = mybir.dt.float32
BF16 = mybir.dt.bfloat16
AF = mybir.ActivationFunctionType
ALU = mybir.AluOpType

D = 1024
SEQ = 8192
DEPTH = 2
NS = 16
BW = 512
TT = 256
NBK = TT // 128
NTILE = SEQ // TT
NCH = 24
ALPHA = (2 * DEPTH) ** 0.25
LN_EPS = 1e-5
GK = 0.044715
GS = 2.0 * math.sqrt(2.0 / math.pi)
NDS = 24
DBG_TILES = None
DBG_NOKV = False
DBG_SKIPKV = False

C_Q, C_KD, C_KV, C_ZA, C_UB, C_ZB, C_UC, C_VC, C_ZC = range(9)
C_G = 9
C_R = 15
C_O = 21
C_GLU = 23


class Buf:
    __slots__ = ("name", "w", "r", "excl")

    def __init__(self, name, excl=False):
        self.name = name
        self.w = None
        self.r = {}
        self.excl = excl


class Sched:
    def __init__(self, nc, es):
        self.nc = nc
        self.engs = {"pe": nc.tensor, "dve": nc.vector, "act": nc.scalar, "pool": nc.gpsimd, "sp": nc.sync}
        self.sem = {k: es.enter_context(nc.semaphore("sem_" + k)) for k in ["pe", "dve", "act", "pool"]}
        self.cnt = {k: 0 for k in self.sem}
        self.seen = {e: {} for e in self.engs}
        self.dsem = [es.enter_context(nc.semaphore("dsem%d" % i)) for i in range(NDS)]
        self.dcnt = [0] * NDS
        self.drange = {"sp": (0, 16), "pool": (16, NDS)}
        self.drr = {"sp": 0, "pool": 16}
        self.pend = []
        self.pendset = set()
        self.ninst = 0

    def _wait(self, e, ev):
        if ev is None:
            return
        key, val = ev
        if key == e and e == "pe":
            return
        if self.seen[e].get(key, 0) >= val:
            return
        sem = self.sem[key] if isinstance(key, str) else self.dsem[key[1]]
        self.engs[e].wait_ge(sem, val)
        self.seen[e][key] = val

    def _deps(self, e, r, w):
        for b in r:
            assert id(b) not in self.pendset or e == "pe", b.name
            self._wait(e, b.w)
            if b.excl:
                for k, v in list(b.r.items()):
                    if k != e:
                        self._wait(e, (k, v))
        for b in w:
            assert id(b) not in self.pendset or e == "pe", b.name
            self._wait(e, b.w)
            for k, v in list(b.r.items()):
                self._wait(e, (k, v))

    def _commit(self, ev, r, w):
        for b in w:
            b.w = ev
            b.r = {}
        for b in r:
            if all(b is not x for x in w):
                b.r[ev[0]] = ev[1]

    def op(self, e, fn, r=(), w=(), sig=True):
        self._deps(e, r, w)
        inst = fn(self.engs[e])
        self.ninst += 1
        if e == "pe" and not sig:
            for b in r:
                self.pend.append((b, 0))
                self.pendset.add(id(b))
            for b in w:
                self.pend.append((b, 1))
                self.pendset.add(id(b))
            return inst
        self.cnt[e] += 1
        inst.then_inc(self.sem[e], 1)
        ev = (e, self.cnt[e])
        rr = list(r)
        ww = list(w)
        if e == "pe":
            for b, k in self.pend:
                (ww if k else rr).append(b)
            self.pend = []
            self.pendset = set()
        self._commit(ev, rr, ww)
        return inst

    def dma(self, q, out_ap, in_ap, r=(), w=()):
        lo, hi = self.drange[q]
        i = self.drr[q]
        self.drr[q] = lo + (i + 1 - lo) % (hi - lo)
        if self.dcnt[i] > 0:
            self._wait(q, (("d", i), self.dcnt[i]))
        self._deps(q, r, w)
        inst = self.engs[q].dma_start(out=out_ap, in_=in_ap)
        self.ninst += 1
        self.dcnt[i] += 16
        inst.then_inc(self.dsem[i], 16)
        self._commit((("d", i), self.dcnt[i]), list(r), list(w))
        return inst

    def finish(self, bufs):
        for b in bufs:
            self._wait("sp", b.w)


class T:
    def __init__(self, t, name):
        self.t = t
        self.b = Buf(name)

    def __getitem__(self, k):
        return self.t[k]


def build_program():
    rec = []
    _build(rec, True)
    return _build(rec, False)


def _build(order, recording):
    nc = bass.Bass("TRN2", target_bir_lowering=False)
    es = ExitStack()
    with es:
        def din(name, shape):
            return nc.dram_tensor(name, list(shape), F32, kind="ExternalInput").ap()

        def dout(name, shape):
            return nc.dram_tensor(name, list(shape), F32, kind="ExternalOutput").ap()

        xp = din("xp", [SEQ, D])
        xs = din("xs", [NS, D])
        wch = din("wch", [DEPTH, NCH, 128, 4096])
        ident_d = din("ident", [128, 128])
        swap_d = din("swapm", [128, 128])
        masks_d = din("masks", [2, 128, 512])
        ak_d = din("ak", [2, 256])
        aq_d = din("aq", [2, 1024])
        colp_d = din("colp", [DEPTH, 128, 3 * 32])
        rowp_d = din("rowp", [DEPTH, 3, 128, 2048])
        bpad_d = din("bpad", [DEPTH, 2, 32, 128, 64])
        cpad_d = din("cpad", [DEPTH, 32, 128, 128])
        rmask_d = din("rmask", [128, 8])
        dsk_d = din("dsk", [DEPTH, 128, 4])
        glub_d = din("glub", [DEPTH, 128, 4])
        sgn_d = din("sgn", [DEPTH, 2, 128, 512])
        sgw_d = din("sgw", [DEPTH, 4, 128, 128])
        triu_d = din("triu", [128, 128])
        sgb_d = din("sgb", [DEPTH, 128, 512])
        lng_d = din("lng", [DEPTH, 2, 128, 1024])
        snk_d = din("snk", [DEPTH, 128, 8])

        ckT_d = din("ckT", [DEPTH, 128, 32, 128])
        cvd_d = din("cvd", [DEPTH, 128, 32, 128])
        h0_d = din("h0", [DEPTH, 128, 32, NS])
        h0s_d = din("h0s", [DEPTH, 128, 32, NS])
        aqs_d = din("aqs", [2, 8])
        hsel_d = din("hsel", [128, 2])
        hselT_d = din("hselT", [2, 128])
        signc_d = din("signc", [128, 1])
        w00_d = din("w00", [DEPTH, NS, 4])
        b00_d = din("b00", [DEPTH, 128, 4])
        esks_d = din("esks", [DEPTH, 128, 4])
        ysm = dout("ysm", [NS, D])
        kvs = dout("kvs", [DEPTH, NS, 256])
        hss = dout("hss", [DEPTH, 128, 32, NS])
        vns = dout("vns", [DEPTH, NS, 512])
        xs1 = nc.dram_tensor("xs1", [NS, D], F32).ap()
        yp = dout("yp", [SEQ, D])
        kvw = dout("kvw", [DEPTH, 128, 256])
        hst_o = dout("hst", [DEPTH, 128, 32])

        x1 = nc.dram_tensor("x1", [SEQ, D], F32).ap()
        wbf = nc.dram_tensor("wbf", [DEPTH, NCH, 128, 4096], BF16).ap()

        S = Sched(nc, es)

        def sb(name, shape, dt=F32):
            return T(es.enter_context(nc.sbuf_tensor("s_" + name, list(shape), dt)), name)

        ident = sb("ident", [128, 128])
        swapm = sb("swapm", [128, 128])
        masks = sb("masks", [128, 2, 128], BF16)
        ak = sb("ak", [2, 256], BF16)
        aq = sb("aq", [2, 1024], BF16)
        ones = sb("ones", [128, 128], BF16)
        rmask = sb("rmask", [128, 8])
        xtok = [sb("xtok%d" % i, [128, D]) for i in range(2 * NBK)]
        xT = sb("xT", [128, 8, TT], BF16)
        wbuf = [sb("wbuf%d" % i, [128, 4096], BF16) for i in range(3)]
        qT = sb("qT", [128, 4, TT], BF16)
        kT = sb("kT", [128, 2, TT + 128], BF16)
        vtok = sb("vtok", [128, NBK + 1, 2, 128], BF16)
        sza = sb("sza", [128, 4, TT], BF16)
        szb = sb("szb", [128, 4, TT], BF16)
        uT = sb("uT", [128, 4, TT], BF16)
        ucz = sb("ucz", [128, 4, TT], BF16)
        vn = sb("vn", [128, 4, BW], BF16)
        ys = sb("ys", [128, 12, TT], BF16)
        ypre = sb("ypre", [128, 4, TT])
        gtmp = sb("gtmp", [128, 4, TT])
        gtmp2 = sb("gtmp2", [128, 4, TT], BF16)
        sig = sb("sig", [128, 4, TT], BF16)
        mtmp = [sb("mtmp%d" % i, [128, 512]) for i in range(2)]
        mrgT = sb("mrgT", [128, 8, TT], BF16)
        pT = [[sb("pT%d%d" % (kv, pt), [128, 512], BF16) for pt in range(2)] for kv in range(2)]
        rden = [sb("rden%d" % kv, [128, 512]) for kv in range(2)]
        otmp = [sb("otmp%d" % kv, [128, 512], BF16) for kv in range(2)]
        lnst = sb("lnst", [128, 12])
        lnmv = sb("lnmv", [128, 4])
        lntmp = sb("lntmp", [128, 512])
        lnout = [sb("lnout%d" % i, [128, D]) for i in range(2)]
        colp = sb("colp", [128, 96])
        cdt = sb("cdt", [128, 32])
        crr = sb("crr", [128, 32])
        cth = sb("cth", [128, 32])
        ctmp = sb("ctmp", [128, 32])
        cc1 = sb("cc1", [128, 32])
        ctmp2 = sb("ctmp2", [128, 32])
        citmp = sb("citmp", [128, 32], mybir.dt.int32)
        cs1 = sb("cs1", [128, 32])
        tabc = sb("tabc", [128, 32, TT], BF16)
        tabs = sb("tabs", [128, 32, TT], BF16)
        bbw = sb("bbw", [128, 32, 128], BF16)
        bbs = sb("bbs", [128, 32, 128], BF16)
        cpw = sb("cpw", [128, 32, 128], BF16)
        dsk = sb("dsk", [128, 4])
        glub = sb("glub", [128, 4])
        sgn = sb("sgn", [128, 2, 512])
        sgw = sb("sgw", [128, 4, 128], BF16)
        triu = sb("triu", [128, 128])
        sgb = sb("sgb", [128, 512])
        lng = sb("lng", [128, 2, 1024])
        esk = sb("esk", [128, 8])
        sigB = sb("sigB", [128, 8, TT], BF16)
        hst = sb("hst", [128, 32])
        s5a = [sb("s5a%d" % i, [128, TT]) for i in range(2)]
        s5b = [sb("s5b%d" % i, [128, TT]) for i in range(2)]
        s5v = [sb("s5v%d" % i, [128, TT]) for i in range(2)]
        s5g = [sb("s5g%d" % i, [128, TT]) for i in range(2)]
        s5c = [sb("s5c%d" % i, [128, TT]) for i in range(3)]
        s5d = [sb("s5d%d" % i, [128, TT]) for i in range(2)]
        s5hb = [sb("s5hb%d" % i, [128, TT], BF16) for i in range(2)]

        aqs = sb("aqs", [2, 8], BF16)
        hsel = sb("hsel", [128, 2], BF16)
        hselT = sb("hselT", [2, 128], BF16)
        signc = sb("signc", [128, 1])
        w00 = sb("w00", [NS, 4])
        w00I = sb("w00I", [NS, 4, NS], BF16)
        b00 = sb("b00", [128, 4])
        esks = sb("esks", [128, 4])
        car = sb("car", [128, 32])
        cais = sb("cais", [128, 32])
        ksd = sb("ksd", [128, 2, NS], BF16)
        vsd = sb("vsd", [128, 2, NS])
        prod = sb("prod", [128, 4, NS], BF16)
        pself = sb("pself", [2, 64], BF16)
        pbs = sb("pbs", [128, 4, NS])
        st1 = sb("st1", [128, NS])
        st2 = sb("st2", [128, NS])

        class AV:
            def __init__(self, ap, b):
                self.ap = ap
                self.b = b

            def __getitem__(self, k):
                return self.ap[k]

        ck = AV(tabc[:, :, 0:128], tabc.b)
        kvout = AV(lnout[1][:, 0:256], lnout[1].b)
        kvs_sb = AV(rden[0][0:NS, 0:256], rden[0].b)
        vnS = AV(otmp[0][0:NS, :], otmp[0].b)
        cv = AV(tabs[:, :, 0:128], tabs.b)
        h0 = AV(mtmp[0][:, 0:32 * NS].rearrange("p (g n) -> p g n", n=NS), mtmp[0].b)
        h0s = AV(mtmp[1][:, 0:32 * NS].rearrange("p (g n) -> p g n", n=NS), mtmp[1].b)
        hbs = AV(pT[0][0][:, 0:32 * NS].rearrange("p (g n) -> p g n", n=NS), pT[0][0].b)
        pSb = AV(pT[0][1][:, 0:128], pT[0][1].b)
        sdn = AV(rden[0][:, 0:128], rden[0].b)
        sso = AV(rden[1][:, 0:128], rden[1].b)

        psum = [T(es.enter_context(nc.psum_tensor("ps%d" % i, [128, 512], F32)), "ps%d" % i) for i in range(8)]
        for p_ in psum:
            p_.b.excl = True
        pools = {"A": [0, 1], "B": [2, 3], "C": [4], "Y": [5], "D": [6, 7]}
        prr = {k: 0 for k in pools}

        def ps_alloc(pool):
            i = prr[pool]
            prr[pool] = (i + 1) % len(pools[pool])
            return psum[pools[pool][i]]

        rr = {}

        def rot(lst, key):
            i = rr.get(key, 0)
            rr[key] = (i + 1) % len(lst)
            return lst[i]

        wbf_b = [[Buf("wbf%d_%d" % (l, c)) for c in range(NCH)] for l in range(DEPTH)]
        for l in range(DEPTH):
            for c in range(NCH):
                S.dma("pool", wbf[l, c], wch[l, c], w=[wbf_b[l][c]])

        S.dma("sp", ident[:], ident_d, w=[ident.b])
        S.dma("sp", swapm[:], swap_d, w=[swapm.b])
        S.dma("pool", masks[:], masks_d.rearrange("a p c -> p a c")[:, :, 0:128], w=[masks.b])
        S.dma("pool", ak[:], ak_d, w=[ak.b])
        S.dma("pool", aq[:], aq_d, w=[aq.b])
        S.dma("sp", rmask[:], rmask_d, w=[rmask.b])
        S.dma("sp", triu[:], triu_d, w=[triu.b])
        S.op("dve", lambda e: e.memset(ones[:], 1.0), w=[ones.b])
        S.dma("pool", aqs[:], aqs_d, w=[aqs.b])
        S.dma("pool", hsel[:], hsel_d, w=[hsel.b])
        S.dma("pool", hselT[:], hselT_d, w=[hselT.b])
        S.dma("sp", signc[:], signc_d, w=[signc.b])

        def nk_of(c):
            return 4 if (c == C_GLU or C_R <= c < C_R + 6) else 8

        wst = {"issued": 0, "used": 0, "hold": False}

        def w_issue():
            i = wst["issued"]
            if i >= len(order):
                return
            l, c, nk = order[i]
            wb = wbuf[i % 3]
            S.dma("sp", wb[:, 0:nk * 512], wbf[l, c, :, 0:nk * 512], r=[wbf_b[l][c]], w=[wb.b])
            wst["issued"] = i + 1

        def w_next(l, c):
            i = wst["used"]
            if recording:
                order.append((l, c, nk_of(c)))
            assert order[i][0] == l and order[i][1] == c, (order[i], l, c)
            while wst["issued"] < min(i + (1 if recording else 2), len(order)):
                w_issue()
            wst["used"] = i + 1
            wst["hold"] = True
            return wbuf[i % 3]

        def w_done():
            wst["hold"] = False
            if recording:
                return
            while wst["issued"] < min(wst["used"] + 2, len(order)):
                w_issue()

        def evac(eng, fn, r, w):
            S.op(eng, fn, r=r, w=w)

        def mm_fm(ps, wb, j, nk, rhs_fn, rbufs, n):
            for kb in range(nk):
                S.op("pe", lambda e, kb=kb: e.matmul(ps[:, 0:n], lhsT=wb[:, kb * 512 + j * 128: kb * 512 + (j + 1) * 128],
                                                     rhs=rhs_fn(kb), start=(kb == 0), stop=(kb == nk - 1)),
                     r=[wb.b] + rbufs, w=[ps.b], sig=(kb == nk - 1))

        def v2(t, width):
            ap = t[:]
            if len(ap.shape) == 3:
                ap = ap.rearrange("p a b -> p (a b)")
            return ap[:, 0:width]

        def setup_layer(l):
            S.dma("sp", colp[:], colp_d[l], w=[colp.b])
            S.dma("pool", cpw[:], cpad_d[l].rearrange("g p m -> p g m"), w=[cpw.b])
            S.dma("sp", dsk[:], dsk_d[l], w=[dsk.b])
            S.dma("sp", glub[:], glub_d[l], w=[glub.b])
            S.dma("sp", sgn[:], sgn_d[l].rearrange("a p c -> p a c"), w=[sgn.b])
            sgw32 = v2(ypre, 512).rearrange("p (g t) -> p g t", g=4)
            S.dma("sp", sgw32, sgw_d[l].rearrange("g s t -> s g t"), w=[ypre.b])
            S.dma("sp", sgb[:], sgb_d[l], w=[sgb.b])
            S.dma("sp", lng[:], lng_d[l].rearrange("a p c -> p a c"), w=[lng.b])
            S.dma("sp", esk[:], snk_d[l], w=[esk.b])
            for g4 in range(4):
                S.op("dve", lambda e, g4=g4: e.tensor_tensor(out=sgw[:, g4, :], in0=sgw32[:, g4, :], in1=triu[:], op=ALU.mult),
                     r=[ypre.b, triu.b], w=[sgw.b])
            S.op("act", lambda e: e.activation(out=esk[:], in_=esk[:], func=AF.Exp), r=[esk.b], w=[esk.b])
            S.op("dve", lambda e: e.tensor_scalar(out=cpw[64:128], in0=cpw[64:128], scalar1=-1.0, scalar2=None, op0=ALU.mult),
                 r=[cpw.b], w=[cpw.b])
            S.op("dve", lambda e: e.memset(hst[:], 0.0), w=[hst.b] + hstb)

            lr_c, li_c, ld_c = colp[:, 0:32], colp[:, 32:64], colp[:, 64:96]
            S.op("act", lambda e: e.activation(out=cdt[:], in_=ld_c, func=AF.Exp), r=[colp.b], w=[cdt.b])
            S.op("dve", lambda e: e.tensor_tensor(out=ctmp[:], in0=lr_c, in1=cdt[:], op=ALU.mult), r=[colp.b, cdt.b], w=[ctmp.b])
            S.op("act", lambda e: e.activation(out=crr[:], in_=ctmp[:], func=AF.Exp), r=[ctmp.b], w=[crr.b])
            S.op("dve", lambda e: e.tensor_tensor(out=cth[:], in0=li_c, in1=cdt[:], op=ALU.mult), r=[colp.b, cdt.b], w=[cth.b])

            def sincos(out_s, out_c, th, thb, tmp, tb, m, mb, itmp, ib, sb_, cb):
                I32 = mybir.dt.int32
                for dst, db, shift in ((out_s, sb_, 0.0), (out_c, cb, 0.5 * math.pi)):
                    S.op("dve", lambda e, shift=shift: e.tensor_scalar(out=tmp, in0=th, scalar1=shift, scalar2=1.0 / (2 * math.pi),
                                                                       op0=ALU.add, op1=ALU.mult), r=[thb], w=[tb])
                    S.op("dve", lambda e: e.tensor_copy(out=itmp, in_=tmp), r=[tb], w=[ib])
                    S.op("dve", lambda e: e.tensor_copy(out=tmp, in_=itmp), r=[ib], w=[tb])
                    S.op("dve", lambda e: e.scalar_tensor_tensor(out=tmp, in0=tmp, scalar=-2 * math.pi, in1=th, op0=ALU.mult, op1=ALU.add),
                         r=[tb, thb], w=[tb])
                    S.op("dve", lambda e, shift=shift: e.tensor_scalar(out=tmp, in0=tmp, scalar1=shift, scalar2=None, op0=ALU.add),
                         r=[tb], w=[tb])
                    S.op("dve", lambda e: e.tensor_scalar(out=m, in0=tmp, scalar1=math.pi, scalar2=-2 * math.pi, op0=ALU.is_gt, op1=ALU.mult),
                         r=[tb], w=[mb])
                    S.op("dve", lambda e: e.tensor_tensor(out=tmp, in0=tmp, in1=m, op=ALU.add), r=[tb, mb], w=[tb])
                    S.op("dve", lambda e: e.tensor_scalar(out=m, in0=tmp, scalar1=-math.pi, scalar2=2 * math.pi, op0=ALU.is_lt, op1=ALU.mult),
                         r=[tb], w=[mb])
                    S.op("dve", lambda e: e.tensor_tensor(out=tmp, in0=tmp, in1=m, op=ALU.add), r=[tb, mb], w=[tb])
                    S.op("dve", lambda e: e.tensor_scalar(out=tmp, in0=tmp, scalar1=math.pi, scalar2=-math.pi, op0=ALU.min, op1=ALU.max),
                         r=[tb], w=[tb])
                    S.op("act", lambda e, dst=dst: e.activation(out=dst, in_=tmp, func=AF.Sin), r=[tb], w=[db])

            sincos(cs1[:], cc1[:], cth[:], cth.b, ctmp[:], ctmp.b, ctmp2[:], ctmp2.b, citmp[:], citmp.b, cs1.b, cc1.b)
            S.op("dve", lambda e: e.tensor_tensor(out=car[:], in0=crr[:], in1=cc1[:], op=ALU.mult), r=[crr.b, cc1.b], w=[car.b])
            S.op("dve", lambda e: e.tensor_tensor(out=cais[:], in0=crr[:], in1=cs1[:], op=ALU.mult), r=[crr.b, cs1.b], w=[cais.b])
            S.op("dve", lambda e: e.tensor_scalar(out=cais[:], in0=cais[:], scalar1=signc[:, 0:1], scalar2=None, op0=ALU.mult),
                 r=[cais.b, signc.b], w=[cais.b])
            S.dma("sp", w00[:], w00_d[l], w=[w00.b])
            S.dma("sp", b00[:], b00_d[l], w=[b00.b])
            S.dma("sp", esks[:], esks_d[l], w=[esks.b])
            S.op("act", lambda e: e.activation(out=esks[:], in_=esks[:], func=AF.Exp), r=[esks.b], w=[esks.b])
            for g4 in range(4):
                S.op("dve", lambda e, g4=g4: e.tensor_scalar(out=w00I[:, g4, :], in0=ident[0:NS, 0:NS], scalar1=w00[:, g4:g4 + 1],
                                                             scalar2=None, op0=ALU.mult), r=[ident.b, w00.b], w=[w00I.b])
            for qq in range(8):
                gs = slice(qq * 4, (qq + 1) * 4)
                ccos = v2(ypre, 4 * TT).rearrange("p (g t) -> p g t", g=4)
                csin = v2(gtmp, 4 * TT).rearrange("p (g t) -> p g t", g=4)
                ctm = v2(mtmp[0], 2 * TT).rearrange("p (g t) -> p g t", g=4)
                ctm2 = v2(mtmp[1], 2 * TT).rearrange("p (g t) -> p g t", g=4)
                S.op("dve", lambda e: e.tensor_copy(out=ccos[:, :, 0], in_=cc1[:, gs]), r=[cc1.b], w=[ypre.b])
                S.op("dve", lambda e: e.tensor_copy(out=csin[:, :, 0], in_=cs1[:, gs]), r=[cs1.b], w=[gtmp.b])
                n = 1
                while n < TT:
                    cn = ccos[:, :, n - 1:n].to_broadcast([128, 4, n])
                    sn = csin[:, :, n - 1:n].to_broadcast([128, 4, n])
                    c0 = ccos[:, :, 0:n]
                    s0 = csin[:, :, 0:n]
                    t1 = ctm[:, :, 0:n]
                    t2 = ctm2[:, :, 0:n]
                    S.op("dve", lambda e, c0=c0, cn=cn, t1=t1: e.tensor_tensor(out=t1, in0=c0, in1=cn, op=ALU.mult), r=[ypre.b], w=[mtmp[0].b])
                    S.op("dve", lambda e, s0=s0, sn=sn, t2=t2: e.tensor_tensor(out=t2, in0=s0, in1=sn, op=ALU.mult), r=[gtmp.b], w=[mtmp[1].b])
                    S.op("dve", lambda e, t1=t1, t2=t2, n=n: e.tensor_tensor(out=ccos[:, :, n:2 * n], in0=t1, in1=t2, op=ALU.subtract),
                         r=[mtmp[0].b, mtmp[1].b], w=[ypre.b])
                    S.op("dve", lambda e, c0=c0, sn=sn, t1=t1: e.tensor_tensor(out=t1, in0=c0, in1=sn, op=ALU.mult), r=[ypre.b, gtmp.b], w=[mtmp[0].b])
                    S.op("dve", lambda e, s0=s0, cn=cn, t2=t2: e.tensor_tensor(out=t2, in0=s0, in1=cn, op=ALU.mult), r=[ypre.b, gtmp.b], w=[mtmp[1].b])
                    S.op("dve", lambda e, t1=t1, t2=t2, n=n: e.tensor_tensor(out=csin[:, :, n:2 * n], in0=t1, in1=t2, op=ALU.add),
                         r=[mtmp[0].b, mtmp[1].b], w=[gtmp.b])
                    n *= 2
                S.op("act", lambda e: e.copy(out=tabc[:, gs, :], in_=ccos), r=[ypre.b], w=[tabc.b])
                S.op("act", lambda e: e.copy(out=tabs[:, gs, :], in_=csin), r=[gtmp.b], w=[tabs.b])

            tl = [T(None, "x")] * 0
            hold = [(xtok[0], 0), (xtok[0], 512), (xtok[1], 0), (xtok[1], 512), (lnout[0], 0), (lnout[0], 512),
                    (lnout[1], 0), (lnout[1], 512)]

            class V:
                def __init__(self, t, off):
                    self.ap = t[:, off:off + 512]
                    self.b = t.b

            for qq in range(4):
                gs = slice(qq * 8, (qq + 1) * 8)
                cs_q = slice(qq * 512, (qq + 1) * 512)
                lr, li, ld, dt_, mg, th, sn_, cs_ = [V(t, o) for t, o in hold]
                for i, dst in enumerate((lr, li, ld)):
                    S.dma("sp", dst.ap, rowp_d[l, i][:, cs_q], w=[dst.b])
                bld = v2(xtok[2], 1024).rearrange("k (a g p) -> k a g p", a=2, g=8)
                for a in range(2):
                    S.dma("sp", bld[:, a], bpad_d[l, a, qq * 8:(qq + 1) * 8].rearrange("g k p -> k g p"), w=[xtok[2].b])

                def tt(out, a, b_, op):
                    S.op("dve", lambda e: e.tensor_tensor(out=out.ap, in0=a.ap, in1=b_.ap, op=op), r=[a.b, b_.b], w=[out.b])

                S.op("act", lambda e: e.activation(out=dt_.ap, in_=ld.ap, func=AF.Exp), r=[ld.b], w=[dt_.b])
                tt(mg, lr, dt_, ALU.mult)
                S.op("act", lambda e: e.activation(out=mg.ap, in_=mg.ap, func=AF.Exp), r=[mg.b], w=[mg.b])
                tt(th, li, dt_, ALU.mult)
                sincos(sn_.ap, cs_.ap, th.ap, th.b, ld.ap, ld.b, v2(gtmp, 512), gtmp.b, v2(ypre, 512).bitcast(mybir.dt.int32), ypre.b, sn_.b, cs_.b)
                tt(cs_, cs_, mg, ALU.mult)
                S.op("dve", lambda e: e.tensor_scalar(out=cs_.ap, in0=cs_.ap, scalar1=-1.0, scalar2=None, op0=ALU.add), r=[cs_.b], w=[cs_.b])
                tt(sn_, sn_, mg, ALU.mult)
                tt(mg, lr, lr, ALU.mult)
                tt(th, li, li, ALU.mult)
                tt(mg, mg, th, ALU.add)
                S.op("dve", lambda e: e.reciprocal(out=mg.ap, in_=mg.ap), r=[mg.b], w=[mg.b])
                tt(dt_, cs_, lr, ALU.mult)
                tt(ld, sn_, li, ALU.mult)
                tt(dt_, dt_, ld, ALU.add)
                tt(dt_, dt_, mg, ALU.mult)
                tt(th, sn_, lr, ALU.mult)
                tt(ld, cs_, li, ALU.mult)
                tt(th, th, ld, ALU.subtract)
                tt(th, th, mg, ALU.mult)
                crv = dt_.ap.rearrange("k (g p) -> k g p", p=64)
                civ = th.ap.rearrange("k (g p) -> k g p", p=64)
                t1 = sn_.ap.rearrange("k (g p) -> k g p", p=64)
                t2 = cs_.ap.rearrange("k (g p) -> k g p", p=64)
                bre = bld[:, 0]
                bim = bld[:, 1]
                S.op("dve", lambda e: e.tensor_tensor(out=t1, in0=crv, in1=bre, op=ALU.mult), r=[dt_.b, xtok[2].b], w=[sn_.b])
                S.op("dve", lambda e: e.tensor_tensor(out=t2, in0=civ, in1=bim, op=ALU.mult), r=[th.b, xtok[2].b], w=[cs_.b])
                S.op("dve", lambda e: e.tensor_tensor(out=bbw[:, gs, 0:64], in0=t1, in1=t2, op=ALU.subtract), r=[sn_.b, cs_.b], w=[bbw.b])
                S.op("dve", lambda e: e.tensor_tensor(out=bbs[:, gs, 64:128], in0=t2, in1=t1, op=ALU.subtract), r=[sn_.b, cs_.b], w=[bbs.b])
                S.op("dve", lambda e: e.tensor_tensor(out=t1, in0=crv, in1=bim, op=ALU.mult), r=[dt_.b, xtok[2].b], w=[sn_.b])
                S.op("dve", lambda e: e.tensor_tensor(out=t2, in0=civ, in1=bre, op=ALU.mult), r=[th.b, xtok[2].b], w=[cs_.b])
                S.op("dve", lambda e: e.tensor_tensor(out=bbw[:, gs, 64:128], in0=t1, in1=t2, op=ALU.add), r=[sn_.b, cs_.b], w=[bbw.b])
                S.op("dve", lambda e: e.tensor_tensor(out=bbs[:, gs, 0:64], in0=t1, in1=t2, op=ALU.add), r=[sn_.b, cs_.b], w=[bbs.b])

        hstb = [Buf("hst%d" % g) for g in range(32)]
        ysb = [Buf("ys_a"), Buf("ys_b"), Buf("ys_c")]

        def s5_pipe(n):
            cols = slice(0, n)
            st = {}
            psy_cur = {}
            for it in range(32 + 7):
                g = it - 7
                if 0 <= g < 32:
                    fb, gl = g // 8, g % 8
                    if gl == 0:
                        psy_cur["p"] = ps_alloc("Y")
                    psy = psy_cur["p"]
                    hb = st[g]["hb"]
                    S.op("pe", lambda e, g=g, gl=gl, psy=psy, hb=hb: e.matmul(psy[:, 0:n], lhsT=cpw[:, g, :], rhs=hb[:, 0:n],
                                                                               start=(gl == 0), stop=(gl == 7)),
                         r=[cpw.b, hb.b], w=[psy.b], sig=True)
                    if gl == 7:
                        S.op("dve", lambda e, fb=fb, psy=psy: e.scalar_tensor_tensor(out=ypre[:, fb, cols], in0=uT[:, fb, cols],
                                                                                    scalar=dsk[:, fb:fb + 1], in1=psy[:, 0:n],
                                                                                    op0=ALU.mult, op1=ALU.add),
                             r=[uT.b, dsk.b, psy.b], w=[ypre.b])
                    del st[g]
                g = it - 6
                if 0 <= g < 32:
                    c, d = st[g]["c"], st[g]["d"]
                    hb = rot(s5hb, "hb")
                    st[g]["hb"] = hb
                    S.op("pool", lambda e, c=c, d=d, hb=hb: e.tensor_tensor(out=hb[:, 0:n], in0=c[:, 0:n], in1=d[:, 0:n], op=ALU.add),
                         r=[c.b, d.b], w=[hb.b])
                    S.op("pool", lambda e, c=c, d=d, g=g: e.tensor_tensor(out=hst[:, g:g + 1], in0=c[:, n - 1:n], in1=d[:, n - 1:n], op=ALU.add),
                         r=[c.b, d.b], w=[hstb[g]])
                g = it - 5
                if 0 <= g < 32:
                    psw = st[g]["psw"]
                    d = rot(s5d, "d")
                    st[g]["d"] = d
                    S.op("dve", lambda e, g=g, psw=psw, d=d: e.tensor_tensor(out=d[:, 0:n], in0=psw[:, 0:n], in1=tabs[:, g, 0:n], op=ALU.mult),
                         r=[psw.b, tabs.b], w=[d.b])
                g = it - 4
                if 0 <= g < 32:
                    gg = st[g]["gg"]
                    psw = ps_alloc("C")
                    st[g]["psw"] = psw
                    S.op("pe", lambda e, gg=gg, psw=psw: e.matmul(psw[:, 0:n], lhsT=swapm[:], rhs=gg[:, 0:n], start=True, stop=True),
                         r=[swapm.b, gg.b], w=[psw.b])
                    c = rot(s5c, "c")
                    st[g]["c"] = c
                    S.op("dve", lambda e, g=g, gg=gg, c=c: e.tensor_tensor(out=c[:, 0:n], in0=gg[:, 0:n], in1=tabc[:, g, 0:n], op=ALU.mult),
                         r=[gg.b, tabc.b], w=[c.b])
                g = it - 3
                if 0 <= g < 32:
                    v = st[g]["v"]
                    gg = rot(s5g, "g")
                    st[g]["gg"] = gg
                    S.op("dve", lambda e, g=g, v=v, gg=gg: e.tensor_tensor_scan(out=gg[:, 0:n], data0=crr[:, g:g + 1].to_broadcast([128, n]),
                                                                               data1=v[:, 0:n], initial=hst[:, g:g + 1], op0=ALU.mult, op1=ALU.add),
                         r=[crr.b, v.b, hstb[g]], w=[gg.b])
                g = it - 2
                if 0 <= g < 32:
                    a, b_ = st[g]["a"], st[g]["b"]
                    v = rot(s5v, "v")
                    st[g]["v"] = v
                    S.op("pool", lambda e, a=a, b_=b_, v=v: e.tensor_tensor(out=v[:, 0:n], in0=a[:, 0:n], in1=b_[:, 0:n], op=ALU.add),
                         r=[a.b, b_.b], w=[v.b])
                g = it - 1
                if 0 <= g < 32:
                    pv = st[g]["pv"]
                    a = rot(s5a, "a")
                    b_ = rot(s5b, "b")
                    st[g]["a"], st[g]["b"] = a, b_
                    S.op("dve", lambda e, g=g, pv=pv, a=a: e.tensor_tensor(out=a[:, 0:n], in0=pv[:, 0:n], in1=tabc[:, g, 0:n], op=ALU.mult),
                         r=[pv.b, tabc.b], w=[a.b])
                    S.op("dve", lambda e, g=g, pv=pv, b_=b_: e.tensor_tensor(out=b_[:, 0:n], in0=pv[:, TT:TT + n], in1=tabs[:, g, 0:n], op=ALU.mult),
                         r=[pv.b, tabs.b], w=[b_.b])
                g = it
                if 0 <= g < 32:
                    fb = g // 8
                    pv = ps_alloc("B")
                    st[g] = {"pv": pv}
                    S.op("pe", lambda e, g=g, fb=fb, pv=pv: e.matmul(pv[:, 0:n], lhsT=bbw[:, g, :], rhs=uT[:, fb, cols], start=True, stop=True),
                         r=[bbw.b, uT.b], w=[pv.b], sig=False)
                    S.op("pe", lambda e, g=g, fb=fb, pv=pv: e.matmul(pv[:, TT:TT + n], lhsT=bbs[:, g, :], rhs=uT[:, fb, cols], start=True, stop=True),
                         r=[bbs.b, uT.b], w=[pv.b])
                yield

        def gelu_glu_gen(l, n):
            yv = ypre[:, :, 0:n]
            g1 = gtmp[:, :, 0:n]
            S.op("act", lambda e: e.activation(out=g1, in_=yv, func=AF.Square, scale=math.sqrt(GK)), r=[ypre.b], w=[gtmp.b])
            yield
            S.op("dve", lambda e: e.scalar_tensor_tensor(out=g1, in0=g1, scalar=1.0, in1=yv, op0=ALU.add, op1=ALU.mult),
                 r=[gtmp.b, ypre.b], w=[gtmp.b])
            yield
            S.op("act", lambda e: e.activation(out=g1, in_=g1, func=AF.Sigmoid, scale=GS), r=[gtmp.b], w=[gtmp.b])
            yield
            S.op("dve", lambda e: e.tensor_tensor(out=yv, in0=yv, in1=g1, op=ALU.mult), r=[gtmp.b, ypre.b], w=[ypre.b])
            yield
            S.op("act", lambda e: e.copy(out=gtmp2[:, :, 0:n], in_=yv), r=[ypre.b], w=[gtmp2.b])
            yield
            wb = w_next(l, C_GLU)
            for j in range(4):
                ps = ps_alloc("A")
                mm_fm(ps, wb, j, 4, lambda kb: gtmp2[:, kb, 0:n], [gtmp2.b], n)
                S.op("act", lambda e, j=j, ps=ps: e.activation(out=gtmp[:, j, 0:n], in_=ps[:, 0:n], func=AF.Sigmoid, bias=glub[:, j:j + 1]),
                     r=[ps.b, glub.b], w=[gtmp.b])
            w_done()
            yield
            S.op("dve", lambda e: e.tensor_tensor(out=yv, in0=yv, in1=g1, op=ALU.mult), r=[gtmp.b, ypre.b], w=[ypre.b])
            yield
            S.op("pool", lambda e: e.tensor_tensor(out=ys[:, 4:8, 0:n], in0=yv, in1=szb[:, :, 0:n], op=ALU.mult),
                 r=[ypre.b, szb.b], w=[ysb[1]])

        def gelu_glu(l, n):
            for _ in gelu_glu_gen(l, n):
                pass

        def layernorm_rows(src, nrow, width, gam, bet, dst, tmp=None, eng2="pool"):
            if tmp is None:
                tmp = (lntmp.b, lntmp[0:nrow, 0:width])
            nchk = width // 512
            for i in range(nchk):
                S.op("dve", lambda e, i=i: e.bn_stats(out=lnst[0:nrow, i * 6:(i + 1) * 6], in_=src[1][:, i * 512:(i + 1) * 512]),
                     r=[src[0]], w=[lnst.b])
            S.op("dve", lambda e: e.bn_aggr(out=lnmv[0:nrow, 0:2], in_=lnst[0:nrow, 0:6 * nchk]), r=[lnst.b], w=[lnmv.b])
            S.op("dve", lambda e: e.tensor_scalar(out=lnmv[0:nrow, 2:3], in0=lnmv[0:nrow, 1:2], scalar1=LN_EPS, scalar2=None,
                                                  op0=ALU.add), r=[lnmv.b], w=[lnmv.b])
            S.op("act", lambda e: e.sqrt(out=lnmv[0:nrow, 2:3], in_=lnmv[0:nrow, 2:3]), r=[lnmv.b], w=[lnmv.b])
            S.op("dve", lambda e: e.reciprocal(out=lnmv[0:nrow, 2:3], in_=lnmv[0:nrow, 2:3]), r=[lnmv.b], w=[lnmv.b])
            S.op("dve", lambda e: e.tensor_scalar(out=tmp[1], in0=src[1], scalar1=lnmv[0:nrow, 0:1],
                                                  scalar2=lnmv[0:nrow, 2:3], op0=ALU.subtract, op1=ALU.mult),
                 r=[src[0], lnmv.b], w=[tmp[0]])
            S.op("dve", lambda e: e.tensor_tensor(out=tmp[1], in0=tmp[1], in1=gam[1], op=ALU.mult),
                 r=[tmp[0], gam[0]], w=[tmp[0]])
            S.op(eng2, lambda e: e.tensor_tensor(out=dst[1], in0=tmp[1], in1=bet[1], op=ALU.add),
                 r=[tmp[0], bet[0]], w=[dst[0]])

        def xload(l, ti):
            t0 = ti * TT
            src = xp if l == 0 else x1
            for b in range(NBK):
                xt = xtok[(ti % 2) * NBK + b]
                S.dma("sp", xt[:], src[t0 + b * 128: t0 + (b + 1) * 128, :], r=([x1_b] if l == 1 else []), w=[xt.b])

        def front(l, ti):
            n = TT
            xts = [xtok[(ti % 2) * NBK + b] for b in range(NBK)]
            for kb in range(8):
                ps = ps_alloc("A")
                for b in range(NBK):
                    S.op("pe", lambda e, b=b, ps=ps, kb=kb: e.transpose(out=ps[:, b * 128:(b + 1) * 128],
                                                                       in_=xts[b][:, kb * 128:(kb + 1) * 128], identity=ident[:]),
                         r=[xts[b].b, ident.b], w=[ps.b], sig=(b == NBK - 1))
                S.op("act", lambda e, ps=ps, kb=kb: e.copy(out=xT[:, kb, :], in_=ps[:, 0:TT]), r=[ps.b], w=[xT.b])
            xrhs = lambda kb: xT[:, kb, :]
            wb = w_next(l, C_UB)
            for j in range(4):
                ps = ps_alloc("A")
                mm_fm(ps, wb, j, 8, xrhs, [xT.b], n)
                S.op("act", lambda e, j=j, ps=ps: e.copy(out=uT[:, j, :], in_=ps[:, 0:TT]), r=[ps.b], w=[uT.b])
            w_done()

        def tail(l, ti):
            t0 = ti * TT
            dst = x1 if l == 0 else yp
            for _ in gelu_glu_gen(l, TT):
                yield
            yield
            for _ in readout_br(l, TT, (1,), False, pre=True):
                yield
            blocks = [(xtok[(ti % 2) * NBK + b], 128, dst[t0 + b * 128: t0 + (b + 1) * 128, :]) for b in range(NBK)]
            for _ in wo_ln(l, TT, blocks):
                yield
            if ti + 2 < (NTILE if DBG_TILES is None else DBG_TILES):
                xload(l, ti + 2)

        def rest_gen(l, ti):
            n = TT
            xrhs = lambda kb: xT[:, kb, :]

            wb = w_next(l, C_Q)
            for j in range(4):
                ps = ps_alloc("A")
                mm_fm(ps, wb, j, 8, xrhs, [xT.b], n)
                S.op("act", lambda e, j=j, ps=ps: e.activation(out=qT[:, j, :], in_=ps[:, 0:TT], func=AF.Copy, scale=0.125),
                     r=[ps.b], w=[qT.b])
                yield
            w_done()
            yield
            wb = w_next(l, C_KD)
            for j in range(2):
                ps = ps_alloc("A")
                mm_fm(ps, wb, j, 8, xrhs, [xT.b], n)
                S.op("act", lambda e, j=j, ps=ps: e.copy(out=kT[:, j, 128:128 + TT], in_=ps[:, 0:TT]), r=[ps.b], w=[kT.b])
                yield
            w_done()
            yield
            wb = w_next(l, C_KV)
            for b in range(NBK):
                ps = ps_alloc("A")
                for kb in range(8):
                    S.op("pe", lambda e, kb=kb, b=b, ps=ps: e.matmul(ps[:, 0:256], lhsT=xT[:, kb, b * 128:(b + 1) * 128],
                                                                    rhs=wb[:, kb * 512: kb * 512 + 256], start=(kb == 0), stop=(kb == 7)),
                         r=[wb.b, xT.b], w=[ps.b], sig=(kb == 7))
                vsrc = ps[:, 128:256].rearrange("p (k d) -> p k d", k=2)
                S.op("act", lambda e, b=b, vsrc=vsrc: e.copy(out=vtok[:, b + 1, :, 0:64], in_=vsrc), r=[ps.b], w=[vtok.b])
                S.op("act", lambda e, b=b, vsrc=vsrc: e.copy(out=vtok[:, b + 1, :, 64:128], in_=vsrc), r=[ps.b], w=[vtok.b])
                if ti == (NTILE if DBG_TILES is None else DBG_TILES) - 1 and b == NBK - 1:
                    S.op("act", lambda e, ps=ps: e.copy(out=kvout[:], in_=ps[:, 0:256]), r=[ps.b], w=[kvout.b])
                    S.dma("pool", kvw[l], kvout[:], r=[kvout.b])
                yield
            w_done()
            yield
            for cid, dstT, fn in ((C_ZA, sza, AF.Silu), (C_UC, ucz, None)):
                wb = w_next(l, cid)
                for j in range(4):
                    ps = ps_alloc("A")
                    mm_fm(ps, wb, j, 8, xrhs, [xT.b], n)
                    if fn is None:
                        S.op("act", lambda e, j=j, ps=ps, dstT=dstT: e.copy(out=dstT[:, j, :], in_=ps[:, 0:TT]), r=[ps.b], w=[dstT.b])
                    else:
                        S.op("act", lambda e, j=j, ps=ps, dstT=dstT, fn=fn: e.activation(out=dstT[:, j, :], in_=ps[:, 0:TT], func=fn),
                             r=[ps.b], w=[dstT.b])
                    yield
                w_done()
                yield
            wb = w_next(l, C_VC)
            for b in range(NBK):
                ps = ps_alloc("A")
                for kb in range(8):
                    S.op("pe", lambda e, kb=kb, b=b, ps=ps: e.matmul(ps[:, :], lhsT=xT[:, kb, b * 128:(b + 1) * 128],
                                                                    rhs=wb[:, kb * 512:(kb + 1) * 512], start=(kb == 0), stop=(kb == 7)),
                         r=[wb.b, xT.b], w=[ps.b], sig=(kb == 7))
                layernorm_rows((ps.b, ps[:, :]), 128, 512, (sgn.b, sgn[:, 0, :]), (sgn.b, sgn[:, 1, :]), (vn.b, vn[:, b, :]))
                yield
            w_done()
            yield
            wb = w_next(l, C_ZC)
            for j in range(4):
                ps = ps_alloc("A")
                mm_fm(ps, wb, j, 8, xrhs, [xT.b], n)
                S.op("act", lambda e, j=j, ps=ps: e.activation(out=sig[:, j, :], in_=ps[:, 0:TT], func=AF.Silu), r=[ps.b], w=[sig.b])
                yield
            w_done()
            S.op("pool", lambda e: e.tensor_tensor(out=ucz[:], in0=ucz[:], in1=sig[:], op=ALU.mult), r=[ucz.b, sig.b], w=[ucz.b])
            yield

            for b in range(NBK):
                for g4 in range(4):
                    ps = ps_alloc("A")
                    S.op("pe", lambda e, b=b, g4=g4, ps=ps: e.matmul(ps[:, 0:128], lhsT=vn[:, b, g4 * 128:(g4 + 1) * 128],
                                                                    rhs=sgw[:, g4, :], start=True, stop=True),
                         r=[vn.b, sgw.b], w=[ps.b])
                    mt = rot(mtmp, "mt")
                    S.op("dve", lambda e, g4=g4, ps=ps, mt=mt: e.tensor_tensor(out=mt[:, 0:128], in0=ps[:, 0:128],
                                                                              in1=sgb[:, g4 * 128:(g4 + 1) * 128], op=ALU.add),
                         r=[ps.b, sgb.b], w=[mt.b])
                    S.op("pool", lambda e, b=b, g4=g4, mt=mt: e.tensor_tensor(out=ys[:, 8 + g4, b * 128:(b + 1) * 128], in0=mt[:, 0:128],
                                                                             in1=ucz[:, g4, b * 128:(b + 1) * 128], op=ALU.mult),
                         r=[mt.b, ucz.b], w=[ysb[2]])
                yield

            for b in range(NBK):
                first = (ti == 0 and b == 0)
                parts = [1] if first else [0, 1]
                for kv in range(2):
                    for pt in parts:
                        pss = ps_alloc("D")
                        kc = b * 128 + pt * 128
                        for par in range(2):
                            rows = slice(par * 64, (par + 1) * 64)
                            o = pss[:, par * 256:(par + 1) * 256].rearrange("p (j q) -> p j q", j=2)
                            S.op("pe", lambda e, o=o, rows=rows, kc=kc, kv=kv, b=b: e.matmul(
                                o, lhsT=kT[rows, kv, kc:kc + 128], rhs=qT[rows, 2 * kv:2 * kv + 2, b * 128:(b + 1) * 128],
                                start=True, stop=False), r=[kT.b, qT.b], w=[pss.b], sig=False)
                            aqv = aq[:, (kv * 2 + par) * 256:(kv * 2 + par + 1) * 256].rearrange("p (j q) -> p j q", j=2)
                            S.op("pe", lambda e, o=o, aqv=aqv, pt=pt: e.matmul(o, lhsT=ak[:, pt * 128:(pt + 1) * 128], rhs=aqv,
                                                                               start=False, stop=True),
                                 r=[ak.b, aq.b], w=[pss.b], sig=(par == 1))
                        S.op("act", lambda e, pss=pss, pt=pt, kv=kv: e.activation(out=pT[kv][pt][:], in_=pss[:, :], func=AF.Exp),
                             r=[pss.b], w=[pT[kv][pt].b])
                        S.op("pool", lambda e, pt=pt, kv=kv: e.tensor_tensor(
                            out=pT[kv][pt][:].rearrange("p (h q) -> p h q", h=4), in0=pT[kv][pt][:].rearrange("p (h q) -> p h q", h=4),
                            in1=masks[:, pt, :].unsqueeze(1).to_broadcast([128, 4, 128]), op=ALU.mult),
                             r=[pT[kv][pt].b, masks.b], w=[pT[kv][pt].b])
                    yield
                    psd = ps_alloc("D")
                    for i, pt in enumerate(parts):
                        S.op("pe", lambda e, psd=psd, pt=pt, i=i, kv=kv: e.matmul(psd[:, :], lhsT=ones[:], rhs=pT[kv][pt][:],
                                                                                 start=(i == 0), stop=(i == len(parts) - 1)),
                             r=[ones.b, pT[kv][pt].b], w=[psd.b], sig=(i == len(parts) - 1))
                    S.op("dve", lambda e, psd=psd, kv=kv: e.tensor_tensor(
                        out=rden[kv][:].rearrange("p (h q) -> p h q", h=4), in0=psd[:, :].rearrange("p (h q) -> p h q", h=4),
                        in1=esk[:, kv * 4:(kv + 1) * 4].unsqueeze(2).to_broadcast([128, 4, 128]), op=ALU.add),
                         r=[psd.b, esk.b], w=[rden[kv].b])
                    S.op("act", lambda e, kv=kv: e.activation(out=rden[kv][:], in_=rden[kv][:], func=AF.Ln), r=[rden[kv].b], w=[rden[kv].b])
                    S.op("act", lambda e, kv=kv: e.activation(out=rden[kv][:], in_=rden[kv][:], func=AF.Exp, scale=-1.0),
                         r=[rden[kv].b], w=[rden[kv].b])
                    pso = ps_alloc("D")
                    for hh in range(4):
                        for i, pt in enumerate(parts):
                            S.op("pe", lambda e, pso=pso, hh=hh, pt=pt, i=i, kv=kv, b=b: e.matmul(
                                pso[:, hh * 128:(hh + 1) * 128], lhsT=vtok[:, b + pt, kv, :], rhs=pT[kv][pt][:, hh * 128:(hh + 1) * 128],
                                start=(i == 0), stop=(i == len(parts) - 1)),
                                 r=[vtok.b, pT[kv][pt].b], w=[pso.b], sig=(hh == 3 and i == len(parts) - 1))
                    S.op("dve", lambda e, pso=pso, kv=kv: e.tensor_tensor(out=otmp[kv][:], in0=pso[:, :], in1=rden[kv][:], op=ALU.mult),
                         r=[pso.b, rden[kv].b], w=[otmp[kv].b])
                    for par in range(2):
                        rows = slice(par * 64, (par + 1) * 64)
                        ov = otmp[kv][rows, par * 256:(par + 1) * 256].rearrange("p (j q) -> p j q", j=2)
                        S.op("pool", lambda e, rows=rows, ov=ov, kv=kv, b=b: e.tensor_tensor(
                            out=ys[rows, 2 * kv:2 * kv + 2, b * 128:(b + 1) * 128], in0=ov,
                            in1=sza[rows, 2 * kv:2 * kv + 2, b * 128:(b + 1) * 128], op=ALU.mult),
                             r=[otmp[kv].b, sza.b], w=[ysb[0]])
                    yield
            S.op("act", lambda e: e.copy(out=kT[:, :, 0:128], in_=kT[:, :, TT:TT + 128]), r=[kT.b], w=[kT.b])
            S.op("pool", lambda e: e.tensor_copy(out=vtok[:, 0], in_=vtok[:, NBK]), r=[vtok.b], w=[vtok.b])
            for _ in readout_br(l, n, (0, 2), True):
                yield
            for _ in gates_b(l, n):
                yield
            wb = w_next(l, C_ZB)
            for j in range(4):
                ps = ps_alloc("A")
                mm_fm(ps, wb, j, 8, xrhs, [xT.b], n)
                S.op("act", lambda e, j=j, ps=ps: e.activation(out=szb[:, j, :], in_=ps[:, 0:TT], func=AF.Silu), r=[ps.b], w=[szb.b])
            w_done()
            yield


        def gates_b(l, n):
            xrhs = lambda kb: xT[:, kb, 0:n]
            for hf in range(2):
                wb = w_next(l, C_G + hf * 3 + 1)
                for j in range(4):
                    ps = ps_alloc("A")
                    mm_fm(ps, wb, j, 8, xrhs, [xT.b], n)
                    S.op("act", lambda e, j=j, ps=ps: e.activation(out=sigB[:, hf * 4 + j, 0:n], in_=ps[:, 0:n], func=AF.Sigmoid),
                         r=[ps.b], w=[sigB.b])
                    yield
                w_done()
                yield

        def readout_br(l, n, branches, init_first, pre=False):
            xrhs = lambda kb: xT[:, kb, 0:n]
            for hf in range(2):
                for bi, b in enumerate(branches):
                    if pre:
                        gsrc, gbuf, goff = sigB, sigB.b, hf * 4
                    else:
                        gsrc, gbuf, goff = sig, sig.b, 0
                        wb = w_next(l, C_G + hf * 3 + b)
                        for j in range(4):
                            ps = ps_alloc("A")
                            mm_fm(ps, wb, j, 8, xrhs, [xT.b], n)
                            S.op("act", lambda e, j=j, ps=ps: e.activation(out=sig[:, j, 0:n], in_=ps[:, 0:n], func=AF.Sigmoid),
                                 r=[ps.b], w=[sig.b])
                            yield
                        w_done()
                        yield
                    wb = w_next(l, C_R + hf * 3 + b)
                    for j in range(4):
                        ps = ps_alloc("A")
                        mm_fm(ps, wb, j, 4, lambda kb, b=b: ys[:, b * 4 + kb, 0:n], [ysb[b]], n)
                        if init_first and bi == 0:
                            S.op("dve", lambda e, j=j, ps=ps, gsrc=gsrc, goff=goff: e.tensor_tensor(
                                out=mrgT[:, hf * 4 + j, 0:n], in0=ps[:, 0:n], in1=gsrc[:, goff + j, 0:n], op=ALU.mult),
                                 r=[ps.b, gbuf], w=[mrgT.b])
                        else:
                            mt = rot(mtmp, "mt")
                            S.op("dve", lambda e, j=j, ps=ps, mt=mt, gsrc=gsrc, goff=goff: e.tensor_tensor(
                                out=mt[:, 0:n], in0=ps[:, 0:n], in1=gsrc[:, goff + j, 0:n], op=ALU.mult),
                                 r=[ps.b, gbuf], w=[mt.b])
                            S.op("pool", lambda e, j=j, mt=mt: e.tensor_tensor(out=mrgT[:, hf * 4 + j, 0:n], in0=mrgT[:, hf * 4 + j, 0:n],
                                                                              in1=mt[:, 0:n], op=ALU.add),
                                 r=[mrgT.b, mt.b], w=[mrgT.b])
                        yield
                    w_done()
                    yield

        def drain(gen):
            for _ in gen:
                pass

        def wo_ln(l, n, blocks):
            for hf in range(2):
                wb = w_next(l, C_O + hf)
                c0 = 0
                for (xt, nrow, _) in blocks:
                    ps = ps_alloc("A")
                    for kb in range(8):
                        S.op("pe", lambda e, kb=kb, ps=ps, c0=c0, nrow=nrow: e.matmul(
                            ps[0:nrow, :], lhsT=mrgT[:, kb, c0:c0 + nrow], rhs=wb[:, kb * 512:(kb + 1) * 512],
                            start=(kb == 0), stop=(kb == 7)), r=[wb.b, mrgT.b], w=[ps.b], sig=(kb == 7))
                    S.op("dve", lambda e, ps=ps, xt=xt, nrow=nrow: e.scalar_tensor_tensor(
                        out=xt[0:nrow, hf * 512:(hf + 1) * 512], in0=xt[0:nrow, hf * 512:(hf + 1) * 512], scalar=ALPHA,
                        in1=ps[0:nrow, :], op0=ALU.mult, op1=ALU.add), r=[ps.b, xt.b], w=[xt.b])
                    c0 += nrow
                    yield
                w_done()
                yield
            for (xt, nrow, dst_ap) in blocks:
                lo = rot(lnout, "lo")
                layernorm_rows((xt.b, xt[0:nrow, :]), nrow, D, (lng.b, lng[0:nrow, 0, :]), (lng.b, lng[0:nrow, 1, :]),
                               (lo.b, lo[0:nrow, :]), tmp=(xt.b, xt[0:nrow, :]))
                S.dma("pool", dst_ap, lo[0:nrow, :], r=[lo.b], w=[x1_b])
                yield

        def sample_tile(l):
            n = NS
            src = xs if l == 0 else xs1
            dst = xs1 if l == 0 else ysm
            xt = xtok[0]
            S.dma("sp", xt[0:n, :], src, r=([x1_b] if l == 1 else []), w=[xt.b])
            for kb in range(8):
                ps = ps_alloc("A")
                S.op("pe", lambda e, ps=ps, kb=kb: e.transpose(out=ps[:, 0:n], in_=xt[0:n, kb * 128:(kb + 1) * 128], identity=ident[0:n, 0:n]),
                     r=[xt.b, ident.b], w=[ps.b])
                S.op("dve", lambda e, ps=ps, kb=kb: e.tensor_copy(out=xT[:, kb, 0:n], in_=ps[:, 0:n]), r=[ps.b], w=[xT.b])
            xrhs = lambda kb: xT[:, kb, 0:n]
            wb = w_next(l, C_UB)
            for j in range(4):
                ps = ps_alloc("A")
                mm_fm(ps, wb, j, 8, xrhs, [xT.b], n)
                S.op("dve", lambda e, j=j, ps=ps: e.tensor_copy(out=uT[:, j, 0:n], in_=ps[:, 0:n]), r=[ps.b], w=[uT.b])
            w_done()
            wb = w_next(l, C_Q)
            for j in range(4):
                ps = ps_alloc("A")
                mm_fm(ps, wb, j, 8, xrhs, [xT.b], n)
                S.op("act", lambda e, j=j, ps=ps: e.activation(out=qT[:, j, 0:n], in_=ps[:, 0:n], func=AF.Copy, scale=0.125), r=[ps.b], w=[qT.b])
            w_done()
            wb = w_next(l, C_KD)
            for j in range(4):
                ps = ps_alloc("A")
                mm_fm(ps, wb, j, 8, xrhs, [xT.b], n)
                if j < 2:
                    S.op("dve", lambda e, j=j, ps=ps: e.tensor_copy(out=ksd[:, j, :], in_=ps[:, 0:n]), r=[ps.b], w=[ksd.b])
                else:
                    S.op("dve", lambda e, j=j, ps=ps: e.tensor_copy(out=vsd[:, j - 2, :], in_=ps[:, 0:n]), r=[ps.b], w=[vsd.b])
            w_done()
            wb = w_next(l, C_KV)
            ps = ps_alloc("A")
            for kb in range(8):
                S.op("pe", lambda e, kb=kb, ps=ps: e.matmul(ps[0:n, 0:256], lhsT=xT[:, kb, 0:n], rhs=wb[:, kb * 512: kb * 512 + 256],
                                                           start=(kb == 0), stop=(kb == 7)), r=[wb.b, xT.b], w=[ps.b], sig=(kb == 7))
            S.op("dve", lambda e, ps=ps: e.tensor_copy(out=kvs_sb[:], in_=ps[0:n, 0:256]), r=[ps.b], w=[kvs_sb.b])
            S.dma("pool", kvs[l], kvs_sb[:], r=[kvs_sb.b])
            w_done()
            for cid, dstT, fn in ((C_ZA, sza, AF.Silu), (C_ZB, szb, AF.Silu), (C_UC, ucz, None)):
                wb = w_next(l, cid)
                for j in range(4):
                    ps = ps_alloc("A")
                    mm_fm(ps, wb, j, 8, xrhs, [xT.b], n)
                    if fn is None:
                        S.op("dve", lambda e, j=j, ps=ps, dstT=dstT: e.tensor_copy(out=dstT[:, j, 0:n], in_=ps[:, 0:n]), r=[ps.b], w=[dstT.b])
                    else:
                        S.op("act", lambda e, j=j, ps=ps, dstT=dstT, fn=fn: e.activation(out=dstT[:, j, 0:n], in_=ps[:, 0:n], func=fn),
                             r=[ps.b], w=[dstT.b])
                w_done()
            wb = w_next(l, C_VC)
            ps = ps_alloc("A")
            for kb in range(8):
                S.op("pe", lambda e, kb=kb, ps=ps: e.matmul(ps[0:n, :], lhsT=xT[:, kb, 0:n], rhs=wb[:, kb * 512:(kb + 1) * 512],
                                                           start=(kb == 0), stop=(kb == 7)), r=[wb.b, xT.b], w=[ps.b], sig=(kb == 7))
            lo = rot(lnout, "lo")
            layernorm_rows((ps.b, ps[0:n, :]), n, 512, (sgn.b, sgn[0:n, 0, :]), (sgn.b, sgn[0:n, 1, :]), (lo.b, lo[0:n, 0:512]))
            S.dma("pool", vns[l], lo[0:n, 0:512], r=[lo.b])
            S.op("act", lambda e: e.copy(out=vnS[:], in_=lo[0:n, 0:512]), r=[lo.b], w=[vnS.b])
            w_done()
            wb = w_next(l, C_ZC)
            for j in range(4):
                ps = ps_alloc("A")
                mm_fm(ps, wb, j, 8, xrhs, [xT.b], n)
                S.op("act", lambda e, j=j, ps=ps: e.activation(out=gtmp2[:, j, 0:n], in_=ps[:, 0:n], func=AF.Silu), r=[ps.b], w=[gtmp2.b])
            w_done()
            S.op("pool", lambda e: e.tensor_tensor(out=ucz[:, :, 0:n], in0=ucz[:, :, 0:n], in1=gtmp2[:, :, 0:n], op=ALU.mult),
                 r=[ucz.b, gtmp2.b], w=[ucz.b])

            S.dma("sp", h0[:], h0_d[l], w=[h0.b])
            S.dma("sp", h0s[:], h0s_d[l], w=[h0s.b])
            psv = ps_alloc("B")
            for g in range(32):
                S.op("pe", lambda e, g=g: e.matmul(psv[:, g * n:(g + 1) * n], lhsT=bbw[:, g, :], rhs=uT[:, g // 8, 0:n], start=True, stop=True),
                     r=[bbw.b, uT.b], w=[psv.b], sig=(g == 31))
            S.op("dve", lambda e: e.tensor_tensor(out=h0[:], in0=h0[:], in1=car[:].unsqueeze(2).to_broadcast([128, 32, n]), op=ALU.mult),
                 r=[h0.b, car.b], w=[h0.b])
            S.op("dve", lambda e: e.tensor_tensor(out=h0s[:], in0=h0s[:], in1=cais[:].unsqueeze(2).to_broadcast([128, 32, n]), op=ALU.mult),
                 r=[h0s.b, cais.b], w=[h0s.b])
            S.op("dve", lambda e: e.tensor_tensor(out=h0[:], in0=h0[:], in1=h0s[:], op=ALU.add), r=[h0.b, h0s.b], w=[h0.b])
            S.op("dve", lambda e: e.tensor_tensor(out=h0[:], in0=h0[:], in1=psv[:, 0:32 * n].rearrange("p (g n) -> p g n", n=n), op=ALU.add),
                 r=[h0.b, psv.b], w=[h0.b])
            S.dma("pool", hss[l], h0[:], r=[h0.b])
            S.op("act", lambda e: e.copy(out=hbs[:], in_=h0[:]), r=[h0.b], w=[hbs.b])
            for fb in range(4):
                psy = ps_alloc("Y")
                for gl in range(8):
                    g = fb * 8 + gl
                    S.op("pe", lambda e, g=g, gl=gl, psy=psy: e.matmul(psy[:, 0:n], lhsT=cpw[:, g, :], rhs=hbs[:, g, :], start=(gl == 0), stop=(gl == 7)),
                         r=[cpw.b, hbs.b], w=[psy.b], sig=(gl == 7))
                S.op("dve", lambda e, fb=fb, psy=psy: e.scalar_tensor_tensor(out=ypre[:, fb, 0:n], in0=uT[:, fb, 0:n], scalar=dsk[:, fb:fb + 1],
                                                                            in1=psy[:, 0:n], op0=ALU.mult, op1=ALU.add),
                     r=[uT.b, dsk.b, psy.b], w=[ypre.b])

            for g4 in range(4):
                ps = ps_alloc("A")
                S.op("pe", lambda e, g4=g4, ps=ps: e.matmul(ps[:, 0:n], lhsT=vnS[:, g4 * 128:(g4 + 1) * 128], rhs=w00I[:, g4, :], start=True, stop=True),
                     r=[vnS.b, w00I.b], w=[ps.b])
                S.op("dve", lambda e, g4=g4, ps=ps: e.tensor_scalar(out=st1[:], in0=ps[:, 0:n], scalar1=b00[:, g4:g4 + 1], scalar2=None, op0=ALU.add),
                     r=[ps.b, b00.b], w=[st1.b])
                S.op("dve", lambda e, g4=g4: e.tensor_tensor(out=ys[:, 8 + g4, 0:n], in0=st1[:], in1=ucz[:, g4, 0:n], op=ALU.mult),
                     r=[st1.b, ucz.b], w=[ysb[2]])

            S.dma("pool", ck[:], ckT_d[l], w=[ck.b])
            S.dma("pool", cv[:], cvd_d[l], w=[cv.b])
            pss = ps_alloc("D")
            for nn in range(n):
                for kv in range(2):
                    for par in range(2):
                        rows = slice(par * 64, (par + 1) * 64)
                        c0 = (kv * 2 + par) * 32
                        o = pss[:, c0:c0 + 32].rearrange("p (j n) -> p j n", j=2)[:, :, nn]
                        last = (nn == n - 1 and kv == 1 and par == 1)
                        S.op("pe", lambda e, o=o, rows=rows, nn=nn, kv=kv: e.matmul(o, lhsT=ck[rows, nn * 2 + kv, :], rhs=qT[rows, 2 * kv:2 * kv + 2, nn],
                                                                                   start=True, stop=False), r=[ck.b, qT.b], w=[pss.b], sig=False)
                        S.op("pe", lambda e, o=o, c0=c0: e.matmul(o, lhsT=ak[:, 0:128], rhs=aqs[:, c0 // 16: c0 // 16 + 2], start=False, stop=True),
                             r=[ak.b, aqs.b], w=[pss.b], sig=last)
            S.op("act", lambda e: e.activation(out=pSb[:], in_=pss[:, 0:128], func=AF.Exp), r=[pss.b], w=[pSb.b])
            for j in range(4):
                S.op("dve", lambda e, j=j: e.tensor_tensor(out=prod[:, j, :], in0=qT[:, j, 0:n], in1=ksd[:, j // 2, :], op=ALU.mult),
                     r=[qT.b, ksd.b], w=[prod.b])
            ps1 = ps_alloc("A")
            S.op("pe", lambda e: e.matmul(ps1[0:2, 0:64], lhsT=hsel[:], rhs=prod[:].rearrange("p j n -> p (j n)"), start=True, stop=True),
                 r=[hsel.b, prod.b], w=[ps1.b])
            S.op("act", lambda e: e.activation(out=pself[:], in_=ps1[0:2, 0:64], func=AF.Exp), r=[ps1.b], w=[pself.b])
            ps2 = ps_alloc("A")
            S.op("pe", lambda e: e.matmul(ps2[:, 0:64], lhsT=hselT[:], rhs=pself[:], start=True, stop=True), r=[hselT.b, pself.b], w=[ps2.b])
            S.op("dve", lambda e: e.tensor_copy(out=pbs[:].rearrange("p j n -> p (j n)"), in_=ps2[:, 0:64]), r=[ps2.b], w=[pbs.b])
            psd = ps_alloc("A")
            S.op("pe", lambda e: e.matmul(psd[:, 0:128], lhsT=ones[:], rhs=pSb[:], start=True, stop=True), r=[ones.b, pSb.b], w=[psd.b])
            S.op("dve", lambda e: e.tensor_copy(out=sdn[:], in_=psd[:, 0:128]), r=[psd.b], w=[sdn.b])
            pso = ps_alloc("D")
            for nn in range(n):
                for kv in range(2):
                    o = pso[:, kv * 64:(kv + 1) * 64].rearrange("p (a n) -> p a n", n=n)[:, :, nn]
                    r_ = pSb[:, kv * 64:(kv + 1) * 64].rearrange("p (a n) -> p a n", n=n)[:, :, nn]
                    S.op("pe", lambda e, o=o, r_=r_, nn=nn, kv=kv: e.matmul(o, lhsT=cv[:, nn * 2 + kv, :], rhs=r_, start=True, stop=True),
                         r=[cv.b, pSb.b], w=[pso.b], sig=(nn == n - 1 and kv == 1))
            S.op("dve", lambda e: e.tensor_copy(out=sso[:], in_=pso[:, 0:128]), r=[pso.b], w=[sso.b])
            for par in range(2):
                rows = slice(par * 64, (par + 1) * 64)
                for kv in range(2):
                    for jj in range(2):
                        blk = 2 * kv + jj
                        c0 = ((kv * 2 + par) * 2 + jj) * n
                        a1 = st1[rows, :]
                        a2 = st2[rows, :]
                        S.op("dve", lambda e, a1=a1, rows=rows, blk=blk, kv=kv: e.tensor_tensor(out=a1, in0=pbs[rows, blk, :], in1=vsd[rows, kv, :], op=ALU.mult),
                             r=[pbs.b, vsd.b], w=[st1.b])
                        S.op("dve", lambda e, a1=a1, rows=rows, c0=c0: e.tensor_tensor(out=a1, in0=a1, in1=sso[rows, c0:c0 + n], op=ALU.add),
                             r=[st1.b, sso.b], w=[st1.b])
                        S.op("dve", lambda e, a2=a2, rows=rows, blk=blk, c0=c0: e.tensor_tensor(out=a2, in0=pbs[rows, blk, :], in1=sdn[rows, c0:c0 + n], op=ALU.add),
                             r=[pbs.b, sdn.b], w=[st2.b])
                        S.op("dve", lambda e, a2=a2, rows=rows, blk=blk: e.tensor_scalar(out=a2, in0=a2, scalar1=esks[rows, blk:blk + 1], scalar2=None, op0=ALU.add),
                             r=[st2.b, esks.b], w=[st2.b])
                        S.op("dve", lambda e, a2=a2: e.reciprocal(out=a2, in_=a2), r=[st2.b], w=[st2.b])
                        S.op("dve", lambda e, a1=a1, a2=a2: e.tensor_tensor(out=a1, in0=a1, in1=a2, op=ALU.mult), r=[st1.b, st2.b], w=[st1.b])
                        S.op("dve", lambda e, a1=a1, rows=rows, blk=blk: e.tensor_tensor(out=ys[rows, blk, 0:n], in0=a1, in1=sza[rows, blk, 0:n], op=ALU.mult),
                             r=[st1.b, sza.b], w=[ysb[0]])
            drain(readout_br(l, n, (0, 2), True))
            gelu_glu(l, n)
            drain(readout_br(l, n, (1,), False))
            drain(wo_ln(l, n, [(xt, n, dst)]))

        x1_b = Buf("x1")

        def roundrobin(gens):
            gens = list(gens)
            while gens:
                for g_ in list(gens):
                    try:
                        next(g_)
                        while wst["hold"]:
                            next(g_)
                    except StopIteration:
                        gens.remove(g_)

        for l in range(DEPTH):
            setup_layer(l)
            nt = NTILE if DBG_TILES is None else DBG_TILES
            xload(l, 0)
            if nt > 1:
                xload(l, 1)
            front(l, 0)
            prev_tail = None
            for ti in range(nt):
                gens = [s5_pipe(TT), rest_gen(l, ti)]
                if prev_tail is not None:
                    gens.append(prev_tail)
                roundrobin(gens)
                if ti + 1 < nt:
                    front(l, ti + 1)
                prev_tail = tail(l, ti)
            if prev_tail is not None:
                drain(prev_tail)
            S.dma("pool", hst_o[l], hst[:], r=[hst.b] + hstb)
            sample_tile(l)
        allb = [x1_b, hst.b, h0.b, kvs_sb.b] + [t.b for t in lnout] + ([] if DBG_NOKV else [kvout.b])
        S.finish(allb)
        for i in range(NDS):
            if S.dcnt[i] > 0:
                S._wait("sp", (("d", i), S.dcnt[i]))
        if not recording:
            print("instructions:", S.ninst, "sem counts:", S.cnt)
    return nc


_PROG = None


def _consts():
    ident = np.eye(128, dtype=np.float32)
    swapm = np.zeros((128, 128), np.float32)
    for m in range(64):
        swapm[64 + m, m] = -1.0
        swapm[m, 64 + m] = 1.0
    s = np.arange(128)[:, None]
    q = np.arange(128)[None, :]
    mprev = (s >= q).astype(np.float32)
    mcur = (s <= q).astype(np.float32)
    masks = np.stack([np.tile(mprev, (1, 4)), np.tile(mcur, (1, 4))]).astype(np.float32)
    ak = np.zeros((2, 256), np.float32)
    ak[0, :128] = np.arange(128) - 128
    ak[0, 128:] = np.arange(128)
    ak[1, :] = 1.0
    slopes = 2.0 ** (-(np.arange(1, 9)))
    aq = np.zeros((2, 2, 2, 2, 128), np.float32)
    for kv in range(2):
        for par in range(2):
            for jj in range(2):
                h = 4 * kv + 2 * jj + par
                aq[0, kv, par, jj, :] = slopes[h]
                aq[1, kv, par, jj, :] = -slopes[h] * np.arange(128)
    rmask = np.zeros((128, 8), np.float32)
    for gl in range(8):
        rmask[gl * 16:(gl + 1) * 16, gl] = 1.0
    triu = (s <= q).astype(np.float32)
    aqs = np.zeros((2, 8), np.float32)
    for kv in range(2):
        for par in range(2):
            for jj in range(2):
                aqs[0, (kv * 2 + par) * 2 + jj] = slopes[4 * kv + 2 * jj + par]
    hsel = np.zeros((128, 2), np.float32)
    hsel[0:64, 0] = 1.0
    hsel[64:128, 1] = 1.0
    signc = np.ones((128, 1), np.float32)
    signc[0:64] = -1.0
    return dict(ident=ident, swapm=swapm, masks=masks, ak=ak, aq=aq.reshape(2, 1024), rmask=rmask, triu=triu,
                aqs=aqs, hsel=hsel, hselT=np.ascontiguousarray(hsel.T), signc=signc)


def _layout_weights(w_in, w_read, w_o, glu_w):
    out = np.zeros((DEPTH, NCH, 128, 8, 512), np.float32)

    def put(l, c, mat, kb0=0):
        k = mat.shape[0] // 128
        out[l, c, :, kb0:kb0 + k, :mat.shape[1]] = mat.reshape(k, 128, mat.shape[1]).transpose(1, 0, 2)

    for l in range(DEPTH):
        W = w_in[l]
        put(l, C_Q, W[:, 0:512])
        k0, k1 = W[:, 512:576], W[:, 576:640]
        v0, v1 = W[:, 640:704], W[:, 704:768]
        put(l, C_KD, np.concatenate([k0, k0, k1, k1, v0, v0, v1, v1], axis=1))
        put(l, C_KV, W[:, 512:768])
        put(l, C_ZA, W[:, 768:1280])
        put(l, C_UB, W[:, 1280:1792])
        put(l, C_ZB, W[:, 1792:2304])
        put(l, C_UC, W[:, 2304:2816])
        put(l, C_VC, W[:, 2816:3328])
        put(l, C_ZC, W[:, 3328:3840])
        for hf in range(2):
            for b in range(3):
                c0 = 3840 + b * 1024 + hf * 512
                put(l, C_G + hf * 3 + b, W[:, c0:c0 + 512])
                put(l, C_R + hf * 3 + b, w_read[l, b][:, hf * 512:(hf + 1) * 512])
            put(l, C_O + hf, w_o[l][:, hf * 512:(hf + 1) * 512])
        put(l, C_GLU, glu_w[l])
    return out.reshape(DEPTH, NCH, 128, 4096)


def _layout_params(inp):
    f = np.float32
    lam_re, lam_im, log_dt = inp["ssm_lambda_re"], inp["ssm_lambda_im"], inp["ssm_log_dt"]
    colp = np.zeros((DEPTH, 128, 96), f)
    rowp = np.zeros((DEPTH, 3, 128, 2048), f)
    for l in range(DEPTH):
        colp[l, :, 0:32] = np.concatenate([lam_re[l].T, lam_re[l].T], 0)
        colp[l, :, 32:64] = np.concatenate([lam_im[l].T, lam_im[l].T], 0)
        colp[l, :, 64:96] = log_dt[l][None, :]
        rowp[l, 0] = lam_re[l].reshape(1, 2048)
        rowp[l, 1] = lam_im[l].reshape(1, 2048)
        rowp[l, 2] = np.repeat(log_dt[l], 64)[None, :]
    bpad = np.zeros((DEPTH, 2, 32, 128, 64), f)
    cpad = np.zeros((DEPTH, 32, 128, 128), f)
    for l in range(DEPTH):
        for g in range(32):
            gl = g % 8
            bpad[l, 0, g, gl * 16:(gl + 1) * 16, :] = inp["ssm_b_re"][l, g].T
            bpad[l, 1, g, gl * 16:(gl + 1) * 16, :] = inp["ssm_b_im"][l, g].T
            cpad[l, g, 0:64, gl * 16:(gl + 1) * 16] = inp["ssm_c_re"][l, g].T
            cpad[l, g, 64:128, gl * 16:(gl + 1) * 16] = inp["ssm_c_im"][l, g].T
    dsk = inp["ssm_d"].reshape(DEPTH, 4, 128).transpose(0, 2, 1).copy()
    glub = inp["glu_b"].reshape(DEPTH, 4, 128).transpose(0, 2, 1).copy()
    sgn = np.zeros((DEPTH, 2, 128, 512), f)
    sgn[:, 0] = inp["sgu_ln_g"][:, None, :]
    sgn[:, 1] = inp["sgu_ln_b"][:, None, :]
    sgw = inp["sgu_w"].transpose(0, 1, 3, 2).copy()
    sgb = np.broadcast_to(inp["sgu_b"].reshape(DEPTH, 1, 512), (DEPTH, 128, 512)).copy()
    lng = np.zeros((DEPTH, 2, 128, 1024), f)
    lng[:, 0] = inp["ln_g"][:, None, :]
    lng[:, 1] = inp["ln_b"][:, None, :]
    snk = np.zeros((DEPTH, 128, 2, 2, 2), f)
    for kv in range(2):
        for par in range(2):
            for jj in range(2):
                h = 4 * kv + 2 * jj + par
                snk[:, :, kv, par, jj] = inp["attn_sinks"][:, h][:, None]
    return dict(colp=colp, rowp=rowp, bpad=bpad, cpad=cpad, dsk=dsk, glub=glub, sgn=sgn, sgw=sgw, sgb=sgb, lng=lng,
                snk=snk.reshape(DEPTH, 128, 8))


def kernel(**inp):
    global _PROG
    inp = {k: np.asarray(v) for k, v in inp.items()}
    if _PROG is None:
        _PROG = build_program()
    nc = _PROG
    consts = _consts()
    wch = _layout_weights(inp["w_in"], inp["w_read"], inp["w_o"], inp["glu_w"])
    prm = _layout_params(inp)
    in_maps = []
    for c in range(8):
        m = dict(consts)
        m.update(prm)
        m["wch"] = wch
        m["xp"] = np.ascontiguousarray(inp["x_prompt"][c % 2])
        sl = slice(c * NS, (c + 1) * NS)
        m["xs"] = np.ascontiguousarray(inp["x_sample"][sl, 0, :])
        ckc = inp["cache_k_win"][:, sl]
        t = ckc.transpose(0, 4, 1, 3, 2).reshape(DEPTH, 64, 32, 128)
        m["ckT"] = np.ascontiguousarray(np.concatenate([t, t], axis=1))
        cvc = inp["cache_v_win"][:, sl]
        t = cvc.transpose(0, 2, 1, 3, 4).reshape(DEPTH, 128, 32, 64)
        m["cvd"] = np.ascontiguousarray(np.concatenate([t, t], axis=3))
        sr = inp["state_ssm_re"][:, sl].transpose(0, 3, 2, 1)
        si = inp["state_ssm_im"][:, sl].transpose(0, 3, 2, 1)
        m["h0"] = np.ascontiguousarray(np.concatenate([sr, si], axis=1))
        m["h0s"] = np.ascontiguousarray(np.concatenate([si, sr], axis=1))
        m["w00"] = np.ascontiguousarray(np.broadcast_to(inp["sgu_w"][:, None, :, 0, 0], (DEPTH, NS, 4)))
        m["b00"] = np.ascontiguousarray(np.broadcast_to(inp["sgu_b"][:, None, :, 0], (DEPTH, 128, 4)))
        es_ = np.zeros((DEPTH, 128, 4), np.float32)
        for blk in range(4):
            es_[:, 0:64, blk] = inp["attn_sinks"][:, 2 * blk][:, None]
            es_[:, 64:128, blk] = inp["attn_sinks"][:, 2 * blk + 1][:, None]
        m["esks"] = es_
        in_maps.append(m)
    res = run_bass_kernel_spmd(nc, in_maps, core_ids=list(range(8)))
    R = res.results
    y_prompt = np.stack([R[0]["yp"], R[1]["yp"]])
    kvw = np.stack([R[0]["kvw"], R[1]["kvw"]], axis=1)
    k_win = kvw[..., 0:128].reshape(DEPTH, 2, 128, 2, 64)
    v_win = kvw[..., 128:256].reshape(DEPTH, 2, 128, 2, 64)
    hs = np.stack([R[0]["hst"], R[1]["hst"]], axis=1)
    h_re = hs[:, :, 0:64, :].transpose(0, 1, 3, 2)
    h_im = hs[:, :, 64:128, :].transpose(0, 1, 3, 2)
    y_sample = np.concatenate([R[c]["ysm"] for c in range(8)], axis=0).reshape(128, 1, D)
    kvsm = np.concatenate([R[c]["kvs"] for c in range(8)], axis=1)
    k_s = kvsm[..., 0:128].reshape(DEPTH, 128, 1, 2, 64)
    v_s = kvsm[..., 128:256].reshape(DEPTH, 128, 1, 2, 64)
    hsm = np.concatenate([R[c]["hss"] for c in range(8)], axis=3)
    hs_re = hsm[:, 0:64].transpose(0, 3, 2, 1)
    hs_im = hsm[:, 64:128].transpose(0, 3, 2, 1)
    vn_s = np.concatenate([R[c]["vns"] for c in range(8)], axis=1).reshape(DEPTH, 128, 1, 512)
    f = np.float32
    outs = (y_prompt, y_sample, k_win, v_win, h_re, h_im, k_s, v_s, hs_re, hs_im, vn_s)
    return tuple(np.ascontiguousarray(o, dtype=f) for o in outs)
```

```python
import math
from contextlib import ExitStack

import numpy as np
import concourse.bass as bass
import concourse.mybir as mybir
from concourse.bass_utils import run_bass_kernel_spmd

F32 = mybir.dt.float32
BF16 = mybir.dt.bfloat16
AF = mybir.ActivationFunctionType
ALU = mybir.AluOpType

D = 1024
SEQ = 8192
DEPTH = 2
NS = 16
BW = 512
TT = 256
NBK = TT // 128
NTILE = SEQ // TT
NCH = 24
ALPHA = (2 * DEPTH) ** 0.25
LN_EPS = 1e-5
GK = 0.044715
GS = 2.0 * math.sqrt(2.0 / math.pi)
NDS = 24
DBG_TILES = None
DBG_NOKV = False
DBG_SKIPKV = False

C_Q, C_KD, C_KV, C_ZA, C_UB, C_ZB, C_UC, C_VC, C_ZC = range(9)
C_G = 9
C_R = 15
C_O = 21
C_GLU = 23


class Buf:
    __slots__ = ("name", "w", "r", "excl")

    def __init__(self, name, excl=False):
        self.name = name
        self.w = None
        self.r = {}
        self.excl = excl


class Sched:
    def __init__(self, nc, es):
        self.nc = nc
        self.engs = {"pe": nc.tensor, "dve": nc.vector, "act": nc.scalar, "pool": nc.gpsimd, "sp": nc.sync}
        self.sem = {k: es.enter_context(nc.semaphore("sem_" + k)) for k in ["pe", "dve", "act", "pool"]}
        self.cnt = {k: 0 for k in self.sem}
        self.seen = {e: {} for e in self.engs}
        self.dsem = [es.enter_context(nc.semaphore("dsem%d" % i)) for i in range(NDS)]
        self.dcnt = [0] * NDS
        self.drange = {"sp": (0, 16), "pool": (16, NDS)}
        self.drr = {"sp": 0, "pool": 16}
        self.pend = []
        self.pendset = set()
        self.ninst = 0
        self.nwait = 0
        self.clock = {}

    def _wait(self, e, ev):
        if ev is None:
            return
        key, val = ev
        if key == e and e == "pe":
            return
        if self.seen[e].get(key, 0) >= val:
            return
        sem = self.sem[key] if isinstance(key, str) else self.dsem[key[1]]
        self.engs[e].wait_ge(sem, val)
        self.nwait += 1
        sn = self.seen[e]
        sn[key] = val
        for k, v in self.clock.get(ev, {}).items():
            if sn.get(k, 0) < v:
                sn[k] = v

    def _deps(self, e, r, w):
        for b in r:
            assert id(b) not in self.pendset or e == "pe", b.name
            self._wait(e, b.w)
            if b.excl:
                for k, v in list(b.r.items()):
                    if k != e:
                        self._wait(e, (k, v))
        for b in w:
            assert id(b) not in self.pendset or e == "pe", b.name
            self._wait(e, b.w)
            for k, v in list(b.r.items()):
                self._wait(e, (k, v))

    def _commit(self, ev, r, w):
        for b in w:
            b.w = ev
            b.r = {}
        for b in r:
            if all(b is not x for x in w):
                b.r[ev[0]] = ev[1]

    def op(self, e, fn, r=(), w=(), sig=True):
        self._deps(e, r, w)
        inst = fn(self.engs[e])
        self.ninst += 1
        if e == "pe" and not sig:
            for b in r:
                self.pend.append((b, 0))
                self.pendset.add(id(b))
            for b in w:
                self.pend.append((b, 1))
                self.pendset.add(id(b))
            return inst
        self.cnt[e] += 1
        inst.then_inc(self.sem[e], 1)
        ev = (e, self.cnt[e])
        self.clock[ev] = dict(self.seen[e])
        rr = list(r)
        ww = list(w)
        if e == "pe":
            for b, k in self.pend:
                (ww if k else rr).append(b)
            self.pend = []
            self.pendset = set()
        self._commit(ev, rr, ww)
        return inst

    def dma(self, q, out_ap, in_ap, r=(), w=()):
        lo, hi = self.drange[q]
        i = self.drr[q]
        self.drr[q] = lo + (i + 1 - lo) % (hi - lo)
        if self.dcnt[i] > 0:
            self._wait(q, (("d", i), self.dcnt[i]))
        self._deps(q, r, w)
        inst = self.engs[q].dma_start(out=out_ap, in_=in_ap)
        self.ninst += 1
        self.dcnt[i] += 16
        inst.then_inc(self.dsem[i], 16)
        self.clock[(("d", i), self.dcnt[i])] = dict(self.seen[q])
        self._commit((("d", i), self.dcnt[i]), list(r), list(w))
        return inst

    def finish(self, bufs):
        for b in bufs:
            self._wait("sp", b.w)


class T:
    def __init__(self, t, name):
        self.t = t
        self.b = Buf(name)

    def __getitem__(self, k):
        return self.t[k]


def build_program():
    rec = []
    _build(rec, True)
    return _build(rec, False)


def _build(order, recording):
    nc = bass.Bass("TRN2", target_bir_lowering=False)
    es = ExitStack()
    with es:
        def din(name, shape):
            return nc.dram_tensor(name, list(shape), F32, kind="ExternalInput").ap()

        def dout(name, shape):
            return nc.dram_tensor(name, list(shape), F32, kind="ExternalOutput").ap()

        xp = din("xp", [SEQ, D])
        xs = din("xs", [NS, D])
        wch = din("wch", [DEPTH, NCH, 128, 4096])
        ident_d = din("ident", [128, 128])
        swap_d = din("swapm", [128, 128])
        masks_d = din("masks", [2, 128, 512])
        ak_d = din("ak", [2, 256])
        aq_d = din("aq", [2, 1024])
        colp_d = din("colp", [DEPTH, 128, 3 * 32])
        rowp_d = din("rowp", [DEPTH, 3, 128, 2048])
        bpad_d = din("bpad", [DEPTH, 2, 32, 128, 64])
        cpad_d = din("cpad", [DEPTH, 32, 128, 128])
        rmask_d = din("rmask", [128, 8])
        dsk_d = din("dsk", [DEPTH, 128, 4])
        glub_d = din("glub", [DEPTH, 128, 4])
        sgn_d = din("sgn", [DEPTH, 2, 128, 512])
        sgw_d = din("sgw", [DEPTH, 4, 128, 128])
        triu_d = din("triu", [128, 128])
        sgb_d = din("sgb", [DEPTH, 128, 512])
        lng_d = din("lng", [DEPTH, 2, 128, 1024])
        snk_d = din("snk", [DEPTH, 128, 8])

        ckT_d = din("ckT", [DEPTH, 128, 32, 128])
        cvd_d = din("cvd", [DEPTH, 128, 32, 128])
        h0_d = din("h0", [DEPTH, 128, 32, NS])
        h0s_d = din("h0s", [DEPTH, 128, 32, NS])
        aqs_d = din("aqs", [2, 8])
        hsel_d = din("hsel", [128, 2])
        hselT_d = din("hselT", [2, 128])
        signc_d = din("signc", [128, 1])
        w00_d = din("w00", [DEPTH, NS, 4])
        b00_d = din("b00", [DEPTH, 128, 4])
        esks_d = din("esks", [DEPTH, 128, 4])
        ysm = dout("ysm", [NS, D])
        kvs = dout("kvs", [DEPTH, NS, 256])
        hss = dout("hss", [DEPTH, 128, 32, NS])
        vns = dout("vns", [DEPTH, NS, 512])
        xs1 = nc.dram_tensor("xs1", [NS, D], F32).ap()
        yp = dout("yp", [SEQ, D])
        kvw = dout("kvw", [DEPTH, 128, 256])
        hst_o = dout("hst", [DEPTH, 128, 32])

        x1 = nc.dram_tensor("x1", [SEQ, D], F32).ap()
        wbf = nc.dram_tensor("wbf", [DEPTH, NCH, 128, 4096], BF16).ap()

        S = Sched(nc, es)

        def sb(name, shape, dt=F32):
            return T(es.enter_context(nc.sbuf_tensor("s_" + name, list(shape), dt)), name)

        ident = sb("ident", [128, 128])
        swapm = sb("swapm", [128, 128])
        masks = sb("masks", [128, 2, 128], BF16)
        ak = sb("ak", [2, 256], BF16)
        aq = sb("aq", [2, 1024], BF16)
        ones = sb("ones", [128, 128], BF16)
        rmask = sb("rmask", [128, 8])
        xtok = [sb("xtok%d" % i, [128, D]) for i in range(2 * NBK)]
        xT = sb("xT", [128, 8, TT], BF16)
        wbuf = [sb("wbuf%d" % i, [128, 4096], BF16) for i in range(3)]
        qT = sb("qT", [128, 4, TT], BF16)
        kT = sb("kT", [128, 2, TT + 128], BF16)
        vtok = sb("vtok", [128, NBK + 1, 2, 128], BF16)
        sza = sb("sza", [128, 4, TT], BF16)
        szb = sb("szb", [128, 4, TT], BF16)
        uT = sb("uT", [128, 4, TT], BF16)
        ucz = sb("ucz", [128, 4, TT], BF16)
        vn = sb("vn", [128, 4, BW], BF16)
        ys = sb("ys", [128, 12, TT], BF16)
        ypre = sb("ypre", [128, 4, TT])
        gtmp = sb("gtmp", [128, 4, TT])
        gtmp2 = sb("gtmp2", [128, 4, TT], BF16)
        sig = sb("sig", [128, 4, TT], BF16)
        mtmp = [sb("mtmp%d" % i, [128, 512]) for i in range(2)]
        mrgT = sb("mrgT", [128, 8, TT], BF16)
        pT = [[sb("pT%d%d" % (kv, pt), [128, 512], BF16) for pt in range(2)] for kv in range(2)]
        rden = [sb("rden%d" % kv, [128, 512]) for kv in range(2)]
        otmp = [sb("otmp%d" % kv, [128, 512], BF16) for kv in range(2)]
        lnst = sb("lnst", [128, 12])
        lnmv = sb("lnmv", [128, 4])
        lntmp = sb("lntmp", [128, 512])
        lnout = [sb("lnout%d" % i, [128, D]) for i in range(2)]
        colp = sb("colp", [128, 96])
        cdt = sb("cdt", [128, 32])
        crr = sb("crr", [128, 32])
        cth = sb("cth", [128, 32])
        ctmp = sb("ctmp", [128, 32])
        cc1 = sb("cc1", [128, 32])
        ctmp2 = sb("ctmp2", [128, 32])
        citmp = sb("citmp", [128, 32], mybir.dt.int32)
        cs1 = sb("cs1", [128, 32])
        tabc = sb("tabc", [128, 32, TT], BF16)
        tabs = sb("tabs", [128, 32, TT], BF16)
        bbw = sb("bbw", [128, 32, 128], BF16)
        bbs = sb("bbs", [128, 32, 128], BF16)
        cpw = sb("cpw", [128, 32, 128], BF16)
        dsk = sb("dsk", [128, 4])
        glub = sb("glub", [128, 4])
        sgn = sb("sgn", [128, 2, 512])
        sgw = sb("sgw", [128, 4, 128], BF16)
        triu = sb("triu", [128, 128])
        sgb = sb("sgb", [128, 512])
        lng = sb("lng", [128, 2, 1024])
        esk = sb("esk", [128, 8])
        sigB = sb("sigB", [128, 8, TT], BF16)
        hst = sb("hst", [128, 32])
        s5a = [sb("s5a%d" % i, [128, TT]) for i in range(2)]
        s5b = [sb("s5b%d" % i, [128, TT]) for i in range(2)]
        s5v = [sb("s5v%d" % i, [128, TT]) for i in range(2)]
        s5g = [sb("s5g%d" % i, [128, TT]) for i in range(2)]
        s5c = [sb("s5c%d" % i, [128, TT]) for i in range(3)]
        s5d = [sb("s5d%d" % i, [128, TT]) for i in range(2)]
        s5hb = [sb("s5hb%d" % i, [128, TT], BF16) for i in range(2)]

        aqs = sb("aqs", [2, 8], BF16)
        hsel = sb("hsel", [128, 2], BF16)
        hselT = sb("hselT", [2, 128], BF16)
        signc = sb("signc", [128, 1])
        w00 = sb("w00", [NS, 4])
        w00I = sb("w00I", [NS, 4, NS], BF16)
        b00 = sb("b00", [128, 4])
        esks = sb("esks", [128, 4])
        car = sb("car", [128, 32])
        cais = sb("cais", [128, 32])
        ksd = sb("ksd", [128, 2, NS], BF16)
        vsd = sb("vsd", [128, 2, NS])
        prod = sb("prod", [128, 4, NS], BF16)
        pself = sb("pself", [2, 64], BF16)
        pbs = sb("pbs", [128, 4, NS])
        st1 = sb("st1", [128, NS])
        st2 = sb("st2", [128, NS])

        class AV:
            def __init__(self, ap, b):
                self.ap = ap
                self.b = b

            def __getitem__(self, k):
                return self.ap[k]

        ck = AV(tabc[:, :, 0:128], tabc.b)
        kvout = AV(lnout[1][:, 0:256], lnout[1].b)
        kvs_sb = AV(rden[0][0:NS, 0:256], rden[0].b)
        vnS = AV(otmp[0][0:NS, :], otmp[0].b)
        cv = AV(tabs[:, :, 0:128], tabs.b)
        h0 = AV(mtmp[0][:, 0:32 * NS].rearrange("p (g n) -> p g n", n=NS), mtmp[0].b)
        h0s = AV(mtmp[1][:, 0:32 * NS].rearrange("p (g n) -> p g n", n=NS), mtmp[1].b)
        hbs = AV(pT[0][0][:, 0:32 * NS].rearrange("p (g n) -> p g n", n=NS), pT[0][0].b)
        pSb = AV(pT[0][1][:, 0:128], pT[0][1].b)
        sdn = AV(rden[0][:, 0:128], rden[0].b)
        sso = AV(rden[1][:, 0:128], rden[1].b)

        psum = [T(es.enter_context(nc.psum_tensor("ps%d" % i, [128, 512], F32)), "ps%d" % i) for i in range(8)]
        for p_ in psum:
            p_.b.excl = True
        pools = {"A": [0, 1], "B": [2, 3], "C": [4], "Y": [5], "D": [6, 7]}
        prr = {k: 0 for k in pools}

        def ps_alloc(pool):
            i = prr[pool]
            prr[pool] = (i + 1) % len(pools[pool])
            return psum[pools[pool][i]]

        rr = {}

        def rot(lst, key):
            i = rr.get(key, 0)
            rr[key] = (i + 1) % len(lst)
            return lst[i]

        wbf_b = [[Buf("wbf%d_%d" % (l, c)) for c in range(NCH)] for l in range(DEPTH)]
        for l in range(DEPTH):
            for c in range(NCH):
                S.dma("pool", wbf[l, c], wch[l, c], w=[wbf_b[l][c]])

        S.dma("sp", ident[:], ident_d, w=[ident.b])
        S.dma("sp", swapm[:], swap_d, w=[swapm.b])
        S.dma("pool", masks[:], masks_d.rearrange("a p c -> p a c")[:, :, 0:128], w=[masks.b])
        S.dma("pool", ak[:], ak_d, w=[ak.b])
        S.dma("pool", aq[:], aq_d, w=[aq.b])
        S.dma("sp", rmask[:], rmask_d, w=[rmask.b])
        S.dma("sp", triu[:], triu_d, w=[triu.b])
        S.op("dve", lambda e: e.memset(ones[:], 1.0), w=[ones.b])
        S.dma("pool", aqs[:], aqs_d, w=[aqs.b])
        S.dma("pool", hsel[:], hsel_d, w=[hsel.b])
        S.dma("pool", hselT[:], hselT_d, w=[hselT.b])
        S.dma("sp", signc[:], signc_d, w=[signc.b])

        def nk_of(c):
            return 4 if (c == C_GLU or C_R <= c < C_R + 6) else 8

        wst = {"issued": 0, "used": 0, "hold": False}

        def w_issue():
            i = wst["issued"]
            if i >= len(order):
                return
            l, c, nk = order[i]
            wb = wbuf[i % 3]
            S.dma("sp", wb[:, 0:nk * 512], wbf[l, c, :, 0:nk * 512], r=[wbf_b[l][c]], w=[wb.b])
            wst["issued"] = i + 1

        def w_next(l, c):
            i = wst["used"]
            if recording:
                order.append((l, c, nk_of(c)))
            assert order[i][0] == l and order[i][1] == c, (order[i], l, c)
            while wst["issued"] < min(i + (1 if recording else 2), len(order)):
                w_issue()
            wst["used"] = i + 1
            wst["hold"] = True
            return wbuf[i % 3]

        def w_done():
            wst["hold"] = False
            if recording:
                return
            while wst["issued"] < min(wst["used"] + 2, len(order)):
                w_issue()

        def evac(eng, fn, r, w):
            S.op(eng, fn, r=r, w=w)

        def mm_fm(ps, wb, j, nk, rhs_fn, rbufs, n):
            for kb in range(nk):
                S.op("pe", lambda e, kb=kb: e.matmul(ps[:, 0:n], lhsT=wb[:, kb * 512 + j * 128: kb * 512 + (j + 1) * 128],
                                                     rhs=rhs_fn(kb), start=(kb == 0), stop=(kb == nk - 1)),
                     r=[wb.b] + rbufs, w=[ps.b], sig=(kb == nk - 1))

        def v2(t, width):
            ap = t[:]
            if len(ap.shape) == 3:
                ap = ap.rearrange("p a b -> p (a b)")
            return ap[:, 0:width]

        def setup_layer(l):
            S.dma("sp", colp[:], colp_d[l], w=[colp.b])
            S.dma("pool", cpw[:], cpad_d[l].rearrange("g p m -> p g m"), w=[cpw.b])
            S.dma("sp", dsk[:], dsk_d[l], w=[dsk.b])
            S.dma("sp", glub[:], glub_d[l], w=[glub.b])
            S.dma("sp", sgn[:], sgn_d[l].rearrange("a p c -> p a c"), w=[sgn.b])
            sgw32 = v2(ypre, 512).rearrange("p (g t) -> p g t", g=4)
            S.dma("sp", sgw32, sgw_d[l].rearrange("g s t -> s g t"), w=[ypre.b])
            S.dma("sp", sgb[:], sgb_d[l], w=[sgb.b])
            S.dma("sp", lng[:], lng_d[l].rearrange("a p c -> p a c"), w=[lng.b])
            S.dma("sp", esk[:], snk_d[l], w=[esk.b])
            for g4 in range(4):
                S.op("dve", lambda e, g4=g4: e.tensor_tensor(out=sgw[:, g4, :], in0=sgw32[:, g4, :], in1=triu[:], op=ALU.mult),
                     r=[ypre.b, triu.b], w=[sgw.b])
            S.op("act", lambda e: e.activation(out=esk[:], in_=esk[:], func=AF.Exp), r=[esk.b], w=[esk.b])
            S.op("dve", lambda e: e.tensor_scalar(out=cpw[64:128], in0=cpw[64:128], scalar1=-1.0, scalar2=None, op0=ALU.mult),
                 r=[cpw.b], w=[cpw.b])
            S.op("dve", lambda e: e.memset(hst[:], 0.0), w=[hst.b] + hstb)

            lr_c, li_c, ld_c = colp[:, 0:32], colp[:, 32:64], colp[:, 64:96]
            S.op("act", lambda e: e.activation(out=cdt[:], in_=ld_c, func=AF.Exp), r=[colp.b], w=[cdt.b])
            S.op("dve", lambda e: e.tensor_tensor(out=ctmp[:], in0=lr_c, in1=cdt[:], op=ALU.mult), r=[colp.b, cdt.b], w=[ctmp.b])
            S.op("act", lambda e: e.activation(out=crr[:], in_=ctmp[:], func=AF.Exp), r=[ctmp.b], w=[crr.b])
            S.op("dve", lambda e: e.tensor_tensor(out=cth[:], in0=li_c, in1=cdt[:], op=ALU.mult), r=[colp.b, cdt.b], w=[cth.b])

            def sincos(out_s, out_c, th, thb, tmp, tb, m, mb, itmp, ib, sb_, cb):
                I32 = mybir.dt.int32
                for dst, db, shift in ((out_s, sb_, 0.0), (out_c, cb, 0.5 * math.pi)):
                    S.op("dve", lambda e, shift=shift: e.tensor_scalar(out=tmp, in0=th, scalar1=shift, scalar2=1.0 / (2 * math.pi),
                                                                       op0=ALU.add, op1=ALU.mult), r=[thb], w=[tb])
                    S.op("dve", lambda e: e.tensor_copy(out=itmp, in_=tmp), r=[tb], w=[ib])
                    S.op("dve", lambda e: e.tensor_copy(out=tmp, in_=itmp), r=[ib], w=[tb])
                    S.op("dve", lambda e: e.scalar_tensor_tensor(out=tmp, in0=tmp, scalar=-2 * math.pi, in1=th, op0=ALU.mult, op1=ALU.add),
                         r=[tb, thb], w=[tb])
                    S.op("dve", lambda e, shift=shift: e.tensor_scalar(out=tmp, in0=tmp, scalar1=shift, scalar2=None, op0=ALU.add),
                         r=[tb], w=[tb])
                    S.op("dve", lambda e: e.tensor_scalar(out=m, in0=tmp, scalar1=math.pi, scalar2=-2 * math.pi, op0=ALU.is_gt, op1=ALU.mult),
                         r=[tb], w=[mb])
                    S.op("dve", lambda e: e.tensor_tensor(out=tmp, in0=tmp, in1=m, op=ALU.add), r=[tb, mb], w=[tb])
                    S.op("dve", lambda e: e.tensor_scalar(out=m, in0=tmp, scalar1=-math.pi, scalar2=2 * math.pi, op0=ALU.is_lt, op1=ALU.mult),
                         r=[tb], w=[mb])
                    S.op("dve", lambda e: e.tensor_tensor(out=tmp, in0=tmp, in1=m, op=ALU.add), r=[tb, mb], w=[tb])
                    S.op("dve", lambda e: e.tensor_scalar(out=tmp, in0=tmp, scalar1=math.pi, scalar2=-math.pi, op0=ALU.min, op1=ALU.max),
                         r=[tb], w=[tb])
                    S.op("act", lambda e, dst=dst: e.activation(out=dst, in_=tmp, func=AF.Sin), r=[tb], w=[db])

            sincos(cs1[:], cc1[:], cth[:], cth.b, ctmp[:], ctmp.b, ctmp2[:], ctmp2.b, citmp[:], citmp.b, cs1.b, cc1.b)
            S.op("dve", lambda e: e.tensor_tensor(out=car[:], in0=crr[:], in1=cc1[:], op=ALU.mult), r=[crr.b, cc1.b], w=[car.b])
            S.op("dve", lambda e: e.tensor_tensor(out=cais[:], in0=crr[:], in1=cs1[:], op=ALU.mult), r=[crr.b, cs1.b], w=[cais.b])
            S.op("dve", lambda e: e.tensor_scalar(out=cais[:], in0=cais[:], scalar1=signc[:, 0:1], scalar2=None, op0=ALU.mult),
                 r=[cais.b, signc.b], w=[cais.b])
            S.dma("sp", w00[:], w00_d[l], w=[w00.b])
            S.dma("sp", b00[:], b00_d[l], w=[b00.b])
            S.dma("sp", esks[:], esks_d[l], w=[esks.b])
            S.op("act", lambda e: e.activation(out=esks[:], in_=esks[:], func=AF.Exp), r=[esks.b], w=[esks.b])
            for g4 in range(4):
                S.op("dve", lambda e, g4=g4: e.tensor_scalar(out=w00I[:, g4, :], in0=ident[0:NS, 0:NS], scalar1=w00[:, g4:g4 + 1],
                                                             scalar2=None, op0=ALU.mult), r=[ident.b, w00.b], w=[w00I.b])
            for qq in range(8):
                gs = slice(qq * 4, (qq + 1) * 4)
                ccos = v2(ypre, 4 * TT).rearrange("p (g t) -> p g t", g=4)
                csin = v2(gtmp, 4 * TT).rearrange("p (g t) -> p g t", g=4)
                ctm = v2(mtmp[0], 2 * TT).rearrange("p (g t) -> p g t", g=4)
                ctm2 = v2(mtmp[1], 2 * TT).rearrange("p (g t) -> p g t", g=4)
                S.op("dve", lambda e: e.tensor_copy(out=ccos[:, :, 0], in_=cc1[:, gs]), r=[cc1.b], w=[ypre.b])
                S.op("dve", lambda e: e.tensor_copy(out=csin[:, :, 0], in_=cs1[:, gs]), r=[cs1.b], w=[gtmp.b])
                n = 1
                while n < TT:
                    cn = ccos[:, :, n - 1:n].to_broadcast([128, 4, n])
                    sn = csin[:, :, n - 1:n].to_broadcast([128, 4, n])
                    c0 = ccos[:, :, 0:n]
                    s0 = csin[:, :, 0:n]
                    t1 = ctm[:, :, 0:n]
                    t2 = ctm2[:, :, 0:n]
                    S.op("dve", lambda e, c0=c0, cn=cn, t1=t1: e.tensor_tensor(out=t1, in0=c0, in1=cn, op=ALU.mult), r=[ypre.b], w=[mtmp[0].b])
                    S.op("dve", lambda e, s0=s0, sn=sn, t2=t2: e.tensor_tensor(out=t2, in0=s0, in1=sn, op=ALU.mult), r=[gtmp.b], w=[mtmp[1].b])
                    S.op("dve", lambda e, t1=t1, t2=t2, n=n: e.tensor_tensor(out=ccos[:, :, n:2 * n], in0=t1, in1=t2, op=ALU.subtract),
                         r=[mtmp[0].b, mtmp[1].b], w=[ypre.b])
                    S.op("dve", lambda e, c0=c0, sn=sn, t1=t1: e.tensor_tensor(out=t1, in0=c0, in1=sn, op=ALU.mult), r=[ypre.b, gtmp.b], w=[mtmp[0].b])
                    S.op("dve", lambda e, s0=s0, cn=cn, t2=t2: e.tensor_tensor(out=t2, in0=s0, in1=cn, op=ALU.mult), r=[ypre.b, gtmp.b], w=[mtmp[1].b])
                    S.op("dve", lambda e, t1=t1, t2=t2, n=n: e.tensor_tensor(out=csin[:, :, n:2 * n], in0=t1, in1=t2, op=ALU.add),
                         r=[mtmp[0].b, mtmp[1].b], w=[gtmp.b])
                    n *= 2
                S.op("act", lambda e: e.copy(out=tabc[:, gs, :], in_=ccos), r=[ypre.b], w=[tabc.b])
                S.op("act", lambda e: e.copy(out=tabs[:, gs, :], in_=csin), r=[gtmp.b], w=[tabs.b])

            tl = [T(None, "x")] * 0
            hold = [(xtok[0], 0), (xtok[0], 512), (xtok[1], 0), (xtok[1], 512), (lnout[0], 0), (lnout[0], 512),
                    (lnout[1], 0), (lnout[1], 512)]

            class V:
                def __init__(self, t, off):
                    self.ap = t[:, off:off + 512]
                    self.b = t.b

            for qq in range(4):
                gs = slice(qq * 8, (qq + 1) * 8)
                cs_q = slice(qq * 512, (qq + 1) * 512)
                lr, li, ld, dt_, mg, th, sn_, cs_ = [V(t, o) for t, o in hold]
                for i, dst in enumerate((lr, li, ld)):
                    S.dma("sp", dst.ap, rowp_d[l, i][:, cs_q], w=[dst.b])
                bld = v2(xtok[2], 1024).rearrange("k (a g p) -> k a g p", a=2, g=8)
                for a in range(2):
                    S.dma("sp", bld[:, a], bpad_d[l, a, qq * 8:(qq + 1) * 8].rearrange("g k p -> k g p"), w=[xtok[2].b])

                def tt(out, a, b_, op):
                    S.op("dve", lambda e: e.tensor_tensor(out=out.ap, in0=a.ap, in1=b_.ap, op=op), r=[a.b, b_.b], w=[out.b])

                S.op("act", lambda e: e.activation(out=dt_.ap, in_=ld.ap, func=AF.Exp), r=[ld.b], w=[dt_.b])
                tt(mg, lr, dt_, ALU.mult)
                S.op("act", lambda e: e.activation(out=mg.ap, in_=mg.ap, func=AF.Exp), r=[mg.b], w=[mg.b])
                tt(th, li, dt_, ALU.mult)
                sincos(sn_.ap, cs_.ap, th.ap, th.b, ld.ap, ld.b, v2(gtmp, 512), gtmp.b, v2(ypre, 512).bitcast(mybir.dt.int32), ypre.b, sn_.b, cs_.b)
                tt(cs_, cs_, mg, ALU.mult)
                S.op("dve", lambda e: e.tensor_scalar(out=cs_.ap, in0=cs_.ap, scalar1=-1.0, scalar2=None, op0=ALU.add), r=[cs_.b], w=[cs_.b])
                tt(sn_, sn_, mg, ALU.mult)
                tt(mg, lr, lr, ALU.mult)
                tt(th, li, li, ALU.mult)
                tt(mg, mg, th, ALU.add)
                S.op("dve", lambda e: e.reciprocal(out=mg.ap, in_=mg.ap), r=[mg.b], w=[mg.b])
                tt(dt_, cs_, lr, ALU.mult)
                tt(ld, sn_, li, ALU.mult)
                tt(dt_, dt_, ld, ALU.add)
                tt(dt_, dt_, mg, ALU.mult)
                tt(th, sn_, lr, ALU.mult)
                tt(ld, cs_, li, ALU.mult)
                tt(th, th, ld, ALU.subtract)
                tt(th, th, mg, ALU.mult)
                crv = dt_.ap.rearrange("k (g p) -> k g p", p=64)
                civ = th.ap.rearrange("k (g p) -> k g p", p=64)
                t1 = sn_.ap.rearrange("k (g p) -> k g p", p=64)
                t2 = cs_.ap.rearrange("k (g p) -> k g p", p=64)
                bre = bld[:, 0]
                bim = bld[:, 1]
                S.op("dve", lambda e: e.tensor_tensor(out=t1, in0=crv, in1=bre, op=ALU.mult), r=[dt_.b, xtok[2].b], w=[sn_.b])
                S.op("dve", lambda e: e.tensor_tensor(out=t2, in0=civ, in1=bim, op=ALU.mult), r=[th.b, xtok[2].b], w=[cs_.b])
                S.op("dve", lambda e: e.tensor_tensor(out=bbw[:, gs, 0:64], in0=t1, in1=t2, op=ALU.subtract), r=[sn_.b, cs_.b], w=[bbw.b])
                S.op("dve", lambda e: e.tensor_tensor(out=bbs[:, gs, 64:128], in0=t2, in1=t1, op=ALU.subtract), r=[sn_.b, cs_.b], w=[bbs.b])
                S.op("dve", lambda e: e.tensor_tensor(out=t1, in0=crv, in1=bim, op=ALU.mult), r=[dt_.b, xtok[2].b], w=[sn_.b])
                S.op("dve", lambda e: e.tensor_tensor(out=t2, in0=civ, in1=bre, op=ALU.mult), r=[th.b, xtok[2].b], w=[cs_.b])
                S.op("dve", lambda e: e.tensor_tensor(out=bbw[:, gs, 64:128], in0=t1, in1=t2, op=ALU.add), r=[sn_.b, cs_.b], w=[bbw.b])
                S.op("dve", lambda e: e.tensor_tensor(out=bbs[:, gs, 0:64], in0=t1, in1=t2, op=ALU.add), r=[sn_.b, cs_.b], w=[bbs.b])

        hstb = [Buf("hst%d" % g) for g in range(32)]
        ysb = [Buf("ys_a"), Buf("ys_b"), Buf("ys_c")]

        def s5_pipe(n):
            cols = slice(0, n)
            st = {}
            psy_cur = {}
            for it in range(32 + 7):
                g = it - 7
                if 0 <= g < 32:
                    fb, gl = g // 8, g % 8
                    if gl == 0:
                        psy_cur["p"] = ps_alloc("Y")
                    psy = psy_cur["p"]
                    hb = st[g]["hb"]
                    S.op("pe", lambda e, g=g, gl=gl, psy=psy, hb=hb: e.matmul(psy[:, 0:n], lhsT=cpw[:, g, :], rhs=hb[:, 0:n],
                                                                               start=(gl == 0), stop=(gl == 7)),
                         r=[cpw.b, hb.b], w=[psy.b], sig=True)
                    if gl == 7:
                        S.op("dve", lambda e, fb=fb, psy=psy: e.scalar_tensor_tensor(out=ypre[:, fb, cols], in0=uT[:, fb, cols],
                                                                                    scalar=dsk[:, fb:fb + 1], in1=psy[:, 0:n],
                                                                                    op0=ALU.mult, op1=ALU.add),
                             r=[uT.b, dsk.b, psy.b], w=[ypre.b])
                    del st[g]
                g = it - 6
                if 0 <= g < 32:
                    c, d = st[g]["c"], st[g]["d"]
                    hb = rot(s5hb, "hb")
                    st[g]["hb"] = hb
                    S.op("pool", lambda e, c=c, d=d, hb=hb: e.tensor_tensor(out=hb[:, 0:n], in0=c[:, 0:n], in1=d[:, 0:n], op=ALU.add),
                         r=[c.b, d.b], w=[hb.b])
                    S.op("pool", lambda e, c=c, d=d, g=g: e.tensor_tensor(out=hst[:, g:g + 1], in0=c[:, n - 1:n], in1=d[:, n - 1:n], op=ALU.add),
                         r=[c.b, d.b], w=[hstb[g]])
                g = it - 5
                if 0 <= g < 32:
                    psw = st[g]["psw"]
                    d = rot(s5d, "d")
                    st[g]["d"] = d
                    S.op("dve", lambda e, g=g, psw=psw, d=d: e.tensor_tensor(out=d[:, 0:n], in0=psw[:, 0:n], in1=tabs[:, g, 0:n], op=ALU.mult),
                         r=[psw.b, tabs.b], w=[d.b])
                g = it - 4
                if 0 <= g < 32:
                    gg = st[g]["gg"]
                    psw = ps_alloc("C")
                    st[g]["psw"] = psw
                    S.op("pe", lambda e, gg=gg, psw=psw: e.matmul(psw[:, 0:n], lhsT=swapm[:], rhs=gg[:, 0:n], start=True, stop=True),
                         r=[swapm.b, gg.b], w=[psw.b])
                    c = rot(s5c, "c")
                    st[g]["c"] = c
                    S.op("dve", lambda e, g=g, gg=gg, c=c: e.tensor_tensor(out=c[:, 0:n], in0=gg[:, 0:n], in1=tabc[:, g, 0:n], op=ALU.mult),
                         r=[gg.b, tabc.b], w=[c.b])
                g = it - 3
                if 0 <= g < 32:
                    v = st[g]["v"]
                    gg = rot(s5g, "g")
                    st[g]["gg"] = gg
                    S.op("dve", lambda e, g=g, v=v, gg=gg: e.tensor_tensor_scan(out=gg[:, 0:n], data0=crr[:, g:g + 1].to_broadcast([128, n]),
                                                                               data1=v[:, 0:n], initial=hst[:, g:g + 1], op0=ALU.mult, op1=ALU.add),
                         r=[crr.b, v.b, hstb[g]], w=[gg.b])
                g = it - 2
                if 0 <= g < 32:
                    a, b_ = st[g]["a"], st[g]["b"]
                    v = rot(s5v, "v")
                    st[g]["v"] = v
                    S.op("pool", lambda e, a=a, b_=b_, v=v: e.tensor_tensor(out=v[:, 0:n], in0=a[:, 0:n], in1=b_[:, 0:n], op=ALU.add),
                         r=[a.b, b_.b], w=[v.b])
                g = it - 1
                if 0 <= g < 32:
                    pv = st[g]["pv"]
                    a = rot(s5a, "a")
                    b_ = rot(s5b, "b")
                    st[g]["a"], st[g]["b"] = a, b_
                    S.op("dve", lambda e, g=g, pv=pv, a=a: e.tensor_tensor(out=a[:, 0:n], in0=pv[:, 0:n], in1=tabc[:, g, 0:n], op=ALU.mult),
                         r=[pv.b, tabc.b], w=[a.b])
                    S.op("dve", lambda e, g=g, pv=pv, b_=b_: e.tensor_tensor(out=b_[:, 0:n], in0=pv[:, TT:TT + n], in1=tabs[:, g, 0:n], op=ALU.mult),
                         r=[pv.b, tabs.b], w=[b_.b])
                g = it
                if 0 <= g < 32:
                    fb = g // 8
                    pv = ps_alloc("B")
                    st[g] = {"pv": pv}
                    S.op("pe", lambda e, g=g, fb=fb, pv=pv: e.matmul(pv[:, 0:n], lhsT=bbw[:, g, :], rhs=uT[:, fb, cols], start=True, stop=True),
                         r=[bbw.b, uT.b], w=[pv.b], sig=False)
                    S.op("pe", lambda e, g=g, fb=fb, pv=pv: e.matmul(pv[:, TT:TT + n], lhsT=bbs[:, g, :], rhs=uT[:, fb, cols], start=True, stop=True),
                         r=[bbs.b, uT.b], w=[pv.b])
                yield

        def gelu_glu_gen(l, n):
            yv = ypre[:, :, 0:n]
            g1 = gtmp[:, :, 0:n]
            S.op("dve", lambda e: e.tensor_tensor(out=g1, in0=yv, in1=yv, op=ALU.mult), r=[ypre.b], w=[gtmp.b])
            S.op("dve", lambda e: e.tensor_scalar(out=g1, in0=g1, scalar1=GK, scalar2=1.0, op0=ALU.mult, op1=ALU.add), r=[gtmp.b], w=[gtmp.b])
            yield
            S.op("pool", lambda e: e.tensor_tensor(out=g1, in0=g1, in1=yv, op=ALU.mult), r=[gtmp.b, ypre.b], w=[gtmp.b])
            yield
            S.op("act", lambda e: e.activation(out=g1, in_=g1, func=AF.Sigmoid, scale=GS), r=[gtmp.b], w=[gtmp.b])
            yield
            S.op("dve", lambda e: e.tensor_tensor(out=yv, in0=yv, in1=g1, op=ALU.mult), r=[gtmp.b, ypre.b], w=[ypre.b])
            yield
            S.op("act", lambda e: e.copy(out=gtmp2[:, :, 0:n], in_=yv), r=[ypre.b], w=[gtmp2.b])
            yield
            wb = w_next(l, C_GLU)
            for j in range(4):
                ps = ps_alloc("A")
                mm_fm(ps, wb, j, 4, lambda kb: gtmp2[:, kb, 0:n], [gtmp2.b], n)
                S.op("act", lambda e, j=j, ps=ps: e.activation(out=gtmp[:, j, 0:n], in_=ps[:, 0:n], func=AF.Sigmoid, bias=glub[:, j:j + 1]),
                     r=[ps.b, glub.b], w=[gtmp.b])
            w_done()
            yield
            S.op("dve", lambda e: e.tensor_tensor(out=yv, in0=yv, in1=g1, op=ALU.mult), r=[gtmp.b, ypre.b], w=[ypre.b])
            yield
            S.op("pool", lambda e: e.tensor_tensor(out=ys[:, 4:8, 0:n], in0=yv, in1=szb[:, :, 0:n], op=ALU.mult),
                 r=[ypre.b, szb.b], w=[ysb[1]])

        def gelu_glu(l, n):
            for _ in gelu_glu_gen(l, n):
                pass

        def layernorm_rows(src, nrow, width, gam, bet, dst, tmp=None, eng2="pool"):
            if tmp is None:
                tmp = (lntmp.b, lntmp[0:nrow, 0:width])
            nchk = width // 512
            for i in range(nchk):
                S.op("dve", lambda e, i=i: e.bn_stats(out=lnst[0:nrow, i * 6:(i + 1) * 6], in_=src[1][:, i * 512:(i + 1) * 512]),
                     r=[src[0]], w=[lnst.b])
            S.op("dve", lambda e: e.bn_aggr(out=lnmv[0:nrow, 0:2], in_=lnst[0:nrow, 0:6 * nchk]), r=[lnst.b], w=[lnmv.b])
            S.op("dve", lambda e: e.tensor_scalar(out=lnmv[0:nrow, 2:3], in0=lnmv[0:nrow, 1:2], scalar1=LN_EPS, scalar2=None,
                                                  op0=ALU.add), r=[lnmv.b], w=[lnmv.b])
            S.op("act", lambda e: e.sqrt(out=lnmv[0:nrow, 2:3], in_=lnmv[0:nrow, 2:3]), r=[lnmv.b], w=[lnmv.b])
            S.op("dve", lambda e: e.reciprocal(out=lnmv[0:nrow, 2:3], in_=lnmv[0:nrow, 2:3]), r=[lnmv.b], w=[lnmv.b])
            S.op("dve", lambda e: e.tensor_scalar(out=tmp[1], in0=src[1], scalar1=lnmv[0:nrow, 0:1],
                                                  scalar2=lnmv[0:nrow, 2:3], op0=ALU.subtract, op1=ALU.mult),
                 r=[src[0], lnmv.b], w=[tmp[0]])
            S.op("dve", lambda e: e.tensor_tensor(out=tmp[1], in0=tmp[1], in1=gam[1], op=ALU.mult),
                 r=[tmp[0], gam[0]], w=[tmp[0]])
            S.op(eng2, lambda e: e.tensor_tensor(out=dst[1], in0=tmp[1], in1=bet[1], op=ALU.add),
                 r=[tmp[0], bet[0]], w=[dst[0]])

        def xload(l, ti):
            t0 = ti * TT
            src = xp if l == 0 else x1
            for b in range(NBK):
                xt = xtok[(ti % 2) * NBK + b]
                S.dma("sp", xt[:], src[t0 + b * 128: t0 + (b + 1) * 128, :], r=([x1_b] if l == 1 else []), w=[xt.b])

        def front(l, ti):
            n = TT
            xts = [xtok[(ti % 2) * NBK + b] for b in range(NBK)]
            for kb in range(8):
                ps = ps_alloc("A")
                for b in range(NBK):
                    S.op("pe", lambda e, b=b, ps=ps, kb=kb: e.transpose(out=ps[:, b * 128:(b + 1) * 128],
                                                                       in_=xts[b][:, kb * 128:(kb + 1) * 128], identity=ident[:]),
                         r=[xts[b].b, ident.b], w=[ps.b], sig=(b == NBK - 1))
                S.op("act" if kb % 2 else "dve",
                     (lambda e, ps=ps, kb=kb: e.copy(out=xT[:, kb, :], in_=ps[:, 0:TT])) if kb % 2 else
                     (lambda e, ps=ps, kb=kb: e.tensor_copy(out=xT[:, kb, :], in_=ps[:, 0:TT])),
                     r=[ps.b], w=[xT.b])
            xrhs = lambda kb: xT[:, kb, :]
            wb = w_next(l, C_UB)
            for j in range(4):
                ps = ps_alloc("A")
                mm_fm(ps, wb, j, 8, xrhs, [xT.b], n)
                S.op("act", lambda e, j=j, ps=ps: e.copy(out=uT[:, j, :], in_=ps[:, 0:TT]), r=[ps.b], w=[uT.b])
            w_done()

        def tail(l, ti):
            t0 = ti * TT
            dst = x1 if l == 0 else yp
            for _ in gelu_glu_gen(l, TT):
                yield
            yield
            for _ in readout_br(l, TT, (1,), False, pre=True):
                yield
            blocks = [(xtok[(ti % 2) * NBK + b], 128, dst[t0 + b * 128: t0 + (b + 1) * 128, :]) for b in range(NBK)]
            for _ in wo_ln(l, TT, blocks):
                yield
            if ti + 2 < (NTILE if DBG_TILES is None else DBG_TILES):
                xload(l, ti + 2)

        def rest_gen(l, ti):
            n = TT
            xrhs = lambda kb: xT[:, kb, :]

            wb = w_next(l, C_Q)
            for j in range(4):
                ps = ps_alloc("A")
                mm_fm(ps, wb, j, 8, xrhs, [xT.b], n)
                S.op("act", lambda e, j=j, ps=ps: e.activation(out=qT[:, j, :], in_=ps[:, 0:TT], func=AF.Copy, scale=0.125),
                     r=[ps.b], w=[qT.b])
                yield
            w_done()
            yield
            wb = w_next(l, C_KD)
            for j in range(2):
                ps = ps_alloc("A")
                mm_fm(ps, wb, j, 8, xrhs, [xT.b], n)
                S.op("act", lambda e, j=j, ps=ps: e.copy(out=kT[:, j, 128:128 + TT], in_=ps[:, 0:TT]), r=[ps.b], w=[kT.b])
                yield
            w_done()
            yield
            wb = w_next(l, C_KV)
            for b in range(NBK):
                ps = ps_alloc("A")
                for kb in range(8):
                    S.op("pe", lambda e, kb=kb, b=b, ps=ps: e.matmul(ps[:, 0:256], lhsT=xT[:, kb, b * 128:(b + 1) * 128],
                                                                    rhs=wb[:, kb * 512: kb * 512 + 256], start=(kb == 0), stop=(kb == 7)),
                         r=[wb.b, xT.b], w=[ps.b], sig=(kb == 7))
                vsrc = ps[:, 128:256].rearrange("p (k d) -> p k d", k=2)
                S.op("act", lambda e, b=b, vsrc=vsrc: e.copy(out=vtok[:, b + 1, :, 0:64], in_=vsrc), r=[ps.b], w=[vtok.b])
                S.op("act", lambda e, b=b, vsrc=vsrc: e.copy(out=vtok[:, b + 1, :, 64:128], in_=vsrc), r=[ps.b], w=[vtok.b])
                if ti == (NTILE if DBG_TILES is None else DBG_TILES) - 1 and b == NBK - 1:
                    S.op("act", lambda e, ps=ps: e.copy(out=kvout[:], in_=ps[:, 0:256]), r=[ps.b], w=[kvout.b])
                    S.dma("pool", kvw[l], kvout[:], r=[kvout.b])
                yield
            w_done()
            yield
            for cid, dstT, fn in ((C_ZA, sza, AF.Silu), (C_UC, ucz, None)):
                wb = w_next(l, cid)
                for j in range(4):
                    ps = ps_alloc("A")
                    mm_fm(ps, wb, j, 8, xrhs, [xT.b], n)
                    if fn is None:
                        S.op("act", lambda e, j=j, ps=ps, dstT=dstT: e.copy(out=dstT[:, j, :], in_=ps[:, 0:TT]), r=[ps.b], w=[dstT.b])
                    else:
                        S.op("act", lambda e, j=j, ps=ps, dstT=dstT, fn=fn: e.activation(out=dstT[:, j, :], in_=ps[:, 0:TT], func=fn),
                             r=[ps.b], w=[dstT.b])
                    yield
                w_done()
                yield
            wb = w_next(l, C_VC)
            for b in range(NBK):
                ps = ps_alloc("A")
                for kb in range(8):
                    S.op("pe", lambda e, kb=kb, b=b, ps=ps: e.matmul(ps[:, :], lhsT=xT[:, kb, b * 128:(b + 1) * 128],
                                                                    rhs=wb[:, kb * 512:(kb + 1) * 512], start=(kb == 0), stop=(kb == 7)),
                         r=[wb.b, xT.b], w=[ps.b], sig=(kb == 7))
                layernorm_rows((ps.b, ps[:, :]), 128, 512, (sgn.b, sgn[:, 0, :]), (sgn.b, sgn[:, 1, :]), (vn.b, vn[:, b, :]))
                yield
            w_done()
            yield
            wb = w_next(l, C_ZC)
            for j in range(4):
                ps = ps_alloc("A")
                mm_fm(ps, wb, j, 8, xrhs, [xT.b], n)
                S.op("act", lambda e, j=j, ps=ps: e.activation(out=sig[:, j, :], in_=ps[:, 0:TT], func=AF.Silu), r=[ps.b], w=[sig.b])
                yield
            w_done()
            S.op("pool", lambda e: e.tensor_tensor(out=ucz[:], in0=ucz[:], in1=sig[:], op=ALU.mult), r=[ucz.b, sig.b], w=[ucz.b])
            yield

            for b in range(NBK):
                for g4 in range(4):
                    ps = ps_alloc("A")
                    S.op("pe", lambda e, b=b, g4=g4, ps=ps: e.matmul(ps[:, 0:128], lhsT=vn[:, b, g4 * 128:(g4 + 1) * 128],
                                                                    rhs=sgw[:, g4, :], start=True, stop=True),
                         r=[vn.b, sgw.b], w=[ps.b])
                    mt = rot(mtmp, "mt")
                    S.op("dve", lambda e, g4=g4, ps=ps, mt=mt: e.tensor_tensor(out=mt[:, 0:128], in0=ps[:, 0:128],
                                                                              in1=sgb[:, g4 * 128:(g4 + 1) * 128], op=ALU.add),
                         r=[ps.b, sgb.b], w=[mt.b])
                    S.op("pool", lambda e, b=b, g4=g4, mt=mt: e.tensor_tensor(out=ys[:, 8 + g4, b * 128:(b + 1) * 128], in0=mt[:, 0:128],
                                                                             in1=ucz[:, g4, b * 128:(b + 1) * 128], op=ALU.mult),
                         r=[mt.b, ucz.b], w=[ysb[2]])
                yield

            for b in range(NBK):
                first = (ti == 0 and b == 0)
                parts = [1] if first else [0, 1]
                for kv in range(2):
                    for pt in parts:
                        pss = ps_alloc("D")
                        kc = b * 128 + pt * 128
                        for par in range(2):
                            rows = slice(par * 64, (par + 1) * 64)
                            o = pss[:, par * 256:(par + 1) * 256].rearrange("p (j q) -> p j q", j=2)
                            S.op("pe", lambda e, o=o, rows=rows, kc=kc, kv=kv, b=b: e.matmul(
                                o, lhsT=kT[rows, kv, kc:kc + 128], rhs=qT[rows, 2 * kv:2 * kv + 2, b * 128:(b + 1) * 128],
                                start=True, stop=False), r=[kT.b, qT.b], w=[pss.b], sig=False)
                            aqv = aq[:, (kv * 2 + par) * 256:(kv * 2 + par + 1) * 256].rearrange("p (j q) -> p j q", j=2)
                            S.op("pe", lambda e, o=o, aqv=aqv, pt=pt: e.matmul(o, lhsT=ak[:, pt * 128:(pt + 1) * 128], rhs=aqv,
                                                                               start=False, stop=True),
                                 r=[ak.b, aq.b], w=[pss.b], sig=(par == 1))
                        S.op("act", lambda e, pss=pss, pt=pt, kv=kv: e.activation(out=pT[kv][pt][:], in_=pss[:, :], func=AF.Exp),
                             r=[pss.b], w=[pT[kv][pt].b])
                        S.op("pool", lambda e, pt=pt, kv=kv: e.tensor_tensor(
                            out=pT[kv][pt][:].rearrange("p (h q) -> p h q", h=4), in0=pT[kv][pt][:].rearrange("p (h q) -> p h q", h=4),
                            in1=masks[:, pt, :].unsqueeze(1).to_broadcast([128, 4, 128]), op=ALU.mult),
                             r=[pT[kv][pt].b, masks.b], w=[pT[kv][pt].b])
                    yield
                    psd = ps_alloc("D")
                    for i, pt in enumerate(parts):
                        S.op("pe", lambda e, psd=psd, pt=pt, i=i, kv=kv: e.matmul(psd[:, :], lhsT=ones[:], rhs=pT[kv][pt][:],
                                                                                 start=(i == 0), stop=(i == len(parts) - 1)),
                             r=[ones.b, pT[kv][pt].b], w=[psd.b], sig=(i == len(parts) - 1))
                    S.op("dve", lambda e, psd=psd, kv=kv: e.tensor_tensor(
                        out=rden[kv][:].rearrange("p (h q) -> p h q", h=4), in0=psd[:, :].rearrange("p (h q) -> p h q", h=4),
                        in1=esk[:, kv * 4:(kv + 1) * 4].unsqueeze(2).to_broadcast([128, 4, 128]), op=ALU.add),
                         r=[psd.b, esk.b], w=[rden[kv].b])
                    S.op("act", lambda e, kv=kv: e.activation(out=rden[kv][:], in_=rden[kv][:], func=AF.Ln), r=[rden[kv].b], w=[rden[kv].b])
                    S.op("act", lambda e, kv=kv: e.activation(out=rden[kv][:], in_=rden[kv][:], func=AF.Exp, scale=-1.0),
                         r=[rden[kv].b], w=[rden[kv].b])
                    pso = ps_alloc("D")
                    for hh in range(4):
                        for i, pt in enumerate(parts):
                            S.op("pe", lambda e, pso=pso, hh=hh, pt=pt, i=i, kv=kv, b=b: e.matmul(
                                pso[:, hh * 128:(hh + 1) * 128], lhsT=vtok[:, b + pt, kv, :], rhs=pT[kv][pt][:, hh * 128:(hh + 1) * 128],
                                start=(i == 0), stop=(i == len(parts) - 1)),
                                 r=[vtok.b, pT[kv][pt].b], w=[pso.b], sig=(hh == 3 and i == len(parts) - 1))
                    S.op("dve", lambda e, pso=pso, kv=kv: e.tensor_tensor(out=otmp[kv][:], in0=pso[:, :], in1=rden[kv][:], op=ALU.mult),
                         r=[pso.b, rden[kv].b], w=[otmp[kv].b])
                    for par in range(2):
                        rows = slice(par * 64, (par + 1) * 64)
                        ov = otmp[kv][rows, par * 256:(par + 1) * 256].rearrange("p (j q) -> p j q", j=2)
                        S.op("pool", lambda e, rows=rows, ov=ov, kv=kv, b=b: e.tensor_tensor(
                            out=ys[rows, 2 * kv:2 * kv + 2, b * 128:(b + 1) * 128], in0=ov,
                            in1=sza[rows, 2 * kv:2 * kv + 2, b * 128:(b + 1) * 128], op=ALU.mult),
                             r=[otmp[kv].b, sza.b], w=[ysb[0]])
                    yield
            S.op("act", lambda e: e.copy(out=kT[:, :, 0:128], in_=kT[:, :, TT:TT + 128]), r=[kT.b], w=[kT.b])
            S.op("pool", lambda e: e.tensor_copy(out=vtok[:, 0], in_=vtok[:, NBK]), r=[vtok.b], w=[vtok.b])
            for _ in readout_br(l, n, (0, 2), True):
                yield
            for _ in gates_b(l, n):
                yield
            wb = w_next(l, C_ZB)
            for j in range(4):
                ps = ps_alloc("A")
                mm_fm(ps, wb, j, 8, xrhs, [xT.b], n)
                S.op("act", lambda e, j=j, ps=ps: e.activation(out=szb[:, j, :], in_=ps[:, 0:TT], func=AF.Silu), r=[ps.b], w=[szb.b])
            w_done()
            yield


        def gates_b(l, n):
            xrhs = lambda kb: xT[:, kb, 0:n]
            for hf in range(2):
                wb = w_next(l, C_G + hf * 3 + 1)
                for j in range(4):
                    ps = ps_alloc("A")
                    mm_fm(ps, wb, j, 8, xrhs, [xT.b], n)
                    S.op("act", lambda e, j=j, ps=ps: e.activation(out=sigB[:, hf * 4 + j, 0:n], in_=ps[:, 0:n], func=AF.Sigmoid),
                         r=[ps.b], w=[sigB.b])
                    yield
                w_done()
                yield

        def readout_br(l, n, branches, init_first, pre=False):
            xrhs = lambda kb: xT[:, kb, 0:n]
            for hf in range(2):
                for bi, b in enumerate(branches):
                    if pre:
                        gsrc, gbuf, goff = sigB, sigB.b, hf * 4
                    else:
                        gsrc, gbuf, goff = sig, sig.b, 0
                        wb = w_next(l, C_G + hf * 3 + b)
                        for j in range(4):
                            ps = ps_alloc("A")
                            mm_fm(ps, wb, j, 8, xrhs, [xT.b], n)
                            S.op("act", lambda e, j=j, ps=ps: e.activation(out=sig[:, j, 0:n], in_=ps[:, 0:n], func=AF.Sigmoid),
                                 r=[ps.b], w=[sig.b])
                            yield
                        w_done()
                        yield
                    wb = w_next(l, C_R + hf * 3 + b)
                    for j in range(4):
                        ps = ps_alloc("A")
                        mm_fm(ps, wb, j, 4, lambda kb, b=b: ys[:, b * 4 + kb, 0:n], [ysb[b]], n)
                        if init_first and bi == 0:
                            S.op("dve", lambda e, j=j, ps=ps, gsrc=gsrc, goff=goff: e.tensor_tensor(
                                out=mrgT[:, hf * 4 + j, 0:n], in0=ps[:, 0:n], in1=gsrc[:, goff + j, 0:n], op=ALU.mult),
                                 r=[ps.b, gbuf], w=[mrgT.b])
                        else:
                            mt = rot(mtmp, "mt")
                            S.op("dve", lambda e, j=j, ps=ps, mt=mt, gsrc=gsrc, goff=goff: e.tensor_tensor(
                                out=mt[:, 0:n], in0=ps[:, 0:n], in1=gsrc[:, goff + j, 0:n], op=ALU.mult),
                                 r=[ps.b, gbuf], w=[mt.b])
                            S.op("pool", lambda e, j=j, mt=mt: e.tensor_tensor(out=mrgT[:, hf * 4 + j, 0:n], in0=mrgT[:, hf * 4 + j, 0:n],
                                                                              in1=mt[:, 0:n], op=ALU.add),
                                 r=[mrgT.b, mt.b], w=[mrgT.b])
                        yield
                    w_done()
                    yield

        def drain(gen):
            for _ in gen:
                pass

        def wo_ln(l, n, blocks):
            for hf in range(2):
                wb = w_next(l, C_O + hf)
                c0 = 0
                for (xt, nrow, _) in blocks:
                    ps = ps_alloc("A")
                    for kb in range(8):
                        S.op("pe", lambda e, kb=kb, ps=ps, c0=c0, nrow=nrow: e.matmul(
                            ps[0:nrow, :], lhsT=mrgT[:, kb, c0:c0 + nrow], rhs=wb[:, kb * 512:(kb + 1) * 512],
                            start=(kb == 0), stop=(kb == 7)), r=[wb.b, mrgT.b], w=[ps.b], sig=(kb == 7))
                    S.op("dve", lambda e, ps=ps, xt=xt, nrow=nrow: e.scalar_tensor_tensor(
                        out=xt[0:nrow, hf * 512:(hf + 1) * 512], in0=xt[0:nrow, hf * 512:(hf + 1) * 512], scalar=ALPHA,
                        in1=ps[0:nrow, :], op0=ALU.mult, op1=ALU.add), r=[ps.b, xt.b], w=[xt.b])
                    c0 += nrow
                    yield
                w_done()
                yield
            for (xt, nrow, dst_ap) in blocks:
                lo = rot(lnout, "lo")
                layernorm_rows((xt.b, xt[0:nrow, :]), nrow, D, (lng.b, lng[0:nrow, 0, :]), (lng.b, lng[0:nrow, 1, :]),
                               (lo.b, lo[0:nrow, :]), tmp=(xt.b, xt[0:nrow, :]))
                S.dma("pool", dst_ap, lo[0:nrow, :], r=[lo.b], w=[x1_b])
                yield

        def sample_tile(l):
            n = NS
            src = xs if l == 0 else xs1
            dst = xs1 if l == 0 else ysm
            xt = xtok[0]
            S.dma("sp", xt[0:n, :], src, r=([x1_b] if l == 1 else []), w=[xt.b])
            for kb in range(8):
                ps = ps_alloc("A")
                S.op("pe", lambda e, ps=ps, kb=kb: e.transpose(out=ps[:, 0:n], in_=xt[0:n, kb * 128:(kb + 1) * 128], identity=ident[0:n, 0:n]),
                     r=[xt.b, ident.b], w=[ps.b])
                S.op("dve", lambda e, ps=ps, kb=kb: e.tensor_copy(out=xT[:, kb, 0:n], in_=ps[:, 0:n]), r=[ps.b], w=[xT.b])
            xrhs = lambda kb: xT[:, kb, 0:n]
            wb = w_next(l, C_UB)
            for j in range(4):
                ps = ps_alloc("A")
                mm_fm(ps, wb, j, 8, xrhs, [xT.b], n)
                S.op("dve", lambda e, j=j, ps=ps: e.tensor_copy(out=uT[:, j, 0:n], in_=ps[:, 0:n]), r=[ps.b], w=[uT.b])
            w_done()
            wb = w_next(l, C_Q)
            for j in range(4):
                ps = ps_alloc("A")
                mm_fm(ps, wb, j, 8, xrhs, [xT.b], n)
                S.op("act", lambda e, j=j, ps=ps: e.activation(out=qT[:, j, 0:n], in_=ps[:, 0:n], func=AF.Copy, scale=0.125), r=[ps.b], w=[qT.b])
            w_done()
            wb = w_next(l, C_KD)
            for j in range(4):
                ps = ps_alloc("A")
                mm_fm(ps, wb, j, 8, xrhs, [xT.b], n)
                if j < 2:
                    S.op("dve", lambda e, j=j, ps=ps: e.tensor_copy(out=ksd[:, j, :], in_=ps[:, 0:n]), r=[ps.b], w=[ksd.b])
                else:
                    S.op("dve", lambda e, j=j, ps=ps: e.tensor_copy(out=vsd[:, j - 2, :], in_=ps[:, 0:n]), r=[ps.b], w=[vsd.b])
            w_done()
            wb = w_next(l, C_KV)
            ps = ps_alloc("A")
            for kb in range(8):
                S.op("pe", lambda e, kb=kb, ps=ps: e.matmul(ps[0:n, 0:256], lhsT=xT[:, kb, 0:n], rhs=wb[:, kb * 512: kb * 512 + 256],
                                                           start=(kb == 0), stop=(kb == 7)), r=[wb.b, xT.b], w=[ps.b], sig=(kb == 7))
            S.op("dve", lambda e, ps=ps: e.tensor_copy(out=kvs_sb[:], in_=ps[0:n, 0:256]), r=[ps.b], w=[kvs_sb.b])
            S.dma("pool", kvs[l], kvs_sb[:], r=[kvs_sb.b])
            w_done()
            for cid, dstT, fn in ((C_ZA, sza, AF.Silu), (C_ZB, szb, AF.Silu), (C_UC, ucz, None)):
                wb = w_next(l, cid)
                for j in range(4):
                    ps = ps_alloc("A")
                    mm_fm(ps, wb, j, 8, xrhs, [xT.b], n)
                    if fn is None:
                        S.op("dve", lambda e, j=j, ps=ps, dstT=dstT: e.tensor_copy(out=dstT[:, j, 0:n], in_=ps[:, 0:n]), r=[ps.b], w=[dstT.b])
                    else:
                        S.op("act", lambda e, j=j, ps=ps, dstT=dstT, fn=fn: e.activation(out=dstT[:, j, 0:n], in_=ps[:, 0:n], func=fn),
                             r=[ps.b], w=[dstT.b])
                w_done()
            wb = w_next(l, C_VC)
            ps = ps_alloc("A")
            for kb in range(8):
                S.op("pe", lambda e, kb=kb, ps=ps: e.matmul(ps[0:n, :], lhsT=xT[:, kb, 0:n], rhs=wb[:, kb * 512:(kb + 1) * 512],
                                                           start=(kb == 0), stop=(kb == 7)), r=[wb.b, xT.b], w=[ps.b], sig=(kb == 7))
            lo = rot(lnout, "lo")
            layernorm_rows((ps.b, ps[0:n, :]), n, 512, (sgn.b, sgn[0:n, 0, :]), (sgn.b, sgn[0:n, 1, :]), (lo.b, lo[0:n, 0:512]))
            S.dma("pool", vns[l], lo[0:n, 0:512], r=[lo.b])
            S.op("act", lambda e: e.copy(out=vnS[:], in_=lo[0:n, 0:512]), r=[lo.b], w=[vnS.b])
            w_done()
            wb = w_next(l, C_ZC)
            for j in range(4):
                ps = ps_alloc("A")
                mm_fm(ps, wb, j, 8, xrhs, [xT.b], n)
                S.op("act", lambda e, j=j, ps=ps: e.activation(out=gtmp2[:, j, 0:n], in_=ps[:, 0:n], func=AF.Silu), r=[ps.b], w=[gtmp2.b])
            w_done()
            S.op("pool", lambda e: e.tensor_tensor(out=ucz[:, :, 0:n], in0=ucz[:, :, 0:n], in1=gtmp2[:, :, 0:n], op=ALU.mult),
                 r=[ucz.b, gtmp2.b], w=[ucz.b])

            S.dma("sp", h0[:], h0_d[l], w=[h0.b])
            S.dma("sp", h0s[:], h0s_d[l], w=[h0s.b])
            psv = ps_alloc("B")
            for g in range(32):
                S.op("pe", lambda e, g=g: e.matmul(psv[:, g * n:(g + 1) * n], lhsT=bbw[:, g, :], rhs=uT[:, g // 8, 0:n], start=True, stop=True),
                     r=[bbw.b, uT.b], w=[psv.b], sig=(g == 31))
            S.op("dve", lambda e: e.tensor_tensor(out=h0[:], in0=h0[:], in1=car[:].unsqueeze(2).to_broadcast([128, 32, n]), op=ALU.mult),
                 r=[h0.b, car.b], w=[h0.b])
            S.op("dve", lambda e: e.tensor_tensor(out=h0s[:], in0=h0s[:], in1=cais[:].unsqueeze(2).to_broadcast([128, 32, n]), op=ALU.mult),
                 r=[h0s.b, cais.b], w=[h0s.b])
            S.op("dve", lambda e: e.tensor_tensor(out=h0[:], in0=h0[:], in1=h0s[:], op=ALU.add), r=[h0.b, h0s.b], w=[h0.b])
            S.op("dve", lambda e: e.tensor_tensor(out=h0[:], in0=h0[:], in1=psv[:, 0:32 * n].rearrange("p (g n) -> p g n", n=n), op=ALU.add),
                 r=[h0.b, psv.b], w=[h0.b])
            S.dma("pool", hss[l], h0[:], r=[h0.b])
            S.op("act", lambda e: e.copy(out=hbs[:], in_=h0[:]), r=[h0.b], w=[hbs.b])
            for fb in range(4):
                psy = ps_alloc("Y")
                for gl in range(8):
                    g = fb * 8 + gl
                    S.op("pe", lambda e, g=g, gl=gl, psy=psy: e.matmul(psy[:, 0:n], lhsT=cpw[:, g, :], rhs=hbs[:, g, :], start=(gl == 0), stop=(gl == 7)),
                         r=[cpw.b, hbs.b], w=[psy.b], sig=(gl == 7))
                S.op("dve", lambda e, fb=fb, psy=psy: e.scalar_tensor_tensor(out=ypre[:, fb, 0:n], in0=uT[:, fb, 0:n], scalar=dsk[:, fb:fb + 1],
                                                                            in1=psy[:, 0:n], op0=ALU.mult, op1=ALU.add),
                     r=[uT.b, dsk.b, psy.b], w=[ypre.b])

            for g4 in range(4):
                ps = ps_alloc("A")
                S.op("pe", lambda e, g4=g4, ps=ps: e.matmul(ps[:, 0:n], lhsT=vnS[:, g4 * 128:(g4 + 1) * 128], rhs=w00I[:, g4, :], start=True, stop=True),
                     r=[vnS.b, w00I.b], w=[ps.b])
                S.op("dve", lambda e, g4=g4, ps=ps: e.tensor_scalar(out=st1[:], in0=ps[:, 0:n], scalar1=b00[:, g4:g4 + 1], scalar2=None, op0=ALU.add),
                     r=[ps.b, b00.b], w=[st1.b])
                S.op("dve", lambda e, g4=g4: e.tensor_tensor(out=ys[:, 8 + g4, 0:n], in0=st1[:], in1=ucz[:, g4, 0:n], op=ALU.mult),
                     r=[st1.b, ucz.b], w=[ysb[2]])

            S.dma("pool", ck[:], ckT_d[l], w=[ck.b])
            S.dma("pool", cv[:], cvd_d[l], w=[cv.b])
            pss = ps_alloc("D")
            for nn in range(n):
                for kv in range(2):
                    for par in range(2):
                        rows = slice(par * 64, (par + 1) * 64)
                        c0 = (kv * 2 + par) * 32
                        o = pss[:, c0:c0 + 32].rearrange("p (j n) -> p j n", j=2)[:, :, nn]
                        last = (nn == n - 1 and kv == 1 and par == 1)
                        S.op("pe", lambda e, o=o, rows=rows, nn=nn, kv=kv: e.matmul(o, lhsT=ck[rows, nn * 2 + kv, :], rhs=qT[rows, 2 * kv:2 * kv + 2, nn],
                                                                                   start=True, stop=False), r=[ck.b, qT.b], w=[pss.b], sig=False)
                        S.op("pe", lambda e, o=o, c0=c0: e.matmul(o, lhsT=ak[:, 0:128], rhs=aqs[:, c0 // 16: c0 // 16 + 2], start=False, stop=True),
                             r=[ak.b, aqs.b], w=[pss.b], sig=last)
            S.op("act", lambda e: e.activation(out=pSb[:], in_=pss[:, 0:128], func=AF.Exp), r=[pss.b], w=[pSb.b])
            for j in range(4):
                S.op("dve", lambda e, j=j: e.tensor_tensor(out=prod[:, j, :], in0=qT[:, j, 0:n], in1=ksd[:, j // 2, :], op=ALU.mult),
                     r=[qT.b, ksd.b], w=[prod.b])
            ps1 = ps_alloc("A")
            S.op("pe", lambda e: e.matmul(ps1[0:2, 0:64], lhsT=hsel[:], rhs=prod[:].rearrange("p j n -> p (j n)"), start=True, stop=True),
                 r=[hsel.b, prod.b], w=[ps1.b])
            S.op("act", lambda e: e.activation(out=pself[:], in_=ps1[0:2, 0:64], func=AF.Exp), r=[ps1.b], w=[pself.b])
            ps2 = ps_alloc("A")
            S.op("pe", lambda e: e.matmul(ps2[:, 0:64], lhsT=hselT[:], rhs=pself[:], start=True, stop=True), r=[hselT.b, pself.b], w=[ps2.b])
            S.op("dve", lambda e: e.tensor_copy(out=pbs[:].rearrange("p j n -> p (j n)"), in_=ps2[:, 0:64]), r=[ps2.b], w=[pbs.b])
            psd = ps_alloc("A")
            S.op("pe", lambda e: e.matmul(psd[:, 0:128], lhsT=ones[:], rhs=pSb[:], start=True, stop=True), r=[ones.b, pSb.b], w=[psd.b])
            S.op("dve", lambda e: e.tensor_copy(out=sdn[:], in_=psd[:, 0:128]), r=[psd.b], w=[sdn.b])
            pso = ps_alloc("D")
            for nn in range(n):
                for kv in range(2):
                    o = pso[:, kv * 64:(kv + 1) * 64].rearrange("p (a n) -> p a n", n=n)[:, :, nn]
                    r_ = pSb[:, kv * 64:(kv + 1) * 64].rearrange("p (a n) -> p a n", n=n)[:, :, nn]
                    S.op("pe", lambda e, o=o, r_=r_, nn=nn, kv=kv: e.matmul(o, lhsT=cv[:, nn * 2 + kv, :], rhs=r_, start=True, stop=True),
                         r=[cv.b, pSb.b], w=[pso.b], sig=(nn == n - 1 and kv == 1))
            S.op("dve", lambda e: e.tensor_copy(out=sso[:], in_=pso[:, 0:128]), r=[pso.b], w=[sso.b])
            for par in range(2):
                rows = slice(par * 64, (par + 1) * 64)
                for kv in range(2):
                    for jj in range(2):
                        blk = 2 * kv + jj
                        c0 = ((kv * 2 + par) * 2 + jj) * n
                        a1 = st1[rows, :]
                        a2 = st2[rows, :]
                        S.op("dve", lambda e, a1=a1, rows=rows, blk=blk, kv=kv: e.tensor_tensor(out=a1, in0=pbs[rows, blk, :], in1=vsd[rows, kv, :], op=ALU.mult),
                             r=[pbs.b, vsd.b], w=[st1.b])
                        S.op("dve", lambda e, a1=a1, rows=rows, c0=c0: e.tensor_tensor(out=a1, in0=a1, in1=sso[rows, c0:c0 + n], op=ALU.add),
                             r=[st1.b, sso.b], w=[st1.b])
                        S.op("dve", lambda e, a2=a2, rows=rows, blk=blk, c0=c0: e.tensor_tensor(out=a2, in0=pbs[rows, blk, :], in1=sdn[rows, c0:c0 + n], op=ALU.add),
                             r=[pbs.b, sdn.b], w=[st2.b])
                        S.op("dve", lambda e, a2=a2, rows=rows, blk=blk: e.tensor_scalar(out=a2, in0=a2, scalar1=esks[rows, blk:blk + 1], scalar2=None, op0=ALU.add),
                             r=[st2.b, esks.b], w=[st2.b])
                        S.op("dve", lambda e, a2=a2: e.reciprocal(out=a2, in_=a2), r=[st2.b], w=[st2.b])
                        S.op("dve", lambda e, a1=a1, a2=a2: e.tensor_tensor(out=a1, in0=a1, in1=a2, op=ALU.mult), r=[st1.b, st2.b], w=[st1.b])
                        S.op("dve", lambda e, a1=a1, rows=rows, blk=blk: e.tensor_tensor(out=ys[rows, blk, 0:n], in0=a1, in1=sza[rows, blk, 0:n], op=ALU.mult),
                             r=[st1.b, sza.b], w=[ysb[0]])
            drain(readout_br(l, n, (0, 2), True))
            gelu_glu(l, n)
            drain(readout_br(l, n, (1,), False))
            drain(wo_ln(l, n, [(xt, n, dst)]))

        x1_b = Buf("x1")

        def roundrobin(gens):
            gens = list(gens)
            while gens:
                for g_ in list(gens):
                    try:
                        next(g_)
                        while wst["hold"]:
                            next(g_)
                    except StopIteration:
                        gens.remove(g_)

        for l in range(DEPTH):
            setup_layer(l)
            nt = NTILE if DBG_TILES is None else DBG_TILES
            xload(l, 0)
            if nt > 1:
                xload(l, 1)
            front(l, 0)
            prev_tail = None
            for ti in range(nt):
                gens = [s5_pipe(TT), rest_gen(l, ti)]
                if prev_tail is not None:
                    gens.append(prev_tail)
                roundrobin(gens)
                if ti + 1 < nt:
                    front(l, ti + 1)
                prev_tail = tail(l, ti)
            if prev_tail is not None:
                drain(prev_tail)
            S.dma("pool", hst_o[l], hst[:], r=[hst.b] + hstb)
            sample_tile(l)
        allb = [x1_b, hst.b, h0.b, kvs_sb.b] + [t.b for t in lnout] + ([] if DBG_NOKV else [kvout.b])
        S.finish(allb)
        for i in range(NDS):
            if S.dcnt[i] > 0:
                S._wait("sp", (("d", i), S.dcnt[i]))
        if not recording:
            print("instructions:", S.ninst, "waits:", S.nwait, "sem counts:", S.cnt)
    return nc


_PROG = None


def _consts():
    ident = np.eye(128, dtype=np.float32)
    swapm = np.zeros((128, 128), np.float32)
    for m in range(64):
        swapm[64 + m, m] = -1.0
        swapm[m, 64 + m] = 1.0
    s = np.arange(128)[:, None]
    q = np.arange(128)[None, :]
    mprev = (s >= q).astype(np.float32)
    mcur = (s <= q).astype(np.float32)
    masks = np.stack([np.tile(mprev, (1, 4)), np.tile(mcur, (1, 4))]).astype(np.float32)
    ak = np.zeros((2, 256), np.float32)
    ak[0, :128] = np.arange(128) - 128
    ak[0, 128:] = np.arange(128)
    ak[1, :] = 1.0
    slopes = 2.0 ** (-(np.arange(1, 9)))
    aq = np.zeros((2, 2, 2, 2, 128), np.float32)
    for kv in range(2):
        for par in range(2):
            for jj in range(2):
                h = 4 * kv + 2 * jj + par
                aq[0, kv, par, jj, :] = slopes[h]
                aq[1, kv, par, jj, :] = -slopes[h] * np.arange(128)
    rmask = np.zeros((128, 8), np.float32)
    for gl in range(8):
        rmask[gl * 16:(gl + 1) * 16, gl] = 1.0
    triu = (s <= q).astype(np.float32)
    aqs = np.zeros((2, 8), np.float32)
    for kv in range(2):
        for par in range(2):
            for jj in range(2):
                aqs[0, (kv * 2 + par) * 2 + jj] = slopes[4 * kv + 2 * jj + par]
    hsel = np.zeros((128, 2), np.float32)
    hsel[0:64, 0] = 1.0
    hsel[64:128, 1] = 1.0
    signc = np.ones((128, 1), np.float32)
    signc[0:64] = -1.0
    return dict(ident=ident, swapm=swapm, masks=masks, ak=ak, aq=aq.reshape(2, 1024), rmask=rmask, triu=triu,
                aqs=aqs, hsel=hsel, hselT=np.ascontiguousarray(hsel.T), signc=signc)


def _layout_weights(w_in, w_read, w_o, glu_w):
    out = np.zeros((DEPTH, NCH, 128, 8, 512), np.float32)

    def put(l, c, mat, kb0=0):
        k = mat.shape[0] // 128
        out[l, c, :, kb0:kb0 + k, :mat.shape[1]] = mat.reshape(k, 128, mat.shape[1]).transpose(1, 0, 2)

    for l in range(DEPTH):
        W = w_in[l]
        put(l, C_Q, W[:, 0:512])
        k0, k1 = W[:, 512:576], W[:, 576:640]
        v0, v1 = W[:, 640:704], W[:, 704:768]
        put(l, C_KD, np.concatenate([k0, k0, k1, k1, v0, v0, v1, v1], axis=1))
        put(l, C_KV, W[:, 512:768])
        put(l, C_ZA, W[:, 768:1280])
        put(l, C_UB, W[:, 1280:1792])
        put(l, C_ZB, W[:, 1792:2304])
        put(l, C_UC, W[:, 2304:2816])
        put(l, C_VC, W[:, 2816:3328])
        put(l, C_ZC, W[:, 3328:3840])
        for hf in range(2):
            for b in range(3):
                c0 = 3840 + b * 1024 + hf * 512
                put(l, C_G + hf * 3 + b, W[:, c0:c0 + 512])
                put(l, C_R + hf * 3 + b, w_read[l, b][:, hf * 512:(hf + 1) * 512])
            put(l, C_O + hf, w_o[l][:, hf * 512:(hf + 1) * 512])
        put(l, C_GLU, glu_w[l])
    return out.reshape(DEPTH, NCH, 128, 4096)


def _layout_params(inp):
    f = np.float32
    lam_re, lam_im, log_dt = inp["ssm_lambda_re"], inp["ssm_lambda_im"], inp["ssm_log_dt"]
    colp = np.zeros((DEPTH, 128, 96), f)
    rowp = np.zeros((DEPTH, 3, 128, 2048), f)
    for l in range(DEPTH):
        colp[l, :, 0:32] = np.concatenate([lam_re[l].T, lam_re[l].T], 0)
        colp[l, :, 32:64] = np.concatenate([lam_im[l].T, lam_im[l].T], 0)
        colp[l, :, 64:96] = log_dt[l][None, :]
        rowp[l, 0] = lam_re[l].reshape(1, 2048)
        rowp[l, 1] = lam_im[l].reshape(1, 2048)
        rowp[l, 2] = np.repeat(log_dt[l], 64)[None, :]
    bpad = np.zeros((DEPTH, 2, 32, 128, 64), f)
    cpad = np.zeros((DEPTH, 32, 128, 128), f)
    for l in range(DEPTH):
        for g in range(32):
            gl = g % 8
            bpad[l, 0, g, gl * 16:(gl + 1) * 16, :] = inp["ssm_b_re"][l, g].T
            bpad[l, 1, g, gl * 16:(gl + 1) * 16, :] = inp["ssm_b_im"][l, g].T
            cpad[l, g, 0:64, gl * 16:(gl + 1) * 16] = inp["ssm_c_re"][l, g].T
            cpad[l, g, 64:128, gl * 16:(gl + 1) * 16] = inp["ssm_c_im"][l, g].T
    dsk = inp["ssm_d"].reshape(DEPTH, 4, 128).transpose(0, 2, 1).copy()
    glub = inp["glu_b"].reshape(DEPTH, 4, 128).transpose(0, 2, 1).copy()
    sgn = np.zeros((DEPTH, 2, 128, 512), f)
    sgn[:, 0] = inp["sgu_ln_g"][:, None, :]
    sgn[:, 1] = inp["sgu_ln_b"][:, None, :]
    sgw = inp["sgu_w"].transpose(0, 1, 3, 2).copy()
    sgb = np.broadcast_to(inp["sgu_b"].reshape(DEPTH, 1, 512), (DEPTH, 128, 512)).copy()
    lng = np.zeros((DEPTH, 2, 128, 1024), f)
    lng[:, 0] = inp["ln_g"][:, None, :]
    lng[:, 1] = inp["ln_b"][:, None, :]
    snk = np.zeros((DEPTH, 128, 2, 2, 2), f)
    for kv in range(2):
        for par in range(2):
            for jj in range(2):
                h = 4 * kv + 2 * jj + par
                snk[:, :, kv, par, jj] = inp["attn_sinks"][:, h][:, None]
    return dict(colp=colp, rowp=rowp, bpad=bpad, cpad=cpad, dsk=dsk, glub=glub, sgn=sgn, sgw=sgw, sgb=sgb, lng=lng,
                snk=snk.reshape(DEPTH, 128, 8))


def kernel(**inp):
    global _PROG
    inp = {k: np.asarray(v) for k, v in inp.items()}
    if _PROG is None:
        _PROG = build_program()
    nc = _PROG
    consts = _consts()
    wch = _layout_weights(inp["w_in"], inp["w_read"], inp["w_o"], inp["glu_w"])
    prm = _layout_params(inp)
    in_maps = []
    for c in range(8):
        m = dict(consts)
        m.update(prm)
        m["wch"] = wch
        m["xp"] = np.ascontiguousarray(inp["x_prompt"][c % 2])
        sl = slice(c * NS, (c + 1) * NS)
        m["xs"] = np.ascontiguousarray(inp["x_sample"][sl, 0, :])
        ckc = inp["cache_k_win"][:, sl]
        t = ckc.transpose(0, 4, 1, 3, 2).reshape(DEPTH, 64, 32, 128)
        m["ckT"] = np.ascontiguousarray(np.concatenate([t, t], axis=1))
        cvc = inp["cache_v_win"][:, sl]
        t = cvc.transpose(0, 2, 1, 3, 4).reshape(DEPTH, 128, 32, 64)
        m["cvd"] = np.ascontiguousarray(np.concatenate([t, t], axis=3))
        sr = inp["state_ssm_re"][:, sl].transpose(0, 3, 2, 1)
        si = inp["state_ssm_im"][:, sl].transpose(0, 3, 2, 1)
        m["h0"] = np.ascontiguousarray(np.concatenate([sr, si], axis=1))
        m["h0s"] = np.ascontiguousarray(np.concatenate([si, sr], axis=1))
        m["w00"] = np.ascontiguousarray(np.broadcast_to(inp["sgu_w"][:, None, :, 0, 0], (DEPTH, NS, 4)))
        m["b00"] = np.ascontiguousarray(np.broadcast_to(inp["sgu_b"][:, None, :, 0], (DEPTH, 128, 4)))
        es_ = np.zeros((DEPTH, 128, 4), np.float32)
        for blk in range(4):
            es_[:, 0:64, blk] = inp["attn_sinks"][:, 2 * blk][:, None]
            es_[:, 64:128, blk] = inp["attn_sinks"][:, 2 * blk + 1][:, None]
        m["esks"] = es_
        in_maps.append(m)
    res = run_bass_kernel_spmd(nc, in_maps, core_ids=list(range(8)))
    R = res.results
    y_prompt = np.stack([R[0]["yp"], R[1]["yp"]])
    kvw = np.stack([R[0]["kvw"], R[1]["kvw"]], axis=1)
    k_win = kvw[..., 0:128].reshape(DEPTH, 2, 128, 2, 64)
    v_win = kvw[..., 128:256].reshape(DEPTH, 2, 128, 2, 64)
    hs = np.stack([R[0]["hst"], R[1]["hst"]], axis=1)
    h_re = hs[:, :, 0:64, :].transpose(0, 1, 3, 2)
    h_im = hs[:, :, 64:128, :].transpose(0, 1, 3, 2)
    y_sample = np.concatenate([R[c]["ysm"] for c in range(8)], axis=0).reshape(128, 1, D)
    kvsm = np.concatenate([R[c]["kvs"] for c in range(8)], axis=1)
    k_s = kvsm[..., 0:128].reshape(DEPTH, 128, 1, 2, 64)
    v_s = kvsm[..., 128:256].reshape(DEPTH, 128, 1, 2, 64)
    hsm = np.concatenate([R[c]["hss"] for c in range(8)], axis=3)
    hs_re = hsm[:, 0:64].transpose(0, 3, 2, 1)
    hs_im = hsm[:, 64:128].transpose(0, 3, 2, 1)
    vn_s = np.concatenate([R[c]["vns"] for c in range(8)], axis=1).reshape(DEPTH, 128, 1, 512)
    f = np.float32
    outs = (y_prompt, y_sample, k_win, v_win, h_re, h_im, k_s, v_s, hs_re, hs_im, vn_s)
    return tuple(np.ascontiguousarray(o, dtype=f) for o in outs)
```

```python
import math
from contextlib import ExitStack

import numpy as np
import concourse.bass as bass
import concourse.mybir as mybir
from concourse.bass_utils import run_bass_kernel_spmd

F32 = mybir.dt.float32
BF16 = mybir.dt.bfloat16
AF = mybir.ActivationFunctionType
ALU = mybir.AluOpType

D = 1024
SEQ = 8192
DEPTH = 2
NS = 16
BW = 512
TT = 256
NBK = TT // 128
NTILE = SEQ // TT
NCH = 24
ALPHA = (2 * DEPTH) ** 0.25
LN_EPS = 1e-5
GK = 0.044715
GS = 2.0 * math.sqrt(2.0 / math.pi)
NDS = 24
DBG_TILES = None
DBG_NOKV = False
DBG_SKIPKV = False

C_Q, C_KD, C_KV, C_ZA, C_UB, C_ZB, C_UC, C_VC, C_ZC = range(9)
C_G = 9
C_R = 15
C_O = 21
C_GLU = 23


class Buf:
    __slots__ = ("name", "w", "r", "excl")

    def __init__(self, name, excl=False):
        self.name = name
        self.w = None
        self.r = {}
        self.excl = excl


class Sched:
    def __init__(self, nc, es):
        self.nc = nc
        self.engs = {"pe": nc.tensor, "dve": nc.vector, "act": nc.scalar, "pool": nc.gpsimd, "sp": nc.sync}
        self.sem = {k: es.enter_context(nc.semaphore("sem_" + k)) for k in ["pe", "dve", "act", "pool"]}
        self.cnt = {k: 0 for k in self.sem}
        self.seen = {e: {} for e in self.engs}
        self.dsem = [es.enter_context(nc.semaphore("dsem%d" % i)) for i in range(NDS)]
        self.dcnt = [0] * NDS
        self.drange = {"sp": (0, 16), "pool": (16, NDS)}
        self.drr = {"sp": 0, "pool": 16}
        self.pend = []
        self.pendset = set()
        self.ninst = 0

    def _wait(self, e, ev):
        if ev is None:
            return
        key, val = ev
        if key == e and e == "pe":
            return
        if self.seen[e].get(key, 0) >= val:
            return
        sem = self.sem[key] if isinstance(key, str) else self.dsem[key[1]]
        self.engs[e].wait_ge(sem, val)
        self.seen[e][key] = val

    def _deps(self, e, r, w):
        for b in r:
            assert id(b) not in self.pendset or e == "pe", b.name
            self._wait(e, b.w)
            if b.excl:
                for k, v in list(b.r.items()):
                    if k != e:
                        self._wait(e, (k, v))
        for b in w:
            assert id(b) not in self.pendset or e == "pe", b.name
            self._wait(e, b.w)
            for k, v in list(b.r.items()):
                self._wait(e, (k, v))

    def _commit(self, ev, r, w):
        for b in w:
            b.w = ev
            b.r = {}
        for b in r:
            if all(b is not x for x in w):
                b.r[ev[0]] = ev[1]

    def op(self, e, fn, r=(), w=(), sig=True):
        self._deps(e, r, w)
        inst = fn(self.engs[e])
        self.ninst += 1
        if e == "pe" and not sig:
            for b in r:
                self.pend.append((b, 0))
                self.pendset.add(id(b))
            for b in w:
                self.pend.append((b, 1))
                self.pendset.add(id(b))
            return inst
        self.cnt[e] += 1
        inst.then_inc(self.sem[e], 1)
        ev = (e, self.cnt[e])
        rr = list(r)
        ww = list(w)
        if e == "pe":
            for b, k in self.pend:
                (ww if k else rr).append(b)
            self.pend = []
            self.pendset = set()
        self._commit(ev, rr, ww)
        return inst

    def dma(self, q, out_ap, in_ap, r=(), w=()):
        lo, hi = self.drange[q]
        i = self.drr[q]
        self.drr[q] = lo + (i + 1 - lo) % (hi - lo)
        if self.dcnt[i] > 0:
            self._wait(q, (("d", i), self.dcnt[i]))
        self._deps(q, r, w)
        inst = self.engs[q].dma_start(out=out_ap, in_=in_ap)
        self.ninst += 1
        self.dcnt[i] += 16
        inst.then_inc(self.dsem[i], 16)
        self._commit((("d", i), self.dcnt[i]), list(r), list(w))
        return inst

    def finish(self, bufs):
        for b in bufs:
            self._wait("sp", b.w)


class T:
    def __init__(self, t, name):
        self.t = t
        self.b = Buf(name)

    def __getitem__(self, k):
        return self.t[k]


def build_program():
    rec = []
    _build(rec, True)
    return _build(rec, False)


def _build(order, recording):
    nc = bass.Bass("TRN2", target_bir_lowering=False)
    es = ExitStack()
    with es:
        def din(name, shape):
            return nc.dram_tensor(name, list(shape), F32, kind="ExternalInput").ap()

        def dout(name, shape):
            return nc.dram_tensor(name, list(shape), F32, kind="ExternalOutput").ap()

        xp = din("xp", [SEQ, D])
        xs = din("xs", [NS, D])
        wch = din("wch", [DEPTH, NCH, 128, 4096])
        ident_d = din("ident", [128, 128])
        swap_d = din("swapm", [128, 128])
        masks_d = din("masks", [2, 128, 512])
        ak_d = din("ak", [2, 256])
        aq_d = din("aq", [2, 1024])
        colp_d = din("colp", [DEPTH, 128, 3 * 32])
        rowp_d = din("rowp", [DEPTH, 3, 128, 2048])
        bpad_d = din("bpad", [DEPTH, 2, 32, 128, 64])
        cpad_d = din("cpad", [DEPTH, 32, 128, 128])
        rmask_d = din("rmask", [128, 8])
        dsk_d = din("dsk", [DEPTH, 128, 4])
        glub_d = din("glub", [DEPTH, 128, 4])
        sgn_d = din("sgn", [DEPTH, 2, 128, 512])
        sgw_d = din("sgw", [DEPTH, 4, 128, 128])
        triu_d = din("triu", [128, 128])
        sgb_d = din("sgb", [DEPTH, 128, 512])
        lng_d = din("lng", [DEPTH, 2, 128, 1024])
        snk_d = din("snk", [DEPTH, 128, 8])

        ckT_d = din("ckT", [DEPTH, 128, 32, 128])
        cvd_d = din("cvd", [DEPTH, 128, 32, 128])
        h0_d = din("h0", [DEPTH, 128, 32, NS])
        h0s_d = din("h0s", [DEPTH, 128, 32, NS])
        aqs_d = din("aqs", [2, 8])
        hsel_d = din("hsel", [128, 2])
        hselT_d = din("hselT", [2, 128])
        signc_d = din("signc", [128, 1])
        w00_d = din("w00", [DEPTH, NS, 4])
        b00_d = din("b00", [DEPTH, 128, 4])
        esks_d = din("esks", [DEPTH, 128, 4])
        ysm = dout("ysm", [NS, D])
        kvs = dout("kvs", [DEPTH, NS, 256])
        hss = dout("hss", [DEPTH, 128, 32, NS])
        vns = dout("vns", [DEPTH, NS, 512])
        xs1 = nc.dram_tensor("xs1", [NS, D], F32).ap()
        yp = dout("yp", [SEQ, D])
        kvw = dout("kvw", [DEPTH, 128, 256])
        hst_o = dout("hst", [DEPTH, 128, 32])

        x1 = nc.dram_tensor("x1", [SEQ, D], F32).ap()
        wbf = nc.dram_tensor("wbf", [DEPTH, NCH, 128, 4096], BF16).ap()

        S = Sched(nc, es)

        def sb(name, shape, dt=F32):
            return T(es.enter_context(nc.sbuf_tensor("s_" + name, list(shape), dt)), name)

        ident = sb("ident", [128, 128])
        swapm = sb("swapm", [128, 128])
        masks = sb("masks", [128, 2, 128], BF16)
        ak = sb("ak", [2, 256], BF16)
        aq = sb("aq", [2, 1024], BF16)
        ones = sb("ones", [128, 128], BF16)
        rmask = sb("rmask", [128, 8])
        xtok = [sb("xtok%d" % i, [128, D]) for i in range(2 * NBK)]
        xT = sb("xT", [128, 8, TT], BF16)
        wbuf = [sb("wbuf%d" % i, [128, 4096], BF16) for i in range(3)]
        qT = sb("qT", [128, 4, TT], BF16)
        kT = sb("kT", [128, 2, TT + 128], BF16)
        vtok = sb("vtok", [128, NBK + 1, 2, 128], BF16)
        sza = sb("sza", [128, 4, TT], BF16)
        szb = sb("szb", [128, 4, TT], BF16)
        uT = sb("uT", [128, 4, TT], BF16)
        ucz = sb("ucz", [128, 4, TT], BF16)
        vn = sb("vn", [128, 4, BW], BF16)
        ys = sb("ys", [128, 12, TT], BF16)
        ypre = sb("ypre", [128, 4, TT])
        gtmp = sb("gtmp", [128, 4, TT])
        gtmp2 = sb("gtmp2", [128, 4, TT], BF16)
        sig = sb("sig", [128, 4, TT], BF16)
        mtmp = [sb("mtmp%d" % i, [128, 512]) for i in range(2)]
        mrgT = sb("mrgT", [128, 8, TT], BF16)
        pT = [[sb("pT%d%d" % (kv, pt), [128, 512], BF16) for pt in range(2)] for kv in range(2)]
        rden = [sb("rden%d" % kv, [128, 512]) for kv in range(2)]
        otmp = [sb("otmp%d" % kv, [128, 512], BF16) for kv in range(2)]
        lnst = sb("lnst", [128, 12])
        lnmv = sb("lnmv", [128, 4])
        lntmp = sb("lntmp", [128, 512])
        lnout = [sb("lnout%d" % i, [128, D]) for i in range(2)]
        colp = sb("colp", [128, 96])
        cdt = sb("cdt", [128, 32])
        crr = sb("crr", [128, 32])
        cth = sb("cth", [128, 32])
        ctmp = sb("ctmp", [128, 32])
        cc1 = sb("cc1", [128, 32])
        ctmp2 = sb("ctmp2", [128, 32])
        citmp = sb("citmp", [128, 32], mybir.dt.int32)
        cs1 = sb("cs1", [128, 32])
        tabc = sb("tabc", [128, 32, TT], BF16)
        tabs = sb("tabs", [128, 32, TT], BF16)
        bbw = sb("bbw", [128, 32, 128], BF16)
        bbs = sb("bbs", [128, 32, 128], BF16)
        cpw = sb("cpw", [128, 32, 128], BF16)
        dsk = sb("dsk", [128, 4])
        glub = sb("glub", [128, 4])
        sgn = sb("sgn", [128, 2, 512])
        sgw = sb("sgw", [128, 4, 128], BF16)
        triu = sb("triu", [128, 128])
        sgb = sb("sgb", [128, 512])
        lng = sb("lng", [128, 2, 1024])
        esk = sb("esk", [128, 8])
        sigB = sb("sigB", [128, 8, TT], BF16)
        hst = sb("hst", [128, 32])
        s5a = [sb("s5a%d" % i, [128, TT]) for i in range(2)]
        s5b = [sb("s5b%d" % i, [128, TT]) for i in range(2)]
        s5v = [sb("s5v%d" % i, [128, TT]) for i in range(2)]
        s5g = [sb("s5g%d" % i, [128, TT]) for i in range(2)]
        s5c = [sb("s5c%d" % i, [128, TT]) for i in range(3)]
        s5d = [sb("s5d%d" % i, [128, TT]) for i in range(2)]
        s5hb = [sb("s5hb%d" % i, [128, TT], BF16) for i in range(2)]

        aqs = sb("aqs", [2, 8], BF16)
        hsel = sb("hsel", [128, 2], BF16)
        hselT = sb("hselT", [2, 128], BF16)
        signc = sb("signc", [128, 1])
        w00 = sb("w00", [NS, 4])
        w00I = sb("w00I", [NS, 4, NS], BF16)
        b00 = sb("b00", [128, 4])
        esks = sb("esks", [128, 4])
        car = sb("car", [128, 32])
        cais = sb("cais", [128, 32])
        ksd = sb("ksd", [128, 2, NS], BF16)
        vsd = sb("vsd", [128, 2, NS])
        prod = sb("prod", [128, 4, NS], BF16)
        pself = sb("pself", [2, 64], BF16)
        pbs = sb("pbs", [128, 4, NS])
        st1 = sb("st1", [128, NS])
        st2 = sb("st2", [128, NS])

        class AV:
            def __init__(self, ap, b):
                self.ap = ap
                self.b = b

            def __getitem__(self, k):
                return self.ap[k]

        ck = AV(tabc[:, :, 0:128], tabc.b)
        kvout = AV(lnout[1][:, 0:256], lnout[1].b)
        kvs_sb = AV(rden[0][0:NS, 0:256], rden[0].b)
        vnS = AV(otmp[0][0:NS, :], otmp[0].b)
        cv = AV(tabs[:, :, 0:128], tabs.b)
        h0 = AV(mtmp[0][:, 0:32 * NS].rearrange("p (g n) -> p g n", n=NS), mtmp[0].b)
        h0s = AV(mtmp[1][:, 0:32 * NS].rearrange("p (g n) -> p g n", n=NS), mtmp[1].b)
        hbs = AV(pT[0][0][:, 0:32 * NS].rearrange("p (g n) -> p g n", n=NS), pT[0][0].b)
        pSb = AV(pT[0][1][:, 0:128], pT[0][1].b)
        sdn = AV(rden[0][:, 0:128], rden[0].b)
        sso = AV(rden[1][:, 0:128], rden[1].b)

        psum = [T(es.enter_context(nc.psum_tensor("ps%d" % i, [128, 512], F32)), "ps%d" % i) for i in range(8)]
        for p_ in psum:
            p_.b.excl = True
        pools = {"A": [0, 1], "B": [2, 3], "C": [4, 6], "Y": [5], "D": [7]}
        prr = {k: 0 for k in pools}

        def ps_alloc(pool):
            i = prr[pool]
            prr[pool] = (i + 1) % len(pools[pool])
            return psum[pools[pool][i]]

        rr = {}

        def rot(lst, key):
            i = rr.get(key, 0)
            rr[key] = (i + 1) % len(lst)
            return lst[i]

        wbf_b = [[Buf("wbf%d_%d" % (l, c)) for c in range(NCH)] for l in range(DEPTH)]
        for l in range(DEPTH):
            for c in range(NCH):
                S.dma("pool", wbf[l, c], wch[l, c], w=[wbf_b[l][c]])

        S.dma("sp", ident[:], ident_d, w=[ident.b])
        S.dma("sp", swapm[:], swap_d, w=[swapm.b])
        S.dma("pool", masks[:], masks_d.rearrange("a p c -> p a c")[:, :, 0:128], w=[masks.b])
        S.dma("pool", ak[:], ak_d, w=[ak.b])
        S.dma("pool", aq[:], aq_d, w=[aq.b])
        S.dma("sp", rmask[:], rmask_d, w=[rmask.b])
        S.dma("sp", triu[:], triu_d, w=[triu.b])
        S.op("dve", lambda e: e.memset(ones[:], 1.0), w=[ones.b])
        S.dma("pool", aqs[:], aqs_d, w=[aqs.b])
        S.dma("pool", hsel[:], hsel_d, w=[hsel.b])
        S.dma("pool", hselT[:], hselT_d, w=[hselT.b])
        S.dma("sp", signc[:], signc_d, w=[signc.b])

        def nk_of(c):
            return 4 if (c == C_GLU or C_R <= c < C_R + 6) else 8

        wst = {"issued": 0, "used": 0, "hold": False}

        def w_issue():
            i = wst["issued"]
            if i >= len(order):
                return
            l, c, nk = order[i]
            wb = wbuf[i % 3]
            S.dma("sp", wb[:, 0:nk * 512], wbf[l, c, :, 0:nk * 512], r=[wbf_b[l][c]], w=[wb.b])
            wst["issued"] = i + 1

        def w_next(l, c):
            i = wst["used"]
            if recording:
                order.append((l, c, nk_of(c)))
            assert order[i][0] == l and order[i][1] == c, (order[i], l, c)
            while wst["issued"] < min(i + (1 if recording else 2), len(order)):
                w_issue()
            wst["used"] = i + 1
            wst["hold"] = True
            return wbuf[i % 3]

        def w_done():
            wst["hold"] = False
            if recording:
                return
            while wst["issued"] < min(wst["used"] + 2, len(order)):
                w_issue()

        def evac(eng, fn, r, w):
            S.op(eng, fn, r=r, w=w)

        def mm_fm(ps, wb, j, nk, rhs_fn, rbufs, n):
            for kb in range(nk):
                S.op("pe", lambda e, kb=kb: e.matmul(ps[:, 0:n], lhsT=wb[:, kb * 512 + j * 128: kb * 512 + (j + 1) * 128],
                                                     rhs=rhs_fn(kb), start=(kb == 0), stop=(kb == nk - 1)),
                     r=[wb.b] + rbufs, w=[ps.b], sig=(kb == nk - 1))

        def v2(t, width):
            ap = t[:]
            if len(ap.shape) == 3:
                ap = ap.rearrange("p a b -> p (a b)")
            return ap[:, 0:width]

        def setup_layer(l):
            S.dma("sp", colp[:], colp_d[l], w=[colp.b])
            S.dma("pool", cpw[:], cpad_d[l].rearrange("g p m -> p g m"), w=[cpw.b])
            S.dma("sp", dsk[:], dsk_d[l], w=[dsk.b])
            S.dma("sp", glub[:], glub_d[l], w=[glub.b])
            S.dma("sp", sgn[:], sgn_d[l].rearrange("a p c -> p a c"), w=[sgn.b])
            sgw32 = v2(ypre, 512).rearrange("p (g t) -> p g t", g=4)
            S.dma("sp", sgw32, sgw_d[l].rearrange("g s t -> s g t"), w=[ypre.b])
            S.dma("sp", sgb[:], sgb_d[l], w=[sgb.b])
            S.dma("sp", lng[:], lng_d[l].rearrange("a p c -> p a c"), w=[lng.b])
            S.dma("sp", esk[:], snk_d[l], w=[esk.b])
            for g4 in range(4):
                S.op("dve", lambda e, g4=g4: e.tensor_tensor(out=sgw[:, g4, :], in0=sgw32[:, g4, :], in1=triu[:], op=ALU.mult),
                     r=[ypre.b, triu.b], w=[sgw.b])
            S.op("act", lambda e: e.activation(out=esk[:], in_=esk[:], func=AF.Exp), r=[esk.b], w=[esk.b])
            S.op("dve", lambda e: e.tensor_scalar(out=cpw[64:128], in0=cpw[64:128], scalar1=-1.0, scalar2=None, op0=ALU.mult),
                 r=[cpw.b], w=[cpw.b])
            S.op("dve", lambda e: e.memset(hst[:], 0.0), w=[hst.b] + hstb)

            lr_c, li_c, ld_c = colp[:, 0:32], colp[:, 32:64], colp[:, 64:96]
            S.op("act", lambda e: e.activation(out=cdt[:], in_=ld_c, func=AF.Exp), r=[colp.b], w=[cdt.b])
            S.op("dve", lambda e: e.tensor_tensor(out=ctmp[:], in0=lr_c, in1=cdt[:], op=ALU.mult), r=[colp.b, cdt.b], w=[ctmp.b])
            S.op("act", lambda e: e.activation(out=crr[:], in_=ctmp[:], func=AF.Exp), r=[ctmp.b], w=[crr.b])
            S.op("dve", lambda e: e.tensor_tensor(out=cth[:], in0=li_c, in1=cdt[:], op=ALU.mult), r=[colp.b, cdt.b], w=[cth.b])

            def sincos(out_s, out_c, th, thb, tmp, tb, m, mb, itmp, ib, sb_, cb):
                I32 = mybir.dt.int32
                for dst, db, shift in ((out_s, sb_, 0.0), (out_c, cb, 0.5 * math.pi)):
                    S.op("dve", lambda e, shift=shift: e.tensor_scalar(out=tmp, in0=th, scalar1=shift, scalar2=1.0 / (2 * math.pi),
                                                                       op0=ALU.add, op1=ALU.mult), r=[thb], w=[tb])
                    S.op("dve", lambda e: e.tensor_copy(out=itmp, in_=tmp), r=[tb], w=[ib])
                    S.op("dve", lambda e: e.tensor_copy(out=tmp, in_=itmp), r=[ib], w=[tb])
                    S.op("dve", lambda e: e.scalar_tensor_tensor(out=tmp, in0=tmp, scalar=-2 * math.pi, in1=th, op0=ALU.mult, op1=ALU.add),
                         r=[tb, thb], w=[tb])
                    S.op("dve", lambda e, shift=shift: e.tensor_scalar(out=tmp, in0=tmp, scalar1=shift, scalar2=None, op0=ALU.add),
                         r=[tb], w=[tb])
                    S.op("dve", lambda e: e.tensor_scalar(out=m, in0=tmp, scalar1=math.pi, scalar2=-2 * math.pi, op0=ALU.is_gt, op1=ALU.mult),
                         r=[tb], w=[mb])
                    S.op("dve", lambda e: e.tensor_tensor(out=tmp, in0=tmp, in1=m, op=ALU.add), r=[tb, mb], w=[tb])
                    S.op("dve", lambda e: e.tensor_scalar(out=m, in0=tmp, scalar1=-math.pi, scalar2=2 * math.pi, op0=ALU.is_lt, op1=ALU.mult),
                         r=[tb], w=[mb])
                    S.op("dve", lambda e: e.tensor_tensor(out=tmp, in0=tmp, in1=m, op=ALU.add), r=[tb, mb], w=[tb])
                    S.op("dve", lambda e: e.tensor_scalar(out=tmp, in0=tmp, scalar1=math.pi, scalar2=-math.pi, op0=ALU.min, op1=ALU.max),
                         r=[tb], w=[tb])
                    S.op("act", lambda e, dst=dst: e.activation(out=dst, in_=tmp, func=AF.Sin), r=[tb], w=[db])

            sincos(cs1[:], cc1[:], cth[:], cth.b, ctmp[:], ctmp.b, ctmp2[:], ctmp2.b, citmp[:], citmp.b, cs1.b, cc1.b)
            S.op("dve", lambda e: e.tensor_tensor(out=car[:], in0=crr[:], in1=cc1[:], op=ALU.mult), r=[crr.b, cc1.b], w=[car.b])
            S.op("dve", lambda e: e.tensor_tensor(out=cais[:], in0=crr[:], in1=cs1[:], op=ALU.mult), r=[crr.b, cs1.b], w=[cais.b])
            S.op("dve", lambda e: e.tensor_scalar(out=cais[:], in0=cais[:], scalar1=signc[:, 0:1], scalar2=None, op0=ALU.mult),
                 r=[cais.b, signc.b], w=[cais.b])
            S.dma("sp", w00[:], w00_d[l], w=[w00.b])
            S.dma("sp", b00[:], b00_d[l], w=[b00.b])
            S.dma("sp", esks[:], esks_d[l], w=[esks.b])
            S.op("act", lambda e: e.activation(out=esks[:], in_=esks[:], func=AF.Exp), r=[esks.b], w=[esks.b])
            for g4 in range(4):
                S.op("dve", lambda e, g4=g4: e.tensor_scalar(out=w00I[:, g4, :], in0=ident[0:NS, 0:NS], scalar1=w00[:, g4:g4 + 1],
                                                             scalar2=None, op0=ALU.mult), r=[ident.b, w00.b], w=[w00I.b])
            for qq in range(8):
                gs = slice(qq * 4, (qq + 1) * 4)
                ccos = v2(ypre, 4 * TT).rearrange("p (g t) -> p g t", g=4)
                csin = v2(gtmp, 4 * TT).rearrange("p (g t) -> p g t", g=4)
                ctm = v2(mtmp[0], 2 * TT).rearrange("p (g t) -> p g t", g=4)
                ctm2 = v2(mtmp[1], 2 * TT).rearrange("p (g t) -> p g t", g=4)
                S.op("dve", lambda e: e.tensor_copy(out=ccos[:, :, 0], in_=cc1[:, gs]), r=[cc1.b], w=[ypre.b])
                S.op("dve", lambda e: e.tensor_copy(out=csin[:, :, 0], in_=cs1[:, gs]), r=[cs1.b], w=[gtmp.b])
                n = 1
                while n < TT:
                    cn = ccos[:, :, n - 1:n].to_broadcast([128, 4, n])
                    sn = csin[:, :, n - 1:n].to_broadcast([128, 4, n])
                    c0 = ccos[:, :, 0:n]
                    s0 = csin[:, :, 0:n]
                    t1 = ctm[:, :, 0:n]
                    t2 = ctm2[:, :, 0:n]
                    S.op("dve", lambda e, c0=c0, cn=cn, t1=t1: e.tensor_tensor(out=t1, in0=c0, in1=cn, op=ALU.mult), r=[ypre.b], w=[mtmp[0].b])
                    S.op("dve", lambda e, s0=s0, sn=sn, t2=t2: e.tensor_tensor(out=t2, in0=s0, in1=sn, op=ALU.mult), r=[gtmp.b], w=[mtmp[1].b])
                    S.op("dve", lambda e, t1=t1, t2=t2, n=n: e.tensor_tensor(out=ccos[:, :, n:2 * n], in0=t1, in1=t2, op=ALU.subtract),
                         r=[mtmp[0].b, mtmp[1].b], w=[ypre.b])
                    S.op("dve", lambda e, c0=c0, sn=sn, t1=t1: e.tensor_tensor(out=t1, in0=c0, in1=sn, op=ALU.mult), r=[ypre.b, gtmp.b], w=[mtmp[0].b])
                    S.op("dve", lambda e, s0=s0, cn=cn, t2=t2: e.tensor_tensor(out=t2, in0=s0, in1=cn, op=ALU.mult), r=[ypre.b, gtmp.b], w=[mtmp[1].b])
                    S.op("dve", lambda e, t1=t1, t2=t2, n=n: e.tensor_tensor(out=csin[:, :, n:2 * n], in0=t1, in1=t2, op=ALU.add),
                         r=[mtmp[0].b, mtmp[1].b], w=[gtmp.b])
                    n *= 2
                S.op("act", lambda e: e.copy(out=tabc[:, gs, :], in_=ccos), r=[ypre.b], w=[tabc.b])
                S.op("act", lambda e: e.copy(out=tabs[:, gs, :], in_=csin), r=[gtmp.b], w=[tabs.b])

            tl = [T(None, "x")] * 0
            hold = [(xtok[0], 0), (xtok[0], 512), (xtok[1], 0), (xtok[1], 512), (lnout[0], 0), (lnout[0], 512),
                    (lnout[1], 0), (lnout[1], 512)]

            class V:
                def __init__(self, t, off):
                    self.ap = t[:, off:off + 512]
                    self.b = t.b

            for qq in range(4):
                gs = slice(qq * 8, (qq + 1) * 8)
                cs_q = slice(qq * 512, (qq + 1) * 512)
                lr, li, ld, dt_, mg, th, sn_, cs_ = [V(t, o) for t, o in hold]
                for i, dst in enumerate((lr, li, ld)):
                    S.dma("sp", dst.ap, rowp_d[l, i][:, cs_q], w=[dst.b])
                bld = v2(xtok[2], 1024).rearrange("k (a g p) -> k a g p", a=2, g=8)
                for a in range(2):
                    S.dma("sp", bld[:, a], bpad_d[l, a, qq * 8:(qq + 1) * 8].rearrange("g k p -> k g p"), w=[xtok[2].b])

                def tt(out, a, b_, op):
                    S.op("dve", lambda e: e.tensor_tensor(out=out.ap, in0=a.ap, in1=b_.ap, op=op), r=[a.b, b_.b], w=[out.b])

                S.op("act", lambda e: e.activation(out=dt_.ap, in_=ld.ap, func=AF.Exp), r=[ld.b], w=[dt_.b])
                tt(mg, lr, dt_, ALU.mult)
                S.op("act", lambda e: e.activation(out=mg.ap, in_=mg.ap, func=AF.Exp), r=[mg.b], w=[mg.b])
                tt(th, li, dt_, ALU.mult)
                sincos(sn_.ap, cs_.ap, th.ap, th.b, ld.ap, ld.b, v2(gtmp, 512), gtmp.b, v2(ypre, 512).bitcast(mybir.dt.int32), ypre.b, sn_.b, cs_.b)
                tt(cs_, cs_, mg, ALU.mult)
                S.op("dve", lambda e: e.tensor_scalar(out=cs_.ap, in0=cs_.ap, scalar1=-1.0, scalar2=None, op0=ALU.add), r=[cs_.b], w=[cs_.b])
                tt(sn_, sn_, mg, ALU.mult)
                tt(mg, lr, lr, ALU.mult)
                tt(th, li, li, ALU.mult)
                tt(mg, mg, th, ALU.add)
                S.op("dve", lambda e: e.reciprocal(out=mg.ap, in_=mg.ap), r=[mg.b], w=[mg.b])
                tt(dt_, cs_, lr, ALU.mult)
                tt(ld, sn_, li, ALU.mult)
                tt(dt_, dt_, ld, ALU.add)
                tt(dt_, dt_, mg, ALU.mult)
                tt(th, sn_, lr, ALU.mult)
                tt(ld, cs_, li, ALU.mult)
                tt(th, th, ld, ALU.subtract)
                tt(th, th, mg, ALU.mult)
                crv = dt_.ap.rearrange("k (g p) -> k g p", p=64)
                civ = th.ap.rearrange("k (g p) -> k g p", p=64)
                t1 = sn_.ap.rearrange("k (g p) -> k g p", p=64)
                t2 = cs_.ap.rearrange("k (g p) -> k g p", p=64)
                bre = bld[:, 0]
                bim = bld[:, 1]
                S.op("dve", lambda e: e.tensor_tensor(out=t1, in0=crv, in1=bre, op=ALU.mult), r=[dt_.b, xtok[2].b], w=[sn_.b])
                S.op("dve", lambda e: e.tensor_tensor(out=t2, in0=civ, in1=bim, op=ALU.mult), r=[th.b, xtok[2].b], w=[cs_.b])
                S.op("dve", lambda e: e.tensor_tensor(out=bbw[:, gs, 0:64], in0=t1, in1=t2, op=ALU.subtract), r=[sn_.b, cs_.b], w=[bbw.b])
                S.op("dve", lambda e: e.tensor_tensor(out=bbs[:, gs, 64:128], in0=t2, in1=t1, op=ALU.subtract), r=[sn_.b, cs_.b], w=[bbs.b])
                S.op("dve", lambda e: e.tensor_tensor(out=t1, in0=crv, in1=bim, op=ALU.mult), r=[dt_.b, xtok[2].b], w=[sn_.b])
                S.op("dve", lambda e: e.tensor_tensor(out=t2, in0=civ, in1=bre, op=ALU.mult), r=[th.b, xtok[2].b], w=[cs_.b])
                S.op("dve", lambda e: e.tensor_tensor(out=bbw[:, gs, 64:128], in0=t1, in1=t2, op=ALU.add), r=[sn_.b, cs_.b], w=[bbw.b])
                S.op("dve", lambda e: e.tensor_tensor(out=bbs[:, gs, 0:64], in0=t1, in1=t2, op=ALU.add), r=[sn_.b, cs_.b], w=[bbs.b])

        hstb = [Buf("hst%d" % g) for g in range(32)]
        ysb = [Buf("ys_a"), Buf("ys_b"), Buf("ys_c")]

        def s5_pipe(n):
            cols = slice(0, n)
            st = {}
            psy_cur = {}
            for it in range(32 + 7):
                g = it - 7
                if 0 <= g < 32:
                    fb, gl = g // 8, g % 8
                    if gl == 0:
                        psy_cur["p"] = ps_alloc("Y")
                    psy = psy_cur["p"]
                    hb = st[g]["hb"]
                    S.op("pe", lambda e, g=g, gl=gl, psy=psy, hb=hb: e.matmul(psy[:, 0:n], lhsT=cpw[:, g, :], rhs=hb[:, 0:n],
                                                                               start=(gl == 0), stop=(gl == 7)),
                         r=[cpw.b, hb.b], w=[psy.b], sig=True)
                    if gl == 7:
                        S.op("dve", lambda e, fb=fb, psy=psy: e.scalar_tensor_tensor(out=ypre[:, fb, cols], in0=uT[:, fb, cols],
                                                                                    scalar=dsk[:, fb:fb + 1], in1=psy[:, 0:n],
                                                                                    op0=ALU.mult, op1=ALU.add),
                             r=[uT.b, dsk.b, psy.b], w=[ypre.b])
                    del st[g]
                g = it - 6
                if 0 <= g < 32:
                    c, d = st[g]["c"], st[g]["d"]
                    hb = rot(s5hb, "hb")
                    st[g]["hb"] = hb
                    S.op("pool", lambda e, c=c, d=d, hb=hb: e.tensor_tensor(out=hb[:, 0:n], in0=c[:, 0:n], in1=d[:, 0:n], op=ALU.add),
                         r=[c.b, d.b], w=[hb.b])
                    S.op("pool", lambda e, c=c, d=d, g=g: e.tensor_tensor(out=hst[:, g:g + 1], in0=c[:, n - 1:n], in1=d[:, n - 1:n], op=ALU.add),
                         r=[c.b, d.b], w=[hstb[g]])
                g = it - 5
                if 0 <= g < 32:
                    psw = st[g]["psw"]
                    d = rot(s5d, "d")
                    st[g]["d"] = d
                    S.op("dve", lambda e, g=g, psw=psw, d=d: e.tensor_tensor(out=d[:, 0:n], in0=psw[:, 0:n], in1=tabs[:, g, 0:n], op=ALU.mult),
                         r=[psw.b, tabs.b], w=[d.b])
                g = it - 4
                if 0 <= g < 32:
                    gg = st[g]["gg"]
                    psw = ps_alloc("C")
                    st[g]["psw"] = psw
                    S.op("pe", lambda e, gg=gg, psw=psw: e.matmul(psw[:, 0:n], lhsT=swapm[:], rhs=gg[:, 0:n], start=True, stop=True),
                         r=[swapm.b, gg.b], w=[psw.b])
                    c = rot(s5c, "c")
                    st[g]["c"] = c
                    S.op("dve", lambda e, g=g, gg=gg, c=c: e.tensor_tensor(out=c[:, 0:n], in0=gg[:, 0:n], in1=tabc[:, g, 0:n], op=ALU.mult),
                         r=[gg.b, tabc.b], w=[c.b])
                g = it - 3
                if 0 <= g < 32:
                    v = st[g]["v"]
                    gg = rot(s5g, "g")
                    st[g]["gg"] = gg
                    S.op("dve", lambda e, g=g, v=v, gg=gg: e.tensor_tensor_scan(out=gg[:, 0:n], data0=crr[:, g:g + 1].to_broadcast([128, n]),
                                                                               data1=v[:, 0:n], initial=hst[:, g:g + 1], op0=ALU.mult, op1=ALU.add),
                         r=[crr.b, v.b, hstb[g]], w=[gg.b])
                g = it - 2
                if 0 <= g < 32:
                    a, b_ = st[g]["a"], st[g]["b"]
                    v = rot(s5v, "v")
                    st[g]["v"] = v
                    S.op("pool", lambda e, a=a, b_=b_, v=v: e.tensor_tensor(out=v[:, 0:n], in0=a[:, 0:n], in1=b_[:, 0:n], op=ALU.add),
                         r=[a.b, b_.b], w=[v.b])
                g = it - 1
                if 0 <= g < 32:
                    pv = st[g]["pv"]
                    a = rot(s5a, "a")
                    b_ = rot(s5b, "b")
                    st[g]["a"], st[g]["b"] = a, b_
                    S.op("dve", lambda e, g=g, pv=pv, a=a: e.tensor_tensor(out=a[:, 0:n], in0=pv[:, 0:n], in1=tabc[:, g, 0:n], op=ALU.mult),
                         r=[pv.b, tabc.b], w=[a.b])
                    S.op("dve", lambda e, g=g, pv=pv, b_=b_: e.tensor_tensor(out=b_[:, 0:n], in0=pv[:, TT:TT + n], in1=tabs[:, g, 0:n], op=ALU.mult),
                         r=[pv.b, tabs.b], w=[b_.b])
                g = it
                if 0 <= g < 32:
                    fb = g // 8
                    pv = ps_alloc("B")
                    st[g] = {"pv": pv}
                    S.op("pe", lambda e, g=g, fb=fb, pv=pv: e.matmul(pv[:, 0:n], lhsT=bbw[:, g, :], rhs=uT[:, fb, cols], start=True, stop=True),
                         r=[bbw.b, uT.b], w=[pv.b], sig=False)
                    S.op("pe", lambda e, g=g, fb=fb, pv=pv: e.matmul(pv[:, TT:TT + n], lhsT=bbs[:, g, :], rhs=uT[:, fb, cols], start=True, stop=True),
                         r=[bbs.b, uT.b], w=[pv.b])
                yield

        def gelu_glu_gen(l, n):
            yv = ypre[:, :, 0:n]
            g1 = gtmp[:, :, 0:n]
            S.op("dve", lambda e: e.tensor_tensor(out=g1, in0=yv, in1=yv, op=ALU.mult), r=[ypre.b], w=[gtmp.b])
            S.op("dve", lambda e: e.tensor_scalar(out=g1, in0=g1, scalar1=GK, scalar2=1.0, op0=ALU.mult, op1=ALU.add), r=[gtmp.b], w=[gtmp.b])
            yield
            S.op("pool", lambda e: e.tensor_tensor(out=g1, in0=g1, in1=yv, op=ALU.mult), r=[gtmp.b, ypre.b], w=[gtmp.b])
            yield
            S.op("act", lambda e: e.activation(out=g1, in_=g1, func=AF.Sigmoid, scale=GS), r=[gtmp.b], w=[gtmp.b])
            yield
            S.op("dve", lambda e: e.tensor_tensor(out=yv, in0=yv, in1=g1, op=ALU.mult), r=[gtmp.b, ypre.b], w=[ypre.b])
            yield
            S.op("act", lambda e: e.copy(out=gtmp2[:, :, 0:n], in_=yv), r=[ypre.b], w=[gtmp2.b])
            yield
            wb = w_next(l, C_GLU)
            for j in range(4):
                ps = ps_alloc("A")
                mm_fm(ps, wb, j, 4, lambda kb: gtmp2[:, kb, 0:n], [gtmp2.b], n)
                S.op("act", lambda e, j=j, ps=ps: e.activation(out=gtmp[:, j, 0:n], in_=ps[:, 0:n], func=AF.Sigmoid, bias=glub[:, j:j + 1]),
                     r=[ps.b, glub.b], w=[gtmp.b])
            w_done()
            yield
            S.op("dve", lambda e: e.tensor_tensor(out=yv, in0=yv, in1=g1, op=ALU.mult), r=[gtmp.b, ypre.b], w=[ypre.b])
            yield
            S.op("pool", lambda e: e.tensor_tensor(out=ys[:, 4:8, 0:n], in0=yv, in1=szb[:, :, 0:n], op=ALU.mult),
                 r=[ypre.b, szb.b], w=[ysb[1]])

        def gelu_glu(l, n):
            for _ in gelu_glu_gen(l, n):
                pass

        def layernorm_rows(src, nrow, width, gam, bet, dst, tmp=None, eng2="pool"):
            if tmp is None:
                tmp = (lntmp.b, lntmp[0:nrow, 0:width])
            nchk = width // 512
            for i in range(nchk):
                S.op("dve", lambda e, i=i: e.bn_stats(out=lnst[0:nrow, i * 6:(i + 1) * 6], in_=src[1][:, i * 512:(i + 1) * 512]),
                     r=[src[0]], w=[lnst.b])
            S.op("dve", lambda e: e.bn_aggr(out=lnmv[0:nrow, 0:2], in_=lnst[0:nrow, 0:6 * nchk]), r=[lnst.b], w=[lnmv.b])
            S.op("dve", lambda e: e.tensor_scalar(out=lnmv[0:nrow, 2:3], in0=lnmv[0:nrow, 1:2], scalar1=LN_EPS, scalar2=None,
                                                  op0=ALU.add), r=[lnmv.b], w=[lnmv.b])
            S.op("act", lambda e: e.sqrt(out=lnmv[0:nrow, 2:3], in_=lnmv[0:nrow, 2:3]), r=[lnmv.b], w=[lnmv.b])
            S.op("dve", lambda e: e.reciprocal(out=lnmv[0:nrow, 2:3], in_=lnmv[0:nrow, 2:3]), r=[lnmv.b], w=[lnmv.b])
            S.op("dve", lambda e: e.tensor_scalar(out=tmp[1], in0=src[1], scalar1=lnmv[0:nrow, 0:1],
                                                  scalar2=lnmv[0:nrow, 2:3], op0=ALU.subtract, op1=ALU.mult),
                 r=[src[0], lnmv.b], w=[tmp[0]])
            S.op("dve", lambda e: e.tensor_tensor(out=tmp[1], in0=tmp[1], in1=gam[1], op=ALU.mult),
                 r=[tmp[0], gam[0]], w=[tmp[0]])
            S.op(eng2, lambda e: e.tensor_tensor(out=dst[1], in0=tmp[1], in1=bet[1], op=ALU.add),
                 r=[tmp[0], bet[0]], w=[dst[0]])

        def xload(l, ti):
            t0 = ti * TT
            src = xp if l == 0 else x1
            for b in range(NBK):
                xt = xtok[(ti % 2) * NBK + b]
                S.dma("sp", xt[:], src[t0 + b * 128: t0 + (b + 1) * 128, :], r=([x1_b] if l == 1 else []), w=[xt.b])

        def front(l, ti):
            n = TT
            xts = [xtok[(ti % 2) * NBK + b] for b in range(NBK)]
            for kb in range(8):
                ps = ps_alloc("A")
                for b in range(NBK):
                    S.op("pe", lambda e, b=b, ps=ps, kb=kb: e.transpose(out=ps[:, b * 128:(b + 1) * 128],
                                                                       in_=xts[b][:, kb * 128:(kb + 1) * 128], identity=ident[:]),
                         r=[xts[b].b, ident.b], w=[ps.b], sig=(b == NBK - 1))
                S.op("act" if kb % 2 else "dve",
                     (lambda e, ps=ps, kb=kb: e.copy(out=xT[:, kb, :], in_=ps[:, 0:TT])) if kb % 2 else
                     (lambda e, ps=ps, kb=kb: e.tensor_copy(out=xT[:, kb, :], in_=ps[:, 0:TT])),
                     r=[ps.b], w=[xT.b])
            xrhs = lambda kb: xT[:, kb, :]
            wb = w_next(l, C_UB)
            for j in range(4):
                ps = ps_alloc("A")
                mm_fm(ps, wb, j, 8, xrhs, [xT.b], n)
                S.op("act", lambda e, j=j, ps=ps: e.copy(out=uT[:, j, :], in_=ps[:, 0:TT]), r=[ps.b], w=[uT.b])
            w_done()

        def tail(l, ti):
            t0 = ti * TT
            dst = x1 if l == 0 else yp
            for _ in gelu_glu_gen(l, TT):
                yield
            yield
            for _ in readout_br(l, TT, (1,), False, pre=True):
                yield
            blocks = [(xtok[(ti % 2) * NBK + b], 128, dst[t0 + b * 128: t0 + (b + 1) * 128, :]) for b in range(NBK)]
            for _ in wo_ln(l, TT, blocks):
                yield
            if ti + 2 < (NTILE if DBG_TILES is None else DBG_TILES):
                xload(l, ti + 2)

        def rest_gen(l, ti):
            n = TT
            xrhs = lambda kb: xT[:, kb, :]

            wb = w_next(l, C_Q)
            for j in range(4):
                ps = ps_alloc("A")
                mm_fm(ps, wb, j, 8, xrhs, [xT.b], n)
                S.op("act", lambda e, j=j, ps=ps: e.activation(out=qT[:, j, :], in_=ps[:, 0:TT], func=AF.Copy, scale=0.125),
                     r=[ps.b], w=[qT.b])
                yield
            w_done()
            yield
            wb = w_next(l, C_KD)
            for j in range(2):
                ps = ps_alloc("A")
                mm_fm(ps, wb, j, 8, xrhs, [xT.b], n)
                S.op("act", lambda e, j=j, ps=ps: e.copy(out=kT[:, j, 128:128 + TT], in_=ps[:, 0:TT]), r=[ps.b], w=[kT.b])
                yield
            w_done()
            yield
            wb = w_next(l, C_KV)
            for b in range(NBK):
                ps = ps_alloc("A")
                for kb in range(8):
                    S.op("pe", lambda e, kb=kb, b=b, ps=ps: e.matmul(ps[:, 0:256], lhsT=xT[:, kb, b * 128:(b + 1) * 128],
                                                                    rhs=wb[:, kb * 512: kb * 512 + 256], start=(kb == 0), stop=(kb == 7)),
                         r=[wb.b, xT.b], w=[ps.b], sig=(kb == 7))
                vsrc = ps[:, 128:256].rearrange("p (k d) -> p k d", k=2)
                S.op("act", lambda e, b=b, vsrc=vsrc: e.copy(out=vtok[:, b + 1, :, 0:64], in_=vsrc), r=[ps.b], w=[vtok.b])
                S.op("act", lambda e, b=b, vsrc=vsrc: e.copy(out=vtok[:, b + 1, :, 64:128], in_=vsrc), r=[ps.b], w=[vtok.b])
                if ti == (NTILE if DBG_TILES is None else DBG_TILES) - 1 and b == NBK - 1:
                    S.op("act", lambda e, ps=ps: e.copy(out=kvout[:], in_=ps[:, 0:256]), r=[ps.b], w=[kvout.b])
                    S.dma("pool", kvw[l], kvout[:], r=[kvout.b])
                yield
            w_done()
            yield
            for cid, dstT, fn in ((C_ZA, sza, AF.Silu), (C_UC, ucz, None)):
                wb = w_next(l, cid)
                for j in range(4):
                    ps = ps_alloc("A")
                    mm_fm(ps, wb, j, 8, xrhs, [xT.b], n)
                    if fn is None:
                        S.op("act", lambda e, j=j, ps=ps, dstT=dstT: e.copy(out=dstT[:, j, :], in_=ps[:, 0:TT]), r=[ps.b], w=[dstT.b])
                    else:
                        S.op("act", lambda e, j=j, ps=ps, dstT=dstT, fn=fn: e.activation(out=dstT[:, j, :], in_=ps[:, 0:TT], func=fn),
                             r=[ps.b], w=[dstT.b])
                    yield
                w_done()
                yield
            wb = w_next(l, C_VC)
            for b in range(NBK):
                ps = ps_alloc("A")
                for kb in range(8):
                    S.op("pe", lambda e, kb=kb, b=b, ps=ps: e.matmul(ps[:, :], lhsT=xT[:, kb, b * 128:(b + 1) * 128],
                                                                    rhs=wb[:, kb * 512:(kb + 1) * 512], start=(kb == 0), stop=(kb == 7)),
                         r=[wb.b, xT.b], w=[ps.b], sig=(kb == 7))
                layernorm_rows((ps.b, ps[:, :]), 128, 512, (sgn.b, sgn[:, 0, :]), (sgn.b, sgn[:, 1, :]), (vn.b, vn[:, b, :]))
                yield
            w_done()
            yield
            wb = w_next(l, C_ZC)
            for j in range(4):
                ps = ps_alloc("A")
                mm_fm(ps, wb, j, 8, xrhs, [xT.b], n)
                S.op("act", lambda e, j=j, ps=ps: e.activation(out=sig[:, j, :], in_=ps[:, 0:TT], func=AF.Silu), r=[ps.b], w=[sig.b])
                yield
            w_done()
            S.op("pool", lambda e: e.tensor_tensor(out=ucz[:], in0=ucz[:], in1=sig[:], op=ALU.mult), r=[ucz.b, sig.b], w=[ucz.b])
            yield

            for b in range(NBK):
                for g4 in range(4):
                    ps = ps_alloc("A")
                    S.op("pe", lambda e, b=b, g4=g4, ps=ps: e.matmul(ps[:, 0:128], lhsT=vn[:, b, g4 * 128:(g4 + 1) * 128],
                                                                    rhs=sgw[:, g4, :], start=True, stop=True),
                         r=[vn.b, sgw.b], w=[ps.b])
                    mt = rot(mtmp, "mt")
                    S.op("dve", lambda e, g4=g4, ps=ps, mt=mt: e.tensor_tensor(out=mt[:, 0:128], in0=ps[:, 0:128],
                                                                              in1=sgb[:, g4 * 128:(g4 + 1) * 128], op=ALU.add),
                         r=[ps.b, sgb.b], w=[mt.b])
                    S.op("pool", lambda e, b=b, g4=g4, mt=mt: e.tensor_tensor(out=ys[:, 8 + g4, b * 128:(b + 1) * 128], in0=mt[:, 0:128],
                                                                             in1=ucz[:, g4, b * 128:(b + 1) * 128], op=ALU.mult),
                         r=[mt.b, ucz.b], w=[ysb[2]])
                yield

            for b in range(NBK):
                first = (ti == 0 and b == 0)
                parts = [1] if first else [0, 1]
                for kv in range(2):
                    for pt in parts:
                        pss = ps_alloc("D")
                        kc = b * 128 + pt * 128
                        for par in range(2):
                            rows = slice(par * 64, (par + 1) * 64)
                            o = pss[:, par * 256:(par + 1) * 256].rearrange("p (j q) -> p j q", j=2)
                            S.op("pe", lambda e, o=o, rows=rows, kc=kc, kv=kv, b=b: e.matmul(
                                o, lhsT=kT[rows, kv, kc:kc + 128], rhs=qT[rows, 2 * kv:2 * kv + 2, b * 128:(b + 1) * 128],
                                start=True, stop=False), r=[kT.b, qT.b], w=[pss.b], sig=False)
                            aqv = aq[:, (kv * 2 + par) * 256:(kv * 2 + par + 1) * 256].rearrange("p (j q) -> p j q", j=2)
                            S.op("pe", lambda e, o=o, aqv=aqv, pt=pt: e.matmul(o, lhsT=ak[:, pt * 128:(pt + 1) * 128], rhs=aqv,
                                                                               start=False, stop=True),
                                 r=[ak.b, aq.b], w=[pss.b], sig=(par == 1))
                        S.op("act", lambda e, pss=pss, pt=pt, kv=kv: e.activation(out=pT[kv][pt][:], in_=pss[:, :], func=AF.Exp),
                             r=[pss.b], w=[pT[kv][pt].b])
                        S.op("pool", lambda e, pt=pt, kv=kv: e.tensor_tensor(
                            out=pT[kv][pt][:].rearrange("p (h q) -> p h q", h=4), in0=pT[kv][pt][:].rearrange("p (h q) -> p h q", h=4),
                            in1=masks[:, pt, :].unsqueeze(1).to_broadcast([128, 4, 128]), op=ALU.mult),
                             r=[pT[kv][pt].b, masks.b], w=[pT[kv][pt].b])
                    yield
                    psd = ps_alloc("D")
                    for i, pt in enumerate(parts):
                        S.op("pe", lambda e, psd=psd, pt=pt, i=i, kv=kv: e.matmul(psd[:, :], lhsT=ones[:], rhs=pT[kv][pt][:],
                                                                                 start=(i == 0), stop=(i == len(parts) - 1)),
                             r=[ones.b, pT[kv][pt].b], w=[psd.b], sig=(i == len(parts) - 1))
                    S.op("dve", lambda e, psd=psd, kv=kv: e.tensor_tensor(
                        out=rden[kv][:].rearrange("p (h q) -> p h q", h=4), in0=psd[:, :].rearrange("p (h q) -> p h q", h=4),
                        in1=esk[:, kv * 4:(kv + 1) * 4].unsqueeze(2).to_broadcast([128, 4, 128]), op=ALU.add),
                         r=[psd.b, esk.b], w=[rden[kv].b])
                    S.op("act", lambda e, kv=kv: e.activation(out=rden[kv][:], in_=rden[kv][:], func=AF.Ln), r=[rden[kv].b], w=[rden[kv].b])
                    S.op("act", lambda e, kv=kv: e.activation(out=rden[kv][:], in_=rden[kv][:], func=AF.Exp, scale=-1.0),
                         r=[rden[kv].b], w=[rden[kv].b])
                    pso = ps_alloc("D")
                    for hh in range(4):
                        for i, pt in enumerate(parts):
                            S.op("pe", lambda e, pso=pso, hh=hh, pt=pt, i=i, kv=kv, b=b: e.matmul(
                                pso[:, hh * 128:(hh + 1) * 128], lhsT=vtok[:, b + pt, kv, :], rhs=pT[kv][pt][:, hh * 128:(hh + 1) * 128],
                                start=(i == 0), stop=(i == len(parts) - 1)),
                                 r=[vtok.b, pT[kv][pt].b], w=[pso.b], sig=(hh == 3 and i == len(parts) - 1))
                    S.op("dve", lambda e, pso=pso, kv=kv: e.tensor_tensor(out=otmp[kv][:], in0=pso[:, :], in1=rden[kv][:], op=ALU.mult),
                         r=[pso.b, rden[kv].b], w=[otmp[kv].b])
                    for par in range(2):
                        rows = slice(par * 64, (par + 1) * 64)
                        ov = otmp[kv][rows, par * 256:(par + 1) * 256].rearrange("p (j q) -> p j q", j=2)
                        S.op("pool", lambda e, rows=rows, ov=ov, kv=kv, b=b: e.tensor_tensor(
                            out=ys[rows, 2 * kv:2 * kv + 2, b * 128:(b + 1) * 128], in0=ov,
                            in1=sza[rows, 2 * kv:2 * kv + 2, b * 128:(b + 1) * 128], op=ALU.mult),
                             r=[otmp[kv].b, sza.b], w=[ysb[0]])
                    yield
            S.op("act", lambda e: e.copy(out=kT[:, :, 0:128], in_=kT[:, :, TT:TT + 128]), r=[kT.b], w=[kT.b])
            S.op("pool", lambda e: e.tensor_copy(out=vtok[:, 0], in_=vtok[:, NBK]), r=[vtok.b], w=[vtok.b])
            for _ in readout_br(l, n, (0, 2), True):
                yield
            for _ in gates_b(l, n):
                yield
            wb = w_next(l, C_ZB)
            for j in range(4):
                ps = ps_alloc("A")
                mm_fm(ps, wb, j, 8, xrhs, [xT.b], n)
                S.op("act", lambda e, j=j, ps=ps: e.activation(out=szb[:, j, :], in_=ps[:, 0:TT], func=AF.Silu), r=[ps.b], w=[szb.b])
            w_done()
            yield


        def gates_b(l, n):
            xrhs = lambda kb: xT[:, kb, 0:n]
            for hf in range(2):
                wb = w_next(l, C_G + hf * 3 + 1)
                for j in range(4):
                    ps = ps_alloc("A")
                    mm_fm(ps, wb, j, 8, xrhs, [xT.b], n)
                    S.op("act", lambda e, j=j, ps=ps: e.activation(out=sigB[:, hf * 4 + j, 0:n], in_=ps[:, 0:n], func=AF.Sigmoid),
                         r=[ps.b], w=[sigB.b])
                    yield
                w_done()
                yield

        def readout_br(l, n, branches, init_first, pre=False):
            xrhs = lambda kb: xT[:, kb, 0:n]
            for hf in range(2):
                for bi, b in enumerate(branches):
                    if pre:
                        gsrc, gbuf, goff = sigB, sigB.b, hf * 4
                    else:
                        gsrc, gbuf, goff = sig, sig.b, 0
                        wb = w_next(l, C_G + hf * 3 + b)
                        for j in range(4):
                            ps = ps_alloc("A")
                            mm_fm(ps, wb, j, 8, xrhs, [xT.b], n)
                            S.op("act", lambda e, j=j, ps=ps: e.activation(out=sig[:, j, 0:n], in_=ps[:, 0:n], func=AF.Sigmoid),
                                 r=[ps.b], w=[sig.b])
                            yield
                        w_done()
                        yield
                    wb = w_next(l, C_R + hf * 3 + b)
                    for j in range(4):
                        ps = ps_alloc("A")
                        mm_fm(ps, wb, j, 4, lambda kb, b=b: ys[:, b * 4 + kb, 0:n], [ysb[b]], n)
                        if init_first and bi == 0:
                            S.op("dve", lambda e, j=j, ps=ps, gsrc=gsrc, goff=goff: e.tensor_tensor(
                                out=mrgT[:, hf * 4 + j, 0:n], in0=ps[:, 0:n], in1=gsrc[:, goff + j, 0:n], op=ALU.mult),
                                 r=[ps.b, gbuf], w=[mrgT.b])
                        else:
                            mt = rot(mtmp, "mt")
                            S.op("dve", lambda e, j=j, ps=ps, mt=mt, gsrc=gsrc, goff=goff: e.tensor_tensor(
                                out=mt[:, 0:n], in0=ps[:, 0:n], in1=gsrc[:, goff + j, 0:n], op=ALU.mult),
                                 r=[ps.b, gbuf], w=[mt.b])
                            S.op("pool", lambda e, j=j, mt=mt: e.tensor_tensor(out=mrgT[:, hf * 4 + j, 0:n], in0=mrgT[:, hf * 4 + j, 0:n],
                                                                              in1=mt[:, 0:n], op=ALU.add),
                                 r=[mrgT.b, mt.b], w=[mrgT.b])
                        yield
                    w_done()
                    yield

        def drain(gen):
            for _ in gen:
                pass

        def wo_ln(l, n, blocks):
            for hf in range(2):
                wb = w_next(l, C_O + hf)
                c0 = 0
                for (xt, nrow, _) in blocks:
                    ps = ps_alloc("A")
                    for kb in range(8):
                        S.op("pe", lambda e, kb=kb, ps=ps, c0=c0, nrow=nrow: e.matmul(
                            ps[0:nrow, :], lhsT=mrgT[:, kb, c0:c0 + nrow], rhs=wb[:, kb * 512:(kb + 1) * 512],
                            start=(kb == 0), stop=(kb == 7)), r=[wb.b, mrgT.b], w=[ps.b], sig=(kb == 7))
                    S.op("dve", lambda e, ps=ps, xt=xt, nrow=nrow: e.scalar_tensor_tensor(
                        out=xt[0:nrow, hf * 512:(hf + 1) * 512], in0=xt[0:nrow, hf * 512:(hf + 1) * 512], scalar=ALPHA,
                        in1=ps[0:nrow, :], op0=ALU.mult, op1=ALU.add), r=[ps.b, xt.b], w=[xt.b])
                    c0 += nrow
                    yield
                w_done()
                yield
            for (xt, nrow, dst_ap) in blocks:
                lo = rot(lnout, "lo")
                layernorm_rows((xt.b, xt[0:nrow, :]), nrow, D, (lng.b, lng[0:nrow, 0, :]), (lng.b, lng[0:nrow, 1, :]),
                               (lo.b, lo[0:nrow, :]), tmp=(xt.b, xt[0:nrow, :]))
                S.dma("pool", dst_ap, lo[0:nrow, :], r=[lo.b], w=[x1_b])
                yield

        def sample_tile(l):
            n = NS
            src = xs if l == 0 else xs1
            dst = xs1 if l == 0 else ysm
            xt = xtok[0]
            S.dma("sp", xt[0:n, :], src, r=([x1_b] if l == 1 else []), w=[xt.b])
            for kb in range(8):
                ps = ps_alloc("A")
                S.op("pe", lambda e, ps=ps, kb=kb: e.transpose(out=ps[:, 0:n], in_=xt[0:n, kb * 128:(kb + 1) * 128], identity=ident[0:n, 0:n]),
                     r=[xt.b, ident.b], w=[ps.b])
                S.op("dve", lambda e, ps=ps, kb=kb: e.tensor_copy(out=xT[:, kb, 0:n], in_=ps[:, 0:n]), r=[ps.b], w=[xT.b])
            xrhs = lambda kb: xT[:, kb, 0:n]
            wb = w_next(l, C_UB)
            for j in range(4):
                ps = ps_alloc("A")
                mm_fm(ps, wb, j, 8, xrhs, [xT.b], n)
                S.op("dve", lambda e, j=j, ps=ps: e.tensor_copy(out=uT[:, j, 0:n], in_=ps[:, 0:n]), r=[ps.b], w=[uT.b])
            w_done()
            wb = w_next(l, C_Q)
            for j in range(4):
                ps = ps_alloc("A")
                mm_fm(ps, wb, j, 8, xrhs, [xT.b], n)
                S.op("act", lambda e, j=j, ps=ps: e.activation(out=qT[:, j, 0:n], in_=ps[:, 0:n], func=AF.Copy, scale=0.125), r=[ps.b], w=[qT.b])
            w_done()
            wb = w_next(l, C_KD)
            for j in range(4):
                ps = ps_alloc("A")
                mm_fm(ps, wb, j, 8, xrhs, [xT.b], n)
                if j < 2:
                    S.op("dve", lambda e, j=j, ps=ps: e.tensor_copy(out=ksd[:, j, :], in_=ps[:, 0:n]), r=[ps.b], w=[ksd.b])
                else:
                    S.op("dve", lambda e, j=j, ps=ps: e.tensor_copy(out=vsd[:, j - 2, :], in_=ps[:, 0:n]), r=[ps.b], w=[vsd.b])
            w_done()
            wb = w_next(l, C_KV)
            ps = ps_alloc("A")
            for kb in range(8):
                S.op("pe", lambda e, kb=kb, ps=ps: e.matmul(ps[0:n, 0:256], lhsT=xT[:, kb, 0:n], rhs=wb[:, kb * 512: kb * 512 + 256],
                                                           start=(kb == 0), stop=(kb == 7)), r=[wb.b, xT.b], w=[ps.b], sig=(kb == 7))
            S.op("dve", lambda e, ps=ps: e.tensor_copy(out=kvs_sb[:], in_=ps[0:n, 0:256]), r=[ps.b], w=[kvs_sb.b])
            S.dma("pool", kvs[l], kvs_sb[:], r=[kvs_sb.b])
            w_done()
            for cid, dstT, fn in ((C_ZA, sza, AF.Silu), (C_ZB, szb, AF.Silu), (C_UC, ucz, None)):
                wb = w_next(l, cid)
                for j in range(4):
                    ps = ps_alloc("A")
                    mm_fm(ps, wb, j, 8, xrhs, [xT.b], n)
                    if fn is None:
                        S.op("dve", lambda e, j=j, ps=ps, dstT=dstT: e.tensor_copy(out=dstT[:, j, 0:n], in_=ps[:, 0:n]), r=[ps.b], w=[dstT.b])
                    else:
                        S.op("act", lambda e, j=j, ps=ps, dstT=dstT, fn=fn: e.activation(out=dstT[:, j, 0:n], in_=ps[:, 0:n], func=fn),
                             r=[ps.b], w=[dstT.b])
                w_done()
            wb = w_next(l, C_VC)
            ps = ps_alloc("A")
            for kb in range(8):
                S.op("pe", lambda e, kb=kb, ps=ps: e.matmul(ps[0:n, :], lhsT=xT[:, kb, 0:n], rhs=wb[:, kb * 512:(kb + 1) * 512],
                                                           start=(kb == 0), stop=(kb == 7)), r=[wb.b, xT.b], w=[ps.b], sig=(kb == 7))
            lo = rot(lnout, "lo")
            layernorm_rows((ps.b, ps[0:n, :]), n, 512, (sgn.b, sgn[0:n, 0, :]), (sgn.b, sgn[0:n, 1, :]), (lo.b, lo[0:n, 0:512]))
            S.dma("pool", vns[l], lo[0:n, 0:512], r=[lo.b])
            S.op("act", lambda e: e.copy(out=vnS[:], in_=lo[0:n, 0:512]), r=[lo.b], w=[vnS.b])
            w_done()
            wb = w_next(l, C_ZC)
            for j in range(4):
                ps = ps_alloc("A")
                mm_fm(ps, wb, j, 8, xrhs, [xT.b], n)
                S.op("act", lambda e, j=j, ps=ps: e.activation(out=gtmp2[:, j, 0:n], in_=ps[:, 0:n], func=AF.Silu), r=[ps.b], w=[gtmp2.b])
            w_done()
            S.op("pool", lambda e: e.tensor_tensor(out=ucz[:, :, 0:n], in0=ucz[:, :, 0:n], in1=gtmp2[:, :, 0:n], op=ALU.mult),
                 r=[ucz.b, gtmp2.b], w=[ucz.b])

            S.dma("sp", h0[:], h0_d[l], w=[h0.b])
            S.dma("sp", h0s[:], h0s_d[l], w=[h0s.b])
            psv = ps_alloc("B")
            for g in range(32):
                S.op("pe", lambda e, g=g: e.matmul(psv[:, g * n:(g + 1) * n], lhsT=bbw[:, g, :], rhs=uT[:, g // 8, 0:n], start=True, stop=True),
                     r=[bbw.b, uT.b], w=[psv.b], sig=(g == 31))
            S.op("dve", lambda e: e.tensor_tensor(out=h0[:], in0=h0[:], in1=car[:].unsqueeze(2).to_broadcast([128, 32, n]), op=ALU.mult),
                 r=[h0.b, car.b], w=[h0.b])
            S.op("dve", lambda e: e.tensor_tensor(out=h0s[:], in0=h0s[:], in1=cais[:].unsqueeze(2).to_broadcast([128, 32, n]), op=ALU.mult),
                 r=[h0s.b, cais.b], w=[h0s.b])
            S.op("dve", lambda e: e.tensor_tensor(out=h0[:], in0=h0[:], in1=h0s[:], op=ALU.add), r=[h0.b, h0s.b], w=[h0.b])
            S.op("dve", lambda e: e.tensor_tensor(out=h0[:], in0=h0[:], in1=psv[:, 0:32 * n].rearrange("p (g n) -> p g n", n=n), op=ALU.add),
                 r=[h0.b, psv.b], w=[h0.b])
            S.dma("pool", hss[l], h0[:], r=[h0.b])
            S.op("act", lambda e: e.copy(out=hbs[:], in_=h0[:]), r=[h0.b], w=[hbs.b])
            for fb in range(4):
                psy = ps_alloc("Y")
                for gl in range(8):
                    g = fb * 8 + gl
                    S.op("pe", lambda e, g=g, gl=gl, psy=psy: e.matmul(psy[:, 0:n], lhsT=cpw[:, g, :], rhs=hbs[:, g, :], start=(gl == 0), stop=(gl == 7)),
                         r=[cpw.b, hbs.b], w=[psy.b], sig=(gl == 7))
                S.op("dve", lambda e, fb=fb, psy=psy: e.scalar_tensor_tensor(out=ypre[:, fb, 0:n], in0=uT[:, fb, 0:n], scalar=dsk[:, fb:fb + 1],
                                                                            in1=psy[:, 0:n], op0=ALU.mult, op1=ALU.add),
                     r=[uT.b, dsk.b, psy.b], w=[ypre.b])

            for g4 in range(4):
                ps = ps_alloc("A")
                S.op("pe", lambda e, g4=g4, ps=ps: e.matmul(ps[:, 0:n], lhsT=vnS[:, g4 * 128:(g4 + 1) * 128], rhs=w00I[:, g4, :], start=True, stop=True),
                     r=[vnS.b, w00I.b], w=[ps.b])
                S.op("dve", lambda e, g4=g4, ps=ps: e.tensor_scalar(out=st1[:], in0=ps[:, 0:n], scalar1=b00[:, g4:g4 + 1], scalar2=None, op0=ALU.add),
                     r=[ps.b, b00.b], w=[st1.b])
                S.op("dve", lambda e, g4=g4: e.tensor_tensor(out=ys[:, 8 + g4, 0:n], in0=st1[:], in1=ucz[:, g4, 0:n], op=ALU.mult),
                     r=[st1.b, ucz.b], w=[ysb[2]])

            S.dma("pool", ck[:], ckT_d[l], w=[ck.b])
            S.dma("pool", cv[:], cvd_d[l], w=[cv.b])
            pss = ps_alloc("D")
            for nn in range(n):
                for kv in range(2):
                    for par in range(2):
                        rows = slice(par * 64, (par + 1) * 64)
                        c0 = (kv * 2 + par) * 32
                        o = pss[:, c0:c0 + 32].rearrange("p (j n) -> p j n", j=2)[:, :, nn]
                        last = (nn == n - 1 and kv == 1 and par == 1)
                        S.op("pe", lambda e, o=o, rows=rows, nn=nn, kv=kv: e.matmul(o, lhsT=ck[rows, nn * 2 + kv, :], rhs=qT[rows, 2 * kv:2 * kv + 2, nn],
                                                                                   start=True, stop=False), r=[ck.b, qT.b], w=[pss.b], sig=False)
                        S.op("pe", lambda e, o=o, c0=c0: e.matmul(o, lhsT=ak[:, 0:128], rhs=aqs[:, c0 // 16: c0 // 16 + 2], start=False, stop=True),
                             r=[ak.b, aqs.b], w=[pss.b], sig=last)
            S.op("act", lambda e: e.activation(out=pSb[:], in_=pss[:, 0:128], func=AF.Exp), r=[pss.b], w=[pSb.b])
            for j in range(4):
                S.op("dve", lambda e, j=j: e.tensor_tensor(out=prod[:, j, :], in0=qT[:, j, 0:n], in1=ksd[:, j // 2, :], op=ALU.mult),
                     r=[qT.b, ksd.b], w=[prod.b])
            ps1 = ps_alloc("A")
            S.op("pe", lambda e: e.matmul(ps1[0:2, 0:64], lhsT=hsel[:], rhs=prod[:].rearrange("p j n -> p (j n)"), start=True, stop=True),
                 r=[hsel.b, prod.b], w=[ps1.b])
            S.op("act", lambda e: e.activation(out=pself[:], in_=ps1[0:2, 0:64], func=AF.Exp), r=[ps1.b], w=[pself.b])
            ps2 = ps_alloc("A")
            S.op("pe", lambda e: e.matmul(ps2[:, 0:64], lhsT=hselT[:], rhs=pself[:], start=True, stop=True), r=[hselT.b, pself.b], w=[ps2.b])
            S.op("dve", lambda e: e.tensor_copy(out=pbs[:].rearrange("p j n -> p (j n)"), in_=ps2[:, 0:64]), r=[ps2.b], w=[pbs.b])
            psd = ps_alloc("A")
            S.op("pe", lambda e: e.matmul(psd[:, 0:128], lhsT=ones[:], rhs=pSb[:], start=True, stop=True), r=[ones.b, pSb.b], w=[psd.b])
            S.op("dve", lambda e: e.tensor_copy(out=sdn[:], in_=psd[:, 0:128]), r=[psd.b], w=[sdn.b])
            pso = ps_alloc("D")
            for nn in range(n):
                for kv in range(2):
                    o = pso[:, kv * 64:(kv + 1) * 64].rearrange("p (a n) -> p a n", n=n)[:, :, nn]
                    r_ = pSb[:, kv * 64:(kv + 1) * 64].rearrange("p (a n) -> p a n", n=n)[:, :, nn]
                    S.op("pe", lambda e, o=o, r_=r_, nn=nn, kv=kv: e.matmul(o, lhsT=cv[:, nn * 2 + kv, :], rhs=r_, start=True, stop=True),
                         r=[cv.b, pSb.b], w=[pso.b], sig=(nn == n - 1 and kv == 1))
            S.op("dve", lambda e: e.tensor_copy(out=sso[:], in_=pso[:, 0:128]), r=[pso.b], w=[sso.b])
            for par in range(2):
                rows = slice(par * 64, (par + 1) * 64)
                for kv in range(2):
                    for jj in range(2):
                        blk = 2 * kv + jj
                        c0 = ((kv * 2 + par) * 2 + jj) * n
                        a1 = st1[rows, :]
                        a2 = st2[rows, :]
                        S.op("dve", lambda e, a1=a1, rows=rows, blk=blk, kv=kv: e.tensor_tensor(out=a1, in0=pbs[rows, blk, :], in1=vsd[rows, kv, :], op=ALU.mult),
                             r=[pbs.b, vsd.b], w=[st1.b])
                        S.op("dve", lambda e, a1=a1, rows=rows, c0=c0: e.tensor_tensor(out=a1, in0=a1, in1=sso[rows, c0:c0 + n], op=ALU.add),
                             r=[st1.b, sso.b], w=[st1.b])
                        S.op("dve", lambda e, a2=a2, rows=rows, blk=blk, c0=c0: e.tensor_tensor(out=a2, in0=pbs[rows, blk, :], in1=sdn[rows, c0:c0 + n], op=ALU.add),
                             r=[pbs.b, sdn.b], w=[st2.b])
                        S.op("dve", lambda e, a2=a2, rows=rows, blk=blk: e.tensor_scalar(out=a2, in0=a2, scalar1=esks[rows, blk:blk + 1], scalar2=None, op0=ALU.add),
                             r=[st2.b, esks.b], w=[st2.b])
                        S.op("dve", lambda e, a2=a2: e.reciprocal(out=a2, in_=a2), r=[st2.b], w=[st2.b])
                        S.op("dve", lambda e, a1=a1, a2=a2: e.tensor_tensor(out=a1, in0=a1, in1=a2, op=ALU.mult), r=[st1.b, st2.b], w=[st1.b])
                        S.op("dve", lambda e, a1=a1, rows=rows, blk=blk: e.tensor_tensor(out=ys[rows, blk, 0:n], in0=a1, in1=sza[rows, blk, 0:n], op=ALU.mult),
                             r=[st1.b, sza.b], w=[ysb[0]])
            drain(readout_br(l, n, (0, 2), True))
            gelu_glu(l, n)
            drain(readout_br(l, n, (1,), False))
            drain(wo_ln(l, n, [(xt, n, dst)]))

        x1_b = Buf("x1")

        def roundrobin(gens):
            gens = list(gens)
            while gens:
                for g_ in list(gens):
                    try:
                        next(g_)
                        while wst["hold"]:
                            next(g_)
                    except StopIteration:
                        gens.remove(g_)

        for l in range(DEPTH):
            setup_layer(l)
            nt = NTILE if DBG_TILES is None else DBG_TILES
            xload(l, 0)
            if nt > 1:
                xload(l, 1)
            front(l, 0)
            prev_tail = None
            for ti in range(nt):
                gens = [s5_pipe(TT), rest_gen(l, ti)]
                if prev_tail is not None:
                    gens.append(prev_tail)
                roundrobin(gens)
                if ti + 1 < nt:
                    front(l, ti + 1)
                prev_tail = tail(l, ti)
            if prev_tail is not None:
                drain(prev_tail)
            S.dma("pool", hst_o[l], hst[:], r=[hst.b] + hstb)
            sample_tile(l)
        allb = [x1_b, hst.b, h0.b, kvs_sb.b] + [t.b for t in lnout] + ([] if DBG_NOKV else [kvout.b])
        S.finish(allb)
        for i in range(NDS):
            if S.dcnt[i] > 0:
                S._wait("sp", (("d", i), S.dcnt[i]))
        if not recording:
            print("instructions:", S.ninst, "sem counts:", S.cnt)
    return nc


_PROG = None


def _consts():
    ident = np.eye(128, dtype=np.float32)
    swapm = np.zeros((128, 128), np.float32)
    for m in range(64):
        swapm[64 + m, m] = -1.0
        swapm[m, 64 + m] = 1.0
    s = np.arange(128)[:, None]
    q = np.arange(128)[None, :]
    mprev = (s >= q).astype(np.float32)
    mcur = (s <= q).astype(np.float32)
    masks = np.stack([np.tile(mprev, (1, 4)), np.tile(mcur, (1, 4))]).astype(np.float32)
    ak = np.zeros((2, 256), np.float32)
    ak[0, :128] = np.arange(128) - 128
    ak[0, 128:] = np.arange(128)
    ak[1, :] = 1.0
    slopes = 2.0 ** (-(np.arange(1, 9)))
    aq = np.zeros((2, 2, 2, 2, 128), np.float32)
    for kv in range(2):
        for par in range(2):
            for jj in range(2):
                h = 4 * kv + 2 * jj + par
                aq[0, kv, par, jj, :] = slopes[h]
                aq[1, kv, par, jj, :] = -slopes[h] * np.arange(128)
    rmask = np.zeros((128, 8), np.float32)
    for gl in range(8):
        rmask[gl * 16:(gl + 1) * 16, gl] = 1.0
    triu = (s <= q).astype(np.float32)
    aqs = np.zeros((2, 8), np.float32)
    for kv in range(2):
        for par in range(2):
            for jj in range(2):
                aqs[0, (kv * 2 + par) * 2 + jj] = slopes[4 * kv + 2 * jj + par]
    hsel = np.zeros((128, 2), np.float32)
    hsel[0:64, 0] = 1.0
    hsel[64:128, 1] = 1.0
    signc = np.ones((128, 1), np.float32)
    signc[0:64] = -1.0
    return dict(ident=ident, swapm=swapm, masks=masks, ak=ak, aq=aq.reshape(2, 1024), rmask=rmask, triu=triu,
                aqs=aqs, hsel=hsel, hselT=np.ascontiguousarray(hsel.T), signc=signc)


def _layout_weights(w_in, w_read, w_o, glu_w):
    out = np.zeros((DEPTH, NCH, 128, 8, 512), np.float32)

    def put(l, c, mat, kb0=0):
        k = mat.shape[0] // 128
        out[l, c, :, kb0:kb0 + k, :mat.shape[1]] = mat.reshape(k, 128, mat.shape[1]).transpose(1, 0, 2)

    for l in range(DEPTH):
        W = w_in[l]
        put(l, C_Q, W[:, 0:512])
        k0, k1 = W[:, 512:576], W[:, 576:640]
        v0, v1 = W[:, 640:704], W[:, 704:768]
        put(l, C_KD, np.concatenate([k0, k0, k1, k1, v0, v0, v1, v1], axis=1))
        put(l, C_KV, W[:, 512:768])
        put(l, C_ZA, W[:, 768:1280])
        put(l, C_UB, W[:, 1280:1792])
        put(l, C_ZB, W[:, 1792:2304])
        put(l, C_UC, W[:, 2304:2816])
        put(l, C_VC, W[:, 2816:3328])
        put(l, C_ZC, W[:, 3328:3840])
        for hf in range(2):
            for b in range(3):
                c0 = 3840 + b * 1024 + hf * 512
                put(l, C_G + hf * 3 + b, W[:, c0:c0 + 512])
                put(l, C_R + hf * 3 + b, w_read[l, b][:, hf * 512:(hf + 1) * 512])
            put(l, C_O + hf, w_o[l][:, hf * 512:(hf + 1) * 512])
        put(l, C_GLU, glu_w[l])
    return out.reshape(DEPTH, NCH, 128, 4096)


def _layout_params(inp):
    f = np.float32
    lam_re, lam_im, log_dt = inp["ssm_lambda_re"], inp["ssm_lambda_im"], inp["ssm_log_dt"]
    colp = np.zeros((DEPTH, 128, 96), f)
    rowp = np.zeros((DEPTH, 3, 128, 2048), f)
    for l in range(DEPTH):
        colp[l, :, 0:32] = np.concatenate([lam_re[l].T, lam_re[l].T], 0)
        colp[l, :, 32:64] = np.concatenate([lam_im[l].T, lam_im[l].T], 0)
        colp[l, :, 64:96] = log_dt[l][None, :]
        rowp[l, 0] = lam_re[l].reshape(1, 2048)
        rowp[l, 1] = lam_im[l].reshape(1, 2048)
        rowp[l, 2] = np.repeat(log_dt[l], 64)[None, :]
    bpad = np.zeros((DEPTH, 2, 32, 128, 64), f)
    cpad = np.zeros((DEPTH, 32, 128, 128), f)
    for l in range(DEPTH):
        for g in range(32):
            gl = g % 8
            bpad[l, 0, g, gl * 16:(gl + 1) * 16, :] = inp["ssm_b_re"][l, g].T
            bpad[l, 1, g, gl * 16:(gl + 1) * 16, :] = inp["ssm_b_im"][l, g].T
            cpad[l, g, 0:64, gl * 16:(gl + 1) * 16] = inp["ssm_c_re"][l, g].T
            cpad[l, g, 64:128, gl * 16:(gl + 1) * 16] = inp["ssm_c_im"][l, g].T
    dsk = inp["ssm_d"].reshape(DEPTH, 4, 128).transpose(0, 2, 1).copy()
    glub = inp["glu_b"].reshape(DEPTH, 4, 128).transpose(0, 2, 1).copy()
    sgn = np.zeros((DEPTH, 2, 128, 512), f)
    sgn[:, 0] = inp["sgu_ln_g"][:, None, :]
    sgn[:, 1] = inp["sgu_ln_b"][:, None, :]
    sgw = inp["sgu_w"].transpose(0, 1, 3, 2).copy()
    sgb = np.broadcast_to(inp["sgu_b"].reshape(DEPTH, 1, 512), (DEPTH, 128, 512)).copy()
    lng = np.zeros((DEPTH, 2, 128, 1024), f)
    lng[:, 0] = inp["ln_g"][:, None, :]
    lng[:, 1] = inp["ln_b"][:, None, :]
    snk = np.zeros((DEPTH, 128, 2, 2, 2), f)
    for kv in range(2):
        for par in range(2):
            for jj in range(2):
                h = 4 * kv + 2 * jj + par
                snk[:, :, kv, par, jj] = inp["attn_sinks"][:, h][:, None]
    return dict(colp=colp, rowp=rowp, bpad=bpad, cpad=cpad, dsk=dsk, glub=glub, sgn=sgn, sgw=sgw, sgb=sgb, lng=lng,
                snk=snk.reshape(DEPTH, 128, 8))


def kernel(**inp):
    global _PROG
    inp = {k: np.asarray(v) for k, v in inp.items()}
    if _PROG is None:
        _PROG = build_program()
    nc = _PROG
    consts = _consts()
    wch = _layout_weights(inp["w_in"], inp["w_read"], inp["w_o"], inp["glu_w"])
    prm = _layout_params(inp)
    in_maps = []
    for c in range(8):
        m = dict(consts)
        m.update(prm)
        m["wch"] = wch
        m["xp"] = np.ascontiguousarray(inp["x_prompt"][c % 2])
        sl = slice(c * NS, (c + 1) * NS)
        m["xs"] = np.ascontiguousarray(inp["x_sample"][sl, 0, :])
        ckc = inp["cache_k_win"][:, sl]
        t = ckc.transpose(0, 4, 1, 3, 2).reshape(DEPTH, 64, 32, 128)
        m["ckT"] = np.ascontiguousarray(np.concatenate([t, t], axis=1))
        cvc = inp["cache_v_win"][:, sl]
        t = cvc.transpose(0, 2, 1, 3, 4).reshape(DEPTH, 128, 32, 64)
        m["cvd"] = np.ascontiguousarray(np.concatenate([t, t], axis=3))
        sr = inp["state_ssm_re"][:, sl].transpose(0, 3, 2, 1)
        si = inp["state_ssm_im"][:, sl].transpose(0, 3, 2, 1)
        m["h0"] = np.ascontiguousarray(np.concatenate([sr, si], axis=1))
        m["h0s"] = np.ascontiguousarray(np.concatenate([si, sr], axis=1))
        m["w00"] = np.ascontiguousarray(np.broadcast_to(inp["sgu_w"][:, None, :, 0, 0], (DEPTH, NS, 4)))
        m["b00"] = np.ascontiguousarray(np.broadcast_to(inp["sgu_b"][:, None, :, 0], (DEPTH, 128, 4)))
        es_ = np.zeros((DEPTH, 128, 4), np.float32)
        for blk in range(4):
            es_[:, 0:64, blk] = inp["attn_sinks"][:, 2 * blk][:, None]
            es_[:, 64:128, blk] = inp["attn_sinks"][:, 2 * blk + 1][:, None]
        m["esks"] = es_
        in_maps.append(m)
    res = run_bass_kernel_spmd(nc, in_maps, core_ids=list(range(8)))
    R = res.results
    y_prompt = np.stack([R[0]["yp"], R[1]["yp"]])
    kvw = np.stack([R[0]["kvw"], R[1]["kvw"]], axis=1)
    k_win = kvw[..., 0:128].reshape(DEPTH, 2, 128, 2, 64)
    v_win = kvw[..., 128:256].reshape(DEPTH, 2, 128, 2, 64)
    hs = np.stack([R[0]["hst"], R[1]["hst"]], axis=1)
    h_re = hs[:, :, 0:64, :].transpose(0, 1, 3, 2)
    h_im = hs[:, :, 64:128, :].transpose(0, 1, 3, 2)
    y_sample = np.concatenate([R[c]["ysm"] for c in range(8)], axis=0).reshape(128, 1, D)
    kvsm = np.concatenate([R[c]["kvs"] for c in range(8)], axis=1)
    k_s = kvsm[..., 0:128].reshape(DEPTH, 128, 1, 2, 64)
    v_s = kvsm[..., 128:256].reshape(DEPTH, 128, 1, 2, 64)
    hsm = np.concatenate([R[c]["hss"] for c in range(8)], axis=3)
    hs_re = hsm[:, 0:64].transpose(0, 3, 2, 1)
    hs_im = hsm[:, 64:128].transpose(0, 3, 2, 1)
    vn_s = np.concatenate([R[c]["vns"] for c in range(8)], axis=1).reshape(DEPTH, 128, 1, 512)
    f = np.float32
    outs = (y_prompt, y_sample, k_win, v_win, h_re, h_im, k_s, v_s, hs_re, hs_im, vn_s)
    return tuple(np.ascontiguousarray(o, dtype=f) for o in outs)
```

```python
import math
from contextlib import ExitStack

import numpy as np
import concourse.bass as bass
import concourse.mybir as mybir
from concourse.bass_utils import run_bass_kernel_spmd

F32 = mybir.dt.float32
BF16 = mybir.dt.bfloat16
AF = mybir.ActivationFunctionType
ALU = mybir.AluOpType

D = 1024
SEQ = 8192
DEPTH = 2
NS = 16
BW = 512
TT = 256
NBK = TT // 128
NTILE = SEQ // TT
NCH = 24
ALPHA = (2 * DEPTH) ** 0.25
LN_EPS = 1e-5
GK = 0.044715
GS = 2.0 * math.sqrt(2.0 / math.pi)
NDS = 24
DBG_TILES = None
DBG_NOKV = False
DBG_SKIPKV = False

C_Q, C_KD, C_KV, C_ZA, C_UB, C_ZB, C_UC, C_VC, C_ZC = range(9)
C_G = 9
C_R = 15
C_O = 21
C_GLU = 23


class Buf:
    __slots__ = ("name", "w", "r", "excl")

    def __init__(self, name, excl=False):
        self.name = name
        self.w = None
        self.r = {}
        self.excl = excl


class Sched:
    def __init__(self, nc, es):
        self.nc = nc
        self.engs = {"pe": nc.tensor, "dve": nc.vector, "act": nc.scalar, "pool": nc.gpsimd, "sp": nc.sync}
        self.sem = {k: es.enter_context(nc.semaphore("sem_" + k)) for k in ["pe", "dve", "act", "pool"]}
        self.cnt = {k: 0 for k in self.sem}
        self.seen = {e: {} for e in self.engs}
        self.dsem = [es.enter_context(nc.semaphore("dsem%d" % i)) for i in range(NDS)]
        self.dcnt = [0] * NDS
        self.drange = {"sp": (0, 16), "pool": (16, NDS)}
        self.drr = {"sp": 0, "pool": 16}
        self.pend = []
        self.pendset = set()
        self.ninst = 0

    def _wait(self, e, ev):
        if ev is None:
            return
        key, val = ev
        if key == e and e == "pe":
            return
        if self.seen[e].get(key, 0) >= val:
            return
        sem = self.sem[key] if isinstance(key, str) else self.dsem[key[1]]
        self.engs[e].wait_ge(sem, val)
        self.seen[e][key] = val

    def _deps(self, e, r, w):
        for b in r:
            assert id(b) not in self.pendset or e == "pe", b.name
            self._wait(e, b.w)
            if b.excl:
                for k, v in list(b.r.items()):
                    if k != e:
                        self._wait(e, (k, v))
        for b in w:
            assert id(b) not in self.pendset or e == "pe", b.name
            self._wait(e, b.w)
            for k, v in list(b.r.items()):
                self._wait(e, (k, v))

    def _commit(self, ev, r, w):
        for b in w:
            b.w = ev
            b.r = {}
        for b in r:
            if all(b is not x for x in w):
                b.r[ev[0]] = ev[1]

    def op(self, e, fn, r=(), w=(), sig=True):
        self._deps(e, r, w)
        inst = fn(self.engs[e])
        self.ninst += 1
        if e == "pe" and not sig:
            for b in r:
                self.pend.append((b, 0))
                self.pendset.add(id(b))
            for b in w:
                self.pend.append((b, 1))
                self.pendset.add(id(b))
            return inst
        self.cnt[e] += 1
        inst.then_inc(self.sem[e], 1)
        ev = (e, self.cnt[e])
        rr = list(r)
        ww = list(w)
        if e == "pe":
            for b, k in self.pend:
                (ww if k else rr).append(b)
            self.pend = []
            self.pendset = set()
        self._commit(ev, rr, ww)
        return inst

    def dma(self, q, out_ap, in_ap, r=(), w=()):
        lo, hi = self.drange[q]
        i = self.drr[q]
        self.drr[q] = lo + (i + 1 - lo) % (hi - lo)
        if self.dcnt[i] > 0:
            self._wait(q, (("d", i), self.dcnt[i]))
        self._deps(q, r, w)
        inst = self.engs[q].dma_start(out=out_ap, in_=in_ap)
        self.ninst += 1
        self.dcnt[i] += 16
        inst.then_inc(self.dsem[i], 16)
        self._commit((("d", i), self.dcnt[i]), list(r), list(w))
        return inst

    def finish(self, bufs):
        for b in bufs:
            self._wait("sp", b.w)


class T:
    def __init__(self, t, name):
        self.t = t
        self.b = Buf(name)

    def __getitem__(self, k):
        return self.t[k]


def build_program():
    rec = []
    _build(rec, True)
    return _build(rec, False)


def _build(order, recording):
    nc = bass.Bass("TRN2", target_bir_lowering=False)
    es = ExitStack()
    with es:
        def din(name, shape):
            return nc.dram_tensor(name, list(shape), F32, kind="ExternalInput").ap()

        def dout(name, shape):
            return nc.dram_tensor(name, list(shape), F32, kind="ExternalOutput").ap()

        xp = din("xp", [SEQ, D])
        xs = din("xs", [NS, D])
        wch = din("wch", [DEPTH, NCH, 128, 4096])
        ident_d = din("ident", [128, 128])
        swap_d = din("swapm", [128, 128])
        masks_d = din("masks", [2, 128, 512])
        ak_d = din("ak", [2, 256])
        aq_d = din("aq", [2, 1024])
        colp_d = din("colp", [DEPTH, 128, 3 * 32])
        rowp_d = din("rowp", [DEPTH, 3, 128, 2048])
        bpad_d = din("bpad", [DEPTH, 2, 32, 128, 64])
        cpad_d = din("cpad", [DEPTH, 32, 128, 128])
        rmask_d = din("rmask", [128, 8])
        dsk_d = din("dsk", [DEPTH, 128, 4])
        glub_d = din("glub", [DEPTH, 128, 4])
        sgn_d = din("sgn", [DEPTH, 2, 128, 512])
        sgw_d = din("sgw", [DEPTH, 4, 128, 128])
        triu_d = din("triu", [128, 128])
        sgb_d = din("sgb", [DEPTH, 128, 512])
        lng_d = din("lng", [DEPTH, 2, 128, 1024])
        snk_d = din("snk", [DEPTH, 128, 8])

        ckT_d = din("ckT", [DEPTH, 128, 32, 128])
        cvd_d = din("cvd", [DEPTH, 128, 32, 128])
        h0_d = din("h0", [DEPTH, 128, 32, NS])
        h0s_d = din("h0s", [DEPTH, 128, 32, NS])
        aqs_d = din("aqs", [2, 8])
        hsel_d = din("hsel", [128, 2])
        hselT_d = din("hselT", [2, 128])
        signc_d = din("signc", [128, 1])
        w00_d = din("w00", [DEPTH, NS, 4])
        b00_d = din("b00", [DEPTH, 128, 4])
        esks_d = din("esks", [DEPTH, 128, 4])
        ysm = dout("ysm", [NS, D])
        kvs = dout("kvs", [DEPTH, NS, 256])
        hss = dout("hss", [DEPTH, 128, 32, NS])
        vns = dout("vns", [DEPTH, NS, 512])
        xs1 = nc.dram_tensor("xs1", [NS, D], F32).ap()
        yp = dout("yp", [SEQ, D])
        kvw = dout("kvw", [DEPTH, 128, 256])
        hst_o = dout("hst", [DEPTH, 128, 32])

        x1 = nc.dram_tensor("x1", [SEQ, D], F32).ap()
        wbf = nc.dram_tensor("wbf", [DEPTH, NCH, 128, 4096], BF16).ap()

        S = Sched(nc, es)

        def sb(name, shape, dt=F32):
            return T(es.enter_context(nc.sbuf_tensor("s_" + name, list(shape), dt)), name)

        ident = sb("ident", [128, 128])
        swapm = sb("swapm", [128, 128])
        masks = sb("masks", [128, 2, 128], BF16)
        ak = sb("ak", [2, 256], BF16)
        aq = sb("aq", [2, 1024], BF16)
        ones = sb("ones", [128, 128], BF16)
        rmask = sb("rmask", [128, 8])
        xtok = [sb("xtok%d" % i, [128, D]) for i in range(2 * NBK)]
        xT = sb("xT", [128, 8, TT], BF16)
        wbuf = [sb("wbuf%d" % i, [128, 4096], BF16) for i in range(3)]
        qT = sb("qT", [128, 4, TT], BF16)
        kT = sb("kT", [128, 2, TT + 128], BF16)
        vtok = sb("vtok", [128, NBK + 1, 2, 128], BF16)
        sza = sb("sza", [128, 4, TT], BF16)
        szb = sb("szb", [128, 4, TT], BF16)
        uT = sb("uT", [128, 4, TT], BF16)
        ucz = sb("ucz", [128, 4, TT], BF16)
        vn = sb("vn", [128, 4, BW], BF16)
        ys = sb("ys", [128, 12, TT], BF16)
        ypre = sb("ypre", [128, 4, TT])
        gtmp = sb("gtmp", [128, 4, TT])
        gtmp2 = sb("gtmp2", [128, 4, TT], BF16)
        sig = sb("sig", [128, 4, TT], BF16)
        mtmp = [sb("mtmp%d" % i, [128, 512]) for i in range(2)]
        mrgT = sb("mrgT", [128, 8, TT], BF16)
        pT = [[sb("pT%d%d" % (kv, pt), [128, 512], BF16) for pt in range(2)] for kv in range(2)]
        rden = [sb("rden%d" % kv, [128, 512]) for kv in range(2)]
        otmp = [sb("otmp%d" % kv, [128, 512], BF16) for kv in range(2)]
        lnst = sb("lnst", [128, 12])
        lnmv = sb("lnmv", [128, 4])
        lntmp = sb("lntmp", [128, 512])
        lnout = [sb("lnout%d" % i, [128, D]) for i in range(2)]
        colp = sb("colp", [128, 96])
        cdt = sb("cdt", [128, 32])
        crr = sb("crr", [128, 32])
        cth = sb("cth", [128, 32])
        ctmp = sb("ctmp", [128, 32])
        cc1 = sb("cc1", [128, 32])
        ctmp2 = sb("ctmp2", [128, 32])
        citmp = sb("citmp", [128, 32], mybir.dt.int32)
        cs1 = sb("cs1", [128, 32])
        tabc = sb("tabc", [128, 32, TT], BF16)
        tabs = sb("tabs", [128, 32, TT], BF16)
        bbw = sb("bbw", [128, 32, 128], BF16)
        bbs = sb("bbs", [128, 32, 128], BF16)
        cpw = sb("cpw", [128, 32, 128], BF16)
        dsk = sb("dsk", [128, 4])
        glub = sb("glub", [128, 4])
        sgn = sb("sgn", [128, 2, 512])
        sgw = sb("sgw", [128, 4, 128], BF16)
        triu = sb("triu", [128, 128])
        sgb = sb("sgb", [128, 512])
        lng = sb("lng", [128, 2, 1024])
        esk = sb("esk", [128, 8])
        sigB = sb("sigB", [128, 8, TT], BF16)
        hst = sb("hst", [128, 32])
        s5a = [sb("s5a%d" % i, [128, TT]) for i in range(2)]
        s5b = [sb("s5b%d" % i, [128, TT]) for i in range(2)]
        s5v = [sb("s5v%d" % i, [128, TT]) for i in range(2)]
        s5g = [sb("s5g%d" % i, [128, TT]) for i in range(2)]
        s5c = [sb("s5c%d" % i, [128, TT]) for i in range(3)]
        s5d = [sb("s5d%d" % i, [128, TT]) for i in range(2)]
        s5hb = [sb("s5hb%d" % i, [128, TT], BF16) for i in range(2)]

        aqs = sb("aqs", [2, 8], BF16)
        hsel = sb("hsel", [128, 2], BF16)
        hselT = sb("hselT", [2, 128], BF16)
        signc = sb("signc", [128, 1])
        w00 = sb("w00", [NS, 4])
        w00I = sb("w00I", [NS, 4, NS], BF16)
        b00 = sb("b00", [128, 4])
        esks = sb("esks", [128, 4])
        car = sb("car", [128, 32])
        cais = sb("cais", [128, 32])
        ksd = sb("ksd", [128, 2, NS], BF16)
        vsd = sb("vsd", [128, 2, NS])
        prod = sb("prod", [128, 4, NS], BF16)
        pself = sb("pself", [2, 64], BF16)
        pbs = sb("pbs", [128, 4, NS])
        st1 = sb("st1", [128, NS])
        st2 = sb("st2", [128, NS])

        class AV:
            def __init__(self, ap, b):
                self.ap = ap
                self.b = b

            def __getitem__(self, k):
                return self.ap[k]

        ck = AV(tabc[:, :, 0:128], tabc.b)
        kvout = AV(lnout[1][:, 0:256], lnout[1].b)
        kvs_sb = AV(rden[0][0:NS, 0:256], rden[0].b)
        vnS = AV(otmp[0][0:NS, :], otmp[0].b)
        cv = AV(tabs[:, :, 0:128], tabs.b)
        h0 = AV(mtmp[0][:, 0:32 * NS].rearrange("p (g n) -> p g n", n=NS), mtmp[0].b)
        h0s = AV(mtmp[1][:, 0:32 * NS].rearrange("p (g n) -> p g n", n=NS), mtmp[1].b)
        hbs = AV(pT[0][0][:, 0:32 * NS].rearrange("p (g n) -> p g n", n=NS), pT[0][0].b)
        pSb = AV(pT[0][1][:, 0:128], pT[0][1].b)
        sdn = AV(rden[0][:, 0:128], rden[0].b)
        sso = AV(rden[1][:, 0:128], rden[1].b)

        psum = [T(es.enter_context(nc.psum_tensor("ps%d" % i, [128, 512], F32)), "ps%d" % i) for i in range(8)]
        for p_ in psum:
            p_.b.excl = True
        pools = {"A": [0, 1, 7], "B": [2, 3], "C": [4, 6], "Y": [5]}
        prr = {k: 0 for k in pools}

        def ps_alloc(pool):
            if pool == "D":
                pool = "A"
            i = prr[pool]
            prr[pool] = (i + 1) % len(pools[pool])
            return psum[pools[pool][i]]

        rr = {}

        def rot(lst, key):
            i = rr.get(key, 0)
            rr[key] = (i + 1) % len(lst)
            return lst[i]

        wbf_b = [[Buf("wbf%d_%d" % (l, c)) for c in range(NCH)] for l in range(DEPTH)]
        for l in range(DEPTH):
            for c in range(NCH):
                S.dma("pool", wbf[l, c], wch[l, c], w=[wbf_b[l][c]])

        S.dma("sp", ident[:], ident_d, w=[ident.b])
        S.dma("sp", swapm[:], swap_d, w=[swapm.b])
        S.dma("pool", masks[:], masks_d.rearrange("a p c -> p a c")[:, :, 0:128], w=[masks.b])
        S.dma("pool", ak[:], ak_d, w=[ak.b])
        S.dma("pool", aq[:], aq_d, w=[aq.b])
        S.dma("sp", rmask[:], rmask_d, w=[rmask.b])
        S.dma("sp", triu[:], triu_d, w=[triu.b])
        S.op("dve", lambda e: e.memset(ones[:], 1.0), w=[ones.b])
        S.dma("pool", aqs[:], aqs_d, w=[aqs.b])
        S.dma("pool", hsel[:], hsel_d, w=[hsel.b])
        S.dma("pool", hselT[:], hselT_d, w=[hselT.b])
        S.dma("sp", signc[:], signc_d, w=[signc.b])

        def nk_of(c):
            return 4 if (c == C_GLU or C_R <= c < C_R + 6) else 8

        wst = {"issued": 0, "used": 0, "hold": False}

        def w_issue():
            i = wst["issued"]
            if i >= len(order):
                return
            l, c, nk = order[i]
            wb = wbuf[i % 3]
            S.dma("sp", wb[:, 0:nk * 512], wbf[l, c, :, 0:nk * 512], r=[wbf_b[l][c]], w=[wb.b])
            wst["issued"] = i + 1

        def w_next(l, c):
            i = wst["used"]
            if recording:
                order.append((l, c, nk_of(c)))
            assert order[i][0] == l and order[i][1] == c, (order[i], l, c)
            while wst["issued"] < min(i + (1 if recording else 2), len(order)):
                w_issue()
            wst["used"] = i + 1
            wst["hold"] = True
            return wbuf[i % 3]

        def w_done():
            wst["hold"] = False
            if recording:
                return
            while wst["issued"] < min(wst["used"] + 2, len(order)):
                w_issue()

        def evac(eng, fn, r, w):
            S.op(eng, fn, r=r, w=w)

        def mm_fm(ps, wb, j, nk, rhs_fn, rbufs, n):
            for kb in range(nk):
                S.op("pe", lambda e, kb=kb: e.matmul(ps[:, 0:n], lhsT=wb[:, kb * 512 + j * 128: kb * 512 + (j + 1) * 128],
                                                     rhs=rhs_fn(kb), start=(kb == 0), stop=(kb == nk - 1)),
                     r=[wb.b] + rbufs, w=[ps.b], sig=(kb == nk - 1))

        def v2(t, width):
            ap = t[:]
            if len(ap.shape) == 3:
                ap = ap.rearrange("p a b -> p (a b)")
            return ap[:, 0:width]

        def setup_layer(l):
            S.dma("sp", colp[:], colp_d[l], w=[colp.b])
            S.dma("pool", cpw[:], cpad_d[l].rearrange("g p m -> p g m"), w=[cpw.b])
            S.dma("sp", dsk[:], dsk_d[l], w=[dsk.b])
            S.dma("sp", glub[:], glub_d[l], w=[glub.b])
            S.dma("sp", sgn[:], sgn_d[l].rearrange("a p c -> p a c"), w=[sgn.b])
            sgw32 = v2(ypre, 512).rearrange("p (g t) -> p g t", g=4)
            S.dma("sp", sgw32, sgw_d[l].rearrange("g s t -> s g t"), w=[ypre.b])
            S.dma("sp", sgb[:], sgb_d[l], w=[sgb.b])
            S.dma("sp", lng[:], lng_d[l].rearrange("a p c -> p a c"), w=[lng.b])
            S.dma("sp", esk[:], snk_d[l], w=[esk.b])
            for g4 in range(4):
                S.op("dve", lambda e, g4=g4: e.tensor_tensor(out=sgw[:, g4, :], in0=sgw32[:, g4, :], in1=triu[:], op=ALU.mult),
                     r=[ypre.b, triu.b], w=[sgw.b])
            S.op("act", lambda e: e.activation(out=esk[:], in_=esk[:], func=AF.Exp), r=[esk.b], w=[esk.b])
            S.op("dve", lambda e: e.tensor_scalar(out=cpw[64:128], in0=cpw[64:128], scalar1=-1.0, scalar2=None, op0=ALU.mult),
                 r=[cpw.b], w=[cpw.b])
            S.op("dve", lambda e: e.memset(hst[:], 0.0), w=[hst.b] + hstb)

            lr_c, li_c, ld_c = colp[:, 0:32], colp[:, 32:64], colp[:, 64:96]
            S.op("act", lambda e: e.activation(out=cdt[:], in_=ld_c, func=AF.Exp), r=[colp.b], w=[cdt.b])
            S.op("dve", lambda e: e.tensor_tensor(out=ctmp[:], in0=lr_c, in1=cdt[:], op=ALU.mult), r=[colp.b, cdt.b], w=[ctmp.b])
            S.op("act", lambda e: e.activation(out=crr[:], in_=ctmp[:], func=AF.Exp), r=[ctmp.b], w=[crr.b])
            S.op("dve", lambda e: e.tensor_tensor(out=cth[:], in0=li_c, in1=cdt[:], op=ALU.mult), r=[colp.b, cdt.b], w=[cth.b])

            def sincos(out_s, out_c, th, thb, tmp, tb, m, mb, itmp, ib, sb_, cb):
                I32 = mybir.dt.int32
                for dst, db, shift in ((out_s, sb_, 0.0), (out_c, cb, 0.5 * math.pi)):
                    S.op("dve", lambda e, shift=shift: e.tensor_scalar(out=tmp, in0=th, scalar1=shift, scalar2=1.0 / (2 * math.pi),
                                                                       op0=ALU.add, op1=ALU.mult), r=[thb], w=[tb])
                    S.op("dve", lambda e: e.tensor_copy(out=itmp, in_=tmp), r=[tb], w=[ib])
                    S.op("dve", lambda e: e.tensor_copy(out=tmp, in_=itmp), r=[ib], w=[tb])
                    S.op("dve", lambda e: e.scalar_tensor_tensor(out=tmp, in0=tmp, scalar=-2 * math.pi, in1=th, op0=ALU.mult, op1=ALU.add),
                         r=[tb, thb], w=[tb])
                    S.op("dve", lambda e, shift=shift: e.tensor_scalar(out=tmp, in0=tmp, scalar1=shift, scalar2=None, op0=ALU.add),
                         r=[tb], w=[tb])
                    S.op("dve", lambda e: e.tensor_scalar(out=m, in0=tmp, scalar1=math.pi, scalar2=-2 * math.pi, op0=ALU.is_gt, op1=ALU.mult),
                         r=[tb], w=[mb])
                    S.op("dve", lambda e: e.tensor_tensor(out=tmp, in0=tmp, in1=m, op=ALU.add), r=[tb, mb], w=[tb])
                    S.op("dve", lambda e: e.tensor_scalar(out=m, in0=tmp, scalar1=-math.pi, scalar2=2 * math.pi, op0=ALU.is_lt, op1=ALU.mult),
                         r=[tb], w=[mb])
                    S.op("dve", lambda e: e.tensor_tensor(out=tmp, in0=tmp, in1=m, op=ALU.add), r=[tb, mb], w=[tb])
                    S.op("dve", lambda e: e.tensor_scalar(out=tmp, in0=tmp, scalar1=math.pi, scalar2=-math.pi, op0=ALU.min, op1=ALU.max),
                         r=[tb], w=[tb])
                    S.op("act", lambda e, dst=dst: e.activation(out=dst, in_=tmp, func=AF.Sin), r=[tb], w=[db])

            sincos(cs1[:], cc1[:], cth[:], cth.b, ctmp[:], ctmp.b, ctmp2[:], ctmp2.b, citmp[:], citmp.b, cs1.b, cc1.b)
            S.op("dve", lambda e: e.tensor_tensor(out=car[:], in0=crr[:], in1=cc1[:], op=ALU.mult), r=[crr.b, cc1.b], w=[car.b])
            S.op("dve", lambda e: e.tensor_tensor(out=cais[:], in0=crr[:], in1=cs1[:], op=ALU.mult), r=[crr.b, cs1.b], w=[cais.b])
            S.op("dve", lambda e: e.tensor_scalar(out=cais[:], in0=cais[:], scalar1=signc[:, 0:1], scalar2=None, op0=ALU.mult),
                 r=[cais.b, signc.b], w=[cais.b])
            S.dma("sp", w00[:], w00_d[l], w=[w00.b])
            S.dma("sp", b00[:], b00_d[l], w=[b00.b])
            S.dma("sp", esks[:], esks_d[l], w=[esks.b])
            S.op("act", lambda e: e.activation(out=esks[:], in_=esks[:], func=AF.Exp), r=[esks.b], w=[esks.b])
            for g4 in range(4):
                S.op("dve", lambda e, g4=g4: e.tensor_scalar(out=w00I[:, g4, :], in0=ident[0:NS, 0:NS], scalar1=w00[:, g4:g4 + 1],
                                                             scalar2=None, op0=ALU.mult), r=[ident.b, w00.b], w=[w00I.b])
            for qq in range(8):
                gs = slice(qq * 4, (qq + 1) * 4)
                ccos = v2(ypre, 4 * TT).rearrange("p (g t) -> p g t", g=4)
                csin = v2(gtmp, 4 * TT).rearrange("p (g t) -> p g t", g=4)
                ctm = v2(mtmp[0], 2 * TT).rearrange("p (g t) -> p g t", g=4)
                ctm2 = v2(mtmp[1], 2 * TT).rearrange("p (g t) -> p g t", g=4)
                S.op("dve", lambda e: e.tensor_copy(out=ccos[:, :, 0], in_=cc1[:, gs]), r=[cc1.b], w=[ypre.b])
                S.op("dve", lambda e: e.tensor_copy(out=csin[:, :, 0], in_=cs1[:, gs]), r=[cs1.b], w=[gtmp.b])
                n = 1
                while n < TT:
                    cn = ccos[:, :, n - 1:n].to_broadcast([128, 4, n])
                    sn = csin[:, :, n - 1:n].to_broadcast([128, 4, n])
                    c0 = ccos[:, :, 0:n]
                    s0 = csin[:, :, 0:n]
                    t1 = ctm[:, :, 0:n]
                    t2 = ctm2[:, :, 0:n]
                    S.op("dve", lambda e, c0=c0, cn=cn, t1=t1: e.tensor_tensor(out=t1, in0=c0, in1=cn, op=ALU.mult), r=[ypre.b], w=[mtmp[0].b])
                    S.op("dve", lambda e, s0=s0, sn=sn, t2=t2: e.tensor_tensor(out=t2, in0=s0, in1=sn, op=ALU.mult), r=[gtmp.b], w=[mtmp[1].b])
                    S.op("dve", lambda e, t1=t1, t2=t2, n=n: e.tensor_tensor(out=ccos[:, :, n:2 * n], in0=t1, in1=t2, op=ALU.subtract),
                         r=[mtmp[0].b, mtmp[1].b], w=[ypre.b])
                    S.op("dve", lambda e, c0=c0, sn=sn, t1=t1: e.tensor_tensor(out=t1, in0=c0, in1=sn, op=ALU.mult), r=[ypre.b, gtmp.b], w=[mtmp[0].b])
                    S.op("dve", lambda e, s0=s0, cn=cn, t2=t2: e.tensor_tensor(out=t2, in0=s0, in1=cn, op=ALU.mult), r=[ypre.b, gtmp.b], w=[mtmp[1].b])
                    S.op("dve", lambda e, t1=t1, t2=t2, n=n: e.tensor_tensor(out=csin[:, :, n:2 * n], in0=t1, in1=t2, op=ALU.add),
                         r=[mtmp[0].b, mtmp[1].b], w=[gtmp.b])
                    n *= 2
                S.op("act", lambda e: e.copy(out=tabc[:, gs, :], in_=ccos), r=[ypre.b], w=[tabc.b])
                S.op("act", lambda e: e.copy(out=tabs[:, gs, :], in_=csin), r=[gtmp.b], w=[tabs.b])

            tl = [T(None, "x")] * 0
            hold = [(xtok[0], 0), (xtok[0], 512), (xtok[1], 0), (xtok[1], 512), (lnout[0], 0), (lnout[0], 512),
                    (lnout[1], 0), (lnout[1], 512)]

            class V:
                def __init__(self, t, off):
                    self.ap = t[:, off:off + 512]
                    self.b = t.b

            for qq in range(4):
                gs = slice(qq * 8, (qq + 1) * 8)
                cs_q = slice(qq * 512, (qq + 1) * 512)
                lr, li, ld, dt_, mg, th, sn_, cs_ = [V(t, o) for t, o in hold]
                for i, dst in enumerate((lr, li, ld)):
                    S.dma("sp", dst.ap, rowp_d[l, i][:, cs_q], w=[dst.b])
                bld = v2(xtok[2], 1024).rearrange("k (a g p) -> k a g p", a=2, g=8)
                for a in range(2):
                    S.dma("sp", bld[:, a], bpad_d[l, a, qq * 8:(qq + 1) * 8].rearrange("g k p -> k g p"), w=[xtok[2].b])

                def tt(out, a, b_, op):
                    S.op("dve", lambda e: e.tensor_tensor(out=out.ap, in0=a.ap, in1=b_.ap, op=op), r=[a.b, b_.b], w=[out.b])

                S.op("act", lambda e: e.activation(out=dt_.ap, in_=ld.ap, func=AF.Exp), r=[ld.b], w=[dt_.b])
                tt(mg, lr, dt_, ALU.mult)
                S.op("act", lambda e: e.activation(out=mg.ap, in_=mg.ap, func=AF.Exp), r=[mg.b], w=[mg.b])
                tt(th, li, dt_, ALU.mult)
                sincos(sn_.ap, cs_.ap, th.ap, th.b, ld.ap, ld.b, v2(gtmp, 512), gtmp.b, v2(ypre, 512).bitcast(mybir.dt.int32), ypre.b, sn_.b, cs_.b)
                tt(cs_, cs_, mg, ALU.mult)
                S.op("dve", lambda e: e.tensor_scalar(out=cs_.ap, in0=cs_.ap, scalar1=-1.0, scalar2=None, op0=ALU.add), r=[cs_.b], w=[cs_.b])
                tt(sn_, sn_, mg, ALU.mult)
                tt(mg, lr, lr, ALU.mult)
                tt(th, li, li, ALU.mult)
                tt(mg, mg, th, ALU.add)
                S.op("dve", lambda e: e.reciprocal(out=mg.ap, in_=mg.ap), r=[mg.b], w=[mg.b])
                tt(dt_, cs_, lr, ALU.mult)
                tt(ld, sn_, li, ALU.mult)
                tt(dt_, dt_, ld, ALU.add)
                tt(dt_, dt_, mg, ALU.mult)
                tt(th, sn_, lr, ALU.mult)
                tt(ld, cs_, li, ALU.mult)
                tt(th, th, ld, ALU.subtract)
                tt(th, th, mg, ALU.mult)
                crv = dt_.ap.rearrange("k (g p) -> k g p", p=64)
                civ = th.ap.rearrange("k (g p) -> k g p", p=64)
                t1 = sn_.ap.rearrange("k (g p) -> k g p", p=64)
                t2 = cs_.ap.rearrange("k (g p) -> k g p", p=64)
                bre = bld[:, 0]
                bim = bld[:, 1]
                S.op("dve", lambda e: e.tensor_tensor(out=t1, in0=crv, in1=bre, op=ALU.mult), r=[dt_.b, xtok[2].b], w=[sn_.b])
                S.op("dve", lambda e: e.tensor_tensor(out=t2, in0=civ, in1=bim, op=ALU.mult), r=[th.b, xtok[2].b], w=[cs_.b])
                S.op("dve", lambda e: e.tensor_tensor(out=bbw[:, gs, 0:64], in0=t1, in1=t2, op=ALU.subtract), r=[sn_.b, cs_.b], w=[bbw.b])
                S.op("dve", lambda e: e.tensor_tensor(out=bbs[:, gs, 64:128], in0=t2, in1=t1, op=ALU.subtract), r=[sn_.b, cs_.b], w=[bbs.b])
                S.op("dve", lambda e: e.tensor_tensor(out=t1, in0=crv, in1=bim, op=ALU.mult), r=[dt_.b, xtok[2].b], w=[sn_.b])
                S.op("dve", lambda e: e.tensor_tensor(out=t2, in0=civ, in1=bre, op=ALU.mult), r=[th.b, xtok[2].b], w=[cs_.b])
                S.op("dve", lambda e: e.tensor_tensor(out=bbw[:, gs, 64:128], in0=t1, in1=t2, op=ALU.add), r=[sn_.b, cs_.b], w=[bbw.b])
                S.op("dve", lambda e: e.tensor_tensor(out=bbs[:, gs, 0:64], in0=t1, in1=t2, op=ALU.add), r=[sn_.b, cs_.b], w=[bbs.b])

        hstb = [Buf("hst%d" % g) for g in range(32)]
        ysb = [Buf("ys_a"), Buf("ys_b"), Buf("ys_c")]

        def s5_pipe(n):
            cols = slice(0, n)
            st = {}
            psy_cur = {}
            for it in range(32 + 7):
                g = it - 7
                if 0 <= g < 32:
                    fb, gl = g // 8, g % 8
                    if gl == 0:
                        psy_cur["p"] = ps_alloc("Y")
                    psy = psy_cur["p"]
                    hb = st[g]["hb"]
                    S.op("pe", lambda e, g=g, gl=gl, psy=psy, hb=hb: e.matmul(psy[:, 0:n], lhsT=cpw[:, g, :], rhs=hb[:, 0:n],
                                                                               start=(gl == 0), stop=(gl == 7)),
                         r=[cpw.b, hb.b], w=[psy.b], sig=True)
                    if gl == 7:
                        S.op("dve", lambda e, fb=fb, psy=psy: e.scalar_tensor_tensor(out=ypre[:, fb, cols], in0=uT[:, fb, cols],
                                                                                    scalar=dsk[:, fb:fb + 1], in1=psy[:, 0:n],
                                                                                    op0=ALU.mult, op1=ALU.add),
                             r=[uT.b, dsk.b, psy.b], w=[ypre.b])
                    del st[g]
                g = it - 6
                if 0 <= g < 32:
                    c, d = st[g]["c"], st[g]["d"]
                    hb = rot(s5hb, "hb")
                    st[g]["hb"] = hb
                    S.op("pool", lambda e, c=c, d=d, hb=hb: e.tensor_tensor(out=hb[:, 0:n], in0=c[:, 0:n], in1=d[:, 0:n], op=ALU.add),
                         r=[c.b, d.b], w=[hb.b])
                    S.op("pool", lambda e, c=c, d=d, g=g: e.tensor_tensor(out=hst[:, g:g + 1], in0=c[:, n - 1:n], in1=d[:, n - 1:n], op=ALU.add),
                         r=[c.b, d.b], w=[hstb[g]])
                g = it - 5
                if 0 <= g < 32:
                    psw = st[g]["psw"]
                    d = rot(s5d, "d")
                    st[g]["d"] = d
                    S.op("dve", lambda e, g=g, psw=psw, d=d: e.tensor_tensor(out=d[:, 0:n], in0=psw[:, 0:n], in1=tabs[:, g, 0:n], op=ALU.mult),
                         r=[psw.b, tabs.b], w=[d.b])
                g = it - 4
                if 0 <= g < 32:
                    gg = st[g]["gg"]
                    psw = ps_alloc("C")
                    st[g]["psw"] = psw
                    S.op("pe", lambda e, gg=gg, psw=psw: e.matmul(psw[:, 0:n], lhsT=swapm[:], rhs=gg[:, 0:n], start=True, stop=True),
                         r=[swapm.b, gg.b], w=[psw.b])
                    c = rot(s5c, "c")
                    st[g]["c"] = c
                    S.op("dve", lambda e, g=g, gg=gg, c=c: e.tensor_tensor(out=c[:, 0:n], in0=gg[:, 0:n], in1=tabc[:, g, 0:n], op=ALU.mult),
                         r=[gg.b, tabc.b], w=[c.b])
                g = it - 3
                if 0 <= g < 32:
                    v = st[g]["v"]
                    gg = rot(s5g, "g")
                    st[g]["gg"] = gg
                    S.op("dve", lambda e, g=g, v=v, gg=gg: e.tensor_tensor_scan(out=gg[:, 0:n], data0=crr[:, g:g + 1].to_broadcast([128, n]),
                                                                               data1=v[:, 0:n], initial=hst[:, g:g + 1], op0=ALU.mult, op1=ALU.add),
                         r=[crr.b, v.b, hstb[g]], w=[gg.b])
                g = it - 2
                if 0 <= g < 32:
                    a, b_ = st[g]["a"], st[g]["b"]
                    v = rot(s5v, "v")
                    st[g]["v"] = v
                    S.op("pool", lambda e, a=a, b_=b_, v=v: e.tensor_tensor(out=v[:, 0:n], in0=a[:, 0:n], in1=b_[:, 0:n], op=ALU.add),
                         r=[a.b, b_.b], w=[v.b])
                g = it - 1
                if 0 <= g < 32:
                    pv = st[g]["pv"]
                    a = rot(s5a, "a")
                    b_ = rot(s5b, "b")
                    st[g]["a"], st[g]["b"] = a, b_
                    S.op("dve", lambda e, g=g, pv=pv, a=a: e.tensor_tensor(out=a[:, 0:n], in0=pv[:, 0:n], in1=tabc[:, g, 0:n], op=ALU.mult),
                         r=[pv.b, tabc.b], w=[a.b])
                    S.op("dve", lambda e, g=g, pv=pv, b_=b_: e.tensor_tensor(out=b_[:, 0:n], in0=pv[:, TT:TT + n], in1=tabs[:, g, 0:n], op=ALU.mult),
                         r=[pv.b, tabs.b], w=[b_.b])
                g = it
                if 0 <= g < 32:
                    fb = g // 8
                    pv = ps_alloc("B")
                    st[g] = {"pv": pv}
                    S.op("pe", lambda e, g=g, fb=fb, pv=pv: e.matmul(pv[:, 0:n], lhsT=bbw[:, g, :], rhs=uT[:, fb, cols], start=True, stop=True),
                         r=[bbw.b, uT.b], w=[pv.b], sig=False)
                    S.op("pe", lambda e, g=g, fb=fb, pv=pv: e.matmul(pv[:, TT:TT + n], lhsT=bbs[:, g, :], rhs=uT[:, fb, cols], start=True, stop=True),
                         r=[bbs.b, uT.b], w=[pv.b])
                yield

        def gelu_glu_gen(l, n):
            yv = ypre[:, :, 0:n]
            g1 = gtmp[:, :, 0:n]
            S.op("dve", lambda e: e.tensor_tensor(out=g1, in0=yv, in1=yv, op=ALU.mult), r=[ypre.b], w=[gtmp.b])
            S.op("dve", lambda e: e.tensor_scalar(out=g1, in0=g1, scalar1=GK, scalar2=1.0, op0=ALU.mult, op1=ALU.add), r=[gtmp.b], w=[gtmp.b])
            yield
            S.op("pool", lambda e: e.tensor_tensor(out=g1, in0=g1, in1=yv, op=ALU.mult), r=[gtmp.b, ypre.b], w=[gtmp.b])
            yield
            S.op("act", lambda e: e.activation(out=g1, in_=g1, func=AF.Sigmoid, scale=GS), r=[gtmp.b], w=[gtmp.b])
            yield
            S.op("dve", lambda e: e.tensor_tensor(out=yv, in0=yv, in1=g1, op=ALU.mult), r=[gtmp.b, ypre.b], w=[ypre.b])
            yield
            S.op("act", lambda e: e.copy(out=gtmp2[:, :, 0:n], in_=yv), r=[ypre.b], w=[gtmp2.b])
            yield
            wb = w_next(l, C_GLU)
            for j in range(4):
                ps = ps_alloc("A")
                mm_fm(ps, wb, j, 4, lambda kb: gtmp2[:, kb, 0:n], [gtmp2.b], n)
                S.op("act", lambda e, j=j, ps=ps: e.activation(out=gtmp[:, j, 0:n], in_=ps[:, 0:n], func=AF.Sigmoid, bias=glub[:, j:j + 1]),
                     r=[ps.b, glub.b], w=[gtmp.b])
            w_done()
            yield
            S.op("dve", lambda e: e.tensor_tensor(out=yv, in0=yv, in1=g1, op=ALU.mult), r=[gtmp.b, ypre.b], w=[ypre.b])
            yield
            S.op("pool", lambda e: e.tensor_tensor(out=ys[:, 4:8, 0:n], in0=yv, in1=szb[:, :, 0:n], op=ALU.mult),
                 r=[ypre.b, szb.b], w=[ysb[1]])

        def gelu_glu(l, n):
            for _ in gelu_glu_gen(l, n):
                pass

        def layernorm_rows(src, nrow, width, gam, bet, dst, tmp=None, eng2="pool"):
            if tmp is None:
                tmp = (lntmp.b, lntmp[0:nrow, 0:width])
            nchk = width // 512
            for i in range(nchk):
                S.op("dve", lambda e, i=i: e.bn_stats(out=lnst[0:nrow, i * 6:(i + 1) * 6], in_=src[1][:, i * 512:(i + 1) * 512]),
                     r=[src[0]], w=[lnst.b])
            S.op("dve", lambda e: e.bn_aggr(out=lnmv[0:nrow, 0:2], in_=lnst[0:nrow, 0:6 * nchk]), r=[lnst.b], w=[lnmv.b])
            S.op("dve", lambda e: e.tensor_scalar(out=lnmv[0:nrow, 2:3], in0=lnmv[0:nrow, 1:2], scalar1=LN_EPS, scalar2=None,
                                                  op0=ALU.add), r=[lnmv.b], w=[lnmv.b])
            S.op("act", lambda e: e.sqrt(out=lnmv[0:nrow, 2:3], in_=lnmv[0:nrow, 2:3]), r=[lnmv.b], w=[lnmv.b])
            S.op("dve", lambda e: e.reciprocal(out=lnmv[0:nrow, 2:3], in_=lnmv[0:nrow, 2:3]), r=[lnmv.b], w=[lnmv.b])
            S.op("dve", lambda e: e.tensor_scalar(out=tmp[1], in0=src[1], scalar1=lnmv[0:nrow, 0:1],
                                                  scalar2=lnmv[0:nrow, 2:3], op0=ALU.subtract, op1=ALU.mult),
                 r=[src[0], lnmv.b], w=[tmp[0]])
            S.op("dve", lambda e: e.tensor_tensor(out=tmp[1], in0=tmp[1], in1=gam[1], op=ALU.mult),
                 r=[tmp[0], gam[0]], w=[tmp[0]])
            S.op(eng2, lambda e: e.tensor_tensor(out=dst[1], in0=tmp[1], in1=bet[1], op=ALU.add),
                 r=[tmp[0], bet[0]], w=[dst[0]])

        def xload(l, ti):
            t0 = ti * TT
            src = xp if l == 0 else x1
            for b in range(NBK):
                xt = xtok[(ti % 2) * NBK + b]
                S.dma("sp", xt[:], src[t0 + b * 128: t0 + (b + 1) * 128, :], r=([x1_b] if l == 1 else []), w=[xt.b])

        def front(l, ti):
            n = TT
            xts = [xtok[(ti % 2) * NBK + b] for b in range(NBK)]
            for kb in range(8):
                ps = ps_alloc("A")
                for b in range(NBK):
                    S.op("pe", lambda e, b=b, ps=ps, kb=kb: e.transpose(out=ps[:, b * 128:(b + 1) * 128],
                                                                       in_=xts[b][:, kb * 128:(kb + 1) * 128], identity=ident[:]),
                         r=[xts[b].b, ident.b], w=[ps.b], sig=(b == NBK - 1))
                S.op("act" if kb % 2 else "dve",
                     (lambda e, ps=ps, kb=kb: e.copy(out=xT[:, kb, :], in_=ps[:, 0:TT])) if kb % 2 else
                     (lambda e, ps=ps, kb=kb: e.tensor_copy(out=xT[:, kb, :], in_=ps[:, 0:TT])),
                     r=[ps.b], w=[xT.b])
            xrhs = lambda kb: xT[:, kb, :]
            wb = w_next(l, C_UB)
            for j in range(4):
                ps = ps_alloc("A")
                mm_fm(ps, wb, j, 8, xrhs, [xT.b], n)
                S.op("act", lambda e, j=j, ps=ps: e.copy(out=uT[:, j, :], in_=ps[:, 0:TT]), r=[ps.b], w=[uT.b])
            w_done()

        def tail(l, ti):
            t0 = ti * TT
            dst = x1 if l == 0 else yp
            for _ in gelu_glu_gen(l, TT):
                yield
            yield
            for _ in readout_br(l, TT, (1,), False, pre=True):
                yield
            blocks = [(xtok[(ti % 2) * NBK + b], 128, dst[t0 + b * 128: t0 + (b + 1) * 128, :]) for b in range(NBK)]
            for _ in wo_ln(l, TT, blocks):
                yield
            if ti + 2 < (NTILE if DBG_TILES is None else DBG_TILES):
                xload(l, ti + 2)

        def rest_gen(l, ti):
            n = TT
            xrhs = lambda kb: xT[:, kb, :]

            wb = w_next(l, C_Q)
            for j in range(4):
                ps = ps_alloc("A")
                mm_fm(ps, wb, j, 8, xrhs, [xT.b], n)
                S.op("act", lambda e, j=j, ps=ps: e.activation(out=qT[:, j, :], in_=ps[:, 0:TT], func=AF.Copy, scale=0.125),
                     r=[ps.b], w=[qT.b])
                yield
            w_done()
            yield
            wb = w_next(l, C_KD)
            for j in range(2):
                ps = ps_alloc("A")
                mm_fm(ps, wb, j, 8, xrhs, [xT.b], n)
                S.op("act", lambda e, j=j, ps=ps: e.copy(out=kT[:, j, 128:128 + TT], in_=ps[:, 0:TT]), r=[ps.b], w=[kT.b])
                yield
            w_done()
            yield
            wb = w_next(l, C_KV)
            for b in range(NBK):
                ps = ps_alloc("A")
                for kb in range(8):
                    S.op("pe", lambda e, kb=kb, b=b, ps=ps: e.matmul(ps[:, 0:256], lhsT=xT[:, kb, b * 128:(b + 1) * 128],
                                                                    rhs=wb[:, kb * 512: kb * 512 + 256], start=(kb == 0), stop=(kb == 7)),
                         r=[wb.b, xT.b], w=[ps.b], sig=(kb == 7))
                vsrc = ps[:, 128:256].rearrange("p (k d) -> p k d", k=2)
                S.op("act", lambda e, b=b, vsrc=vsrc: e.copy(out=vtok[:, b + 1, :, 0:64], in_=vsrc), r=[ps.b], w=[vtok.b])
                S.op("act", lambda e, b=b, vsrc=vsrc: e.copy(out=vtok[:, b + 1, :, 64:128], in_=vsrc), r=[ps.b], w=[vtok.b])
                if ti == (NTILE if DBG_TILES is None else DBG_TILES) - 1 and b == NBK - 1:
                    S.op("act", lambda e, ps=ps: e.copy(out=kvout[:], in_=ps[:, 0:256]), r=[ps.b], w=[kvout.b])
                    S.dma("pool", kvw[l], kvout[:], r=[kvout.b])
                yield
            w_done()
            yield
            for cid, dstT, fn in ((C_ZA, sza, AF.Silu), (C_UC, ucz, None)):
                wb = w_next(l, cid)
                for j in range(4):
                    ps = ps_alloc("A")
                    mm_fm(ps, wb, j, 8, xrhs, [xT.b], n)
                    if fn is None:
                        S.op("act", lambda e, j=j, ps=ps, dstT=dstT: e.copy(out=dstT[:, j, :], in_=ps[:, 0:TT]), r=[ps.b], w=[dstT.b])
                    else:
                        S.op("act", lambda e, j=j, ps=ps, dstT=dstT, fn=fn: e.activation(out=dstT[:, j, :], in_=ps[:, 0:TT], func=fn),
                             r=[ps.b], w=[dstT.b])
                    yield
                w_done()
                yield
            wb = w_next(l, C_VC)
            for b in range(NBK):
                ps = ps_alloc("A")
                for kb in range(8):
                    S.op("pe", lambda e, kb=kb, b=b, ps=ps: e.matmul(ps[:, :], lhsT=xT[:, kb, b * 128:(b + 1) * 128],
                                                                    rhs=wb[:, kb * 512:(kb + 1) * 512], start=(kb == 0), stop=(kb == 7)),
                         r=[wb.b, xT.b], w=[ps.b], sig=(kb == 7))
                layernorm_rows((ps.b, ps[:, :]), 128, 512, (sgn.b, sgn[:, 0, :]), (sgn.b, sgn[:, 1, :]), (vn.b, vn[:, b, :]))
                yield
            w_done()
            yield
            wb = w_next(l, C_ZC)
            for j in range(4):
                ps = ps_alloc("A")
                mm_fm(ps, wb, j, 8, xrhs, [xT.b], n)
                S.op("act", lambda e, j=j, ps=ps: e.activation(out=sig[:, j, :], in_=ps[:, 0:TT], func=AF.Silu), r=[ps.b], w=[sig.b])
                yield
            w_done()
            S.op("pool", lambda e: e.tensor_tensor(out=ucz[:], in0=ucz[:], in1=sig[:], op=ALU.mult), r=[ucz.b, sig.b], w=[ucz.b])
            yield

            for b in range(NBK):
                for g4 in range(4):
                    ps = ps_alloc("A")
                    S.op("pe", lambda e, b=b, g4=g4, ps=ps: e.matmul(ps[:, 0:128], lhsT=vn[:, b, g4 * 128:(g4 + 1) * 128],
                                                                    rhs=sgw[:, g4, :], start=True, stop=True),
                         r=[vn.b, sgw.b], w=[ps.b])
                    mt = rot(mtmp, "mt")
                    S.op("dve", lambda e, g4=g4, ps=ps, mt=mt: e.tensor_tensor(out=mt[:, 0:128], in0=ps[:, 0:128],
                                                                              in1=sgb[:, g4 * 128:(g4 + 1) * 128], op=ALU.add),
                         r=[ps.b, sgb.b], w=[mt.b])
                    S.op("pool", lambda e, b=b, g4=g4, mt=mt: e.tensor_tensor(out=ys[:, 8 + g4, b * 128:(b + 1) * 128], in0=mt[:, 0:128],
                                                                             in1=ucz[:, g4, b * 128:(b + 1) * 128], op=ALU.mult),
                         r=[mt.b, ucz.b], w=[ysb[2]])
                yield

            for b in range(NBK):
                first = (ti == 0 and b == 0)
                parts = [1] if first else [0, 1]
                for kv in range(2):
                    for pt in parts:
                        pss = ps_alloc("D")
                        kc = b * 128 + pt * 128
                        for par in range(2):
                            rows = slice(par * 64, (par + 1) * 64)
                            o = pss[:, par * 256:(par + 1) * 256].rearrange("p (j q) -> p j q", j=2)
                            S.op("pe", lambda e, o=o, rows=rows, kc=kc, kv=kv, b=b: e.matmul(
                                o, lhsT=kT[rows, kv, kc:kc + 128], rhs=qT[rows, 2 * kv:2 * kv + 2, b * 128:(b + 1) * 128],
                                start=True, stop=False), r=[kT.b, qT.b], w=[pss.b], sig=False)
                            aqv = aq[:, (kv * 2 + par) * 256:(kv * 2 + par + 1) * 256].rearrange("p (j q) -> p j q", j=2)
                            S.op("pe", lambda e, o=o, aqv=aqv, pt=pt: e.matmul(o, lhsT=ak[:, pt * 128:(pt + 1) * 128], rhs=aqv,
                                                                               start=False, stop=True),
                                 r=[ak.b, aq.b], w=[pss.b], sig=(par == 1))
                        S.op("act", lambda e, pss=pss, pt=pt, kv=kv: e.activation(out=pT[kv][pt][:], in_=pss[:, :], func=AF.Exp),
                             r=[pss.b], w=[pT[kv][pt].b])
                        S.op("pool", lambda e, pt=pt, kv=kv: e.tensor_tensor(
                            out=pT[kv][pt][:].rearrange("p (h q) -> p h q", h=4), in0=pT[kv][pt][:].rearrange("p (h q) -> p h q", h=4),
                            in1=masks[:, pt, :].unsqueeze(1).to_broadcast([128, 4, 128]), op=ALU.mult),
                             r=[pT[kv][pt].b, masks.b], w=[pT[kv][pt].b])
                    yield
                    psd = ps_alloc("D")
                    for i, pt in enumerate(parts):
                        S.op("pe", lambda e, psd=psd, pt=pt, i=i, kv=kv: e.matmul(psd[:, :], lhsT=ones[:], rhs=pT[kv][pt][:],
                                                                                 start=(i == 0), stop=(i == len(parts) - 1)),
                             r=[ones.b, pT[kv][pt].b], w=[psd.b], sig=(i == len(parts) - 1))
                    S.op("dve", lambda e, psd=psd, kv=kv: e.tensor_tensor(
                        out=rden[kv][:].rearrange("p (h q) -> p h q", h=4), in0=psd[:, :].rearrange("p (h q) -> p h q", h=4),
                        in1=esk[:, kv * 4:(kv + 1) * 4].unsqueeze(2).to_broadcast([128, 4, 128]), op=ALU.add),
                         r=[psd.b, esk.b], w=[rden[kv].b])
                    S.op("act", lambda e, kv=kv: e.activation(out=rden[kv][:], in_=rden[kv][:], func=AF.Ln), r=[rden[kv].b], w=[rden[kv].b])
                    S.op("act", lambda e, kv=kv: e.activation(out=rden[kv][:], in_=rden[kv][:], func=AF.Exp, scale=-1.0),
                         r=[rden[kv].b], w=[rden[kv].b])
                    pso = ps_alloc("D")
                    for hh in range(4):
                        for i, pt in enumerate(parts):
                            S.op("pe", lambda e, pso=pso, hh=hh, pt=pt, i=i, kv=kv, b=b: e.matmul(
                                pso[:, hh * 128:(hh + 1) * 128], lhsT=vtok[:, b + pt, kv, :], rhs=pT[kv][pt][:, hh * 128:(hh + 1) * 128],
                                start=(i == 0), stop=(i == len(parts) - 1)),
                                 r=[vtok.b, pT[kv][pt].b], w=[pso.b], sig=(hh == 3 and i == len(parts) - 1))
                    S.op("dve", lambda e, pso=pso, kv=kv: e.tensor_tensor(out=otmp[kv][:], in0=pso[:, :], in1=rden[kv][:], op=ALU.mult),
                         r=[pso.b, rden[kv].b], w=[otmp[kv].b])
                    for par in range(2):
                        rows = slice(par * 64, (par + 1) * 64)
                        ov = otmp[kv][rows, par * 256:(par + 1) * 256].rearrange("p (j q) -> p j q", j=2)
                        S.op("pool", lambda e, rows=rows, ov=ov, kv=kv, b=b: e.tensor_tensor(
                            out=ys[rows, 2 * kv:2 * kv + 2, b * 128:(b + 1) * 128], in0=ov,
                            in1=sza[rows, 2 * kv:2 * kv + 2, b * 128:(b + 1) * 128], op=ALU.mult),
                             r=[otmp[kv].b, sza.b], w=[ysb[0]])
                    yield
            S.op("act", lambda e: e.copy(out=kT[:, :, 0:128], in_=kT[:, :, TT:TT + 128]), r=[kT.b], w=[kT.b])
            S.op("pool", lambda e: e.tensor_copy(out=vtok[:, 0], in_=vtok[:, NBK]), r=[vtok.b], w=[vtok.b])
            for _ in readout_br(l, n, (0, 2), True):
                yield
            for _ in gates_b(l, n):
                yield
            wb = w_next(l, C_ZB)
            for j in range(4):
                ps = ps_alloc("A")
                mm_fm(ps, wb, j, 8, xrhs, [xT.b], n)
                S.op("act", lambda e, j=j, ps=ps: e.activation(out=szb[:, j, :], in_=ps[:, 0:TT], func=AF.Silu), r=[ps.b], w=[szb.b])
            w_done()
            yield


        def gates_b(l, n):
            xrhs = lambda kb: xT[:, kb, 0:n]
            for hf in range(2):
                wb = w_next(l, C_G + hf * 3 + 1)
                for j in range(4):
                    ps = ps_alloc("A")
                    mm_fm(ps, wb, j, 8, xrhs, [xT.b], n)
                    S.op("act", lambda e, j=j, ps=ps: e.activation(out=sigB[:, hf * 4 + j, 0:n], in_=ps[:, 0:n], func=AF.Sigmoid),
                         r=[ps.b], w=[sigB.b])
                    yield
                w_done()
                yield

        def readout_br(l, n, branches, init_first, pre=False):
            xrhs = lambda kb: xT[:, kb, 0:n]
            for hf in range(2):
                for bi, b in enumerate(branches):
                    if pre:
                        gsrc, gbuf, goff = sigB, sigB.b, hf * 4
                    else:
                        gsrc, gbuf, goff = sig, sig.b, 0
                        wb = w_next(l, C_G + hf * 3 + b)
                        for j in range(4):
                            ps = ps_alloc("A")
                            mm_fm(ps, wb, j, 8, xrhs, [xT.b], n)
                            S.op("act", lambda e, j=j, ps=ps: e.activation(out=sig[:, j, 0:n], in_=ps[:, 0:n], func=AF.Sigmoid),
                                 r=[ps.b], w=[sig.b])
                            yield
                        w_done()
                        yield
                    wb = w_next(l, C_R + hf * 3 + b)
                    for j in range(4):
                        ps = ps_alloc("A")
                        mm_fm(ps, wb, j, 4, lambda kb, b=b: ys[:, b * 4 + kb, 0:n], [ysb[b]], n)
                        if init_first and bi == 0:
                            S.op("dve", lambda e, j=j, ps=ps, gsrc=gsrc, goff=goff: e.tensor_tensor(
                                out=mrgT[:, hf * 4 + j, 0:n], in0=ps[:, 0:n], in1=gsrc[:, goff + j, 0:n], op=ALU.mult),
                                 r=[ps.b, gbuf], w=[mrgT.b])
                        else:
                            mt = rot(mtmp, "mt")
                            S.op("dve", lambda e, j=j, ps=ps, mt=mt, gsrc=gsrc, goff=goff: e.tensor_tensor(
                                out=mt[:, 0:n], in0=ps[:, 0:n], in1=gsrc[:, goff + j, 0:n], op=ALU.mult),
                                 r=[ps.b, gbuf], w=[mt.b])
                            S.op("pool", lambda e, j=j, mt=mt: e.tensor_tensor(out=mrgT[:, hf * 4 + j, 0:n], in0=mrgT[:, hf * 4 + j, 0:n],
                                                                              in1=mt[:, 0:n], op=ALU.add),
                                 r=[mrgT.b, mt.b], w=[mrgT.b])
                        yield
                    w_done()
                    yield

        def drain(gen):
            for _ in gen:
                pass

        def wo_ln(l, n, blocks):
            for hf in range(2):
                wb = w_next(l, C_O + hf)
                c0 = 0
                for (xt, nrow, _) in blocks:
                    ps = ps_alloc("A")
                    for kb in range(8):
                        S.op("pe", lambda e, kb=kb, ps=ps, c0=c0, nrow=nrow: e.matmul(
                            ps[0:nrow, :], lhsT=mrgT[:, kb, c0:c0 + nrow], rhs=wb[:, kb * 512:(kb + 1) * 512],
                            start=(kb == 0), stop=(kb == 7)), r=[wb.b, mrgT.b], w=[ps.b], sig=(kb == 7))
                    S.op("dve", lambda e, ps=ps, xt=xt, nrow=nrow: e.scalar_tensor_tensor(
                        out=xt[0:nrow, hf * 512:(hf + 1) * 512], in0=xt[0:nrow, hf * 512:(hf + 1) * 512], scalar=ALPHA,
                        in1=ps[0:nrow, :], op0=ALU.mult, op1=ALU.add), r=[ps.b, xt.b], w=[xt.b])
                    c0 += nrow
                    yield
                w_done()
                yield
            for (xt, nrow, dst_ap) in blocks:
                lo = rot(lnout, "lo")
                layernorm_rows((xt.b, xt[0:nrow, :]), nrow, D, (lng.b, lng[0:nrow, 0, :]), (lng.b, lng[0:nrow, 1, :]),
                               (lo.b, lo[0:nrow, :]), tmp=(xt.b, xt[0:nrow, :]))
                S.dma("pool", dst_ap, lo[0:nrow, :], r=[lo.b], w=[x1_b])
                yield

        def sample_tile(l):
            n = NS
            src = xs if l == 0 else xs1
            dst = xs1 if l == 0 else ysm
            xt = xtok[0]
            S.dma("sp", xt[0:n, :], src, r=([x1_b] if l == 1 else []), w=[xt.b])
            for kb in range(8):
                ps = ps_alloc("A")
                S.op("pe", lambda e, ps=ps, kb=kb: e.transpose(out=ps[:, 0:n], in_=xt[0:n, kb * 128:(kb + 1) * 128], identity=ident[0:n, 0:n]),
                     r=[xt.b, ident.b], w=[ps.b])
                S.op("dve", lambda e, ps=ps, kb=kb: e.tensor_copy(out=xT[:, kb, 0:n], in_=ps[:, 0:n]), r=[ps.b], w=[xT.b])
            xrhs = lambda kb: xT[:, kb, 0:n]
            wb = w_next(l, C_UB)
            for j in range(4):
                ps = ps_alloc("A")
                mm_fm(ps, wb, j, 8, xrhs, [xT.b], n)
                S.op("dve", lambda e, j=j, ps=ps: e.tensor_copy(out=uT[:, j, 0:n], in_=ps[:, 0:n]), r=[ps.b], w=[uT.b])
            w_done()
            wb = w_next(l, C_Q)
            for j in range(4):
                ps = ps_alloc("A")
                mm_fm(ps, wb, j, 8, xrhs, [xT.b], n)
                S.op("act", lambda e, j=j, ps=ps: e.activation(out=qT[:, j, 0:n], in_=ps[:, 0:n], func=AF.Copy, scale=0.125), r=[ps.b], w=[qT.b])
            w_done()
            wb = w_next(l, C_KD)
            for j in range(4):
                ps = ps_alloc("A")
                mm_fm(ps, wb, j, 8, xrhs, [xT.b], n)
                if j < 2:
                    S.op("dve", lambda e, j=j, ps=ps: e.tensor_copy(out=ksd[:, j, :], in_=ps[:, 0:n]), r=[ps.b], w=[ksd.b])
                else:
                    S.op("dve", lambda e, j=j, ps=ps: e.tensor_copy(out=vsd[:, j - 2, :], in_=ps[:, 0:n]), r=[ps.b], w=[vsd.b])
            w_done()
            wb = w_next(l, C_KV)
            ps = ps_alloc("A")
            for kb in range(8):
                S.op("pe", lambda e, kb=kb, ps=ps: e.matmul(ps[0:n, 0:256], lhsT=xT[:, kb, 0:n], rhs=wb[:, kb * 512: kb * 512 + 256],
                                                           start=(kb == 0), stop=(kb == 7)), r=[wb.b, xT.b], w=[ps.b], sig=(kb == 7))
            S.op("dve", lambda e, ps=ps: e.tensor_copy(out=kvs_sb[:], in_=ps[0:n, 0:256]), r=[ps.b], w=[kvs_sb.b])
            S.dma("pool", kvs[l], kvs_sb[:], r=[kvs_sb.b])
            w_done()
            for cid, dstT, fn in ((C_ZA, sza, AF.Silu), (C_ZB, szb, AF.Silu), (C_UC, ucz, None)):
                wb = w_next(l, cid)
                for j in range(4):
                    ps = ps_alloc("A")
                    mm_fm(ps, wb, j, 8, xrhs, [xT.b], n)
                    if fn is None:
                        S.op("dve", lambda e, j=j, ps=ps, dstT=dstT: e.tensor_copy(out=dstT[:, j, 0:n], in_=ps[:, 0:n]), r=[ps.b], w=[dstT.b])
                    else:
                        S.op("act", lambda e, j=j, ps=ps, dstT=dstT, fn=fn: e.activation(out=dstT[:, j, 0:n], in_=ps[:, 0:n], func=fn),
                             r=[ps.b], w=[dstT.b])
                w_done()
            wb = w_next(l, C_VC)
            ps = ps_alloc("A")
            for kb in range(8):
                S.op("pe", lambda e, kb=kb, ps=ps: e.matmul(ps[0:n, :], lhsT=xT[:, kb, 0:n], rhs=wb[:, kb * 512:(kb + 1) * 512],
                                                           start=(kb == 0), stop=(kb == 7)), r=[wb.b, xT.b], w=[ps.b], sig=(kb == 7))
            lo = rot(lnout, "lo")
            layernorm_rows((ps.b, ps[0:n, :]), n, 512, (sgn.b, sgn[0:n, 0, :]), (sgn.b, sgn[0:n, 1, :]), (lo.b, lo[0:n, 0:512]))
            S.dma("pool", vns[l], lo[0:n, 0:512], r=[lo.b])
            S.op("act", lambda e: e.copy(out=vnS[:], in_=lo[0:n, 0:512]), r=[lo.b], w=[vnS.b])
            w_done()
            wb = w_next(l, C_ZC)
            for j in range(4):
                ps = ps_alloc("A")
                mm_fm(ps, wb, j, 8, xrhs, [xT.b], n)
                S.op("act", lambda e, j=j, ps=ps: e.activation(out=gtmp2[:, j, 0:n], in_=ps[:, 0:n], func=AF.Silu), r=[ps.b], w=[gtmp2.b])
            w_done()
            S.op("pool", lambda e: e.tensor_tensor(out=ucz[:, :, 0:n], in0=ucz[:, :, 0:n], in1=gtmp2[:, :, 0:n], op=ALU.mult),
                 r=[ucz.b, gtmp2.b], w=[ucz.b])

            S.dma("sp", h0[:], h0_d[l], w=[h0.b])
            S.dma("sp", h0s[:], h0s_d[l], w=[h0s.b])
            psv = ps_alloc("B")
            for g in range(32):
                S.op("pe", lambda e, g=g: e.matmul(psv[:, g * n:(g + 1) * n], lhsT=bbw[:, g, :], rhs=uT[:, g // 8, 0:n], start=True, stop=True),
                     r=[bbw.b, uT.b], w=[psv.b], sig=(g == 31))
            S.op("dve", lambda e: e.tensor_tensor(out=h0[:], in0=h0[:], in1=car[:].unsqueeze(2).to_broadcast([128, 32, n]), op=ALU.mult),
                 r=[h0.b, car.b], w=[h0.b])
            S.op("dve", lambda e: e.tensor_tensor(out=h0s[:], in0=h0s[:], in1=cais[:].unsqueeze(2).to_broadcast([128, 32, n]), op=ALU.mult),
                 r=[h0s.b, cais.b], w=[h0s.b])
            S.op("dve", lambda e: e.tensor_tensor(out=h0[:], in0=h0[:], in1=h0s[:], op=ALU.add), r=[h0.b, h0s.b], w=[h0.b])
            S.op("dve", lambda e: e.tensor_tensor(out=h0[:], in0=h0[:], in1=psv[:, 0:32 * n].rearrange("p (g n) -> p g n", n=n), op=ALU.add),
                 r=[h0.b, psv.b], w=[h0.b])
            S.dma("pool", hss[l], h0[:], r=[h0.b])
            S.op("act", lambda e: e.copy(out=hbs[:], in_=h0[:]), r=[h0.b], w=[hbs.b])
            for fb in range(4):
                psy = ps_alloc("Y")
                for gl in range(8):
                    g = fb * 8 + gl
                    S.op("pe", lambda e, g=g, gl=gl, psy=psy: e.matmul(psy[:, 0:n], lhsT=cpw[:, g, :], rhs=hbs[:, g, :], start=(gl == 0), stop=(gl == 7)),
                         r=[cpw.b, hbs.b], w=[psy.b], sig=(gl == 7))
                S.op("dve", lambda e, fb=fb, psy=psy: e.scalar_tensor_tensor(out=ypre[:, fb, 0:n], in0=uT[:, fb, 0:n], scalar=dsk[:, fb:fb + 1],
                                                                            in1=psy[:, 0:n], op0=ALU.mult, op1=ALU.add),
                     r=[uT.b, dsk.b, psy.b], w=[ypre.b])

            for g4 in range(4):
                ps = ps_alloc("A")
                S.op("pe", lambda e, g4=g4, ps=ps: e.matmul(ps[:, 0:n], lhsT=vnS[:, g4 * 128:(g4 + 1) * 128], rhs=w00I[:, g4, :], start=True, stop=True),
                     r=[vnS.b, w00I.b], w=[ps.b])
                S.op("dve", lambda e, g4=g4, ps=ps: e.tensor_scalar(out=st1[:], in0=ps[:, 0:n], scalar1=b00[:, g4:g4 + 1], scalar2=None, op0=ALU.add),
                     r=[ps.b, b00.b], w=[st1.b])
                S.op("dve", lambda e, g4=g4: e.tensor_tensor(out=ys[:, 8 + g4, 0:n], in0=st1[:], in1=ucz[:, g4, 0:n], op=ALU.mult),
                     r=[st1.b, ucz.b], w=[ysb[2]])

            S.dma("pool", ck[:], ckT_d[l], w=[ck.b])
            S.dma("pool", cv[:], cvd_d[l], w=[cv.b])
            pss = ps_alloc("D")
            for nn in range(n):
                for kv in range(2):
                    for par in range(2):
                        rows = slice(par * 64, (par + 1) * 64)
                        c0 = (kv * 2 + par) * 32
                        o = pss[:, c0:c0 + 32].rearrange("p (j n) -> p j n", j=2)[:, :, nn]
                        last = (nn == n - 1 and kv == 1 and par == 1)
                        S.op("pe", lambda e, o=o, rows=rows, nn=nn, kv=kv: e.matmul(o, lhsT=ck[rows, nn * 2 + kv, :], rhs=qT[rows, 2 * kv:2 * kv + 2, nn],
                                                                                   start=True, stop=False), r=[ck.b, qT.b], w=[pss.b], sig=False)
                        S.op("pe", lambda e, o=o, c0=c0: e.matmul(o, lhsT=ak[:, 0:128], rhs=aqs[:, c0 // 16: c0 // 16 + 2], start=False, stop=True),
                             r=[ak.b, aqs.b], w=[pss.b], sig=last)
            S.op("act", lambda e: e.activation(out=pSb[:], in_=pss[:, 0:128], func=AF.Exp), r=[pss.b], w=[pSb.b])
            for j in range(4):
                S.op("dve", lambda e, j=j: e.tensor_tensor(out=prod[:, j, :], in0=qT[:, j, 0:n], in1=ksd[:, j // 2, :], op=ALU.mult),
                     r=[qT.b, ksd.b], w=[prod.b])
            ps1 = ps_alloc("A")
            S.op("pe", lambda e: e.matmul(ps1[0:2, 0:64], lhsT=hsel[:], rhs=prod[:].rearrange("p j n -> p (j n)"), start=True, stop=True),
                 r=[hsel.b, prod.b], w=[ps1.b])
            S.op("act", lambda e: e.activation(out=pself[:], in_=ps1[0:2, 0:64], func=AF.Exp), r=[ps1.b], w=[pself.b])
            ps2 = ps_alloc("A")
            S.op("pe", lambda e: e.matmul(ps2[:, 0:64], lhsT=hselT[:], rhs=pself[:], start=True, stop=True), r=[hselT.b, pself.b], w=[ps2.b])
            S.op("dve", lambda e: e.tensor_copy(out=pbs[:].rearrange("p j n -> p (j n)"), in_=ps2[:, 0:64]), r=[ps2.b], w=[pbs.b])
            psd = ps_alloc("A")
            S.op("pe", lambda e: e.matmul(psd[:, 0:128], lhsT=ones[:], rhs=pSb[:], start=True, stop=True), r=[ones.b, pSb.b], w=[psd.b])
            S.op("dve", lambda e: e.tensor_copy(out=sdn[:], in_=psd[:, 0:128]), r=[psd.b], w=[sdn.b])
            pso = ps_alloc("D")
            for nn in range(n):
                for kv in range(2):
                    o = pso[:, kv * 64:(kv + 1) * 64].rearrange("p (a n) -> p a n", n=n)[:, :, nn]
                    r_ = pSb[:, kv * 64:(kv + 1) * 64].rearrange("p (a n) -> p a n", n=n)[:, :, nn]
                    S.op("pe", lambda e, o=o, r_=r_, nn=nn, kv=kv: e.matmul(o, lhsT=cv[:, nn * 2 + kv, :], rhs=r_, start=True, stop=True),
                         r=[cv.b, pSb.b], w=[pso.b], sig=(nn == n - 1 and kv == 1))
            S.op("dve", lambda e: e.tensor_copy(out=sso[:], in_=pso[:, 0:128]), r=[pso.b], w=[sso.b])
            for par in range(2):
                rows = slice(par * 64, (par + 1) * 64)
                for kv in range(2):
                    for jj in range(2):
                        blk = 2 * kv + jj
                        c0 = ((kv * 2 + par) * 2 + jj) * n
                        a1 = st1[rows, :]
                        a2 = st2[rows, :]
                        S.op("dve", lambda e, a1=a1, rows=rows, blk=blk, kv=kv: e.tensor_tensor(out=a1, in0=pbs[rows, blk, :], in1=vsd[rows, kv, :], op=ALU.mult),
                             r=[pbs.b, vsd.b], w=[st1.b])
                        S.op("dve", lambda e, a1=a1, rows=rows, c0=c0: e.tensor_tensor(out=a1, in0=a1, in1=sso[rows, c0:c0 + n], op=ALU.add),
                             r=[st1.b, sso.b], w=[st1.b])
                        S.op("dve", lambda e, a2=a2, rows=rows, blk=blk, c0=c0: e.tensor_tensor(out=a2, in0=pbs[rows, blk, :], in1=sdn[rows, c0:c0 + n], op=ALU.add),
                             r=[pbs.b, sdn.b], w=[st2.b])
                        S.op("dve", lambda e, a2=a2, rows=rows, blk=blk: e.tensor_scalar(out=a2, in0=a2, scalar1=esks[rows, blk:blk + 1], scalar2=None, op0=ALU.add),
                             r=[st2.b, esks.b], w=[st2.b])
                        S.op("dve", lambda e, a2=a2: e.reciprocal(out=a2, in_=a2), r=[st2.b], w=[st2.b])
                        S.op("dve", lambda e, a1=a1, a2=a2: e.tensor_tensor(out=a1, in0=a1, in1=a2, op=ALU.mult), r=[st1.b, st2.b], w=[st1.b])
                        S.op("dve", lambda e, a1=a1, rows=rows, blk=blk: e.tensor_tensor(out=ys[rows, blk, 0:n], in0=a1, in1=sza[rows, blk, 0:n], op=ALU.mult),
                             r=[st1.b, sza.b], w=[ysb[0]])
            drain(readout_br(l, n, (0, 2), True))
            gelu_glu(l, n)
            drain(readout_br(l, n, (1,), False))
            drain(wo_ln(l, n, [(xt, n, dst)]))

        x1_b = Buf("x1")

        def roundrobin(gens):
            gens = list(gens)
            while gens:
                for g_ in list(gens):
                    try:
                        next(g_)
                        while wst["hold"]:
                            next(g_)
                    except StopIteration:
                        gens.remove(g_)

        for l in range(DEPTH):
            setup_layer(l)
            nt = NTILE if DBG_TILES is None else DBG_TILES
            xload(l, 0)
            if nt > 1:
                xload(l, 1)
            front(l, 0)
            prev_tail = None
            for ti in range(nt):
                gens = [s5_pipe(TT), rest_gen(l, ti)]
                if prev_tail is not None:
                    gens.append(prev_tail)
                roundrobin(gens)
                if ti + 1 < nt:
                    front(l, ti + 1)
                prev_tail = tail(l, ti)
            if prev_tail is not None:
                drain(prev_tail)
            S.dma("pool", hst_o[l], hst[:], r=[hst.b] + hstb)
            sample_tile(l)
        allb = [x1_b, hst.b, h0.b, kvs_sb.b] + [t.b for t in lnout] + ([] if DBG_NOKV else [kvout.b])
        S.finish(allb)
        for i in range(NDS):
            if S.dcnt[i] > 0:
                S._wait("sp", (("d", i), S.dcnt[i]))
        if not recording:
            print("instructions:", S.ninst, "sem counts:", S.cnt)
    return nc


_PROG = None


def _consts():
    ident = np.eye(128, dtype=np.float32)
    swapm = np.zeros((128, 128), np.float32)
    for m in range(64):
        swapm[64 + m, m] = -1.0
        swapm[m, 64 + m] = 1.0
    s = np.arange(128)[:, None]
    q = np.arange(128)[None, :]
    mprev = (s >= q).astype(np.float32)
    mcur = (s <= q).astype(np.float32)
    masks = np.stack([np.tile(mprev, (1, 4)), np.tile(mcur, (1, 4))]).astype(np.float32)
    ak = np.zeros((2, 256), np.float32)
    ak[0, :128] = np.arange(128) - 128
    ak[0, 128:] = np.arange(128)
    ak[1, :] = 1.0
    slopes = 2.0 ** (-(np.arange(1, 9)))
    aq = np.zeros((2, 2, 2, 2, 128), np.float32)
    for kv in range(2):
        for par in range(2):
            for jj in range(2):
                h = 4 * kv + 2 * jj + par
                aq[0, kv, par, jj, :] = slopes[h]
                aq[1, kv, par, jj, :] = -slopes[h] * np.arange(128)
    rmask = np.zeros((128, 8), np.float32)
    for gl in range(8):
        rmask[gl * 16:(gl + 1) * 16, gl] = 1.0
    triu = (s <= q).astype(np.float32)
    aqs = np.zeros((2, 8), np.float32)
    for kv in range(2):
        for par in range(2):
            for jj in range(2):
                aqs[0, (kv * 2 + par) * 2 + jj] = slopes[4 * kv + 2 * jj + par]
    hsel = np.zeros((128, 2), np.float32)
    hsel[0:64, 0] = 1.0
    hsel[64:128, 1] = 1.0
    signc = np.ones((128, 1), np.float32)
    signc[0:64] = -1.0
    return dict(ident=ident, swapm=swapm, masks=masks, ak=ak, aq=aq.reshape(2, 1024), rmask=rmask, triu=triu,
                aqs=aqs, hsel=hsel, hselT=np.ascontiguousarray(hsel.T), signc=signc)


def _layout_weights(w_in, w_read, w_o, glu_w):
    out = np.zeros((DEPTH, NCH, 128, 8, 512), np.float32)

    def put(l, c, mat, kb0=0):
        k = mat.shape[0] // 128
        out[l, c, :, kb0:kb0 + k, :mat.shape[1]] = mat.reshape(k, 128, mat.shape[1]).transpose(1, 0, 2)

    for l in range(DEPTH):
        W = w_in[l]
        put(l, C_Q, W[:, 0:512])
        k0, k1 = W[:, 512:576], W[:, 576:640]
        v0, v1 = W[:, 640:704], W[:, 704:768]
        put(l, C_KD, np.concatenate([k0, k0, k1, k1, v0, v0, v1, v1], axis=1))
        put(l, C_KV, W[:, 512:768])
        put(l, C_ZA, W[:, 768:1280])
        put(l, C_UB, W[:, 1280:1792])
        put(l, C_ZB, W[:, 1792:2304])
        put(l, C_UC, W[:, 2304:2816])
        put(l, C_VC, W[:, 2816:3328])
        put(l, C_ZC, W[:, 3328:3840])
        for hf in range(2):
            for b in range(3):
                c0 = 3840 + b * 1024 + hf * 512
                put(l, C_G + hf * 3 + b, W[:, c0:c0 + 512])
                put(l, C_R + hf * 3 + b, w_read[l, b][:, hf * 512:(hf + 1) * 512])
            put(l, C_O + hf, w_o[l][:, hf * 512:(hf + 1) * 512])
        put(l, C_GLU, glu_w[l])
    return out.reshape(DEPTH, NCH, 128, 4096)


def _layout_params(inp):
    f = np.float32
    lam_re, lam_im, log_dt = inp["ssm_lambda_re"], inp["ssm_lambda_im"], inp["ssm_log_dt"]
    colp = np.zeros((DEPTH, 128, 96), f)
    rowp = np.zeros((DEPTH, 3, 128, 2048), f)
    for l in range(DEPTH):
        colp[l, :, 0:32] = np.concatenate([lam_re[l].T, lam_re[l].T], 0)
        colp[l, :, 32:64] = np.concatenate([lam_im[l].T, lam_im[l].T], 0)
        colp[l, :, 64:96] = log_dt[l][None, :]
        rowp[l, 0] = lam_re[l].reshape(1, 2048)
        rowp[l, 1] = lam_im[l].reshape(1, 2048)
        rowp[l, 2] = np.repeat(log_dt[l], 64)[None, :]
    bpad = np.zeros((DEPTH, 2, 32, 128, 64), f)
    cpad = np.zeros((DEPTH, 32, 128, 128), f)
    for l in range(DEPTH):
        for g in range(32):
            gl = g % 8
            bpad[l, 0, g, gl * 16:(gl + 1) * 16, :] = inp["ssm_b_re"][l, g].T
            bpad[l, 1, g, gl * 16:(gl + 1) * 16, :] = inp["ssm_b_im"][l, g].T
            cpad[l, g, 0:64, gl * 16:(gl + 1) * 16] = inp["ssm_c_re"][l, g].T
            cpad[l, g, 64:128, gl * 16:(gl + 1) * 16] = inp["ssm_c_im"][l, g].T
    dsk = inp["ssm_d"].reshape(DEPTH, 4, 128).transpose(0, 2, 1).copy()
    glub = inp["glu_b"].reshape(DEPTH, 4, 128).transpose(0, 2, 1).copy()
    sgn = np.zeros((DEPTH, 2, 128, 512), f)
    sgn[:, 0] = inp["sgu_ln_g"][:, None, :]
    sgn[:, 1] = inp["sgu_ln_b"][:, None, :]
    sgw = inp["sgu_w"].transpose(0, 1, 3, 2).copy()
    sgb = np.broadcast_to(inp["sgu_b"].reshape(DEPTH, 1, 512), (DEPTH, 128, 512)).copy()
    lng = np.zeros((DEPTH, 2, 128, 1024), f)
    lng[:, 0] = inp["ln_g"][:, None, :]
    lng[:, 1] = inp["ln_b"][:, None, :]
    snk = np.zeros((DEPTH, 128, 2, 2, 2), f)
    for kv in range(2):
        for par in range(2):
            for jj in range(2):
                h = 4 * kv + 2 * jj + par
                snk[:, :, kv, par, jj] = inp["attn_sinks"][:, h][:, None]
    return dict(colp=colp, rowp=rowp, bpad=bpad, cpad=cpad, dsk=dsk, glub=glub, sgn=sgn, sgw=sgw, sgb=sgb, lng=lng,
                snk=snk.reshape(DEPTH, 128, 8))


def kernel(**inp):
    global _PROG
    inp = {k: np.asarray(v) for k, v in inp.items()}
    if _PROG is None:
        _PROG = build_program()
    nc = _PROG
    consts = _consts()
    wch = _layout_weights(inp["w_in"], inp["w_read"], inp["w_o"], inp["glu_w"])
    prm = _layout_params(inp)
    in_maps = []
    for c in range(8):
        m = dict(consts)
        m.update(prm)
        m["wch"] = wch
        m["xp"] = np.ascontiguousarray(inp["x_prompt"][c % 2])
        sl = slice(c * NS, (c + 1) * NS)
        m["xs"] = np.ascontiguousarray(inp["x_sample"][sl, 0, :])
        ckc = inp["cache_k_win"][:, sl]
        t = ckc.transpose(0, 4, 1, 3, 2).reshape(DEPTH, 64, 32, 128)
        m["ckT"] = np.ascontiguousarray(np.concatenate([t, t], axis=1))
        cvc = inp["cache_v_win"][:, sl]
        t = cvc.transpose(0, 2, 1, 3, 4).reshape(DEPTH, 128, 32, 64)
        m["cvd"] = np.ascontiguousarray(np.concatenate([t, t], axis=3))
        sr = inp["state_ssm_re"][:, sl].transpose(0, 3, 2, 1)
        si = inp["state_ssm_im"][:, sl].transpose(0, 3, 2, 1)
        m["h0"] = np.ascontiguousarray(np.concatenate([sr, si], axis=1))
        m["h0s"] = np.ascontiguousarray(np.concatenate([si, sr], axis=1))
        m["w00"] = np.ascontiguousarray(np.broadcast_to(inp["sgu_w"][:, None, :, 0, 0], (DEPTH, NS, 4)))
        m["b00"] = np.ascontiguousarray(np.broadcast_to(inp["sgu_b"][:, None, :, 0], (DEPTH, 128, 4)))
        es_ = np.zeros((DEPTH, 128, 4), np.float32)
        for blk in range(4):
            es_[:, 0:64, blk] = inp["attn_sinks"][:, 2 * blk][:, None]
            es_[:, 64:128, blk] = inp["attn_sinks"][:, 2 * blk + 1][:, None]
        m["esks"] = es_
        in_maps.append(m)
    res = run_bass_kernel_spmd(nc, in_maps, core_ids=list(range(8)))
    R = res.results
    y_prompt = np.stack([R[0]["yp"], R[1]["yp"]])
    kvw = np.stack([R[0]["kvw"], R[1]["kvw"]], axis=1)
    k_win = kvw[..., 0:128].reshape(DEPTH, 2, 128, 2, 64)
    v_win = kvw[..., 128:256].reshape(DEPTH, 2, 128, 2, 64)
    hs = np.stack([R[0]["hst"], R[1]["hst"]], axis=1)
    h_re = hs[:, :, 0:64, :].transpose(0, 1, 3, 2)
    h_im = hs[:, :, 64:128, :].transpose(0, 1, 3, 2)
    y_sample = np.concatenate([R[c]["ysm"] for c in range(8)], axis=0).reshape(128, 1, D)
    kvsm = np.concatenate([R[c]["kvs"] for c in range(8)], axis=1)
    k_s = kvsm[..., 0:128].reshape(DEPTH, 128, 1, 2, 64)
    v_s = kvsm[..., 128:256].reshape(DEPTH, 128, 1, 2, 64)
    hsm = np.concatenate([R[c]["hss"] for c in range(8)], axis=3)
    hs_re = hsm[:, 0:64].transpose(0, 3, 2, 1)
    hs_im = hsm[:, 64:128].transpose(0, 3, 2, 1)
    vn_s = np.concatenate([R[c]["vns"] for c in range(8)], axis=1).reshape(DEPTH, 128, 1, 512)
    f = np.float32
    outs = (y_prompt, y_sample, k_win, v_win, h_re, h_im, k_s, v_s, hs_re, hs_im, vn_s)
    return tuple(np.ascontiguousarray(o, dtype=f) for o in outs)
```

```python
import math
from contextlib import ExitStack

import numpy as np
import concourse.bass as bass
import concourse.mybir as mybir
from concourse.bass_utils import run_bass_kernel_spmd

F32 = mybir.dt.float32
BF16 = mybir.dt.bfloat16
AF = mybir.ActivationFunctionType
ALU = mybir.AluOpType

D = 1024
SEQ = 8192
DEPTH = 2
NS = 16
BW = 512
TT = 256
NBK = TT // 128
NTILE = SEQ // TT
NCH = 24
ALPHA = (2 * DEPTH) ** 0.25
LN_EPS = 1e-5
GK = 0.044715
GS = 2.0 * math.sqrt(2.0 / math.pi)
NDS = 24
DBG_TILES = None
DBG_NOKV = False
DBG_SKIPKV = False

C_Q, C_KD, C_KV, C_ZA, C_UB, C_ZB, C_UC, C_VC, C_ZC = range(9)
C_G = 9
C_R = 15
C_O = 21
C_GLU = 23


class Buf:
    __slots__ = ("name", "w", "r", "excl")

    def __init__(self, name, excl=False):
        self.name = name
        self.w = None
        self.r = {}
        self.excl = excl


class Sched:
    def __init__(self, nc, es):
        self.nc = nc
        self.engs = {"pe": nc.tensor, "dve": nc.vector, "act": nc.scalar, "pool": nc.gpsimd, "sp": nc.sync}
        self.sem = {k: es.enter_context(nc.semaphore("sem_" + k)) for k in ["pe", "dve", "act", "pool"]}
        self.cnt = {k: 0 for k in self.sem}
        self.seen = {e: {} for e in self.engs}
        self.dsem = [es.enter_context(nc.semaphore("dsem%d" % i)) for i in range(NDS)]
        self.dcnt = [0] * NDS
        self.drange = {"sp": (0, 16), "pool": (16, NDS)}
        self.drr = {"sp": 0, "pool": 16}
        self.pend = []
        self.pendset = set()
        self.ninst = 0

    def _wait(self, e, ev):
        if ev is None:
            return
        key, val = ev
        if key == e and e == "pe":
            return
        if self.seen[e].get(key, 0) >= val:
            return
        sem = self.sem[key] if isinstance(key, str) else self.dsem[key[1]]
        self.engs[e].wait_ge(sem, val)
        self.seen[e][key] = val

    def _deps(self, e, r, w):
        for b in r:
            assert id(b) not in self.pendset or e == "pe", b.name
            self._wait(e, b.w)
            if b.excl:
                for k, v in list(b.r.items()):
                    if k != e:
                        self._wait(e, (k, v))
        for b in w:
            assert id(b) not in self.pendset or e == "pe", b.name
            self._wait(e, b.w)
            for k, v in list(b.r.items()):
                self._wait(e, (k, v))

    def _commit(self, ev, r, w):
        for b in w:
            b.w = ev
            b.r = {}
        for b in r:
            if all(b is not x for x in w):
                b.r[ev[0]] = ev[1]

    def op(self, e, fn, r=(), w=(), sig=True):
        self._deps(e, r, w)
        inst = fn(self.engs[e])
        self.ninst += 1
        if e == "pe" and not sig:
            for b in r:
                self.pend.append((b, 0))
                self.pendset.add(id(b))
            for b in w:
                self.pend.append((b, 1))
                self.pendset.add(id(b))
            return inst
        self.cnt[e] += 1
        inst.then_inc(self.sem[e], 1)
        ev = (e, self.cnt[e])
        rr = list(r)
        ww = list(w)
        if e == "pe":
            for b, k in self.pend:
                (ww if k else rr).append(b)
            self.pend = []
            self.pendset = set()
        self._commit(ev, rr, ww)
        return inst

    def dma(self, q, out_ap, in_ap, r=(), w=()):
        lo, hi = self.drange[q]
        i = self.drr[q]
        self.drr[q] = lo + (i + 1 - lo) % (hi - lo)
        if self.dcnt[i] > 0:
            self._wait(q, (("d", i), self.dcnt[i]))
        self._deps(q, r, w)
        inst = self.engs[q].dma_start(out=out_ap, in_=in_ap)
        self.ninst += 1
        self.dcnt[i] += 16
        inst.then_inc(self.dsem[i], 16)
        self._commit((("d", i), self.dcnt[i]), list(r), list(w))
        return inst

    def finish(self, bufs):
        for b in bufs:
            self._wait("sp", b.w)


class T:
    def __init__(self, t, name):
        self.t = t
        self.b = Buf(name)

    def __getitem__(self, k):
        return self.t[k]


def build_program():
    rec = []
    _build(rec, True)
    return _build(rec, False)


def _build(order, recording):
    nc = bass.Bass("TRN2", target_bir_lowering=False)
    es = ExitStack()
    with es:
        def din(name, shape):
            return nc.dram_tensor(name, list(shape), F32, kind="ExternalInput").ap()

        def dout(name, shape):
            return nc.dram_tensor(name, list(shape), F32, kind="ExternalOutput").ap()

        xp = din("xp", [SEQ, D])
        xs = din("xs", [NS, D])
        wch = din("wch", [DEPTH, NCH, 128, 4096])
        ident_d = din("ident", [128, 128])
        swap_d = din("swapm", [128, 128])
        masks_d = din("masks", [2, 128, 512])
        ak_d = din("ak", [2, 256])
        aq_d = din("aq", [2, 1024])
        colp_d = din("colp", [DEPTH, 128, 3 * 32])
        rowp_d = din("rowp", [DEPTH, 3, 128, 2048])
        bpad_d = din("bpad", [DEPTH, 2, 32, 128, 64])
        cpad_d = din("cpad", [DEPTH, 32, 128, 128])
        rmask_d = din("rmask", [128, 8])
        dsk_d = din("dsk", [DEPTH, 128, 4])
        glub_d = din("glub", [DEPTH, 128, 4])
        sgn_d = din("sgn", [DEPTH, 2, 128, 512])
        sgw_d = din("sgw", [DEPTH, 4, 128, 128])
        triu_d = din("triu", [128, 128])
        sgb_d = din("sgb", [DEPTH, 128, 512])
        lng_d = din("lng", [DEPTH, 2, 128, 1024])
        snk_d = din("snk", [DEPTH, 128, 8])

        ckT_d = din("ckT", [DEPTH, 128, 32, 128])
        cvd_d = din("cvd", [DEPTH, 128, 32, 128])
        h0_d = din("h0", [DEPTH, 128, 32, NS])
        h0s_d = din("h0s", [DEPTH, 128, 32, NS])
        aqs_d = din("aqs", [2, 8])
        hsel_d = din("hsel", [128, 2])
        hselT_d = din("hselT", [2, 128])
        signc_d = din("signc", [128, 1])
        w00_d = din("w00", [DEPTH, NS, 4])
        b00_d = din("b00", [DEPTH, 128, 4])
        esks_d = din("esks", [DEPTH, 128, 4])
        ysm = dout("ysm", [NS, D])
        kvs = dout("kvs", [DEPTH, NS, 256])
        hss = dout("hss", [DEPTH, 128, 32, NS])
        vns = dout("vns", [DEPTH, NS, 512])
        xs1 = nc.dram_tensor("xs1", [NS, D], F32).ap()
        yp = dout("yp", [SEQ, D])
        kvw = dout("kvw", [DEPTH, 128, 256])
        hst_o = dout("hst", [DEPTH, 128, 32])

        x1 = nc.dram_tensor("x1", [SEQ, D], F32).ap()
        wbf = nc.dram_tensor("wbf", [DEPTH, NCH, 128, 4096], BF16).ap()

        S = Sched(nc, es)

        def sb(name, shape, dt=F32):
            return T(es.enter_context(nc.sbuf_tensor("s_" + name, list(shape), dt)), name)

        ident = sb("ident", [128, 128])
        swapm = sb("swapm", [128, 128])
        masks = sb("masks", [128, 2, 128], BF16)
        ak = sb("ak", [2, 256], BF16)
        aq = sb("aq", [2, 1024], BF16)
        ones = sb("ones", [128, 128], BF16)
        rmask = sb("rmask", [128, 8])
        xtok = [sb("xtok%d" % i, [128, D]) for i in range(2 * NBK)]
        xT = sb("xT", [128, 8, TT], BF16)
        wbuf = [sb("wbuf%d" % i, [128, 4096], BF16) for i in range(3)]
        qT = sb("qT", [128, 4, TT], BF16)
        kT = sb("kT", [128, 2, TT + 128], BF16)
        vtok = sb("vtok", [128, NBK + 1, 2, 128], BF16)
        sza = sb("sza", [128, 4, TT], BF16)
        szb = sb("szb", [128, 4, TT], BF16)
        uT = sb("uT", [128, 4, TT], BF16)
        ucz = sb("ucz", [128, 4, TT], BF16)
        vn = sb("vn", [128, 4, BW], BF16)
        ys = sb("ys", [128, 12, TT], BF16)
        ypre = sb("ypre", [128, 4, TT])
        gtmp = sb("gtmp", [128, 4, TT])
        gtmp2 = sb("gtmp2", [128, 4, TT], BF16)
        sig = sb("sig", [128, 4, TT], BF16)
        mtmp = [sb("mtmp%d" % i, [128, 512]) for i in range(2)]
        mrgT = sb("mrgT", [128, 8, TT], BF16)
        pT = [[sb("pT%d%d" % (kv, pt), [128, 512], BF16) for pt in range(2)] for kv in range(2)]
        rden = [sb("rden%d" % kv, [128, 512]) for kv in range(2)]
        otmp = [sb("otmp%d" % kv, [128, 512], BF16) for kv in range(2)]
        lnst = sb("lnst", [128, 12])
        lnmv = sb("lnmv", [128, 4])
        lntmp = sb("lntmp", [128, 512])
        lnout = [sb("lnout%d" % i, [128, D]) for i in range(2)]
        colp = sb("colp", [128, 96])
        cdt = sb("cdt", [128, 32])
        crr = sb("crr", [128, 32])
        cth = sb("cth", [128, 32])
        ctmp = sb("ctmp", [128, 32])
        cc1 = sb("cc1", [128, 32])
        ctmp2 = sb("ctmp2", [128, 32])
        citmp = sb("citmp", [128, 32], mybir.dt.int32)
        cs1 = sb("cs1", [128, 32])
        tabc = sb("tabc", [128, 32, TT], BF16)
        tabs = sb("tabs", [128, 32, TT], BF16)
        bbw = sb("bbw", [128, 32, 128], BF16)
        bbs = sb("bbs", [128, 32, 128], BF16)
        cpw = sb("cpw", [128, 32, 128], BF16)
        dsk = sb("dsk", [128, 4])
        glub = sb("glub", [128, 4])
        sgn = sb("sgn", [128, 2, 512])
        sgw = sb("sgw", [128, 4, 128], BF16)
        triu = sb("triu", [128, 128])
        sgb = sb("sgb", [128, 512])
        lng = sb("lng", [128, 2, 1024])
        esk = sb("esk", [128, 8])
        sigB = sb("sigB", [128, 8, TT], BF16)
        hst = sb("hst", [128, 32])
        s5a = [sb("s5a%d" % i, [128, TT]) for i in range(2)]
        s5b = [sb("s5b%d" % i, [128, TT]) for i in range(2)]
        s5v = [sb("s5v%d" % i, [128, TT]) for i in range(2)]
        s5g = [sb("s5g%d" % i, [128, TT]) for i in range(2)]
        s5c = [sb("s5c%d" % i, [128, TT]) for i in range(3)]
        s5d = [sb("s5d%d" % i, [128, TT]) for i in range(2)]
        s5hb = [sb("s5hb%d" % i, [128, TT], BF16) for i in range(2)]

        aqs = sb("aqs", [2, 8], BF16)
        hsel = sb("hsel", [128, 2], BF16)
        hselT = sb("hselT", [2, 128], BF16)
        signc = sb("signc", [128, 1])
        w00 = sb("w00", [NS, 4])
        w00I = sb("w00I", [NS, 4, NS], BF16)
        b00 = sb("b00", [128, 4])
        esks = sb("esks", [128, 4])
        car = sb("car", [128, 32])
        cais = sb("cais", [128, 32])
        ksd = sb("ksd", [128, 2, NS], BF16)
        vsd = sb("vsd", [128, 2, NS])
        prod = sb("prod", [128, 4, NS], BF16)
        pself = sb("pself", [2, 64], BF16)
        pbs = sb("pbs", [128, 4, NS])
        st1 = sb("st1", [128, NS])
        st2 = sb("st2", [128, NS])

        class AV:
            def __init__(self, ap, b):
                self.ap = ap
                self.b = b

            def __getitem__(self, k):
                return self.ap[k]

        ck = AV(tabc[:, :, 0:128], tabc.b)
        kvout = AV(lnout[1][:, 0:256], lnout[1].b)
        kvs_sb = AV(rden[0][0:NS, 0:256], rden[0].b)
        vnS = AV(otmp[0][0:NS, :], otmp[0].b)
        cv = AV(tabs[:, :, 0:128], tabs.b)
        h0 = AV(mtmp[0][:, 0:32 * NS].rearrange("p (g n) -> p g n", n=NS), mtmp[0].b)
        h0s = AV(mtmp[1][:, 0:32 * NS].rearrange("p (g n) -> p g n", n=NS), mtmp[1].b)
        hbs = AV(pT[0][0][:, 0:32 * NS].rearrange("p (g n) -> p g n", n=NS), pT[0][0].b)
        pSb = AV(pT[0][1][:, 0:128], pT[0][1].b)
        sdn = AV(rden[0][:, 0:128], rden[0].b)
        sso = AV(rden[1][:, 0:128], rden[1].b)

        psum = [T(es.enter_context(nc.psum_tensor("ps%d" % i, [128, 512], F32)), "ps%d" % i) for i in range(8)]
        for p_ in psum:
            p_.b.excl = True
        pools = {"A": [0, 1, 7], "B": [2, 3], "C": [4, 6], "Y": [5]}
        prr = {k: 0 for k in pools}

        def ps_alloc(pool):
            if pool == "D":
                pool = "A"
            i = prr[pool]
            prr[pool] = (i + 1) % len(pools[pool])
            return psum[pools[pool][i]]

        rr = {}

        def rot(lst, key):
            i = rr.get(key, 0)
            rr[key] = (i + 1) % len(lst)
            return lst[i]

        wbf_b = [[Buf("wbf%d_%d" % (l, c)) for c in range(NCH)] for l in range(DEPTH)]
        for l in range(DEPTH):
            for c in range(NCH):
                S.dma("pool", wbf[l, c], wch[l, c], w=[wbf_b[l][c]])

        S.dma("sp", ident[:], ident_d, w=[ident.b])
        S.dma("sp", swapm[:], swap_d, w=[swapm.b])
        S.dma("pool", masks[:], masks_d.rearrange("a p c -> p a c")[:, :, 0:128], w=[masks.b])
        S.dma("pool", ak[:], ak_d, w=[ak.b])
        S.dma("pool", aq[:], aq_d, w=[aq.b])
        S.dma("sp", rmask[:], rmask_d, w=[rmask.b])
        S.dma("sp", triu[:], triu_d, w=[triu.b])
        S.op("dve", lambda e: e.memset(ones[:], 1.0), w=[ones.b])
        S.dma("pool", aqs[:], aqs_d, w=[aqs.b])
        S.dma("pool", hsel[:], hsel_d, w=[hsel.b])
        S.dma("pool", hselT[:], hselT_d, w=[hselT.b])
        S.dma("sp", signc[:], signc_d, w=[signc.b])

        def nk_of(c):
            return 4 if (c == C_GLU or C_R <= c < C_R + 6) else 8

        wst = {"issued": 0, "used": 0, "hold": False}

        def w_issue():
            i = wst["issued"]
            if i >= len(order):
                return
            l, c, nk = order[i]
            wb = wbuf[i % 3]
            S.dma("sp", wb[:, 0:nk * 512], wbf[l, c, :, 0:nk * 512], r=[wbf_b[l][c]], w=[wb.b])
            wst["issued"] = i + 1

        def w_next(l, c):
            i = wst["used"]
            if recording:
                order.append((l, c, nk_of(c)))
            assert order[i][0] == l and order[i][1] == c, (order[i], l, c)
            while wst["issued"] < min(i + (1 if recording else 2), len(order)):
                w_issue()
            wst["used"] = i + 1
            wst["hold"] = True
            return wbuf[i % 3]

        def w_done():
            wst["hold"] = False
            if recording:
                return
            while wst["issued"] < min(wst["used"] + 2, len(order)):
                w_issue()

        def evac(eng, fn, r, w):
            S.op(eng, fn, r=r, w=w)

        def mm_fm(ps, wb, j, nk, rhs_fn, rbufs, n):
            for kb in range(nk):
                S.op("pe", lambda e, kb=kb: e.matmul(ps[:, 0:n], lhsT=wb[:, kb * 512 + j * 128: kb * 512 + (j + 1) * 128],
                                                     rhs=rhs_fn(kb), start=(kb == 0), stop=(kb == nk - 1)),
                     r=[wb.b] + rbufs, w=[ps.b], sig=(kb == nk - 1))

        def v2(t, width):
            ap = t[:]
            if len(ap.shape) == 3:
                ap = ap.rearrange("p a b -> p (a b)")
            return ap[:, 0:width]

        def setup_layer(l):
            S.dma("sp", colp[:], colp_d[l], w=[colp.b])
            S.dma("pool", cpw[:], cpad_d[l].rearrange("g p m -> p g m"), w=[cpw.b])
            S.dma("sp", dsk[:], dsk_d[l], w=[dsk.b])
            S.dma("sp", glub[:], glub_d[l], w=[glub.b])
            S.dma("sp", sgn[:], sgn_d[l].rearrange("a p c -> p a c"), w=[sgn.b])
            sgw32 = v2(ypre, 512).rearrange("p (g t) -> p g t", g=4)
            S.dma("sp", sgw32, sgw_d[l].rearrange("g s t -> s g t"), w=[ypre.b])
            S.dma("sp", sgb[:], sgb_d[l], w=[sgb.b])
            S.dma("sp", lng[:], lng_d[l].rearrange("a p c -> p a c"), w=[lng.b])
            S.dma("sp", esk[:], snk_d[l], w=[esk.b])
            for g4 in range(4):
                S.op("dve", lambda e, g4=g4: e.tensor_tensor(out=sgw[:, g4, :], in0=sgw32[:, g4, :], in1=triu[:], op=ALU.mult),
                     r=[ypre.b, triu.b], w=[sgw.b])
            S.op("act", lambda e: e.activation(out=esk[:], in_=esk[:], func=AF.Exp), r=[esk.b], w=[esk.b])
            S.op("dve", lambda e: e.tensor_scalar(out=cpw[64:128], in0=cpw[64:128], scalar1=-1.0, scalar2=None, op0=ALU.mult),
                 r=[cpw.b], w=[cpw.b])
            S.op("dve", lambda e: e.memset(hst[:], 0.0), w=[hst.b] + hstb)

            lr_c, li_c, ld_c = colp[:, 0:32], colp[:, 32:64], colp[:, 64:96]
            S.op("act", lambda e: e.activation(out=cdt[:], in_=ld_c, func=AF.Exp), r=[colp.b], w=[cdt.b])
            S.op("dve", lambda e: e.tensor_tensor(out=ctmp[:], in0=lr_c, in1=cdt[:], op=ALU.mult), r=[colp.b, cdt.b], w=[ctmp.b])
            S.op("act", lambda e: e.activation(out=crr[:], in_=ctmp[:], func=AF.Exp), r=[ctmp.b], w=[crr.b])
            S.op("dve", lambda e: e.tensor_tensor(out=cth[:], in0=li_c, in1=cdt[:], op=ALU.mult), r=[colp.b, cdt.b], w=[cth.b])

            def sincos(out_s, out_c, th, thb, tmp, tb, m, mb, itmp, ib, sb_, cb):
                I32 = mybir.dt.int32
                for dst, db, shift in ((out_s, sb_, 0.0), (out_c, cb, 0.5 * math.pi)):
                    S.op("dve", lambda e, shift=shift: e.tensor_scalar(out=tmp, in0=th, scalar1=shift, scalar2=1.0 / (2 * math.pi),
                                                                       op0=ALU.add, op1=ALU.mult), r=[thb], w=[tb])
                    S.op("dve", lambda e: e.tensor_copy(out=itmp, in_=tmp), r=[tb], w=[ib])
                    S.op("dve", lambda e: e.tensor_copy(out=tmp, in_=itmp), r=[ib], w=[tb])
                    S.op("dve", lambda e: e.scalar_tensor_tensor(out=tmp, in0=tmp, scalar=-2 * math.pi, in1=th, op0=ALU.mult, op1=ALU.add),
                         r=[tb, thb], w=[tb])
                    S.op("dve", lambda e, shift=shift: e.tensor_scalar(out=tmp, in0=tmp, scalar1=shift, scalar2=None, op0=ALU.add),
                         r=[tb], w=[tb])
                    S.op("dve", lambda e: e.tensor_scalar(out=m, in0=tmp, scalar1=math.pi, scalar2=-2 * math.pi, op0=ALU.is_gt, op1=ALU.mult),
                         r=[tb], w=[mb])
                    S.op("dve", lambda e: e.tensor_tensor(out=tmp, in0=tmp, in1=m, op=ALU.add), r=[tb, mb], w=[tb])
                    S.op("dve", lambda e: e.tensor_scalar(out=m, in0=tmp, scalar1=-math.pi, scalar2=2 * math.pi, op0=ALU.is_lt, op1=ALU.mult),
                         r=[tb], w=[mb])
                    S.op("dve", lambda e: e.tensor_tensor(out=tmp, in0=tmp, in1=m, op=ALU.add), r=[tb, mb], w=[tb])
                    S.op("dve", lambda e: e.tensor_scalar(out=tmp, in0=tmp, scalar1=math.pi, scalar2=-math.pi, op0=ALU.min, op1=ALU.max),
                         r=[tb], w=[tb])
                    S.op("act", lambda e, dst=dst: e.activation(out=dst, in_=tmp, func=AF.Sin), r=[tb], w=[db])

            sincos(cs1[:], cc1[:], cth[:], cth.b, ctmp[:], ctmp.b, ctmp2[:], ctmp2.b, citmp[:], citmp.b, cs1.b, cc1.b)
            S.op("dve", lambda e: e.tensor_tensor(out=car[:], in0=crr[:], in1=cc1[:], op=ALU.mult), r=[crr.b, cc1.b], w=[car.b])
            S.op("dve", lambda e: e.tensor_tensor(out=cais[:], in0=crr[:], in1=cs1[:], op=ALU.mult), r=[crr.b, cs1.b], w=[cais.b])
            S.op("dve", lambda e: e.tensor_scalar(out=cais[:], in0=cais[:], scalar1=signc[:, 0:1], scalar2=None, op0=ALU.mult),
                 r=[cais.b, signc.b], w=[cais.b])
            S.dma("sp", w00[:], w00_d[l], w=[w00.b])
            S.dma("sp", b00[:], b00_d[l], w=[b00.b])
            S.dma("sp", esks[:], esks_d[l], w=[esks.b])
            S.op("act", lambda e: e.activation(out=esks[:], in_=esks[:], func=AF.Exp), r=[esks.b], w=[esks.b])
            for g4 in range(4):
                S.op("dve", lambda e, g4=g4: e.tensor_scalar(out=w00I[:, g4, :], in0=ident[0:NS, 0:NS], scalar1=w00[:, g4:g4 + 1],
                                                             scalar2=None, op0=ALU.mult), r=[ident.b, w00.b], w=[w00I.b])
            for qq in range(8):
                gs = slice(qq * 4, (qq + 1) * 4)
                ccos = v2(ypre, 4 * TT).rearrange("p (g t) -> p g t", g=4)
                csin = v2(gtmp, 4 * TT).rearrange("p (g t) -> p g t", g=4)
                ctm = v2(mtmp[0], 2 * TT).rearrange("p (g t) -> p g t", g=4)
                ctm2 = v2(mtmp[1], 2 * TT).rearrange("p (g t) -> p g t", g=4)
                S.op("dve", lambda e: e.tensor_copy(out=ccos[:, :, 0], in_=cc1[:, gs]), r=[cc1.b], w=[ypre.b])
                S.op("dve", lambda e: e.tensor_copy(out=csin[:, :, 0], in_=cs1[:, gs]), r=[cs1.b], w=[gtmp.b])
                n = 1
                while n < TT:
                    cn = ccos[:, :, n - 1:n].to_broadcast([128, 4, n])
                    sn = csin[:, :, n - 1:n].to_broadcast([128, 4, n])
                    c0 = ccos[:, :, 0:n]
                    s0 = csin[:, :, 0:n]
                    t1 = ctm[:, :, 0:n]
                    t2 = ctm2[:, :, 0:n]
                    S.op("dve", lambda e, c0=c0, cn=cn, t1=t1: e.tensor_tensor(out=t1, in0=c0, in1=cn, op=ALU.mult), r=[ypre.b], w=[mtmp[0].b])
                    S.op("dve", lambda e, s0=s0, sn=sn, t2=t2: e.tensor_tensor(out=t2, in0=s0, in1=sn, op=ALU.mult), r=[gtmp.b], w=[mtmp[1].b])
                    S.op("dve", lambda e, t1=t1, t2=t2, n=n: e.tensor_tensor(out=ccos[:, :, n:2 * n], in0=t1, in1=t2, op=ALU.subtract),
                         r=[mtmp[0].b, mtmp[1].b], w=[ypre.b])
                    S.op("dve", lambda e, c0=c0, sn=sn, t1=t1: e.tensor_tensor(out=t1, in0=c0, in1=sn, op=ALU.mult), r=[ypre.b, gtmp.b], w=[mtmp[0].b])
                    S.op("dve", lambda e, s0=s0, cn=cn, t2=t2: e.tensor_tensor(out=t2, in0=s0, in1=cn, op=ALU.mult), r=[ypre.b, gtmp.b], w=[mtmp[1].b])
                    S.op("dve", lambda e, t1=t1, t2=t2, n=n: e.tensor_tensor(out=csin[:, :, n:2 * n], in0=t1, in1=t2, op=ALU.add),
                         r=[mtmp[0].b, mtmp[1].b], w=[gtmp.b])
                    n *= 2
                S.op("act", lambda e: e.copy(out=tabc[:, gs, :], in_=ccos), r=[ypre.b], w=[tabc.b])
                S.op("act", lambda e: e.copy(out=tabs[:, gs, :], in_=csin), r=[gtmp.b], w=[tabs.b])

            tl = [T(None, "x")] * 0
            hold = [(xtok[0], 0), (xtok[0], 512), (xtok[1], 0), (xtok[1], 512), (lnout[0], 0), (lnout[0], 512),
                    (lnout[1], 0), (lnout[1], 512)]

            class V:
                def __init__(self, t, off):
                    self.ap = t[:, off:off + 512]
                    self.b = t.b

            for qq in range(4):
                gs = slice(qq * 8, (qq + 1) * 8)
                cs_q = slice(qq * 512, (qq + 1) * 512)
                lr, li, ld, dt_, mg, th, sn_, cs_ = [V(t, o) for t, o in hold]
                for i, dst in enumerate((lr, li, ld)):
                    S.dma("sp", dst.ap, rowp_d[l, i][:, cs_q], w=[dst.b])
                bld = v2(xtok[2], 1024).rearrange("k (a g p) -> k a g p", a=2, g=8)
                for a in range(2):
                    S.dma("sp", bld[:, a], bpad_d[l, a, qq * 8:(qq + 1) * 8].rearrange("g k p -> k g p"), w=[xtok[2].b])

                def tt(out, a, b_, op):
                    S.op("dve", lambda e: e.tensor_tensor(out=out.ap, in0=a.ap, in1=b_.ap, op=op), r=[a.b, b_.b], w=[out.b])

                S.op("act", lambda e: e.activation(out=dt_.ap, in_=ld.ap, func=AF.Exp), r=[ld.b], w=[dt_.b])
                tt(mg, lr, dt_, ALU.mult)
                S.op("act", lambda e: e.activation(out=mg.ap, in_=mg.ap, func=AF.Exp), r=[mg.b], w=[mg.b])
                tt(th, li, dt_, ALU.mult)
                sincos(sn_.ap, cs_.ap, th.ap, th.b, ld.ap, ld.b, v2(gtmp, 512), gtmp.b, v2(ypre, 512).bitcast(mybir.dt.int32), ypre.b, sn_.b, cs_.b)
                tt(cs_, cs_, mg, ALU.mult)
                S.op("dve", lambda e: e.tensor_scalar(out=cs_.ap, in0=cs_.ap, scalar1=-1.0, scalar2=None, op0=ALU.add), r=[cs_.b], w=[cs_.b])
                tt(sn_, sn_, mg, ALU.mult)
                tt(mg, lr, lr, ALU.mult)
                tt(th, li, li, ALU.mult)
                tt(mg, mg, th, ALU.add)
                S.op("dve", lambda e: e.reciprocal(out=mg.ap, in_=mg.ap), r=[mg.b], w=[mg.b])
                tt(dt_, cs_, lr, ALU.mult)
                tt(ld, sn_, li, ALU.mult)
                tt(dt_, dt_, ld, ALU.add)
                tt(dt_, dt_, mg, ALU.mult)
                tt(th, sn_, lr, ALU.mult)
                tt(ld, cs_, li, ALU.mult)
                tt(th, th, ld, ALU.subtract)
                tt(th, th, mg, ALU.mult)
                crv = dt_.ap.rearrange("k (g p) -> k g p", p=64)
                civ = th.ap.rearrange("k (g p) -> k g p", p=64)
                t1 = sn_.ap.rearrange("k (g p) -> k g p", p=64)
                t2 = cs_.ap.rearrange("k (g p) -> k g p", p=64)
                bre = bld[:, 0]
                bim = bld[:, 1]
                S.op("dve", lambda e: e.tensor_tensor(out=t1, in0=crv, in1=bre, op=ALU.mult), r=[dt_.b, xtok[2].b], w=[sn_.b])
                S.op("dve", lambda e: e.tensor_tensor(out=t2, in0=civ, in1=bim, op=ALU.mult), r=[th.b, xtok[2].b], w=[cs_.b])
                S.op("dve", lambda e: e.tensor_tensor(out=bbw[:, gs, 0:64], in0=t1, in1=t2, op=ALU.subtract), r=[sn_.b, cs_.b], w=[bbw.b])
                S.op("dve", lambda e: e.tensor_tensor(out=bbs[:, gs, 64:128], in0=t2, in1=t1, op=ALU.subtract), r=[sn_.b, cs_.b], w=[bbs.b])
                S.op("dve", lambda e: e.tensor_tensor(out=t1, in0=crv, in1=bim, op=ALU.mult), r=[dt_.b, xtok[2].b], w=[sn_.b])
                S.op("dve", lambda e: e.tensor_tensor(out=t2, in0=civ, in1=bre, op=ALU.mult), r=[th.b, xtok[2].b], w=[cs_.b])
                S.op("dve", lambda e: e.tensor_tensor(out=bbw[:, gs, 64:128], in0=t1, in1=t2, op=ALU.add), r=[sn_.b, cs_.b], w=[bbw.b])
                S.op("dve", lambda e: e.tensor_tensor(out=bbs[:, gs, 0:64], in0=t1, in1=t2, op=ALU.add), r=[sn_.b, cs_.b], w=[bbs.b])

        hstb = [Buf("hst%d" % g) for g in range(32)]
        ysb = [Buf("ys_a"), Buf("ys_b"), Buf("ys_c")]
        sigb = [Buf("sig%d" % j) for j in range(4)]

        def s5_pipe(n):
            cols = slice(0, n)
            st = {}
            psy_cur = {}
            for it in range(32 + 7):
                g = it - 7
                if 0 <= g < 32:
                    fb, gl = g // 8, g % 8
                    if gl == 0:
                        psy_cur["p"] = ps_alloc("Y")
                    psy = psy_cur["p"]
                    hb = st[g]["hb"]
                    S.op("pe", lambda e, g=g, gl=gl, psy=psy, hb=hb: e.matmul(psy[:, 0:n], lhsT=cpw[:, g, :], rhs=hb[:, 0:n],
                                                                               start=(gl == 0), stop=(gl == 7)),
                         r=[cpw.b, hb.b], w=[psy.b], sig=True)
                    if gl == 7:
                        S.op("dve", lambda e, fb=fb, psy=psy: e.scalar_tensor_tensor(out=ypre[:, fb, cols], in0=uT[:, fb, cols],
                                                                                    scalar=dsk[:, fb:fb + 1], in1=psy[:, 0:n],
                                                                                    op0=ALU.mult, op1=ALU.add),
                             r=[uT.b, dsk.b, psy.b], w=[ypre.b])
                    del st[g]
                g = it - 6
                if 0 <= g < 32:
                    c, d = st[g]["c"], st[g]["d"]
                    hb = rot(s5hb, "hb")
                    st[g]["hb"] = hb
                    S.op("pool", lambda e, c=c, d=d, hb=hb: e.tensor_tensor(out=hb[:, 0:n], in0=c[:, 0:n], in1=d[:, 0:n], op=ALU.add),
                         r=[c.b, d.b], w=[hb.b])
                    S.op("pool", lambda e, c=c, d=d, g=g: e.tensor_tensor(out=hst[:, g:g + 1], in0=c[:, n - 1:n], in1=d[:, n - 1:n], op=ALU.add),
                         r=[c.b, d.b], w=[hstb[g]])
                g = it - 5
                if 0 <= g < 32:
                    psw = st[g]["psw"]
                    d = rot(s5d, "d")
                    st[g]["d"] = d
                    S.op("dve", lambda e, g=g, psw=psw, d=d: e.tensor_tensor(out=d[:, 0:n], in0=psw[:, 0:n], in1=tabs[:, g, 0:n], op=ALU.mult),
                         r=[psw.b, tabs.b], w=[d.b])
                g = it - 4
                if 0 <= g < 32:
                    gg = st[g]["gg"]
                    psw = ps_alloc("C")
                    st[g]["psw"] = psw
                    S.op("pe", lambda e, gg=gg, psw=psw: e.matmul(psw[:, 0:n], lhsT=swapm[:], rhs=gg[:, 0:n], start=True, stop=True),
                         r=[swapm.b, gg.b], w=[psw.b])
                    c = rot(s5c, "c")
                    st[g]["c"] = c
                    S.op("dve", lambda e, g=g, gg=gg, c=c: e.tensor_tensor(out=c[:, 0:n], in0=gg[:, 0:n], in1=tabc[:, g, 0:n], op=ALU.mult),
                         r=[gg.b, tabc.b], w=[c.b])
                g = it - 3
                if 0 <= g < 32:
                    v = st[g]["v"]
                    gg = rot(s5g, "g")
                    st[g]["gg"] = gg
                    S.op("dve", lambda e, g=g, v=v, gg=gg: e.tensor_tensor_scan(out=gg[:, 0:n], data0=crr[:, g:g + 1].to_broadcast([128, n]),
                                                                               data1=v[:, 0:n], initial=hst[:, g:g + 1], op0=ALU.mult, op1=ALU.add),
                         r=[crr.b, v.b, hstb[g]], w=[gg.b])
                g = it - 2
                if 0 <= g < 32:
                    a, b_ = st[g]["a"], st[g]["b"]
                    v = rot(s5v, "v")
                    st[g]["v"] = v
                    S.op("pool", lambda e, a=a, b_=b_, v=v: e.tensor_tensor(out=v[:, 0:n], in0=a[:, 0:n], in1=b_[:, 0:n], op=ALU.add),
                         r=[a.b, b_.b], w=[v.b])
                g = it - 1
                if 0 <= g < 32:
                    pv = st[g]["pv"]
                    a = rot(s5a, "a")
                    b_ = rot(s5b, "b")
                    st[g]["a"], st[g]["b"] = a, b_
                    S.op("dve", lambda e, g=g, pv=pv, a=a: e.tensor_tensor(out=a[:, 0:n], in0=pv[:, 0:n], in1=tabc[:, g, 0:n], op=ALU.mult),
                         r=[pv.b, tabc.b], w=[a.b])
                    S.op("dve", lambda e, g=g, pv=pv, b_=b_: e.tensor_tensor(out=b_[:, 0:n], in0=pv[:, TT:TT + n], in1=tabs[:, g, 0:n], op=ALU.mult),
                         r=[pv.b, tabs.b], w=[b_.b])
                g = it
                if 0 <= g < 32:
                    fb = g // 8
                    pv = ps_alloc("B")
                    st[g] = {"pv": pv}
                    S.op("pe", lambda e, g=g, fb=fb, pv=pv: e.matmul(pv[:, 0:n], lhsT=bbw[:, g, :], rhs=uT[:, fb, cols], start=True, stop=True),
                         r=[bbw.b, uT.b], w=[pv.b], sig=False)
                    S.op("pe", lambda e, g=g, fb=fb, pv=pv: e.matmul(pv[:, TT:TT + n], lhsT=bbs[:, g, :], rhs=uT[:, fb, cols], start=True, stop=True),
                         r=[bbs.b, uT.b], w=[pv.b])
                yield

        def gelu_glu_gen(l, n):
            yv = ypre[:, :, 0:n]
            g1 = gtmp[:, :, 0:n]
            S.op("dve", lambda e: e.tensor_tensor(out=g1, in0=yv, in1=yv, op=ALU.mult), r=[ypre.b], w=[gtmp.b])
            S.op("dve", lambda e: e.tensor_scalar(out=g1, in0=g1, scalar1=GK, scalar2=1.0, op0=ALU.mult, op1=ALU.add), r=[gtmp.b], w=[gtmp.b])
            yield
            S.op("pool", lambda e: e.tensor_tensor(out=g1, in0=g1, in1=yv, op=ALU.mult), r=[gtmp.b, ypre.b], w=[gtmp.b])
            yield
            S.op("act", lambda e: e.activation(out=g1, in_=g1, func=AF.Sigmoid, scale=GS), r=[gtmp.b], w=[gtmp.b])
            yield
            S.op("dve", lambda e: e.tensor_tensor(out=yv, in0=yv, in1=g1, op=ALU.mult), r=[gtmp.b, ypre.b], w=[ypre.b])
            yield
            S.op("act", lambda e: e.copy(out=gtmp2[:, :, 0:n], in_=yv), r=[ypre.b], w=[gtmp2.b])
            yield
            wb = w_next(l, C_GLU)
            for j in range(4):
                ps = ps_alloc("A")
                mm_fm(ps, wb, j, 4, lambda kb: gtmp2[:, kb, 0:n], [gtmp2.b], n)
                S.op("act", lambda e, j=j, ps=ps: e.activation(out=gtmp[:, j, 0:n], in_=ps[:, 0:n], func=AF.Sigmoid, bias=glub[:, j:j + 1]),
                     r=[ps.b, glub.b], w=[gtmp.b])
            w_done()
            yield
            S.op("dve", lambda e: e.tensor_tensor(out=yv, in0=yv, in1=g1, op=ALU.mult), r=[gtmp.b, ypre.b], w=[ypre.b])
            yield
            S.op("pool", lambda e: e.tensor_tensor(out=ys[:, 4:8, 0:n], in0=yv, in1=szb[:, :, 0:n], op=ALU.mult),
                 r=[ypre.b, szb.b], w=[ysb[1]])

        def gelu_glu(l, n):
            for _ in gelu_glu_gen(l, n):
                pass

        def layernorm_rows(src, nrow, width, gam, bet, dst, tmp=None, eng2="pool"):
            if tmp is None:
                tmp = (lntmp.b, lntmp[0:nrow, 0:width])
            nchk = width // 512
            for i in range(nchk):
                S.op("dve", lambda e, i=i: e.bn_stats(out=lnst[0:nrow, i * 6:(i + 1) * 6], in_=src[1][:, i * 512:(i + 1) * 512]),
                     r=[src[0]], w=[lnst.b])
            S.op("dve", lambda e: e.bn_aggr(out=lnmv[0:nrow, 0:2], in_=lnst[0:nrow, 0:6 * nchk]), r=[lnst.b], w=[lnmv.b])
            S.op("dve", lambda e: e.tensor_scalar(out=lnmv[0:nrow, 2:3], in0=lnmv[0:nrow, 1:2], scalar1=LN_EPS, scalar2=None,
                                                  op0=ALU.add), r=[lnmv.b], w=[lnmv.b])
            S.op("act", lambda e: e.sqrt(out=lnmv[0:nrow, 2:3], in_=lnmv[0:nrow, 2:3]), r=[lnmv.b], w=[lnmv.b])
            S.op("dve", lambda e: e.reciprocal(out=lnmv[0:nrow, 2:3], in_=lnmv[0:nrow, 2:3]), r=[lnmv.b], w=[lnmv.b])
            S.op("dve", lambda e: e.tensor_scalar(out=tmp[1], in0=src[1], scalar1=lnmv[0:nrow, 0:1],
                                                  scalar2=lnmv[0:nrow, 2:3], op0=ALU.subtract, op1=ALU.mult),
                 r=[src[0], lnmv.b], w=[tmp[0]])
            S.op("dve", lambda e: e.tensor_tensor(out=tmp[1], in0=tmp[1], in1=gam[1], op=ALU.mult),
                 r=[tmp[0], gam[0]], w=[tmp[0]])
            S.op(eng2, lambda e: e.tensor_tensor(out=dst[1], in0=tmp[1], in1=bet[1], op=ALU.add),
                 r=[tmp[0], bet[0]], w=[dst[0]])

        def xload(l, ti):
            t0 = ti * TT
            src = xp if l == 0 else x1
            for b in range(NBK):
                xt = xtok[(ti % 2) * NBK + b]
                S.dma("sp", xt[:], src[t0 + b * 128: t0 + (b + 1) * 128, :], r=([x1_b] if l == 1 else []), w=[xt.b])

        def front(l, ti):
            n = TT
            xts = [xtok[(ti % 2) * NBK + b] for b in range(NBK)]
            for kb in range(8):
                ps = ps_alloc("A")
                for b in range(NBK):
                    S.op("pe", lambda e, b=b, ps=ps, kb=kb: e.transpose(out=ps[:, b * 128:(b + 1) * 128],
                                                                       in_=xts[b][:, kb * 128:(kb + 1) * 128], identity=ident[:]),
                         r=[xts[b].b, ident.b], w=[ps.b], sig=(b == NBK - 1))
                S.op("act" if kb % 2 else "dve",
                     (lambda e, ps=ps, kb=kb: e.copy(out=xT[:, kb, :], in_=ps[:, 0:TT])) if kb % 2 else
                     (lambda e, ps=ps, kb=kb: e.tensor_copy(out=xT[:, kb, :], in_=ps[:, 0:TT])),
                     r=[ps.b], w=[xT.b])
            xrhs = lambda kb: xT[:, kb, :]
            wb = w_next(l, C_UB)
            for j in range(4):
                ps = ps_alloc("A")
                mm_fm(ps, wb, j, 8, xrhs, [xT.b], n)
                S.op("act", lambda e, j=j, ps=ps: e.copy(out=uT[:, j, :], in_=ps[:, 0:TT]), r=[ps.b], w=[uT.b])
            w_done()

        def tail(l, ti):
            t0 = ti * TT
            dst = x1 if l == 0 else yp
            for _ in gelu_glu_gen(l, TT):
                yield
            yield
            for _ in readout_br(l, TT, (1,), False, pre=True):
                yield
            blocks = [(xtok[(ti % 2) * NBK + b], 128, dst[t0 + b * 128: t0 + (b + 1) * 128, :]) for b in range(NBK)]
            for _ in wo_ln(l, TT, blocks):
                yield
            if ti + 2 < (NTILE if DBG_TILES is None else DBG_TILES):
                xload(l, ti + 2)

        def rest_gen(l, ti):
            n = TT
            xrhs = lambda kb: xT[:, kb, :]

            wb = w_next(l, C_Q)
            for j in range(4):
                ps = ps_alloc("A")
                mm_fm(ps, wb, j, 8, xrhs, [xT.b], n)
                S.op("act", lambda e, j=j, ps=ps: e.activation(out=qT[:, j, :], in_=ps[:, 0:TT], func=AF.Copy, scale=0.125),
                     r=[ps.b], w=[qT.b])
                yield
            w_done()
            yield
            wb = w_next(l, C_KD)
            for j in range(2):
                ps = ps_alloc("A")
                mm_fm(ps, wb, j, 8, xrhs, [xT.b], n)
                S.op("act", lambda e, j=j, ps=ps: e.copy(out=kT[:, j, 128:128 + TT], in_=ps[:, 0:TT]), r=[ps.b], w=[kT.b])
                yield
            w_done()
            yield
            wb = w_next(l, C_KV)
            for b in range(NBK):
                ps = ps_alloc("A")
                for kb in range(8):
                    S.op("pe", lambda e, kb=kb, b=b, ps=ps: e.matmul(ps[:, 0:256], lhsT=xT[:, kb, b * 128:(b + 1) * 128],
                                                                    rhs=wb[:, kb * 512: kb * 512 + 256], start=(kb == 0), stop=(kb == 7)),
                         r=[wb.b, xT.b], w=[ps.b], sig=(kb == 7))
                vsrc = ps[:, 128:256].rearrange("p (k d) -> p k d", k=2)
                S.op("act", lambda e, b=b, vsrc=vsrc: e.copy(out=vtok[:, b + 1, :, 0:64], in_=vsrc), r=[ps.b], w=[vtok.b])
                S.op("act", lambda e, b=b, vsrc=vsrc: e.copy(out=vtok[:, b + 1, :, 64:128], in_=vsrc), r=[ps.b], w=[vtok.b])
                if ti == (NTILE if DBG_TILES is None else DBG_TILES) - 1 and b == NBK - 1:
                    S.op("act", lambda e, ps=ps: e.copy(out=kvout[:], in_=ps[:, 0:256]), r=[ps.b], w=[kvout.b])
                    S.dma("pool", kvw[l], kvout[:], r=[kvout.b])
                yield
            w_done()
            yield
            for cid, dstT, fn in ((C_ZA, sza, AF.Silu), (C_UC, ucz, None)):
                wb = w_next(l, cid)
                for j in range(4):
                    ps = ps_alloc("A")
                    mm_fm(ps, wb, j, 8, xrhs, [xT.b], n)
                    if fn is None:
                        S.op("act", lambda e, j=j, ps=ps, dstT=dstT: e.copy(out=dstT[:, j, :], in_=ps[:, 0:TT]), r=[ps.b], w=[dstT.b])
                    else:
                        S.op("act", lambda e, j=j, ps=ps, dstT=dstT, fn=fn: e.activation(out=dstT[:, j, :], in_=ps[:, 0:TT], func=fn),
                             r=[ps.b], w=[dstT.b])
                    yield
                w_done()
                yield
            wb = w_next(l, C_VC)
            for b in range(NBK):
                ps = ps_alloc("A")
                for kb in range(8):
                    S.op("pe", lambda e, kb=kb, b=b, ps=ps: e.matmul(ps[:, :], lhsT=xT[:, kb, b * 128:(b + 1) * 128],
                                                                    rhs=wb[:, kb * 512:(kb + 1) * 512], start=(kb == 0), stop=(kb == 7)),
                         r=[wb.b, xT.b], w=[ps.b], sig=(kb == 7))
                layernorm_rows((ps.b, ps[:, :]), 128, 512, (sgn.b, sgn[:, 0, :]), (sgn.b, sgn[:, 1, :]), (vn.b, vn[:, b, :]))
                yield
            w_done()
            yield
            wb = w_next(l, C_ZC)
            for j in range(4):
                ps = ps_alloc("A")
                mm_fm(ps, wb, j, 8, xrhs, [xT.b], n)
                S.op("act", lambda e, j=j, ps=ps: e.activation(out=sig[:, j, :], in_=ps[:, 0:TT], func=AF.Silu), r=[ps.b], w=[sigb[j]])
                yield
            w_done()
            S.op("pool", lambda e: e.tensor_tensor(out=ucz[:], in0=ucz[:], in1=sig[:], op=ALU.mult), r=[ucz.b] + sigb, w=[ucz.b])
            yield

            for b in range(NBK):
                for g4 in range(4):
                    ps = ps_alloc("A")
                    S.op("pe", lambda e, b=b, g4=g4, ps=ps: e.matmul(ps[:, 0:128], lhsT=vn[:, b, g4 * 128:(g4 + 1) * 128],
                                                                    rhs=sgw[:, g4, :], start=True, stop=True),
                         r=[vn.b, sgw.b], w=[ps.b])
                    mt = rot(mtmp, "mt")
                    S.op("dve", lambda e, g4=g4, ps=ps, mt=mt: e.tensor_tensor(out=mt[:, 0:128], in0=ps[:, 0:128],
                                                                              in1=sgb[:, g4 * 128:(g4 + 1) * 128], op=ALU.add),
                         r=[ps.b, sgb.b], w=[mt.b])
                    S.op("pool", lambda e, b=b, g4=g4, mt=mt: e.tensor_tensor(out=ys[:, 8 + g4, b * 128:(b + 1) * 128], in0=mt[:, 0:128],
                                                                             in1=ucz[:, g4, b * 128:(b + 1) * 128], op=ALU.mult),
                         r=[mt.b, ucz.b], w=[ysb[2]])
                yield

            for b in range(NBK):
                first = (ti == 0 and b == 0)
                parts = [1] if first else [0, 1]
                for kv in range(2):
                    for pt in parts:
                        pss = ps_alloc("D")
                        kc = b * 128 + pt * 128
                        for par in range(2):
                            rows = slice(par * 64, (par + 1) * 64)
                            o = pss[:, par * 256:(par + 1) * 256].rearrange("p (j q) -> p j q", j=2)
                            S.op("pe", lambda e, o=o, rows=rows, kc=kc, kv=kv, b=b: e.matmul(
                                o, lhsT=kT[rows, kv, kc:kc + 128], rhs=qT[rows, 2 * kv:2 * kv + 2, b * 128:(b + 1) * 128],
                                start=True, stop=False), r=[kT.b, qT.b], w=[pss.b], sig=False)
                            aqv = aq[:, (kv * 2 + par) * 256:(kv * 2 + par + 1) * 256].rearrange("p (j q) -> p j q", j=2)
                            S.op("pe", lambda e, o=o, aqv=aqv, pt=pt: e.matmul(o, lhsT=ak[:, pt * 128:(pt + 1) * 128], rhs=aqv,
                                                                               start=False, stop=True),
                                 r=[ak.b, aq.b], w=[pss.b], sig=(par == 1))
                        S.op("act", lambda e, pss=pss, pt=pt, kv=kv: e.activation(out=pT[kv][pt][:], in_=pss[:, :], func=AF.Exp),
                             r=[pss.b], w=[pT[kv][pt].b])
                        S.op("pool", lambda e, pt=pt, kv=kv: e.tensor_tensor(
                            out=pT[kv][pt][:].rearrange("p (h q) -> p h q", h=4), in0=pT[kv][pt][:].rearrange("p (h q) -> p h q", h=4),
                            in1=masks[:, pt, :].unsqueeze(1).to_broadcast([128, 4, 128]), op=ALU.mult),
                             r=[pT[kv][pt].b, masks.b], w=[pT[kv][pt].b])
                    yield
                    psd = ps_alloc("D")
                    for i, pt in enumerate(parts):
                        S.op("pe", lambda e, psd=psd, pt=pt, i=i, kv=kv: e.matmul(psd[:, :], lhsT=ones[:], rhs=pT[kv][pt][:],
                                                                                 start=(i == 0), stop=(i == len(parts) - 1)),
                             r=[ones.b, pT[kv][pt].b], w=[psd.b], sig=(i == len(parts) - 1))
                    S.op("dve", lambda e, psd=psd, kv=kv: e.tensor_tensor(
                        out=rden[kv][:].rearrange("p (h q) -> p h q", h=4), in0=psd[:, :].rearrange("p (h q) -> p h q", h=4),
                        in1=esk[:, kv * 4:(kv + 1) * 4].unsqueeze(2).to_broadcast([128, 4, 128]), op=ALU.add),
                         r=[psd.b, esk.b], w=[rden[kv].b])
                    S.op("act", lambda e, kv=kv: e.activation(out=rden[kv][:], in_=rden[kv][:], func=AF.Ln), r=[rden[kv].b], w=[rden[kv].b])
                    S.op("act", lambda e, kv=kv: e.activation(out=rden[kv][:], in_=rden[kv][:], func=AF.Exp, scale=-1.0),
                         r=[rden[kv].b], w=[rden[kv].b])
                    pso = ps_alloc("D")
                    for hh in range(4):
                        for i, pt in enumerate(parts):
                            S.op("pe", lambda e, pso=pso, hh=hh, pt=pt, i=i, kv=kv, b=b: e.matmul(
                                pso[:, hh * 128:(hh + 1) * 128], lhsT=vtok[:, b + pt, kv, :], rhs=pT[kv][pt][:, hh * 128:(hh + 1) * 128],
                                start=(i == 0), stop=(i == len(parts) - 1)),
                                 r=[vtok.b, pT[kv][pt].b], w=[pso.b], sig=(hh == 3 and i == len(parts) - 1))
                    S.op("dve", lambda e, pso=pso, kv=kv: e.tensor_tensor(out=otmp[kv][:], in0=pso[:, :], in1=rden[kv][:], op=ALU.mult),
                         r=[pso.b, rden[kv].b], w=[otmp[kv].b])
                    for par in range(2):
                        rows = slice(par * 64, (par + 1) * 64)
                        ov = otmp[kv][rows, par * 256:(par + 1) * 256].rearrange("p (j q) -> p j q", j=2)
                        S.op("pool", lambda e, rows=rows, ov=ov, kv=kv, b=b: e.tensor_tensor(
                            out=ys[rows, 2 * kv:2 * kv + 2, b * 128:(b + 1) * 128], in0=ov,
                            in1=sza[rows, 2 * kv:2 * kv + 2, b * 128:(b + 1) * 128], op=ALU.mult),
                             r=[otmp[kv].b, sza.b], w=[ysb[0]])
                    yield
            S.op("act", lambda e: e.copy(out=kT[:, :, 0:128], in_=kT[:, :, TT:TT + 128]), r=[kT.b], w=[kT.b])
            S.op("pool", lambda e: e.tensor_copy(out=vtok[:, 0], in_=vtok[:, NBK]), r=[vtok.b], w=[vtok.b])
            for _ in readout_br(l, n, (0, 2), True):
                yield
            for _ in gates_b(l, n):
                yield
            wb = w_next(l, C_ZB)
            for j in range(4):
                ps = ps_alloc("A")
                mm_fm(ps, wb, j, 8, xrhs, [xT.b], n)
                S.op("act", lambda e, j=j, ps=ps: e.activation(out=szb[:, j, :], in_=ps[:, 0:TT], func=AF.Silu), r=[ps.b], w=[szb.b])
            w_done()
            yield


        def gates_b(l, n):
            xrhs = lambda kb: xT[:, kb, 0:n]
            for hf in range(2):
                wb = w_next(l, C_G + hf * 3 + 1)
                for j in range(4):
                    ps = ps_alloc("A")
                    mm_fm(ps, wb, j, 8, xrhs, [xT.b], n)
                    S.op("act", lambda e, j=j, ps=ps: e.activation(out=sigB[:, hf * 4 + j, 0:n], in_=ps[:, 0:n], func=AF.Sigmoid),
                         r=[ps.b], w=[sigB.b])
                    yield
                w_done()
                yield

        def readout_br(l, n, branches, init_first, pre=False):
            xrhs = lambda kb: xT[:, kb, 0:n]
            for hf in range(2):
                for bi, b in enumerate(branches):
                    if pre:
                        gsrc, gbufs, goff = sigB, [sigB.b] * 4, hf * 4
                    else:
                        gsrc, gbufs, goff = sig, sigb, 0
                        wb = w_next(l, C_G + hf * 3 + b)
                        for j in range(4):
                            ps = ps_alloc("A")
                            mm_fm(ps, wb, j, 8, xrhs, [xT.b], n)
                            S.op("act", lambda e, j=j, ps=ps: e.activation(out=sig[:, j, 0:n], in_=ps[:, 0:n], func=AF.Sigmoid),
                                 r=[ps.b], w=[sigb[j]])
                            yield
                        w_done()
                        yield
                    wb = w_next(l, C_R + hf * 3 + b)
                    for j in range(4):
                        ps = ps_alloc("A")
                        mm_fm(ps, wb, j, 4, lambda kb, b=b: ys[:, b * 4 + kb, 0:n], [ysb[b]], n)
                        if init_first and bi == 0:
                            S.op("dve", lambda e, j=j, ps=ps, gsrc=gsrc, goff=goff: e.tensor_tensor(
                                out=mrgT[:, hf * 4 + j, 0:n], in0=ps[:, 0:n], in1=gsrc[:, goff + j, 0:n], op=ALU.mult),
                                 r=[ps.b, gbufs[j]], w=[mrgT.b])
                        else:
                            mt = rot(mtmp, "mt")
                            S.op("dve", lambda e, j=j, ps=ps, mt=mt, gsrc=gsrc, goff=goff: e.tensor_tensor(
                                out=mt[:, 0:n], in0=ps[:, 0:n], in1=gsrc[:, goff + j, 0:n], op=ALU.mult),
                                 r=[ps.b, gbufs[j]], w=[mt.b])
                            S.op("pool", lambda e, j=j, mt=mt: e.tensor_tensor(out=mrgT[:, hf * 4 + j, 0:n], in0=mrgT[:, hf * 4 + j, 0:n],
                                                                              in1=mt[:, 0:n], op=ALU.add),
                                 r=[mrgT.b, mt.b], w=[mrgT.b])
                        yield
                    w_done()
                    yield

        def drain(gen):
            for _ in gen:
                pass

        def wo_ln(l, n, blocks):
            for hf in range(2):
                wb = w_next(l, C_O + hf)
                c0 = 0
                for (xt, nrow, _) in blocks:
                    ps = ps_alloc("A")
                    for kb in range(8):
                        S.op("pe", lambda e, kb=kb, ps=ps, c0=c0, nrow=nrow: e.matmul(
                            ps[0:nrow, :], lhsT=mrgT[:, kb, c0:c0 + nrow], rhs=wb[:, kb * 512:(kb + 1) * 512],
                            start=(kb == 0), stop=(kb == 7)), r=[wb.b, mrgT.b], w=[ps.b], sig=(kb == 7))
                    S.op("dve", lambda e, ps=ps, xt=xt, nrow=nrow: e.scalar_tensor_tensor(
                        out=xt[0:nrow, hf * 512:(hf + 1) * 512], in0=xt[0:nrow, hf * 512:(hf + 1) * 512], scalar=ALPHA,
                        in1=ps[0:nrow, :], op0=ALU.mult, op1=ALU.add), r=[ps.b, xt.b], w=[xt.b])
                    c0 += nrow
                    yield
                w_done()
                yield
            for (xt, nrow, dst_ap) in blocks:
                lo = rot(lnout, "lo")
                layernorm_rows((xt.b, xt[0:nrow, :]), nrow, D, (lng.b, lng[0:nrow, 0, :]), (lng.b, lng[0:nrow, 1, :]),
                               (lo.b, lo[0:nrow, :]), tmp=(xt.b, xt[0:nrow, :]))
                S.dma("pool", dst_ap, lo[0:nrow, :], r=[lo.b], w=[x1_b])
                yield

        def sample_tile(l):
            n = NS
            src = xs if l == 0 else xs1
            dst = xs1 if l == 0 else ysm
            xt = xtok[0]
            S.dma("sp", xt[0:n, :], src, r=([x1_b] if l == 1 else []), w=[xt.b])
            for kb in range(8):
                ps = ps_alloc("A")
                S.op("pe", lambda e, ps=ps, kb=kb: e.transpose(out=ps[:, 0:n], in_=xt[0:n, kb * 128:(kb + 1) * 128], identity=ident[0:n, 0:n]),
                     r=[xt.b, ident.b], w=[ps.b])
                S.op("dve", lambda e, ps=ps, kb=kb: e.tensor_copy(out=xT[:, kb, 0:n], in_=ps[:, 0:n]), r=[ps.b], w=[xT.b])
            xrhs = lambda kb: xT[:, kb, 0:n]
            wb = w_next(l, C_UB)
            for j in range(4):
                ps = ps_alloc("A")
                mm_fm(ps, wb, j, 8, xrhs, [xT.b], n)
                S.op("dve", lambda e, j=j, ps=ps: e.tensor_copy(out=uT[:, j, 0:n], in_=ps[:, 0:n]), r=[ps.b], w=[uT.b])
            w_done()
            wb = w_next(l, C_Q)
            for j in range(4):
                ps = ps_alloc("A")
                mm_fm(ps, wb, j, 8, xrhs, [xT.b], n)
                S.op("act", lambda e, j=j, ps=ps: e.activation(out=qT[:, j, 0:n], in_=ps[:, 0:n], func=AF.Copy, scale=0.125), r=[ps.b], w=[qT.b])
            w_done()
            wb = w_next(l, C_KD)
            for j in range(4):
                ps = ps_alloc("A")
                mm_fm(ps, wb, j, 8, xrhs, [xT.b], n)
                if j < 2:
                    S.op("dve", lambda e, j=j, ps=ps: e.tensor_copy(out=ksd[:, j, :], in_=ps[:, 0:n]), r=[ps.b], w=[ksd.b])
                else:
                    S.op("dve", lambda e, j=j, ps=ps: e.tensor_copy(out=vsd[:, j - 2, :], in_=ps[:, 0:n]), r=[ps.b], w=[vsd.b])
            w_done()
            wb = w_next(l, C_KV)
            ps = ps_alloc("A")
            for kb in range(8):
                S.op("pe", lambda e, kb=kb, ps=ps: e.matmul(ps[0:n, 0:256], lhsT=xT[:, kb, 0:n], rhs=wb[:, kb * 512: kb * 512 + 256],
                                                           start=(kb == 0), stop=(kb == 7)), r=[wb.b, xT.b], w=[ps.b], sig=(kb == 7))
            S.op("dve", lambda e, ps=ps: e.tensor_copy(out=kvs_sb[:], in_=ps[0:n, 0:256]), r=[ps.b], w=[kvs_sb.b])
            S.dma("pool", kvs[l], kvs_sb[:], r=[kvs_sb.b])
            w_done()
            for cid, dstT, fn in ((C_ZA, sza, AF.Silu), (C_ZB, szb, AF.Silu), (C_UC, ucz, None)):
                wb = w_next(l, cid)
                for j in range(4):
                    ps = ps_alloc("A")
                    mm_fm(ps, wb, j, 8, xrhs, [xT.b], n)
                    if fn is None:
                        S.op("dve", lambda e, j=j, ps=ps, dstT=dstT: e.tensor_copy(out=dstT[:, j, 0:n], in_=ps[:, 0:n]), r=[ps.b], w=[dstT.b])
                    else:
                        S.op("act", lambda e, j=j, ps=ps, dstT=dstT, fn=fn: e.activation(out=dstT[:, j, 0:n], in_=ps[:, 0:n], func=fn),
                             r=[ps.b], w=[dstT.b])
                w_done()
            wb = w_next(l, C_VC)
            ps = ps_alloc("A")
            for kb in range(8):
                S.op("pe", lambda e, kb=kb, ps=ps: e.matmul(ps[0:n, :], lhsT=xT[:, kb, 0:n], rhs=wb[:, kb * 512:(kb + 1) * 512],
                                                           start=(kb == 0), stop=(kb == 7)), r=[wb.b, xT.b], w=[ps.b], sig=(kb == 7))
            lo = rot(lnout, "lo")
            layernorm_rows((ps.b, ps[0:n, :]), n, 512, (sgn.b, sgn[0:n, 0, :]), (sgn.b, sgn[0:n, 1, :]), (lo.b, lo[0:n, 0:512]))
            S.dma("pool", vns[l], lo[0:n, 0:512], r=[lo.b])
            S.op("act", lambda e: e.copy(out=vnS[:], in_=lo[0:n, 0:512]), r=[lo.b], w=[vnS.b])
            w_done()
            wb = w_next(l, C_ZC)
            for j in range(4):
                ps = ps_alloc("A")
                mm_fm(ps, wb, j, 8, xrhs, [xT.b], n)
                S.op("act", lambda e, j=j, ps=ps: e.activation(out=gtmp2[:, j, 0:n], in_=ps[:, 0:n], func=AF.Silu), r=[ps.b], w=[gtmp2.b])
            w_done()
            S.op("pool", lambda e: e.tensor_tensor(out=ucz[:, :, 0:n], in0=ucz[:, :, 0:n], in1=gtmp2[:, :, 0:n], op=ALU.mult),
                 r=[ucz.b, gtmp2.b], w=[ucz.b])

            S.dma("sp", h0[:], h0_d[l], w=[h0.b])
            S.dma("sp", h0s[:], h0s_d[l], w=[h0s.b])
            psv = ps_alloc("B")
            for g in range(32):
                S.op("pe", lambda e, g=g: e.matmul(psv[:, g * n:(g + 1) * n], lhsT=bbw[:, g, :], rhs=uT[:, g // 8, 0:n], start=True, stop=True),
                     r=[bbw.b, uT.b], w=[psv.b], sig=(g == 31))
            S.op("dve", lambda e: e.tensor_tensor(out=h0[:], in0=h0[:], in1=car[:].unsqueeze(2).to_broadcast([128, 32, n]), op=ALU.mult),
                 r=[h0.b, car.b], w=[h0.b])
            S.op("dve", lambda e: e.tensor_tensor(out=h0s[:], in0=h0s[:], in1=cais[:].unsqueeze(2).to_broadcast([128, 32, n]), op=ALU.mult),
                 r=[h0s.b, cais.b], w=[h0s.b])
            S.op("dve", lambda e: e.tensor_tensor(out=h0[:], in0=h0[:], in1=h0s[:], op=ALU.add), r=[h0.b, h0s.b], w=[h0.b])
            S.op("dve", lambda e: e.tensor_tensor(out=h0[:], in0=h0[:], in1=psv[:, 0:32 * n].rearrange("p (g n) -> p g n", n=n), op=ALU.add),
                 r=[h0.b, psv.b], w=[h0.b])
            S.dma("pool", hss[l], h0[:], r=[h0.b])
            S.op("act", lambda e: e.copy(out=hbs[:], in_=h0[:]), r=[h0.b], w=[hbs.b])
            for fb in range(4):
                psy = ps_alloc("Y")
                for gl in range(8):
                    g = fb * 8 + gl
                    S.op("pe", lambda e, g=g, gl=gl, psy=psy: e.matmul(psy[:, 0:n], lhsT=cpw[:, g, :], rhs=hbs[:, g, :], start=(gl == 0), stop=(gl == 7)),
                         r=[cpw.b, hbs.b], w=[psy.b], sig=(gl == 7))
                S.op("dve", lambda e, fb=fb, psy=psy: e.scalar_tensor_tensor(out=ypre[:, fb, 0:n], in0=uT[:, fb, 0:n], scalar=dsk[:, fb:fb + 1],
                                                                            in1=psy[:, 0:n], op0=ALU.mult, op1=ALU.add),
                     r=[uT.b, dsk.b, psy.b], w=[ypre.b])

            for g4 in range(4):
                ps = ps_alloc("A")
                S.op("pe", lambda e, g4=g4, ps=ps: e.matmul(ps[:, 0:n], lhsT=vnS[:, g4 * 128:(g4 + 1) * 128], rhs=w00I[:, g4, :], start=True, stop=True),
                     r=[vnS.b, w00I.b], w=[ps.b])
                S.op("dve", lambda e, g4=g4, ps=ps: e.tensor_scalar(out=st1[:], in0=ps[:, 0:n], scalar1=b00[:, g4:g4 + 1], scalar2=None, op0=ALU.add),
                     r=[ps.b, b00.b], w=[st1.b])
                S.op("dve", lambda e, g4=g4: e.tensor_tensor(out=ys[:, 8 + g4, 0:n], in0=st1[:], in1=ucz[:, g4, 0:n], op=ALU.mult),
                     r=[st1.b, ucz.b], w=[ysb[2]])

            S.dma("pool", ck[:], ckT_d[l], w=[ck.b])
            S.dma("pool", cv[:], cvd_d[l], w=[cv.b])
            pss = ps_alloc("D")
            for nn in range(n):
                for kv in range(2):
                    for par in range(2):
                        rows = slice(par * 64, (par + 1) * 64)
                        c0 = (kv * 2 + par) * 32
                        o = pss[:, c0:c0 + 32].rearrange("p (j n) -> p j n", j=2)[:, :, nn]
                        last = (nn == n - 1 and kv == 1 and par == 1)
                        S.op("pe", lambda e, o=o, rows=rows, nn=nn, kv=kv: e.matmul(o, lhsT=ck[rows, nn * 2 + kv, :], rhs=qT[rows, 2 * kv:2 * kv + 2, nn],
                                                                                   start=True, stop=False), r=[ck.b, qT.b], w=[pss.b], sig=False)
                        S.op("pe", lambda e, o=o, c0=c0: e.matmul(o, lhsT=ak[:, 0:128], rhs=aqs[:, c0 // 16: c0 // 16 + 2], start=False, stop=True),
                             r=[ak.b, aqs.b], w=[pss.b], sig=last)
            S.op("act", lambda e: e.activation(out=pSb[:], in_=pss[:, 0:128], func=AF.Exp), r=[pss.b], w=[pSb.b])
            for j in range(4):
                S.op("dve", lambda e, j=j: e.tensor_tensor(out=prod[:, j, :], in0=qT[:, j, 0:n], in1=ksd[:, j // 2, :], op=ALU.mult),
                     r=[qT.b, ksd.b], w=[prod.b])
            ps1 = ps_alloc("A")
            S.op("pe", lambda e: e.matmul(ps1[0:2, 0:64], lhsT=hsel[:], rhs=prod[:].rearrange("p j n -> p (j n)"), start=True, stop=True),
                 r=[hsel.b, prod.b], w=[ps1.b])
            S.op("act", lambda e: e.activation(out=pself[:], in_=ps1[0:2, 0:64], func=AF.Exp), r=[ps1.b], w=[pself.b])
            ps2 = ps_alloc("A")
            S.op("pe", lambda e: e.matmul(ps2[:, 0:64], lhsT=hselT[:], rhs=pself[:], start=True, stop=True), r=[hselT.b, pself.b], w=[ps2.b])
            S.op("dve", lambda e: e.tensor_copy(out=pbs[:].rearrange("p j n -> p (j n)"), in_=ps2[:, 0:64]), r=[ps2.b], w=[pbs.b])
            psd = ps_alloc("A")
            S.op("pe", lambda e: e.matmul(psd[:, 0:128], lhsT=ones[:], rhs=pSb[:], start=True, stop=True), r=[ones.b, pSb.b], w=[psd.b])
            S.op("dve", lambda e: e.tensor_copy(out=sdn[:], in_=psd[:, 0:128]), r=[psd.b], w=[sdn.b])
            pso = ps_alloc("D")
            for nn in range(n):
                for kv in range(2):
                    o = pso[:, kv * 64:(kv + 1) * 64].rearrange("p (a n) -> p a n", n=n)[:, :, nn]
                    r_ = pSb[:, kv * 64:(kv + 1) * 64].rearrange("p (a n) -> p a n", n=n)[:, :, nn]
                    S.op("pe", lambda e, o=o, r_=r_, nn=nn, kv=kv: e.matmul(o, lhsT=cv[:, nn * 2 + kv, :], rhs=r_, start=True, stop=True),
                         r=[cv.b, pSb.b], w=[pso.b], sig=(nn == n - 1 and kv == 1))
            S.op("dve", lambda e: e.tensor_copy(out=sso[:], in_=pso[:, 0:128]), r=[pso.b], w=[sso.b])
            for par in range(2):
                rows = slice(par * 64, (par + 1) * 64)
                for kv in range(2):
                    for jj in range(2):
                        blk = 2 * kv + jj
                        c0 = ((kv * 2 + par) * 2 + jj) * n
                        a1 = st1[rows, :]
                        a2 = st2[rows, :]
                        S.op("dve", lambda e, a1=a1, rows=rows, blk=blk, kv=kv: e.tensor_tensor(out=a1, in0=pbs[rows, blk, :], in1=vsd[rows, kv, :], op=ALU.mult),
                             r=[pbs.b, vsd.b], w=[st1.b])
                        S.op("dve", lambda e, a1=a1, rows=rows, c0=c0: e.tensor_tensor(out=a1, in0=a1, in1=sso[rows, c0:c0 + n], op=ALU.add),
                             r=[st1.b, sso.b], w=[st1.b])
                        S.op("dve", lambda e, a2=a2, rows=rows, blk=blk, c0=c0: e.tensor_tensor(out=a2, in0=pbs[rows, blk, :], in1=sdn[rows, c0:c0 + n], op=ALU.add),
                             r=[pbs.b, sdn.b], w=[st2.b])
                        S.op("dve", lambda e, a2=a2, rows=rows, blk=blk: e.tensor_scalar(out=a2, in0=a2, scalar1=esks[rows, blk:blk + 1], scalar2=None, op0=ALU.add),
                             r=[st2.b, esks.b], w=[st2.b])
                        S.op("dve", lambda e, a2=a2: e.reciprocal(out=a2, in_=a2), r=[st2.b], w=[st2.b])
                        S.op("dve", lambda e, a1=a1, a2=a2: e.tensor_tensor(out=a1, in0=a1, in1=a2, op=ALU.mult), r=[st1.b, st2.b], w=[st1.b])
                        S.op("dve", lambda e, a1=a1, rows=rows, blk=blk: e.tensor_tensor(out=ys[rows, blk, 0:n], in0=a1, in1=sza[rows, blk, 0:n], op=ALU.mult),
                             r=[st1.b, sza.b], w=[ysb[0]])
            drain(readout_br(l, n, (0, 2), True))
            gelu_glu(l, n)
            drain(readout_br(l, n, (1,), False))
            drain(wo_ln(l, n, [(xt, n, dst)]))

        x1_b = Buf("x1")

        def roundrobin(gens):
            gens = list(gens)
            while gens:
                for g_ in list(gens):
                    try:
                        next(g_)
                        while wst["hold"]:
                            next(g_)
                    except StopIteration:
                        gens.remove(g_)

        for l in range(DEPTH):
            setup_layer(l)
            nt = NTILE if DBG_TILES is None else DBG_TILES
            xload(l, 0)
            if nt > 1:
                xload(l, 1)
            front(l, 0)
            prev_tail = None
            for ti in range(nt):
                gens = [s5_pipe(TT), rest_gen(l, ti)]
                if prev_tail is not None:
                    gens.append(prev_tail)
                roundrobin(gens)
                if ti + 1 < nt:
                    front(l, ti + 1)
                prev_tail = tail(l, ti)
            if prev_tail is not None:
                drain(prev_tail)
            S.dma("pool", hst_o[l], hst[:], r=[hst.b] + hstb)
            sample_tile(l)
        allb = [x1_b, hst.b, h0.b, kvs_sb.b] + [t.b for t in lnout] + ([] if DBG_NOKV else [kvout.b])
        S.finish(allb)
        for i in range(NDS):
            if S.dcnt[i] > 0:
                S._wait("sp", (("d", i), S.dcnt[i]))
        if not recording:
            print("instructions:", S.ninst, "sem counts:", S.cnt)
    return nc


_PROG = None


def _consts():
    ident = np.eye(128, dtype=np.float32)
    swapm = np.zeros((128, 128), np.float32)
    for m in range(64):
        swapm[64 + m, m] = -1.0
        swapm[m, 64 + m] = 1.0
    s = np.arange(128)[:, None]
    q = np.arange(128)[None, :]
    mprev = (s >= q).astype(np.float32)
    mcur = (s <= q).astype(np.float32)
    masks = np.stack([np.tile(mprev, (1, 4)), np.tile(mcur, (1, 4))]).astype(np.float32)
    ak = np.zeros((2, 256), np.float32)
    ak[0, :128] = np.arange(128) - 128
    ak[0, 128:] = np.arange(128)
    ak[1, :] = 1.0
    slopes = 2.0 ** (-(np.arange(1, 9)))
    aq = np.zeros((2, 2, 2, 2, 128), np.float32)
    for kv in range(2):
        for par in range(2):
            for jj in range(2):
                h = 4 * kv + 2 * jj + par
                aq[0, kv, par, jj, :] = slopes[h]
                aq[1, kv, par, jj, :] = -slopes[h] * np.arange(128)
    rmask = np.zeros((128, 8), np.float32)
    for gl in range(8):
        rmask[gl * 16:(gl + 1) * 16, gl] = 1.0
    triu = (s <= q).astype(np.float32)
    aqs = np.zeros((2, 8), np.float32)
    for kv in range(2):
        for par in range(2):
            for jj in range(2):
                aqs[0, (kv * 2 + par) * 2 + jj] = slopes[4 * kv + 2 * jj + par]
    hsel = np.zeros((128, 2), np.float32)
    hsel[0:64, 0] = 1.0
    hsel[64:128, 1] = 1.0
    signc = np.ones((128, 1), np.float32)
    signc[0:64] = -1.0
    return dict(ident=ident, swapm=swapm, masks=masks, ak=ak, aq=aq.reshape(2, 1024), rmask=rmask, triu=triu,
                aqs=aqs, hsel=hsel, hselT=np.ascontiguousarray(hsel.T), signc=signc)


def _layout_weights(w_in, w_read, w_o, glu_w):
    out = np.zeros((DEPTH, NCH, 128, 8, 512), np.float32)

    def put(l, c, mat, kb0=0):
        k = mat.shape[0] // 128
        out[l, c, :, kb0:kb0 + k, :mat.shape[1]] = mat.reshape(k, 128, mat.shape[1]).transpose(1, 0, 2)

    for l in range(DEPTH):
        W = w_in[l]
        put(l, C_Q, W[:, 0:512])
        k0, k1 = W[:, 512:576], W[:, 576:640]
        v0, v1 = W[:, 640:704], W[:, 704:768]
        put(l, C_KD, np.concatenate([k0, k0, k1, k1, v0, v0, v1, v1], axis=1))
        put(l, C_KV, W[:, 512:768])
        put(l, C_ZA, W[:, 768:1280])
        put(l, C_UB, W[:, 1280:1792])
        put(l, C_ZB, W[:, 1792:2304])
        put(l, C_UC, W[:, 2304:2816])
        put(l, C_VC, W[:, 2816:3328])
        put(l, C_ZC, W[:, 3328:3840])
        for hf in range(2):
            for b in range(3):
                c0 = 3840 + b * 1024 + hf * 512
                put(l, C_G + hf * 3 + b, W[:, c0:c0 + 512])
                put(l, C_R + hf * 3 + b, w_read[l, b][:, hf * 512:(hf + 1) * 512])
            put(l, C_O + hf, w_o[l][:, hf * 512:(hf + 1) * 512])
        put(l, C_GLU, glu_w[l])
    return out.reshape(DEPTH, NCH, 128, 4096)


def _layout_params(inp):
    f = np.float32
    lam_re, lam_im, log_dt = inp["ssm_lambda_re"], inp["ssm_lambda_im"], inp["ssm_log_dt"]
    colp = np.zeros((DEPTH, 128, 96), f)
    rowp = np.zeros((DEPTH, 3, 128, 2048), f)
    for l in range(DEPTH):
        colp[l, :, 0:32] = np.concatenate([lam_re[l].T, lam_re[l].T], 0)
        colp[l, :, 32:64] = np.concatenate([lam_im[l].T, lam_im[l].T], 0)
        colp[l, :, 64:96] = log_dt[l][None, :]
        rowp[l, 0] = lam_re[l].reshape(1, 2048)
        rowp[l, 1] = lam_im[l].reshape(1, 2048)
        rowp[l, 2] = np.repeat(log_dt[l], 64)[None, :]
    bpad = np.zeros((DEPTH, 2, 32, 128, 64), f)
    cpad = np.zeros((DEPTH, 32, 128, 128), f)
    for l in range(DEPTH):
        for g in range(32):
            gl = g % 8
            bpad[l, 0, g, gl * 16:(gl + 1) * 16, :] = inp["ssm_b_re"][l, g].T
            bpad[l, 1, g, gl * 16:(gl + 1) * 16, :] = inp["ssm_b_im"][l, g].T
            cpad[l, g, 0:64, gl * 16:(gl + 1) * 16] = inp["ssm_c_re"][l, g].T
            cpad[l, g, 64:128, gl * 16:(gl + 1) * 16] = inp["ssm_c_im"][l, g].T
    dsk = inp["ssm_d"].reshape(DEPTH, 4, 128).transpose(0, 2, 1).copy()
    glub = inp["glu_b"].reshape(DEPTH, 4, 128).transpose(0, 2, 1).copy()
    sgn = np.zeros((DEPTH, 2, 128, 512), f)
    sgn[:, 0] = inp["sgu_ln_g"][:, None, :]
    sgn[:, 1] = inp["sgu_ln_b"][:, None, :]
    sgw = inp["sgu_w"].transpose(0, 1, 3, 2).copy()
    sgb = np.broadcast_to(inp["sgu_b"].reshape(DEPTH, 1, 512), (DEPTH, 128, 512)).copy()
    lng = np.zeros((DEPTH, 2, 128, 1024), f)
    lng[:, 0] = inp["ln_g"][:, None, :]
    lng[:, 1] = inp["ln_b"][:, None, :]
    snk = np.zeros((DEPTH, 128, 2, 2, 2), f)
    for kv in range(2):
        for par in range(2):
            for jj in range(2):
                h = 4 * kv + 2 * jj + par
                snk[:, :, kv, par, jj] = inp["attn_sinks"][:, h][:, None]
    return dict(colp=colp, rowp=rowp, bpad=bpad, cpad=cpad, dsk=dsk, glub=glub, sgn=sgn, sgw=sgw, sgb=sgb, lng=lng,
                snk=snk.reshape(DEPTH, 128, 8))


def kernel(**inp):
    global _PROG
    inp = {k: np.asarray(v) for k, v in inp.items()}
    if _PROG is None:
        _PROG = build_program()
    nc = _PROG
    consts = _consts()
    wch = _layout_weights(inp["w_in"], inp["w_read"], inp["w_o"], inp["glu_w"])
    prm = _layout_params(inp)
    in_maps = []
    for c in range(8):
        m = dict(consts)
        m.update(prm)
        m["wch"] = wch
        m["xp"] = np.ascontiguousarray(inp["x_prompt"][c % 2])
        sl = slice(c * NS, (c + 1) * NS)
        m["xs"] = np.ascontiguousarray(inp["x_sample"][sl, 0, :])
        ckc = inp["cache_k_win"][:, sl]
        t = ckc.transpose(0, 4, 1, 3, 2).reshape(DEPTH, 64, 32, 128)
        m["ckT"] = np.ascontiguousarray(np.concatenate([t, t], axis=1))
        cvc = inp["cache_v_win"][:, sl]
        t = cvc.transpose(0, 2, 1, 3, 4).reshape(DEPTH, 128, 32, 64)
        m["cvd"] = np.ascontiguousarray(np.concatenate([t, t], axis=3))
        sr = inp["state_ssm_re"][:, sl].transpose(0, 3, 2, 1)
        si = inp["state_ssm_im"][:, sl].transpose(0, 3, 2, 1)
        m["h0"] = np.ascontiguousarray(np.concatenate([sr, si], axis=1))
        m["h0s"] = np.ascontiguousarray(np.concatenate([si, sr], axis=1))
        m["w00"] = np.ascontiguousarray(np.broadcast_to(inp["sgu_w"][:, None, :, 0, 0], (DEPTH, NS, 4)))
        m["b00"] = np.ascontiguousarray(np.broadcast_to(inp["sgu_b"][:, None, :, 0], (DEPTH, 128, 4)))
        es_ = np.zeros((DEPTH, 128, 4), np.float32)
        for blk in range(4):
            es_[:, 0:64, blk] = inp["attn_sinks"][:, 2 * blk][:, None]
            es_[:, 64:128, blk] = inp["attn_sinks"][:, 2 * blk + 1][:, None]
        m["esks"] = es_
        in_maps.append(m)
    res = run_bass_kernel_spmd(nc, in_maps, core_ids=list(range(8)))
    R = res.results
    y_prompt = np.stack([R[0]["yp"], R[1]["yp"]])
    kvw = np.stack([R[0]["kvw"], R[1]["kvw"]], axis=1)
    k_win = kvw[..., 0:128].reshape(DEPTH, 2, 128, 2, 64)
    v_win = kvw[..., 128:256].reshape(DEPTH, 2, 128, 2, 64)
    hs = np.stack([R[0]["hst"], R[1]["hst"]], axis=1)
    h_re = hs[:, :, 0:64, :].transpose(0, 1, 3, 2)
    h_im = hs[:, :, 64:128, :].transpose(0, 1, 3, 2)
    y_sample = np.concatenate([R[c]["ysm"] for c in range(8)], axis=0).reshape(128, 1, D)
    kvsm = np.concatenate([R[c]["kvs"] for c in range(8)], axis=1)
    k_s = kvsm[..., 0:128].reshape(DEPTH, 128, 1, 2, 64)
    v_s = kvsm[..., 128:256].reshape(DEPTH, 128, 1, 2, 64)
    hsm = np.concatenate([R[c]["hss"] for c in range(8)], axis=3)
    hs_re = hsm[:, 0:64].transpose(0, 3, 2, 1)
    hs_im = hsm[:, 64:128].transpose(0, 3, 2, 1)
    vn_s = np.concatenate([R[c]["vns"] for c in range(8)], axis=1).reshape(DEPTH, 128, 1, 512)
    f = np.float32
    outs = (y_prompt, y_sample, k_win, v_win, h_re, h_im, k_s, v_s, hs_re, hs_im, vn_s)
    return tuple(np.ascontiguousarray(o, dtype=f) for o in outs)
```
